# Optimizing a Trainium2 kernel written in Bass

```python
import jax, jax.numpy as jnp
from jax import lax
import numpy as np

D_MODEL = 1024
BATCH = 8
SEQ = 8192
DEPTH = 1

HEAD_DIM = 64
ROT_DIM = HEAD_DIM // 4
ROPE_THETA = 500000.0
DIL_GROUPS = ((128, 1), (512, 4), (2048, 16))
HEADS_PER_DIL_GROUP = 2
N_DIL_HEADS = HEADS_PER_DIL_GROUP * len(DIL_GROUPS)
DSA_Q_HEADS = 6
DSA_KV_HEADS = 2
DSA_TOPK_MAX = 256
IDX_HEADS = 8
IDX_DIM = 64
MEM_LEN = 256
MEM_HEADS = 4
N_BRANCHES = 3
D_FF = 4 * D_MODEL
Q_BLOCK = 128
EPS = 1e-6

DIL_W = N_DIL_HEADS * HEAD_DIM
DIL_OUT_W = HEADS_PER_DIL_GROUP * HEAD_DIM
DSA_Q_W = DSA_Q_HEADS * HEAD_DIM
DSA_KV_W = DSA_KV_HEADS * HEAD_DIM
IDX_Q_W = IDX_HEADS * IDX_DIM
MEM_Q_W = MEM_HEADS * HEAD_DIM
GATE_W = N_BRANCHES * D_MODEL
IN_SPLITS = (DIL_W, DIL_W, DIL_W, DSA_Q_W, DSA_KV_W, DSA_KV_W, IDX_Q_W, IDX_DIM, IDX_HEADS, MEM_Q_W, GATE_W)
IN_COLS = DIL_W * 3 + DSA_Q_W + DSA_KV_W * 2 + IDX_Q_W + IDX_DIM + IDX_HEADS + MEM_Q_W + GATE_W

kernel_name = "hybrid_gated_dilated_dsa_memory_block"


def rms_norm(x, g):
    xf = x.astype(jnp.float32)
    y = xf * lax.rsqrt(jnp.mean(xf * xf, axis=-1, keepdims=True) + EPS)
    return (y * g.astype(jnp.float32)).astype(x.dtype)


def rotary(x, pos):
    half = ROT_DIM // 2
    inv = jnp.power(jnp.float32(ROPE_THETA), -jnp.arange(half, dtype=jnp.float32) / half)
    ang = pos.astype(jnp.float32)[..., None] * inv
    cos = jnp.cos(ang)[:, :, None, :]
    sin = jnp.sin(ang)[:, :, None, :]
    xr = x[..., :ROT_DIM].astype(jnp.float32)
    x1, x2 = xr[..., :half], xr[..., half:]
    rot = jnp.concatenate([x1 * cos - x2 * sin, x2 * cos + x1 * sin], axis=-1).astype(x.dtype)
    return jnp.concatenate([rot, x[..., ROT_DIM:]], axis=-1)


def split_heads(t, n, dh=HEAD_DIM):
    return t.reshape(t.shape[:-1] + (n, dh))


def dilated_window_attention(q, k, v, window, dilation):
    b, s, h, dh = q.shape
    span = window // dilation
    m = s // dilation
    nblk = -(-m // span)
    mp = nblk * span

    def to_sub(t):
        t = t.reshape(b, m, dilation, h, dh).transpose(0, 2, 1, 3, 4)
        return jnp.pad(t, ((0, 0), (0, 0), (0, mp - m), (0, 0), (0, 0)))

    def band(t):
        tp = jnp.pad(to_sub(t), ((0, 0), (0, 0), (span, 0), (0, 0), (0, 0)))
        tp = tp.reshape(b, dilation, nblk + 1, span, h, dh)
        return jnp.concatenate([tp[:, :, :-1], tp[:, :, 1:]], axis=3)

    qs = to_sub(q).reshape(b, dilation, nblk, span, h, dh)
    kb, vb = band(k), band(v)
    sc = jnp.einsum('brnqhd,brnkhd->brnhqk', qs, kb).astype(jnp.float32) * (dh ** -0.5)
    qi = jnp.arange(span)[:, None]
    kj = jnp.arange(2 * span)[None, :]
    dist = span + qi - kj
    in_band = (dist >= 0) & (dist <= span)
    blk = jnp.arange(nblk)[:, None, None]
    mask = in_band[None] & ((blk > 0) | (kj >= span)[None])
    sc = jnp.where(mask[None, None, :, None], sc, -jnp.inf)
    lse = jax.nn.logsumexp(sc, axis=-1)
    p = jnp.exp(sc - lse[..., None]).astype(v.dtype)
    o = jnp.einsum('brnhqk,brnkhd->brnqhd', p, vb)
    o = o.reshape(b, dilation, mp, h, dh)[:, :, :m].transpose(0, 2, 1, 3, 4).reshape(b, s, h, dh)
    lse = lse.transpose(0, 1, 2, 4, 3).reshape(b, dilation, mp, h)[:, :, :m]
    lse = lse.transpose(0, 2, 1, 3).reshape(b, s, h)
    return o, lse


def dsa_attention(q, k, v, qi, ki, wi):
    b, s, hq, dh = q.shape
    hkv = k.shape[2]
    grp = hq // hkv
    topk = min(DSA_TOPK_MAX, s // 4)
    nblk = s // Q_BLOCK
    key_pos = jnp.arange(s)
    gather = jax.vmap(lambda a, i: a[i])

    def blocks(t):
        return t.reshape((b, nblk, Q_BLOCK) + t.shape[2:]).swapaxes(0, 1)

    def one_block(args):
        n, qb, qib, wib = args
        t = n * Q_BLOCK + jnp.arange(Q_BLOCK)
        isc = jax.nn.relu(jnp.einsum('bqhd,bsd->bqhs', qib, ki))
        iscore = jnp.einsum('bqhs,bqh->bqs', isc, wib).astype(jnp.float32)
        causal = key_pos[None, :] <= t[:, None]
        iscore = jnp.where(causal[None], iscore, -jnp.inf)
        _, sel = lax.top_k(iscore, topk)
        valid = sel <= t[None, :, None]
        ks = gather(k, sel)
        vs = gather(v, sel)
        qg = qb.reshape(b, Q_BLOCK, hkv, grp, dh)
        sc = jnp.einsum('bqcgd,bqkcd->bqcgk', qg, ks).astype(jnp.float32) * (dh ** -0.5)
        sc = jnp.where(valid[:, :, None, None, :], sc, -jnp.inf)
        p = jax.nn.softmax(sc, axis=-1).astype(v.dtype)
        o = jnp.einsum('bqcgk,bqkcd->bqcgd', p, vs)
        return o.reshape(b, Q_BLOCK, hq * dh)

    out = lax.map(one_block, (jnp.arange(nblk), blocks(q), blocks(qi), blocks(wi)))
    return out.swapaxes(0, 1).reshape(b, s, hq * dh)


def memory_attention(qc, mem, g_mem, w_mem_kv, g_qc, g_kc):
    b, s = qc.shape[0], qc.shape[1]
    q = rms_norm(split_heads(qc, MEM_HEADS), g_qc)
    kv = (rms_norm(mem, g_mem) @ w_mem_kv).reshape(b, mem.shape[1], 2, MEM_HEADS, HEAD_DIM)
    km = rms_norm(kv[:, :, 0], g_kc)
    vm = kv[:, :, 1]
    sc = jnp.einsum('bshd,bmhd->bhsm', q, km).astype(jnp.float32) * (HEAD_DIM ** -0.5)
    p = jax.nn.softmax(sc, axis=-1).astype(vm.dtype)
    return jnp.einsum('bhsm,bmhd->bshd', p, vm).reshape(b, s, MEM_Q_W)


def hybrid_layer(x, mem, positions, g_mix, g_mem, w_in, g_qa, g_ka, g_qb, g_kb, g_qc, g_kc,
                 w_mem_kv, w_a, w_b, w_c, w_o, g_mlp, w_1, w_2):
    b, s, d = x.shape
    h = rms_norm(x, g_mix)
    proj = h @ w_in
    offs = []
    acc = 0
    for w in IN_SPLITS[:-1]:
        acc += w
        offs.append(acc)
    qa, ka, va, qb, kb, vb, qi, ki, wi, qc, gl = jnp.split(proj, offs, axis=-1)

    qa = rotary(rms_norm(split_heads(qa, N_DIL_HEADS), g_qa), positions)
    ka = rotary(rms_norm(split_heads(ka, N_DIL_HEADS), g_ka), positions)
    va = split_heads(va, N_DIL_HEADS)
    n_g = len(DIL_GROUPS)
    qa = qa.reshape(b, s, n_g, HEADS_PER_DIL_GROUP, HEAD_DIM)
    ka = ka.reshape(b, s, n_g, HEADS_PER_DIL_GROUP, HEAD_DIM)
    va = va.reshape(b, s, n_g, HEADS_PER_DIL_GROUP, HEAD_DIM)
    outs, lses = [], []
    for gi, (win, dil) in enumerate(DIL_GROUPS):
        o_g, l_g = dilated_window_attention(qa[:, :, gi], ka[:, :, gi], va[:, :, gi], win, dil)
        outs.append(o_g)
        lses.append(l_g)
    o_a = jnp.stack(outs, axis=2)
    alpha = jax.nn.softmax(jnp.stack(lses, axis=2), axis=2)
    y_a = (alpha[..., None].astype(o_a.dtype) * o_a).sum(axis=2).reshape(b, s, DIL_OUT_W) @ w_a

    qb = rotary(rms_norm(split_heads(qb, DSA_Q_HEADS), g_qb), positions)
    kb = rotary(rms_norm(split_heads(kb, DSA_KV_HEADS), g_kb), positions)
    vb = split_heads(vb, DSA_KV_HEADS)
    qi = rotary(split_heads(qi, IDX_HEADS, IDX_DIM), positions)
    ki = rotary(ki[:, :, None, :], positions)[:, :, 0]
    wi = wi * ((IDX_HEADS ** -0.5) * (IDX_DIM ** -0.5))
    y_b = dsa_attention(qb, kb, vb, qi, ki, wi) @ w_b

    y_c = memory_attention(qc, mem, g_mem, w_mem_kv, g_qc, g_kc) @ w_c

    gates = jax.nn.sigmoid(gl.astype(jnp.float32)).astype(x.dtype).reshape(b, s, N_BRANCHES, d)
    merged = gates[:, :, 0] * y_a + gates[:, :, 1] * y_b + gates[:, :, 2] * y_c
    x = x + merged @ w_o

    u = rms_norm(x, g_mlp) @ w_1
    return x + jnp.square(jax.nn.relu(u)) @ w_2


def setup_inputs(seed: int = 0) -> dict:
    key = jax.random.key(seed)
    ks = jax.random.split(key, 24)
    f32 = jnp.float32

    def nrm(k, shape, fan_in):
        return jax.random.normal(k, shape, f32) * (fan_in ** -0.5)

    def gain(k, shape):
        return 1.0 + 0.02 * jax.random.normal(k, shape, f32)

    x = jax.random.normal(ks[0], (BATCH, SEQ, D_MODEL), f32)
    mem = jax.random.normal(ks[1], (BATCH, MEM_LEN, D_MODEL), f32)
    offset = jax.random.randint(ks[2], (BATCH, 1), 0, 4096, dtype=jnp.int32)
    positions = offset + jnp.arange(SEQ, dtype=jnp.int32)[None, :]
    L = DEPTH
    return {
        "x": x,
        "mem": mem,
        "positions": positions,
        "g_mix": gain(ks[3], (L, D_MODEL)),
        "g_mem": gain(ks[4], (L, D_MODEL)),
        "w_in": nrm(ks[5], (L, D_MODEL, IN_COLS), D_MODEL),
        "g_qa": gain(ks[6], (L, HEAD_DIM)),
        "g_ka": gain(ks[7], (L, HEAD_DIM)),
        "g_qb": gain(ks[8], (L, HEAD_DIM)),
        "g_kb": gain(ks[9], (L, HEAD_DIM)),
        "g_qc": gain(ks[10], (L, HEAD_DIM)),
        "g_kc": gain(ks[11], (L, HEAD_DIM)),
        "w_mem_kv": nrm(ks[12], (L, D_MODEL, 2 * MEM_HEADS * HEAD_DIM), D_MODEL),
        "w_a": nrm(ks[13], (L, DIL_OUT_W, D_MODEL), DIL_OUT_W),
        "w_b": nrm(ks[14], (L, DSA_Q_W, D_MODEL), DSA_Q_W),
        "w_c": nrm(ks[15], (L, MEM_Q_W, D_MODEL), MEM_Q_W),
        "w_o": nrm(ks[16], (L, D_MODEL, D_MODEL), D_MODEL),
        "g_mlp": gain(ks[17], (L, D_MODEL)),
        "w_1": nrm(ks[18], (L, D_MODEL, D_FF), D_MODEL),
        "w_2": nrm(ks[19], (L, D_FF, D_MODEL), D_FF),
    }


def reference(x, mem, positions, g_mix, g_mem, w_in, g_qa, g_ka, g_qb, g_kb, g_qc, g_kc,
              w_mem_kv, w_a, w_b, w_c, w_o, g_mlp, w_1, w_2):
    for i in range(DEPTH):
        x = hybrid_layer(x, mem, positions, g_mix[i], g_mem[i], w_in[i], g_qa[i], g_ka[i],
                         g_qb[i], g_kb[i], g_qc[i], g_kc[i], w_mem_kv[i], w_a[i], w_b[i],
                         w_c[i], w_o[i], g_mlp[i], w_1[i], w_2[i])
    return x
```

```python
import bisect
import os
import numpy as np
import concourse.bass as bass
import concourse.mybir as mybir
from concourse.bass_utils import run_bass_kernel_spmd

F32 = mybir.dt.float32
BF16 = mybir.dt.bfloat16
I32 = mybir.dt.int32
ALU = mybir.AluOpType
AF = mybir.ActivationFunctionType
AX = mybir.AxisListType

ENGS = ("pe", "act", "dve", "pool", "sp")

S = 8192
D = 1024
NT = S // 128
EPS = 1e-6
NEG = -30000.0
TOPK = 256
NBIS = 16


class Buf:
    __slots__ = ("name", "w", "r", "excl")

    def __init__(self, name="", excl=False):
        self.name = name
        self.w = None
        self.r = []
        self.excl = excl


class Prog:
    def __init__(self, nc):
        self.nc = nc
        self.ins = []
        self.by_eng = {e: [] for e in ENGS}
        self.groups = {}

    def _deps(self, R, W):
        deps = set()
        for b in R:
            if b.w is not None:
                deps.add(b.w)
        for b in W:
            if b.w is not None:
                deps.add(b.w)
            deps.update(b.r)
        return deps

    def _commit(self, iid, R, W):
        for b in R:
            b.r.append(iid)
        for b in W:
            b.w = iid
            b.r = []

    def I(self, eng, fn, R=(), W=()):
        iid = len(self.ins)
        W = list(W) + [b for b in R if b.excl and b not in W]
        deps = self._deps(R, W)
        raw = set(b.w for b in R if b.w is not None)
        self.ins.append(dict(eng=eng, fn=fn, deps=deps, raw=raw, grp=None))
        self.by_eng[eng].append(iid)
        self._commit(iid, R, W)
        return iid

    def dma(self, eng, out, in_, R=(), W=(), grp="g", **kw):
        iid = len(self.ins)
        deps = self._deps(R, W)
        raw = set(b.w for b in R if b.w is not None)
        self.groups.setdefault(grp, []).append(iid)
        self.ins.append(dict(eng=eng, fn=(lambda e, o=out, i=in_, k=kw: e.dma_start(out=o, in_=i, **k)),
                             deps=deps, raw=raw, grp=grp))
        self.by_eng[eng].append(iid)
        self._commit(iid, R, W)
        return iid

    def barrier(self):
        alld = set()
        for e in ENGS:
            for k in reversed(self.by_eng[e]):
                if self.ins[k]["fn"] is not None:
                    alld.add(k)
                    break
        for g, l in self.groups.items():
            if l:
                alld.add(l[-1])
        for e in ENGS:
            iid = len(self.ins)
            self.ins.append(dict(eng=e, fn=None, deps=set(alld), raw=set(alld), grp=None))
            self.by_eng[e].append(iid)

    def emit(self):
        nc = self.nc
        ins = self.ins
        n = len(ins)
        is_target = [False] * n
        for k in range(n):
            for d in ins[k]["deps"]:
                is_target[d] = True
        ms_val = [0] * n
        cnt = {e: 0 for e in ENGS}
        for k in range(n):
            it = ins[k]
            if it["grp"] is None and it["fn"] is not None and is_target[k]:
                cnt[it["eng"]] += 1
                ms_val[k] = cnt[it["eng"]]
        self.esem = {e: nc.alloc_semaphore("sem_" + e) for e in ENGS}
        self.gsem = {g: nc.alloc_semaphore("dsem_" + g) for g in self.groups}
        self.nwaits = 0
        prog = self

        def run_engine(ename, eobj):
            seen = {}
            for k in prog.by_eng[ename]:
                it = ins[k]
                need = {}
                for d in it["deps"]:
                    dd = ins[d]
                    if dd["grp"] is not None:
                        g = dd["grp"]
                        c = bisect.bisect_left(prog.groups[g], k)
                        key = ("g", g)
                        val = 16 * c
                    else:
                        if dd["fn"] is None:
                            continue
                        if dd["eng"] == ename and it["grp"] is None and it["fn"] is not None:
                            if ename == "pe":
                                continue
                            if d not in it["raw"]:
                                continue
                        key = ("e", dd["eng"])
                        val = ms_val[d]
                    if val > need.get(key, 0):
                        need[key] = val
                for key, val in need.items():
                    if seen.get(key, 0) >= val:
                        continue
                    seen[key] = val
                    sem = prog.esem[key[1]] if key[0] == "e" else prog.gsem[key[1]]
                    eobj.wait_ge(sem, val)
                    prog.nwaits += 1
                if it["fn"] is None:
                    continue
                bi = it["fn"](eobj)
                if it["grp"] is not None:
                    bi.then_inc(prog.gsem[it["grp"]], 16)
                elif is_target[k]:
                    bi.then_inc(prog.esem[ename], 1)

        with nc.Block() as block:
            @block.tensor
            def _(e):
                run_engine("pe", e)

            @block.scalar
            def _(e):
                run_engine("act", e)

            @block.vector
            def _(e):
                run_engine("dve", e)

            @block.gpsimd
            def _(e):
                run_engine("pool", e)

            @block.sync
            def _(e):
                run_engine("sp", e)


class Arena:
    def __init__(self, nc, nbytes):
        self.t = nc.alloc_sbuf_tensor("arena", [128, nbytes // 2], BF16)
        self.n = nbytes // 2
        self.base = 0
        self.p = 0

    def mark(self):
        self.base = self.p

    def reset(self):
        self.p = self.base

    def get(self, shape, dtype):
        ne = int(np.prod(shape))
        w = ne * (2 if dtype in (F32, I32) else 1)
        w = (w + 31) // 32 * 32
        assert self.p + w <= self.n, ("SBUF arena overflow", self.p, w, self.n)
        ap = self.t[:, self.p:self.p + w]
        self.p += w
        if dtype != BF16:
            ap = ap.bitcast(dtype)
        ap = ap[:, 0:ne]
        if len(shape) == 2:
            ap = ap.rearrange("p (a b) -> p a b", a=shape[0])
        elif len(shape) == 3:
            ap = ap.rearrange("p (a b c) -> p a b c", a=shape[0], b=shape[1])
        return ap


IN_COLS = 5704
N_TM = 2560
N_SM = 72
N_G = 3072


def _win_perm():
    o = {}
    acc = 0
    for nme, w in (("qa", 384), ("ka", 384), ("va", 384), ("qb", 384), ("kb", 128), ("vb", 128),
                   ("qi", 512), ("ki", 64), ("wi", 8), ("qc", 256), ("gl", 3072)):
        o[nme] = np.arange(acc, acc + w)
        acc += w
    qb = o["qb"].reshape(2, 3, 64).transpose(1, 0, 2).reshape(-1)
    return np.concatenate([o["qa"], o["ka"], qb, o["kb"], o["qi"], o["qc"], o["va"], o["vb"],
                           o["ki"], o["wi"], o["gl"]])


def build_program(debug=(), phases=(1, 2, 3, 4, 5), ntiles=NT):
    nc = bass.Bass("TRN2", target_bir_lowering=False)
    P = Prog(nc)
    I_ = P.I

    def din(name, shape, dt=F32):
        return nc.dram_tensor(name, shape, dt, kind="ExternalInput").ap()

    def dscr(name, shape, dt):
        return nc.dram_tensor(name, shape, dt, kind=("ExternalOutput" if name in debug else "Internal")).ap()

    x_d = din("x", [S, D])
    mem_d = din("mem", [256, D])
    pos_d = din("pos", [128, NT], I32)
    win_d = din("w_in", [128, 8, IN_COLS])
    gmix_d = din("g_mix", [128, 8])
    gmem_d = din("g_mem", [128, 8])
    gmlp_d = din("g_mlp", [128, 8])
    gains_d = din("gains", [1, 384])
    wkv_d = din("w_mem_kv", [128, 8, 512])
    wa_d = din("w_a", [128, 1, D])
    wb_d = din("w_b", [128, 3, D])
    wc_d = din("w_c", [128, 2, D])
    wo_d = din("w_o", [128, 8, D])
    w1_d = din("w_1", [128, 8, 4096])
    w2_d = din("w_2", [128, 32, D])
    out_d = nc.dram_tensor("out", [S, D], F32, kind="ExternalOutput").ap()

    QKVA = dscr("s_qkva", [S, 1152], BF16)
    QBT = dscr("s_qbt", [128, NT, 384], BF16)
    KBT = dscr("s_kbt", [128, S], BF16)
    VB = dscr("s_vb", [S, 136], BF16)
    QIT = dscr("s_qit", [128, NT, 512], BF16)
    KIT = dscr("s_kit", [64, S], BF16)
    QCT = dscr("s_qct", [128, NT, 256], BF16)
    GT = dscr("s_gt", [24, 128, S], BF16)
    OA = dscr("s_oa", [3, S, 130], F32)
    OBT = dscr("s_obt", [3, 128, S], BF16)
    X1 = dscr("s_x1", [S, D], F32)

    A = Arena(nc, 206 * 1024)
    psum = nc.alloc_psum_tensor("psum", [128, 8, 512], F32)

    def bank(i):
        return psum[:, i, :]

    def bank_bf(i):
        return psum[:, i, :].bitcast(BF16)

    pb = [Buf("bank%d" % i, excl=True) for i in range(8)]

    ident = A.get([128], BF16)
    identf = A.get([128], F32)
    mcur = A.get([128], BF16)
    mprev = A.get([128], BF16)
    i3 = A.get([384], BF16)
    gains = A.get([384], F32)
    cs = A.get([NT, 16], F32)
    b_const = Buf("const")
    I_("pool", lambda e: e.memset(identf, 0.0), W=[b_const])
    I_("pool", lambda e: e.affine_select(out=identf, in_=identf, pattern=[[-1, 128]], compare_op=ALU.not_equal,
                                         fill=1.0, base=0, channel_multiplier=1), R=[b_const], W=[b_const])
    I_("pool", lambda e: e.tensor_copy(out=ident, in_=identf), R=[b_const], W=[b_const])
    for g in range(3):
        I_("pool", lambda e, g=g: e.tensor_copy(out=i3[:, g * 128:(g + 1) * 128], in_=identf), R=[b_const], W=[b_const])
    I_("pool", lambda e: e.memset(mcur, 0.0), W=[b_const], R=[b_const])
    I_("pool", lambda e: e.affine_select(out=mcur, in_=mcur, pattern=[[1, 128]], compare_op=ALU.is_ge,
                                         fill=NEG, base=0, channel_multiplier=-1), R=[b_const], W=[b_const])
    I_("pool", lambda e: e.memset(mprev, 0.0), W=[b_const], R=[b_const])
    I_("pool", lambda e: e.affine_select(out=mprev, in_=mprev, pattern=[[-1, 128]], compare_op=ALU.is_ge,
                                         fill=NEG, base=0, channel_multiplier=1), R=[b_const], W=[b_const])
    P.dma("sp", gains, gains_d.partition_broadcast(128), W=[b_const], grp="c0")
    A.mark()
    posi = A.get([NT], I32)
    posf = A.get([NT], F32)
    inv = A.get([8], F32)
    ang = A.get([NT, 8], F32)
    ang2 = A.get([NT, 16], F32)
    P.dma("sp", posi, pos_d, W=[b_const], grp="c0")
    I_("dve", lambda e: e.tensor_copy(out=posf, in_=posi), R=[b_const], W=[b_const])
    for i in range(8):
        I_("pool", lambda e, i=i: e.memset(inv[:, i:i + 1], float(np.float32(500000.0) ** np.float32(-i / 8.0))),
           R=[b_const], W=[b_const])
    I_("dve", lambda e: e.tensor_tensor(out=ang, in0=posf.unsqueeze(2).to_broadcast([128, NT, 8]),
                                        in1=inv.unsqueeze(1).to_broadcast([128, NT, 8]), op=ALU.mult),
       R=[b_const], W=[b_const])
    TWO_PI = float(2 * np.pi)
    I_("dve", lambda e: e.tensor_scalar(out=ang2[:, :, 0:8], in0=ang, scalar1=float(0.5 * np.pi), scalar2=None,
                                        op0=ALU.add), R=[b_const], W=[b_const])
    I_("dve", lambda e: e.tensor_copy(out=ang2[:, :, 8:16], in_=ang), R=[b_const], W=[b_const])
    angk = A.get([NT, 16], F32)
    angi = A.get([NT, 16], I32)
    I_("dve", lambda e: e.tensor_scalar(out=angk, in0=ang2, scalar1=float(1.0 / (2 * np.pi)), scalar2=None,
                                        op0=ALU.mult), R=[b_const], W=[b_const])
    I_("dve", lambda e: e.tensor_copy(out=angi, in_=angk), R=[b_const], W=[b_const])
    I_("dve", lambda e: e.tensor_copy(out=angk, in_=angi), R=[b_const], W=[b_const])
    I_("dve", lambda e: e.scalar_tensor_tensor(out=ang2, in0=angk, scalar=-TWO_PI, in1=ang2, op0=ALU.mult,
                                               op1=ALU.add), R=[b_const], W=[b_const])
    I_("dve", lambda e: e.tensor_scalar(out=angk, in0=ang2, scalar1=float(np.pi), scalar2=TWO_PI, op0=ALU.is_gt,
                                        op1=ALU.mult), R=[b_const], W=[b_const])
    I_("dve", lambda e: e.tensor_tensor(out=ang2, in0=ang2, in1=angk, op=ALU.subtract), R=[b_const], W=[b_const])
    I_("dve", lambda e: e.tensor_scalar(out=angk, in0=ang2, scalar1=float(-np.pi), scalar2=TWO_PI, op0=ALU.is_lt,
                                        op1=ALU.mult), R=[b_const], W=[b_const])
    I_("dve", lambda e: e.tensor_tensor(out=ang2, in0=ang2, in1=angk, op=ALU.add), R=[b_const], W=[b_const])
    I_("dve", lambda e: e.tensor_scalar(out=ang2, in0=ang2, scalar1=3.141592, scalar2=-3.141592, op0=ALU.min,
                                        op1=ALU.max), R=[b_const], W=[b_const])
    I_("act", lambda e: e.activation(out=cs, in_=ang2, func=AF.Sin), R=[b_const], W=[b_const])
    P.barrier()
    A.reset()

    def rstd_from_ss(ss, rs, n, bufs, width):
        I_("dve", lambda e: e.tensor_scalar(out=rs, in0=ss, scalar1=1.0 / n, scalar2=EPS, op0=ALU.mult, op1=ALU.add),
           R=bufs, W=bufs)
        I_("act", lambda e: e.activation(out=rs, in_=rs, func=AF.Sqrt), R=bufs, W=bufs)
        I_("dve", lambda e: e.reciprocal(out=rs, in_=rs), R=bufs, W=bufs)

    def make_stage(chunk, tag):
        return ([A.get([chunk], F32) for _ in range(2)], [Buf(tag + "stg%d" % i) for i in range(2)], tag)

    def load_weight_bf16(dst, src_d, nk, ncols, gain_d, tag, chunk=2048, stage=None):
        if stage is None:
            stage = make_stage(chunk, tag)
        stg, sb, stag = stage
        bw = Buf(tag)
        gt = None
        if gain_d is not None:
            gt = A.get([8], F32)
            P.dma("sp", gt, gain_d, W=[bw], grp=tag + "g")
        it = 0
        for k in range(nk):
            for c0 in range(0, ncols, chunk):
                cw = min(chunk, ncols - c0)
                s = it % 2
                P.dma("sp", stg[s][:, 0:cw], src_d[:, k, c0:c0 + cw], W=[sb[s]], grp=stag + "s%d" % s)
                eng = ("dve", "pool")[it % 2]
                if gt is not None:
                    I_(eng, lambda e, s=s, k=k, c0=c0, cw=cw: e.tensor_scalar(
                        out=dst[:, k, c0:c0 + cw], in0=stg[s][:, 0:cw], scalar1=gt[:, k:k + 1], scalar2=None,
                        op0=ALU.mult), R=[sb[s], bw], W=[bw])
                else:
                    I_(eng, lambda e, s=s, k=k, c0=c0, cw=cw: e.tensor_copy(out=dst[:, k, c0:c0 + cw],
                                                                          in_=stg[s][:, 0:cw]), R=[sb[s]], W=[bw])
                it += 1
        return bw

    def norm_rows_to_hT(src_tile_d, xt, xb, hT_dst, bx, bh, bxs, grp, pbank, bpb):
        P.dma("sp", xt, src_tile_d, W=[bx], grp=grp)
        st = bxs["st"]
        I_("pool", lambda e: e.memset(st[:, 0:1], 0.0), W=[bxs["b"]])
        I_("act", lambda e: e.activation(out=bxs["sq"], in_=xt, func=AF.Square, accum_out=st[:, 0:1]),
           R=[bx, bxs["b"]], W=[bxs["b"]])
        rstd_from_ss(st[:, 0:1], st[:, 1:2], D, [bxs["b"]], 1)
        I_("dve", lambda e: e.tensor_scalar(out=xb, in0=xt, scalar1=st[:, 1:2], scalar2=None, op0=ALU.mult),
           R=[bx, bxs["b"]], W=[bh])
        pT = bank_bf(pbank)
        for k in range(8):
            I_("pe", lambda e, k=k: e.transpose(out=pT[:, k * 128:(k + 1) * 128], in_=xb[:, k * 128:(k + 1) * 128],
                                                identity=ident), R=[bh, b_const], W=[bpb])

    if 1 in phases:
        W = A.get([8, IN_COLS], BF16)
        bW = load_weight_bf16(W, win_d, 8, IN_COLS, gmix_d, "win", chunk=1426)
        NB = 2
        xt = [A.get([D], F32) for _ in range(NB)]
        bx = [Buf("xt%d" % i) for i in range(NB)]
        xb = [A.get([D], BF16) for _ in range(NB)]
        bh = [Buf("xb%d" % i) for i in range(NB)]
        sq = A.get([D], F32)
        st = A.get([8], F32)
        bxs = dict(sq=sq, st=st, b=Buf("xstat"))
        hTg = [A.get([8, 512], BF16) for _ in range(2)]
        bhT = [Buf("hTg%d" % i) for i in range(2)]
        stg = A.get([2632], F32)
        bstg = Buf("stg")
        tmp = A.get([1536], F32)
        btmp = Buf("tmp")
        ss = A.get([24], F32)
        rs = A.get([24], F32)
        bss = Buf("ss")
        gainrow = A.get([1536], F32)
        bgr = Buf("gainrow")
        gsrc = [0] * 6 + [1] * 6 + [2] * 6 + [3] * 2 + [4] * 4
        for hh, gi in enumerate(gsrc):
            I_("pool", lambda e, hh=hh, gi=gi: e.tensor_copy(out=gainrow[:, hh * 64:(hh + 1) * 64],
                                                            in_=gains[:, gi * 64:(gi + 1) * 64]),
               R=[b_const], W=[bgr])
        rt = A.get([4, 29, 8], F32)
        brt = Buf("rt")
        oA = [A.get([1152], BF16) for _ in range(2)]
        boA = [Buf("oA%d" % i) for i in range(2)]
        oT = [A.get([1344], BF16) for _ in range(2)]
        boT = [Buf("oT%d" % i) for i in range(2)]
        oV = [A.get([136], BF16) for _ in range(2)]
        boV = [Buf("oV%d" % i) for i in range(2)]
        tT = [A.get([1408], BF16) for _ in range(2)]
        btT = [Buf("tT%d" % i) for i in range(2)]
        gsb = [A.get([512], BF16) for _ in range(3)]
        bgsb = [Buf("gsb%d" % i) for i in range(3)]
        WI_SCALE = float((8 ** -0.5) * (64 ** -0.5))
        gcount = 0
        for grp_i in range(ntiles // 4):
            hs = grp_i % 2
            for tt in range(4):
                n = grp_i * 4 + tt
                s = n % NB
                o = n % 2
                norm_rows_to_hT(x_d[n * 128:(n + 1) * 128, :], xt[s], xb[s], None, bx[s], bh[s], bxs, "x%d" % s, 5, pb[5])
                I_("act", lambda e, hs=hs, tt=tt: e.copy(
                    out=hTg[hs][:, :, tt * 128:(tt + 1) * 128],
                    in_=bank_bf(5).rearrange("p (k c) -> p k c", k=8)), R=[pb[5]], W=[bhT[hs]])
                for bnk in range(5):
                    for k in range(8):
                        I_("pe", lambda e, bnk=bnk, k=k, hs=hs, tt=tt: e.matmul(
                            bank(bnk), lhsT=hTg[hs][:, k, tt * 128:(tt + 1) * 128],
                            rhs=W[:, k, bnk * 512:(bnk + 1) * 512], start=(k == 0), stop=(k == 7)),
                           R=[bhT[hs], bW], W=[pb[bnk]])
                for k in range(8):
                    I_("pe", lambda e, k=k, hs=hs, tt=tt: e.matmul(
                        bank(7)[:, 0:N_SM], lhsT=hTg[hs][:, k, tt * 128:(tt + 1) * 128],
                        rhs=W[:, k, N_TM:N_TM + N_SM], start=(k == 0), stop=(k == 7)),
                       R=[bhT[hs], bW], W=[pb[7]])
                I_("act", lambda e: e.copy(out=stg[:, 0:1792], in_=psum[:, 0:4, :].rearrange("p a b -> p (a b)")[:, 0:1792]),
                   R=[pb[0], pb[1], pb[2], pb[3]], W=[bstg])
                I_("dve", lambda e: e.tensor_copy(out=stg[:, 1856:2624],
                                                  in_=psum[:, 3:5, :].rearrange("p a b -> p (a b)")[:, 256:1024]),
                   R=[pb[3], pb[4]], W=[bstg])
                I_("dve", lambda e: e.tensor_copy(out=stg[:, 1792:1856], in_=bank(7)[:, 0:64]), R=[pb[7]], W=[bstg])
                I_("dve", lambda e: e.tensor_scalar(out=stg[:, 2624:2632], in0=bank(7)[:, 64:72], scalar1=WI_SCALE,
                                                    scalar2=None, op0=ALU.mult), R=[pb[7]], W=[bstg])
                I_("pool", lambda e: e.tensor_tensor(out=tmp[:, 0:1280], in0=stg[:, 0:1280], in1=stg[:, 0:1280],
                                                     op=ALU.mult), R=[bstg], W=[btmp])
                I_("pool", lambda e: e.tensor_tensor(out=tmp[:, 1280:1536], in0=stg[:, 1856:2112],
                                                     in1=stg[:, 1856:2112], op=ALU.mult), R=[bstg], W=[btmp])
                I_("dve", lambda e: e.tensor_reduce(out=ss, in_=tmp.rearrange("p (h d) -> p h d", d=64), axis=AX.X,
                                                    op=ALU.add), R=[btmp], W=[bss])
                rstd_from_ss(ss, rs, 64, [bss], 24)
                I_("dve", lambda e: e.tensor_tensor(
                    out=tmp.rearrange("p (h d) -> p h d", d=64), in0=gainrow.rearrange("p (h d) -> p h d", d=64),
                    in1=rs.unsqueeze(2).to_broadcast([128, 24, 64]), op=ALU.mult), R=[bss, bgr, btmp], W=[btmp])
                I_("pool", lambda e: e.tensor_tensor(out=stg[:, 0:1280], in0=stg[:, 0:1280], in1=tmp[:, 0:1280],
                                                     op=ALU.mult), R=[bstg, btmp], W=[bstg])
                I_("pool", lambda e: e.tensor_tensor(out=stg[:, 1856:2112], in0=stg[:, 1856:2112],
                                                     in1=tmp[:, 1280:1536], op=ALU.mult), R=[bstg, btmp], W=[bstg])
                v = stg[:, 0:1856].rearrange("p (h d) -> p h d", d=64)
                x1 = v[:, :, 0:8]
                x2 = v[:, :, 8:16]
                cosb = cs[:, n, 0:8].unsqueeze(1).to_broadcast([128, 29, 8])
                sinb = cs[:, n, 8:16].unsqueeze(1).to_broadcast([128, 29, 8])
                I_("dve", lambda e, x1=x1, cosb=cosb: e.tensor_tensor(out=rt[:, 0], in0=x1, in1=cosb, op=ALU.mult),
                   R=[bstg, b_const], W=[brt])
                I_("dve", lambda e, x2=x2, sinb=sinb: e.tensor_tensor(out=rt[:, 1], in0=x2, in1=sinb, op=ALU.mult),
                   R=[bstg, b_const], W=[brt])
                I_("pool", lambda e, x2=x2, cosb=cosb: e.tensor_tensor(out=rt[:, 2], in0=x2, in1=cosb, op=ALU.mult),
                   R=[bstg, b_const], W=[brt])
                I_("pool", lambda e, x1=x1, sinb=sinb: e.tensor_tensor(out=rt[:, 3], in0=x1, in1=sinb, op=ALU.mult),
                   R=[bstg, b_const], W=[brt])
                I_("dve", lambda e, x1=x1: e.tensor_tensor(out=x1, in0=rt[:, 0], in1=rt[:, 1], op=ALU.subtract),
                   R=[brt], W=[bstg])
                I_("dve", lambda e, x2=x2: e.tensor_tensor(out=x2, in0=rt[:, 2], in1=rt[:, 3], op=ALU.add),
                   R=[brt], W=[bstg])
                I_("act", lambda e, o=o: e.copy(out=oA[o][:, 0:768], in_=stg[:, 0:768]), R=[bstg], W=[boA[o]])
                I_("act", lambda e, o=o: e.copy(out=oA[o][:, 768:1152], in_=stg[:, 2112:2496]), R=[bstg], W=[boA[o]])
                I_("pool", lambda e, o=o: e.tensor_copy(out=oT[o], in_=stg[:, 768:2112]), R=[bstg], W=[boT[o]])
                I_("pool", lambda e, o=o: e.tensor_copy(out=oV[o], in_=stg[:, 2496:2632]), R=[bstg], W=[boV[o]])
                P.dma("pool", QKVA[n * 128:(n + 1) * 128, :], oA[o], R=[boA[o]], grp="oA%d" % o)
                P.dma("pool", VB[n * 128:(n + 1) * 128, :], oV[o], R=[boV[o]], grp="oV%d" % o)
                pT6 = bank_bf(6)
                for c in range(8):
                    I_("pe", lambda e, c=c, o=o: e.transpose(out=pT6[:, c * 128:(c + 1) * 128],
                                                             in_=oT[o][:, c * 128:(c + 1) * 128], identity=ident),
                       R=[boT[o], b_const], W=[pb[6]])
                I_("dve", lambda e, o=o: e.tensor_copy(out=tT[o][:, 0:1024], in_=pT6), R=[pb[6]], W=[btT[o]])
                I_("pe", lambda e, o=o: e.transpose(out=pT6[0:64, 0:128], in_=oT[o][:, 1024:1088], identity=ident),
                   R=[boT[o], b_const], W=[pb[6]])
                for c in range(2):
                    I_("pe", lambda e, c=c, o=o: e.transpose(out=pT6[:, 128 + c * 128:256 + c * 128],
                                                             in_=oT[o][:, 1088 + c * 128:1216 + c * 128], identity=ident),
                       R=[boT[o], b_const], W=[pb[6]])
                I_("dve", lambda e, o=o: e.tensor_copy(out=tT[o][0:64, 1024:1152], in_=pT6[0:64, 0:128]),
                   R=[pb[6]], W=[btT[o]])
                I_("dve", lambda e, o=o: e.tensor_copy(out=tT[o][:, 1152:1408], in_=pT6[:, 128:384]),
                   R=[pb[6]], W=[btT[o]])
                g = "tT%d" % o
                P.dma("pool", QBT[:, n, :], tT[o][:, 0:384], R=[btT[o]], grp=g)
                P.dma("pool", KBT[:, n * 128:(n + 1) * 128], tT[o][:, 384:512], R=[btT[o]], grp=g)
                P.dma("pool", QIT[:, n, :], tT[o][:, 512:1024], R=[btT[o]], grp=g)
                P.dma("pool", KIT[:, n * 128:(n + 1) * 128], tT[o][0:64, 1024:1152], R=[btT[o]], grp=g)
                P.dma("pool", QCT[:, n, :], tT[o][:, 1152:1408], R=[btT[o]], grp=g)
            for c in range(24):
                for k in range(8):
                    I_("pe", lambda e, c=c, k=k, hs=hs: e.matmul(
                        bank(7), lhsT=W[:, k, N_TM + N_SM + c * 128:N_TM + N_SM + (c + 1) * 128],
                        rhs=hTg[hs][:, k, :], start=(k == 0), stop=(k == 7)), R=[bhT[hs], bW], W=[pb[7]])
                gs = gcount % 3
                gcount += 1
                I_("act", lambda e, gs=gs: e.activation(out=gsb[gs], in_=bank(7), func=AF.Sigmoid),
                   R=[pb[7]], W=[bgsb[gs]])
                P.dma("sp", GT[c, :, grp_i * 512:(grp_i + 1) * 512], gsb[gs], R=[bgsb[gs]], grp="gsb%d" % gs)
        P.barrier()
        A.reset()


    ntok = ntiles * 128
    if 2 in phases:
        blk = [A.get([3, 128], BF16) for _ in range(3)]
        bblk = [Buf("blk%d" % i) for i in range(3)]
        qT = [A.get([128], BF16) for _ in range(2)]
        bqT = [Buf("qT%d" % i) for i in range(2)]
        kT = [A.get([128], BF16) for _ in range(3)]
        bkT = [Buf("kT%d" % i) for i in range(3)]
        va_ = [A.get([2, 65], BF16) for _ in range(3)]
        bva = [Buf("va%d" % i) for i in range(3)]
        PT = [A.get([512], BF16) for _ in range(2)]
        bPT = [Buf("PT%d" % i) for i in range(2)]
        oas = [A.get([130], F32) for _ in range(2)]
        boas = [Buf("oas%d" % i) for i in range(2)]
        for i in range(3):
            I_("pool", lambda e, i=i: e.memset(va_[i], 1.0), W=[bva[i]])
        u = 0
        for g, dil in enumerate((1, 4, 16)):
            m = ntok // dil
            nblk = m // 128
            assert nblk >= 1
            for r in range(dil):
                for n in range(nblk):
                    sb3 = u % 3
                    sb2 = u % 2
                    prev3 = (u - 1) % 3
                    start = r + dil * 128 * n
                    rows = QKVA[start:start + dil * 127 + 1:dil, :].rearrange("t (s c) -> t s c", s=3)[:, :, g * 128:(g + 1) * 128]
                    P.dma("sp", blk[sb3], rows, W=[bblk[sb3]], grp="blk%d" % sb3)
                    pT = bank_bf(sb2)
                    I_("pe", lambda e, pT=pT, sb3=sb3: e.transpose(out=pT[:, 0:128], in_=blk[sb3][:, 0, :], identity=ident),
                       R=[bblk[sb3], b_const], W=[pb[sb2]])
                    I_("pe", lambda e, pT=pT, sb3=sb3: e.transpose(out=pT[:, 128:256], in_=blk[sb3][:, 1, :], identity=ident),
                       R=[bblk[sb3], b_const], W=[pb[sb2]])
                    I_("dve", lambda e, pT=pT, sb2=sb2: e.tensor_copy(out=qT[sb2], in_=pT[:, 0:128]), R=[pb[sb2]], W=[bqT[sb2]])
                    I_("dve", lambda e, pT=pT, sb3=sb3: e.tensor_copy(out=kT[sb3], in_=pT[:, 128:256]), R=[pb[sb2]], W=[bkT[sb3]])
                    I_("pool", lambda e, sb3=sb3: e.tensor_copy(out=va_[sb3][:, :, 0:64],
                                                                in_=blk[sb3][:, 2, :].rearrange("p (j d) -> p j d", j=2)),
                       R=[bblk[sb3]], W=[bva[sb3]])
                    psY = bank(2 + sb2)
                    has_prev = n > 0
                    for j in range(2):
                        if has_prev:
                            I_("pe", lambda e, j=j, psY=psY, prev3=prev3, sb2=sb2: e.matmul(
                                psY[:, j * 128:(j + 1) * 128], lhsT=kT[prev3][64 * j:64 * j + 64, :],
                                rhs=qT[sb2][64 * j:64 * j + 64, :], start=True, stop=False),
                               R=[bkT[prev3], bqT[sb2]], W=[pb[2 + sb2]])
                            I_("pe", lambda e, j=j, psY=psY: e.matmul(
                                psY[:, j * 128:(j + 1) * 128], lhsT=ident, rhs=mprev, start=False, stop=True),
                               R=[b_const], W=[pb[2 + sb2]])
                        I_("pe", lambda e, j=j, psY=psY, sb3=sb3, sb2=sb2: e.matmul(
                            psY[:, 256 + j * 128:256 + (j + 1) * 128], lhsT=kT[sb3][64 * j:64 * j + 64, :],
                            rhs=qT[sb2][64 * j:64 * j + 64, :], start=True, stop=False),
                           R=[bkT[sb3], bqT[sb2]], W=[pb[2 + sb2]])
                        I_("pe", lambda e, j=j, psY=psY: e.matmul(
                            psY[:, 256 + j * 128:256 + (j + 1) * 128], lhsT=ident, rhs=mcur, start=False, stop=True),
                           R=[b_const], W=[pb[2 + sb2]])
                    lo = 0 if has_prev else 256
                    I_("act", lambda e, psY=psY, sb2=sb2, lo=lo: e.activation(out=PT[sb2][:, lo:512], in_=psY[:, lo:512],
                                                                             func=AF.Exp, scale=0.125),
                       R=[pb[2 + sb2]], W=[bPT[sb2]])
                    psZ = bank(4 + sb2)
                    for j in range(2):
                        if has_prev:
                            I_("pe", lambda e, j=j, psZ=psZ, sb2=sb2, prev3=prev3: e.matmul(
                                psZ[:, j * 65:(j + 1) * 65], lhsT=PT[sb2][:, j * 128:(j + 1) * 128],
                                rhs=va_[prev3][:, j, :], start=True, stop=False),
                               R=[bPT[sb2], bva[prev3]], W=[pb[4 + sb2]])
                        I_("pe", lambda e, j=j, psZ=psZ, sb2=sb2, sb3=sb3, hp=has_prev: e.matmul(
                            psZ[:, j * 65:(j + 1) * 65], lhsT=PT[sb2][:, 256 + j * 128:256 + (j + 1) * 128],
                            rhs=va_[sb3][:, j, :], start=(not hp), stop=True),
                           R=[bPT[sb2], bva[sb3]], W=[pb[4 + sb2]])
                    I_("dve", lambda e, psZ=psZ, sb2=sb2: e.tensor_copy(out=oas[sb2], in_=psZ[:, 0:130]),
                       R=[pb[4 + sb2]], W=[boas[sb2]])
                    P.dma("pool", OA[g, start:start + dil * 127 + 1:dil, :], oas[sb2], R=[boas[sb2]], grp="oas%d" % sb2)
                    u += 1
        P.barrier()
        A.reset()

    if 3 in phases:
        kiT2 = A.get([S], BF16)
        kbT = A.get([S], BF16)
        vba = A.get([NT, 2, 65], BF16)
        wia = A.get([NT, 8], BF16)
        bK = Buf("dsaK")
        P.dma("sp", kiT2[0:64, 0:ntok], KIT[:, 0:ntok], W=[bK], grp="dk")
        P.dma("sp", kiT2[64:128, 0:ntok], KIT[:, 0:ntok], W=[bK], grp="dk")
        P.dma("sp", kbT[:, 0:ntok], KBT[:, 0:ntok], W=[bK], grp="dk")
        I_("pool", lambda e: e.memset(vba, 1.0), W=[bK])
        vst = [A.get([136], BF16) for _ in range(2)]
        bvst = [Buf("vst%d" % i) for i in range(2)]
        for n in range(ntiles):
            s2 = n % 2
            P.dma("sp", vst[s2], VB[n * 128:(n + 1) * 128, :], W=[bvst[s2]], grp="vst%d" % s2)
            I_("pool", lambda e, n=n, s2=s2: e.tensor_copy(out=vba[:, n, :, 0:64],
                                                          in_=vst[s2][:, 0:128].rearrange("p (c d) -> p c d", c=2)),
               R=[bvst[s2], bK], W=[bK])
            I_("pool", lambda e, n=n, s2=s2: e.tensor_copy(out=wia[:, n, :], in_=vst[s2][:, 128:136]),
               R=[bvst[s2], bK], W=[bK])
        qiT = [A.get([4, 128], BF16) for _ in range(2)]
        bqiT = [Buf("qiT%d" % i) for i in range(2)]
        qbT = [A.get([384], BF16) for _ in range(2)]
        bqbT = [Buf("qbT%d" % i) for i in range(2)]
        Dh = [A.get([8, 128], BF16) for _ in range(2)]
        bDh = [Buf("Dh%d" % i) for i in range(2)]
        isc = [A.get([S], F32) for _ in range(2)]
        bisc = [Buf("isc%d" % i) for i in range(2)]
        junk = A.get([S], BF16)
        bjunk = Buf("junk")
        Rb = [A.get([2, 512], BF16) for _ in range(3)]
        bRb = [Buf("Rb%d" % i) for i in range(3)]
        PTb = [A.get([384], BF16) for _ in range(3)]
        bPTb = [Buf("PTb%d" % i) for i in range(3)]
        MB = [A.get([128], BF16) for _ in range(3)]
        bMB = [Buf("MB%d" % i) for i in range(3)]
        bst = [A.get([8], F32) for _ in range(2)]
        bbst = [Buf("bst%d" % i) for i in range(2)]
        Wk = [A.get([NBIS + 1], F32) for _ in range(2)]
        pow2 = A.get([NBIS + 1], F32)
        cntb = A.get([2], F32)
        bcnt = Buf("cnt")
        rden = A.get([6], F32)
        ob = [A.get([384], BF16) for _ in range(2)]
        bob = [Buf("ob%d" % i) for i in range(2)]
        obT = [A.get([384], BF16) for _ in range(2)]
        bobT = [Buf("obT%d" % i) for i in range(2)]
        bpow = Buf("pow2")
        for k in range(NBIS + 1):
            I_("pool", lambda e, k=k: e.memset(pow2[:, k:k + 1], float(2.0 ** -(k + 1))), W=[bpow], R=[bpow])
        pair = [psum[:, 0:2, :], psum[:, 2:4, :]]
        bpair = [Buf("pair0", excl=True), Buf("pair1", excl=True)]
        accI = bank(4)
        sbank = [bank(5), bank(6)]
        accO = bank(7)
        cnt_pair = [0]
        cnt_rb = [0]
        cnt_unit = [0]
        cnt_mb = [0]

        _fr = []

        def fill_reg(e):
            if not _fr:
                _fr.append(e.to_reg(-1e30))
            return _fr[0]

        def dsa_index(i):
            s2 = i % 2
            L = 128 * (i + 1)
            P.dma("sp", qiT[s2], QIT[:, i, :].rearrange("p (j t) -> p j t", j=4), W=[bqiT[s2]], grp="qiT%d" % s2)
            P.dma("sp", qbT[s2], QBT[:, i, :], W=[bqbT[s2]], grp="qbT%d" % s2)
            I_("pool", lambda e: e.tensor_tensor(out=Dh[s2], in0=identf.unsqueeze(1).to_broadcast([128, 8, 128]),
                                                 in1=wia[:, i, :].unsqueeze(2).to_broadcast([128, 8, 128]), op=ALU.mult),
               R=[bK, b_const], W=[bDh[s2]])
            for c0 in range(0, L, 512):
                cw = min(512, L - c0)
                for jj in range(4):
                    pp = cnt_pair[0] % 2
                    cnt_pair[0] += 1
                    rb = cnt_rb[0] % 3
                    cnt_rb[0] += 1
                    for hh in range(2):
                        I_("pe", lambda e, pp=pp, hh=hh, jj=jj, c0=c0, cw=cw: e.matmul(
                            pair[pp][:, hh, 0:cw], lhsT=qiT[s2][64 * hh:64 * hh + 64, jj, :],
                            rhs=kiT2[64 * hh:64 * hh + 64, c0:c0 + cw], start=True, stop=True),
                           R=[bqiT[s2], bK], W=[bpair[pp]])
                    I_("act", lambda e, pp=pp, rb=rb, cw=cw: e.activation(out=Rb[rb][:, :, 0:cw], in_=pair[pp][:, :, 0:cw],
                                                                         func=AF.Relu), R=[bpair[pp]], W=[bRb[rb]])
                    for hh in range(2):
                        I_("pe", lambda e, rb=rb, hh=hh, jj=jj, cw=cw: e.matmul(
                            accI[:, 0:cw], lhsT=Dh[s2][:, 2 * jj + hh, :], rhs=Rb[rb][:, hh, 0:cw],
                            start=(jj == 0 and hh == 0), stop=(jj == 3 and hh == 1)),
                           R=[bDh[s2], bRb[rb]], W=[pb[4]])
                I_("act", lambda e, c0=c0, cw=cw: e.copy(out=isc[s2][:, c0:c0 + cw], in_=accI[:, 0:cw]),
                   R=[pb[4]], W=[bisc[s2]])
            I_("pool", lambda e: e.affine_select(out=isc[s2][:, 128 * i:128 * (i + 1)], in_=isc[s2][:, 128 * i:128 * (i + 1)],
                                                 pattern=[[-1, 128]], compare_op=ALU.is_ge, fill=fill_reg(e), base=0,
                                                 channel_multiplier=1), R=[bisc[s2]], W=[bisc[s2]])

        def dsa_bisect(i):
            s2 = i % 2
            L = 128 * (i + 1)
            b = bst[s2]
            bb = bbst[s2]
            if i < 2:
                I_("dve", lambda e: e.memset(b[:, 3:4], -1e29), W=[bb])
                return
            I_("dve", lambda e: e.tensor_reduce(out=b[:, 0:1], in_=isc[s2][:, 0:L], axis=AX.X, op=ALU.max),
               R=[bisc[s2]], W=[bb])
            I_("dve", lambda e: e.tensor_reduce(out=b[:, 1:2], in_=isc[s2][:, 0:128 * i], axis=AX.X, op=ALU.min),
               R=[bisc[s2]], W=[bb])
            I_("dve", lambda e: e.tensor_tensor(out=b[:, 2:3], in0=b[:, 0:1], in1=b[:, 1:2], op=ALU.subtract),
               R=[bb], W=[bb])
            I_("dve", lambda e: e.tensor_scalar(out=Wk[s2], in0=pow2, scalar1=b[:, 2:3], scalar2=None, op0=ALU.mult),
               R=[bb, bpow], W=[bb])
            I_("dve", lambda e: e.tensor_tensor(out=b[:, 4:5], in0=b[:, 1:2], in1=Wk[s2][:, 0:1], op=ALU.add),
               R=[bb], W=[bb])
            for k in range(NBIS):
                I_("dve", lambda e: e.tensor_scalar(out=junk[:, 0:L], in0=isc[s2][:, 0:L], scalar1=b[:, 4:5], scalar2=None,
                                                    op0=ALU.is_ge, op1=ALU.add, accum_out=cntb[:, 0:1]),
                   R=[bisc[s2], bb], W=[bjunk, bcnt])
                I_("dve", lambda e: e.tensor_scalar(out=cntb[:, 1:2], in0=cntb[:, 0:1], scalar1=TOPK - 0.5, scalar2=0.5,
                                                    op0=ALU.is_ge, op1=ALU.subtract), R=[bcnt], W=[bcnt])
                I_("dve", lambda e, k=k: e.scalar_tensor_tensor(out=b[:, 4:5], in0=cntb[:, 1:2], scalar=Wk[s2][:, k:k + 1],
                                                               in1=b[:, 4:5], op0=ALU.mult, op1=ALU.add),
                   R=[bcnt, bb], W=[bb])
            I_("dve", lambda e: e.tensor_tensor(out=b[:, 3:4], in0=b[:, 4:5], in1=Wk[s2][:, NBIS:NBIS + 1], op=ALU.subtract),
               R=[bb], W=[bb])

        def dsa_attn(i):
            s2 = i % 2
            b = bst[s2]
            for kb in range(i + 1):
                mb = cnt_mb[0] % 3
                cnt_mb[0] += 1
                I_("pool", lambda e, kb=kb, mb=mb: e.tensor_scalar(out=MB[mb], in0=isc[s2][:, kb * 128:(kb + 1) * 128],
                                                                  scalar1=b[:, 3:4], scalar2=NEG, op0=ALU.is_lt,
                                                                  op1=ALU.mult), R=[bisc[s2], bbst[s2]], W=[bMB[mb]])
                for c in range(2):
                    u = cnt_unit[0]
                    cnt_unit[0] += 1
                    sb = u % 2
                    pt = u % 3
                    psS = sbank[sb]
                    I_("pe", lambda e, c=c, kb=kb, psS=psS: e.matmul(
                        psS[:, 0:384], lhsT=kbT[64 * c:64 * c + 64, kb * 128:(kb + 1) * 128],
                        rhs=qbT[s2][64 * c:64 * c + 64, :], start=True, stop=False),
                       R=[bK, bqbT[s2]], W=[pb[5 + sb]])
                    I_("pe", lambda e, mb=mb, psS=psS: e.matmul(psS[:, 0:384], lhsT=MB[mb], rhs=i3, start=False, stop=True),
                       R=[bMB[mb], b_const], W=[pb[5 + sb]])
                    I_("act", lambda e, pt=pt, psS=psS: e.activation(out=PTb[pt], in_=psS[:, 0:384], func=AF.Exp, scale=0.125),
                       R=[pb[5 + sb]], W=[bPTb[pt]])
                    for g in range(3):
                        h = 3 * c + g
                        I_("pe", lambda e, pt=pt, g=g, h=h, kb=kb, c=c: e.matmul(
                            accO[:, h * 65:(h + 1) * 65], lhsT=PTb[pt][:, g * 128:(g + 1) * 128], rhs=vba[:, kb, c, :],
                            start=(kb == 0 and h == 0), stop=(kb == i), skip_group_check=True),
                            R=[bPTb[pt], bK], W=[pb[7]])
            av = accO[:, 0:390].rearrange("p (h e) -> p h e", e=65)
            I_("dve", lambda e: e.reciprocal(out=rden, in_=av[:, :, 64]), R=[pb[7]], W=[bcnt])
            I_("dve", lambda e: e.tensor_tensor(out=ob[s2].rearrange("p (h d) -> p h d", d=64), in0=av[:, :, 0:64],
                                                in1=rden.unsqueeze(2).to_broadcast([128, 6, 64]), op=ALU.mult),
               R=[pb[7], bcnt], W=[bob[s2]])
            pT = bank_bf(4)
            for cc in range(3):
                I_("pe", lambda e, cc=cc: e.transpose(out=pT[:, cc * 128:(cc + 1) * 128], in_=ob[s2][:, cc * 128:(cc + 1) * 128],
                                                      identity=ident), R=[bob[s2], b_const], W=[pb[4]])
            I_("act", lambda e: e.copy(out=obT[s2], in_=pT[:, 0:384]), R=[pb[4]], W=[bobT[s2]])
            P.dma("pool", OBT[:, :, i * 128:(i + 1) * 128].rearrange("c p t -> p c t"),
                  obT[s2].rearrange("p (c t) -> p c t", c=3), R=[bobT[s2]], grp="obT%d" % s2)

        dsa_index(0)
        for i in range(ntiles):
            if i + 1 < ntiles:
                dsa_index(i + 1)
            dsa_bisect(i)
            dsa_attn(i)
        P.barrier()
        A.reset()


    if 4 in phases:
        wa = A.get([1, D], BF16)
        wb_ = A.get([3, D], BF16)
        wc = A.get([2, D], BF16)
        wo = A.get([8, D], BF16)
        wkv = A.get([8, 512], BF16)
        stage4 = make_stage(1024, "p4")
        bwa = load_weight_bf16(wa, wa_d, 1, D, None, "wa", chunk=1024, stage=stage4)
        bwb = load_weight_bf16(wb_, wb_d, 3, D, None, "wb", chunk=1024, stage=stage4)
        bwc = load_weight_bf16(wc, wc_d, 2, D, None, "wc", chunk=1024, stage=stage4)
        bwo = load_weight_bf16(wo, wo_d, 8, D, None, "wo", chunk=1024, stage=stage4)
        bwkv = load_weight_bf16(wkv, wkv_d, 8, 512, gmem_d, "wkv", chunk=1024, stage=stage4)
        P4STOP = int(os.environ.get('P4STOP', '9'))
        xt4 = [A.get([D], F32) for _ in range(2)]
        bxt4 = [Buf("xt4%d" % i) for i in range(2)]
        xb4 = A.get([D], BF16)
        bxb4 = Buf("xb4")
        sq4 = A.get([D], F32)
        st4 = A.get([8], F32)
        bxs4 = dict(sq=sq4, st=st4, b=Buf("xstat4"))
        memT = A.get([8, 128], BF16)
        bmemT = Buf("memT")
        kmT = A.get([2, 256], BF16)
        vmb = A.get([2, 256], BF16)
        bkm = Buf("kmT")
        bvm = Buf("vmb")
        ksb = A.get([256], F32)
        ksq = A.get([256], F32)
        kss = A.get([8], F32)
        kgr = A.get([256], F32)
        kmb = A.get([256], BF16)
        bks = Buf("ksb")
        ones_bf = A.get([128], BF16)
        I_("pool", lambda e: e.memset(ones_bf, 1.0), W=[bks])
        for hh in range(4):
            I_("pool", lambda e, hh=hh: e.tensor_copy(out=kgr[:, hh * 64:(hh + 1) * 64], in_=gains[:, 5 * 64:6 * 64]),
               R=[b_const], W=[bks])
        for mt in range(2 if P4STOP >= 1 else 0):
            norm_rows_to_hT(mem_d[mt * 128:(mt + 1) * 128, :], xt4[0], xb4, None, bxt4[0], bxb4, bxs4, "xt40", 5, pb[5])
            I_("act", lambda e: e.copy(out=memT, in_=bank_bf(5).rearrange("p (k c) -> p k c", k=8)), R=[pb[5]], W=[bmemT])
            P4SUB = int(os.environ.get('P4SUB', '9'))
            if P4SUB < 1:
                continue
            for k in range(8):
                I_("pe", lambda e, k=k: e.matmul(bank(0), lhsT=memT[:, k, :], rhs=wkv[:, k, :], start=(k == 0), stop=(k == 7)),
                   R=[bmemT, bwkv], W=[pb[0]])
            P4V = int(os.environ.get('P4V', '3'))
            if P4V & 1:
                I_("act", lambda e, mt=mt: e.copy(out=vmb[:, mt, :], in_=bank(0)[:, 256:512]), R=[pb[0]], W=[bvm])
            if P4V & 2:
                I_("dve", lambda e: e.tensor_copy(out=ksb, in_=bank(0)[:, 0:256]), R=[pb[0]], W=[bks])
            if P4SUB < 2:
                continue
            I_("pool", lambda e: e.tensor_tensor(out=ksq, in0=ksb, in1=ksb, op=ALU.mult), R=[bks], W=[bks])
            I_("dve", lambda e: e.tensor_reduce(out=kss[:, 0:4], in_=ksq.rearrange("p (h d) -> p h d", d=64), axis=AX.X,
                                                op=ALU.add), R=[bks], W=[bks])
            rstd_from_ss(kss[:, 0:4], kss[:, 4:8], 64, [bks], 4)
            I_("dve", lambda e: e.tensor_tensor(out=ksq.rearrange("p (h d) -> p h d", d=64),
                                                in0=kgr.rearrange("p (h d) -> p h d", d=64),
                                                in1=kss[:, 4:8].unsqueeze(2).to_broadcast([128, 4, 64]), op=ALU.mult),
               R=[bks], W=[bks])
            I_("dve", lambda e: e.tensor_tensor(out=kmb, in0=ksb, in1=ksq, op=ALU.mult), R=[bks], W=[bks])
            if P4SUB < 3:
                continue
            for j in range(2):
                I_("pe", lambda e, j=j: e.transpose(out=bank_bf(1)[:, j * 128:(j + 1) * 128], in_=kmb[:, j * 128:(j + 1) * 128],
                                                    identity=ident), R=[bks, b_const], W=[pb[1]])
            I_("dve", lambda e, mt=mt: e.tensor_copy(out=kmT[:, :, mt * 128:(mt + 1) * 128],
                                                     in_=bank_bf(1)[:, 0:256].rearrange("p (j m) -> p j m", j=2)),
               R=[pb[1]], W=[bkm])
        gtg = [A.get([24, 512], BF16) for _ in range(2)]
        bgtg = [Buf("gtg%d" % i) for i in range(2)]
        obg = [A.get([3, 512], BF16) for _ in range(2)]
        bobg = [Buf("obg%d" % i) for i in range(2)]
        qcg = [A.get([2, 512], BF16) for _ in range(2)]
        bqcg = [Buf("qcg%d" % i) for i in range(2)]
        oaT = [A.get([512], BF16) for _ in range(2)]
        boaT = [Buf("oaT%d" % i) for i in range(2)]
        ocT = [A.get([2, 512], BF16) for _ in range(2)]
        bocT = [Buf("ocT%d" % i) for i in range(2)]
        mT = [A.get([8, 512], BF16) for _ in range(2)]
        bmT = [Buf("mT%d" % i) for i in range(2)]
        oa3 = [A.get([3, 130], F32) for _ in range(2)]
        boa3 = [Buf("oa3%d" % i) for i in range(2)]
        oasum = A.get([130], F32)
        oard = A.get([2], F32)
        oab = A.get([128], BF16)
        boas4 = Buf("oasum")
        PTc = [A.get([512], BF16) for _ in range(4)]
        bPTc = [Buf("PTc%d" % i) for i in range(4)]
        rdc = [A.get([512], F32) for _ in range(2)]
        brdc = [Buf("rdc%d" % i) for i in range(2)]
        tm = [A.get([512], F32) for _ in range(3)]
        btm = [Buf("tm%d" % i) for i in range(3)]
        x1t = [A.get([D], F32) for _ in range(2)]
        bx1t = [Buf("x1t%d" % i) for i in range(2)]
        ngrp = ntiles // 4
        cpt = [0]
        chd = [0]
        for G in range(ngrp if P4STOP >= 2 else 0):
            g2 = G % 2
            t0 = G * 512
            for c4 in range(0, 24, 4):
                P.dma("sp", gtg[g2][:, c4:c4 + 4, :], GT[c4:c4 + 4, :, t0:t0 + 512].rearrange("c p t -> p c t"),
                      W=[bgtg[g2]], grp="gtg%d" % g2)
            P.dma("sp", obg[g2], OBT[:, :, t0:t0 + 512].rearrange("c p t -> p c t"), W=[bobg[g2]], grp="obg%d" % g2)
            for j in range(2):
                P.dma("sp", qcg[g2][:, j, :].rearrange("p (n t) -> p n t", n=4), QCT[:, 4 * G:4 * G + 4, j * 128:(j + 1) * 128],
                      W=[bqcg[g2]], grp="qcg%d" % g2)
            for tt in range(4):
                n = 4 * G + tt
                o2 = n % 2
                P.dma("sp", oa3[o2], OA[:, n * 128:(n + 1) * 128, :].rearrange("g t e -> t g e"), W=[boa3[o2]],
                      grp="oa3%d" % o2)
                I_("pool", lambda e, o2=o2: e.tensor_tensor(out=oasum, in0=oa3[o2][:, 0, :], in1=oa3[o2][:, 1, :], op=ALU.add),
                   R=[boa3[o2]], W=[boas4])
                I_("pool", lambda e, o2=o2: e.tensor_tensor(out=oasum, in0=oasum, in1=oa3[o2][:, 2, :], op=ALU.add),
                   R=[boa3[o2], boas4], W=[boas4])
                osv = oasum.rearrange("p (j e) -> p j e", e=65)
                I_("dve", lambda e, osv=osv: e.reciprocal(out=oard, in_=osv[:, :, 64]), R=[boas4], W=[boas4])
                I_("dve", lambda e, osv=osv: e.tensor_tensor(out=oab.rearrange("p (j d) -> p j d", d=64), in0=osv[:, :, 0:64],
                                                             in1=oard.unsqueeze(2).to_broadcast([128, 2, 64]), op=ALU.mult),
                   R=[boas4], W=[boas4])
                I_("pe", lambda e, tt=tt: e.transpose(out=bank_bf(0)[:, tt * 128:(tt + 1) * 128], in_=oab, identity=ident),
                   R=[boas4, b_const], W=[pb[0]])
            I_("act", lambda e, g2=g2: e.copy(out=oaT[g2], in_=bank_bf(0)[:, 0:512]), R=[pb[0]], W=[boaT[g2]])
            for h in range(4 if P4STOP >= 3 else 0):
                j, hh = h // 2, h % 2
                hs2 = chd[0] % 2
                chd[0] += 1
                numb, denb = 4 + 2 * hs2, 5 + 2 * hs2
                pts = []
                for mt in range(2):
                    sbk = 2 * hs2 + mt
                    pt = cpt[0] % 4
                    cpt[0] += 1
                    pts.append(pt)
                    I_("pe", lambda e, sbk=sbk, hh=hh, j=j, mt=mt, g2=g2: e.matmul(
                        bank(sbk), lhsT=kmT[64 * hh:64 * hh + 64, j, mt * 128:(mt + 1) * 128],
                        rhs=qcg[g2][64 * hh:64 * hh + 64, j, :], start=True, stop=True),
                       R=[bkm, bqcg[g2]], W=[pb[sbk]])
                    I_("act", lambda e, sbk=sbk, pt=pt: e.activation(out=PTc[pt], in_=bank(sbk), func=AF.Exp, scale=0.125),
                       R=[pb[sbk]], W=[bPTc[pt]])
                for mt in range(2):
                    I_("pe", lambda e, mt=mt, j=j, numb=numb, pt=pts[mt]: e.matmul(
                        bank(numb), lhsT=vmb[:, mt, j * 128:(j + 1) * 128], rhs=PTc[pt], start=(mt == 0), stop=(mt == 1)),
                       R=[bvm, bPTc[pts[mt]]], W=[pb[numb]])
                for mt in range(2):
                    I_("pe", lambda e, mt=mt, denb=denb, pt=pts[mt]: e.matmul(
                        bank(denb), lhsT=ones_bf, rhs=PTc[pt], start=(mt == 0), stop=(mt == 1)),
                       R=[bks, bPTc[pts[mt]]], W=[pb[denb]])
                lo_, hi_ = 64 * hh, 64 * hh + 64
                I_("dve", lambda e, denb=denb, hs2=hs2, lo_=lo_, hi_=hi_: e.reciprocal(out=rdc[hs2][lo_:hi_, :],
                                                                                     in_=bank(denb)[lo_:hi_, :]),
                   R=[pb[denb]], W=[brdc[hs2]])
                I_("dve", lambda e, numb=numb, hs2=hs2, lo_=lo_, hi_=hi_, j=j, g2=g2: e.tensor_tensor(
                    out=ocT[g2][lo_:hi_, j, :], in0=bank(numb)[lo_:hi_, :], in1=rdc[hs2][lo_:hi_, :], op=ALU.mult),
                   R=[pb[numb], brdc[hs2]], W=[bocT[g2]])
            for oc in range(8 if P4STOP >= 4 else 0):
                bs = 3 * (oc % 2)
                cs_ = slice(oc * 128, (oc + 1) * 128)
                I_("pe", lambda e, bs=bs, cs_=cs_, g2=g2: e.matmul(bank(bs), lhsT=wa[:, 0, cs_], rhs=oaT[g2], start=True, stop=True),
                   R=[bwa, boaT[g2]], W=[pb[bs]])
                for k in range(3):
                    I_("pe", lambda e, bs=bs, cs_=cs_, k=k, g2=g2: e.matmul(bank(bs + 1), lhsT=wb_[:, k, cs_], rhs=obg[g2][:, k, :],
                                                                          start=(k == 0), stop=(k == 2)),
                       R=[bwb, bobg[g2]], W=[pb[bs + 1]])
                for k in range(2):
                    I_("pe", lambda e, bs=bs, cs_=cs_, k=k, g2=g2: e.matmul(bank(bs + 2), lhsT=wc[:, k, cs_], rhs=ocT[g2][:, k, :],
                                                                          start=(k == 0), stop=(k == 1)),
                       R=[bwc, bocT[g2]], W=[pb[bs + 2]])
                I_("dve", lambda e, bs=bs, oc=oc, g2=g2: e.tensor_tensor(out=tm[0], in0=bank(bs), in1=gtg[g2][:, oc, :], op=ALU.mult),
                   R=[pb[bs], bgtg[g2]], W=[btm[0]])
                I_("dve", lambda e, bs=bs, oc=oc, g2=g2: e.tensor_tensor(out=tm[1], in0=bank(bs + 1), in1=gtg[g2][:, 8 + oc, :],
                                                                       op=ALU.mult), R=[pb[bs + 1], bgtg[g2]], W=[btm[1]])
                I_("dve", lambda e, bs=bs, oc=oc, g2=g2: e.tensor_tensor(out=tm[2], in0=bank(bs + 2), in1=gtg[g2][:, 16 + oc, :],
                                                                       op=ALU.mult), R=[pb[bs + 2], bgtg[g2]], W=[btm[2]])
                I_("pool", lambda e: e.tensor_tensor(out=tm[0], in0=tm[0], in1=tm[1], op=ALU.add), R=[btm[0], btm[1]], W=[btm[0]])
                I_("pool", lambda e, oc=oc, g2=g2: e.tensor_tensor(out=mT[g2][:, oc, :], in0=tm[0], in1=tm[2], op=ALU.add),
                   R=[btm[0], btm[2]], W=[bmT[g2]])
            for tt in range(4 if P4STOP >= 5 else 0):
                n = 4 * G + tt
                o2 = n % 2
                P.dma("sp", xt4[o2], x_d[n * 128:(n + 1) * 128, :], W=[bxt4[o2]], grp="xt4%d" % o2)
                for hf in range(2):
                    for k in range(8):
                        I_("pe", lambda e, hf=hf, k=k, tt=tt, g2=g2: e.matmul(
                            bank(6 + hf), lhsT=mT[g2][:, k, tt * 128:(tt + 1) * 128], rhs=wo[:, k, hf * 512:(hf + 1) * 512],
                            start=(k == 0), stop=(k == 7)), R=[bmT[g2], bwo], W=[pb[6 + hf]])
                I_("dve", lambda e, o2=o2: e.tensor_tensor(out=x1t[o2], in0=psum[:, 6:8, :].rearrange("p a b -> p (a b)"),
                                                           in1=xt4[o2], op=ALU.add), R=[pb[6], pb[7], bxt4[o2]], W=[bx1t[o2]])
                P.dma("pool", X1[n * 128:(n + 1) * 128, :], x1t[o2], R=[bx1t[o2]], grp="x1t%d" % o2)
        P.barrier()
        A.reset()

    if 5 in phases:
        W1 = A.get([8, 4096], BF16)
        W2 = A.get([32, D], BF16)
        stage5 = make_stage(2048, "p5")
        bW1 = load_weight_bf16(W1, w1_d, 8, 4096, gmlp_d, "w1", chunk=2048, stage=stage5)
        bW2 = load_weight_bf16(W2, w2_d, 32, D, None, "w2", chunk=1024, stage=stage5)
        xt5 = [A.get([D], F32) for _ in range(4)]
        bxt5 = [Buf("xt5%d" % i) for i in range(4)]
        xb5 = [A.get([D], BF16) for _ in range(2)]
        bxb5 = [Buf("xb5%d" % i) for i in range(2)]
        sq5 = A.get([D], F32)
        st5 = A.get([8], F32)
        bxs5 = dict(sq=sq5, st=st5, b=Buf("xstat5"))
        h2T = [A.get([8, 256], BF16) for _ in range(2)]
        bh2T = [Buf("h2T%d" % i) for i in range(2)]
        rr = [A.get([256], BF16) for _ in range(3)]
        brr = [Buf("rr%d" % i) for i in range(3)]
        aa = [A.get([256], BF16) for _ in range(3)]
        baa = [Buf("aa%d" % i) for i in range(3)]
        ot = [A.get([D], F32) for _ in range(2)]
        bot = [Buf("ot%d" % i) for i in range(2)]
        ng5 = ntiles // 2
        cc5 = [0]
        for G in range(ng5):
            g2 = G % 2
            for tt in range(2):
                n = 2 * G + tt
                s4 = n % 4
                s2 = n % 2
                norm_rows_to_hT(X1[n * 128:(n + 1) * 128, :], xt5[s4], xb5[s2], None, bxt5[s4], bxb5[s2], bxs5,
                                "xt5%d" % s4, 7, pb[7])
                I_("act", lambda e, g2=g2, tt=tt: e.copy(out=h2T[g2][:, :, tt * 128:(tt + 1) * 128],
                                                         in_=bank_bf(7).rearrange("p (k c) -> p k c", k=8)),
                   R=[pb[7]], W=[bh2T[g2]])
            for c in range(32):
                ub = 4 + cc5[0] % 3
                r3 = cc5[0] % 3
                cc5[0] += 1
                for k in range(8):
                    I_("pe", lambda e, ub=ub, k=k, c=c, g2=g2: e.matmul(
                        bank(ub)[:, 0:256], lhsT=W1[:, k, c * 128:(c + 1) * 128], rhs=h2T[g2][:, k, :],
                        start=(k == 0), stop=(k == 7)), R=[bW1, bh2T[g2]], W=[pb[ub]])
                I_("act", lambda e, ub=ub, r3=r3: e.activation(out=rr[r3], in_=bank(ub)[:, 0:256], func=AF.Relu),
                   R=[pb[ub]], W=[brr[r3]])
                I_("dve", lambda e, r3=r3: e.tensor_tensor(out=aa[r3], in0=rr[r3], in1=rr[r3], op=ALU.mult),
                   R=[brr[r3]], W=[baa[r3]])
                for tt in range(2):
                    for hf in range(2):
                        I_("pe", lambda e, tt=tt, hf=hf, r3=r3, c=c: e.matmul(
                            bank(tt * 2 + hf), lhsT=aa[r3][:, tt * 128:(tt + 1) * 128], rhs=W2[:, c, hf * 512:(hf + 1) * 512],
                            start=(c == 0), stop=(c == 31)), R=[baa[r3], bW2], W=[pb[tt * 2 + hf]])
            for tt in range(2):
                n = 2 * G + tt
                s4 = n % 4
                s2 = n % 2
                I_("dve", lambda e, tt=tt, s2=s2, s4=s4: e.tensor_tensor(
                    out=ot[s2], in0=psum[:, 2 * tt:2 * tt + 2, :].rearrange("p a b -> p (a b)"), in1=xt5[s4], op=ALU.add),
                   R=[pb[2 * tt], pb[2 * tt + 1], bxt5[s4]], W=[bot[s2]])
                P.dma("pool", out_d[n * 128:(n + 1) * 128, :], ot[s2], R=[bot[s2]], grp="ot%d" % s2)
        P.barrier()
        A.reset()

    P.barrier()
    P.emit()
    return nc, P


def _host_layout(inputs, b):
    def kmaj(w):
        k = w.shape[0] // 128
        return np.ascontiguousarray(w.reshape(k, 128, w.shape[1]).transpose(1, 0, 2))
    perm = _win_perm()
    m = {
        "x": np.ascontiguousarray(inputs["x"][b]),
        "mem": np.ascontiguousarray(inputs["mem"][b]),
        "pos": np.ascontiguousarray(inputs["positions"][b].reshape(NT, 128).T),
        "w_in": kmaj(inputs["w_in"][0][:, perm]),
        "g_mix": np.ascontiguousarray(inputs["g_mix"][0].reshape(8, 128).T),
        "g_mem": np.ascontiguousarray(inputs["g_mem"][0].reshape(8, 128).T),
        "g_mlp": np.ascontiguousarray(inputs["g_mlp"][0].reshape(8, 128).T),
        "gains": np.concatenate([inputs[k][0] for k in ("g_qa", "g_ka", "g_qb", "g_kb", "g_qc", "g_kc")])[None, :],
        "w_mem_kv": kmaj(inputs["w_mem_kv"][0]),
        "w_a": kmaj(inputs["w_a"][0]),
        "w_b": kmaj(inputs["w_b"][0]),
        "w_c": kmaj(inputs["w_c"][0]),
        "w_o": kmaj(inputs["w_o"][0]),
        "w_1": kmaj(inputs["w_1"][0]),
        "w_2": kmaj(inputs["w_2"][0]),
    }
    return {k: np.ascontiguousarray(v) for k, v in m.items()}


def kernel(**inputs):
    inputs = {k: np.asarray(v) for k, v in inputs.items()}
    nc, _ = build_program()
    in_maps = [_host_layout(inputs, b) for b in range(8)]
    res = run_bass_kernel_spmd(nc, in_maps, core_ids=list(range(8)))
    return np.stack([np.asarray(r["out"]) for r in res.results], axis=0).astype(np.float32)
```

```python
import bisect
import os
import numpy as np
import concourse.bass as bass
import concourse.mybir as mybir
from concourse.bass_utils import run_bass_kernel_spmd

F32 = mybir.dt.float32
BF16 = mybir.dt.bfloat16
I32 = mybir.dt.int32
ALU = mybir.AluOpType
AF = mybir.ActivationFunctionType
AX = mybir.AxisListType

ENGS = ("pe", "act", "dve", "pool", "sp")

S = 8192
D = 1024
NT = S // 128
EPS = 1e-6
NEG = -30000.0
TOPK = 256
NBIS = 16


class Buf:
    __slots__ = ("name", "w", "r", "excl")

    def __init__(self, name="", excl=False):
        self.name = name
        self.w = None
        self.r = []
        self.excl = excl


class Prog:
    def __init__(self, nc):
        self.nc = nc
        self.ins = []
        self.by_eng = {e: [] for e in ENGS}
        self.groups = {}

    def _deps(self, R, W):
        deps = set()
        for b in R:
            if b.w is not None:
                deps.add(b.w)
        for b in W:
            if b.w is not None:
                deps.add(b.w)
            deps.update(b.r)
        return deps

    def _commit(self, iid, R, W):
        for b in R:
            b.r.append(iid)
        for b in W:
            b.w = iid
            b.r = []

    def I(self, eng, fn, R=(), W=()):
        iid = len(self.ins)
        W = list(W) + [b for b in R if b.excl and b not in W]
        deps = self._deps(R, W)
        raw = set(b.w for b in R if b.w is not None)
        self.ins.append(dict(eng=eng, fn=fn, deps=deps, raw=raw, grp=None))
        self.by_eng[eng].append(iid)
        self._commit(iid, R, W)
        return iid

    def dma(self, eng, out, in_, R=(), W=(), grp="g", **kw):
        iid = len(self.ins)
        deps = self._deps(R, W)
        raw = set(b.w for b in R if b.w is not None)
        self.groups.setdefault(grp, []).append(iid)
        self.ins.append(dict(eng=eng, fn=(lambda e, o=out, i=in_, k=kw: e.dma_start(out=o, in_=i, **k)),
                             deps=deps, raw=raw, grp=grp))
        self.by_eng[eng].append(iid)
        self._commit(iid, R, W)
        return iid

    def barrier(self):
        alld = set()
        for e in ENGS:
            for k in reversed(self.by_eng[e]):
                if self.ins[k]["fn"] is not None:
                    alld.add(k)
                    break
        for g, l in self.groups.items():
            if l:
                alld.add(l[-1])
        for e in ENGS:
            iid = len(self.ins)
            self.ins.append(dict(eng=e, fn=None, deps=set(alld), raw=set(alld), grp=None))
            self.by_eng[e].append(iid)

    def emit(self):
        nc = self.nc
        ins = self.ins
        n = len(ins)
        is_target = [False] * n
        for k in range(n):
            for d in ins[k]["deps"]:
                is_target[d] = True
        ms_val = [0] * n
        cnt = {e: 0 for e in ENGS}
        for k in range(n):
            it = ins[k]
            if it["grp"] is None and it["fn"] is not None and is_target[k]:
                cnt[it["eng"]] += 1
                ms_val[k] = cnt[it["eng"]]
        self.esem = {e: nc.alloc_semaphore("sem_" + e) for e in ENGS}
        self.gsem = {g: nc.alloc_semaphore("dsem_" + g) for g in self.groups}
        self.nwaits = 0
        prog = self

        def run_engine(ename, eobj):
            seen = {}
            for k in prog.by_eng[ename]:
                it = ins[k]
                need = {}
                for d in it["deps"]:
                    dd = ins[d]
                    if dd["grp"] is not None:
                        g = dd["grp"]
                        c = bisect.bisect_left(prog.groups[g], k)
                        key = ("g", g)
                        val = 16 * c
                    else:
                        if dd["fn"] is None:
                            continue
                        if dd["eng"] == ename and it["grp"] is None and it["fn"] is not None:
                            if ename == "pe":
                                continue
                            if d not in it["raw"]:
                                continue
                        key = ("e", dd["eng"])
                        val = ms_val[d]
                    if val > need.get(key, 0):
                        need[key] = val
                for key, val in need.items():
                    if seen.get(key, 0) >= val:
                        continue
                    seen[key] = val
                    sem = prog.esem[key[1]] if key[0] == "e" else prog.gsem[key[1]]
                    eobj.wait_ge(sem, val)
                    prog.nwaits += 1
                if it["fn"] is None:
                    continue
                bi = it["fn"](eobj)
                if it["grp"] is not None:
                    bi.then_inc(prog.gsem[it["grp"]], 16)
                elif is_target[k]:
                    bi.then_inc(prog.esem[ename], 1)

        with nc.Block() as block:
            @block.tensor
            def _(e):
                run_engine("pe", e)

            @block.scalar
            def _(e):
                run_engine("act", e)

            @block.vector
            def _(e):
                run_engine("dve", e)

            @block.gpsimd
            def _(e):
                run_engine("pool", e)

            @block.sync
            def _(e):
                run_engine("sp", e)


class Arena:
    def __init__(self, nc, nbytes):
        self.t = nc.alloc_sbuf_tensor("arena", [128, nbytes // 2], BF16)
        self.n = nbytes // 2
        self.base = 0
        self.p = 0

    def mark(self):
        self.base = self.p

    def reset(self):
        self.p = self.base

    def get(self, shape, dtype):
        ne = int(np.prod(shape))
        w = ne * (2 if dtype in (F32, I32) else 1)
        w = (w + 31) // 32 * 32
        assert self.p + w <= self.n, ("SBUF arena overflow", self.p, w, self.n)
        ap = self.t[:, self.p:self.p + w]
        self.p += w
        if dtype != BF16:
            ap = ap.bitcast(dtype)
        ap = ap[:, 0:ne]
        if len(shape) == 2:
            ap = ap.rearrange("p (a b) -> p a b", a=shape[0])
        elif len(shape) == 3:
            ap = ap.rearrange("p (a b c) -> p a b c", a=shape[0], b=shape[1])
        return ap


IN_COLS = 5704
N_TM = 2560
N_SM = 72
N_G = 3072


def _win_perm():
    o = {}
    acc = 0
    for nme, w in (("qa", 384), ("ka", 384), ("va", 384), ("qb", 384), ("kb", 128), ("vb", 128),
                   ("qi", 512), ("ki", 64), ("wi", 8), ("qc", 256), ("gl", 3072)):
        o[nme] = np.arange(acc, acc + w)
        acc += w
    qb = o["qb"].reshape(2, 3, 64).transpose(1, 0, 2).reshape(-1)
    return np.concatenate([o["qa"], o["ka"], qb, o["kb"], o["qi"], o["qc"], o["va"], o["vb"],
                           o["ki"], o["wi"], o["gl"]])


def build_program(debug=(), phases=(1, 2, 3, 4, 5), ntiles=NT):
    nc = bass.Bass("TRN2", target_bir_lowering=False)
    P = Prog(nc)
    I_ = P.I

    def din(name, shape, dt=F32):
        return nc.dram_tensor(name, shape, dt, kind="ExternalInput").ap()

    def dscr(name, shape, dt):
        return nc.dram_tensor(name, shape, dt, kind=("ExternalOutput" if name in debug else "Internal")).ap()

    x_d = din("x", [S, D])
    mem_d = din("mem", [256, D])
    pos_d = din("pos", [128, NT], I32)
    win_d = din("w_in", [128, 8, IN_COLS])
    gmix_d = din("g_mix", [128, 8])
    gmem_d = din("g_mem", [128, 8])
    gmlp_d = din("g_mlp", [128, 8])
    gains_d = din("gains", [1, 384])
    wkv_d = din("w_mem_kv", [128, 8, 512])
    wa_d = din("w_a", [128, 1, D])
    wb_d = din("w_b", [128, 3, D])
    wc_d = din("w_c", [128, 2, D])
    wo_d = din("w_o", [128, 8, D])
    w1_d = din("w_1", [128, 8, 4096])
    w2_d = din("w_2", [128, 32, D])
    out_d = nc.dram_tensor("out", [S, D], F32, kind="ExternalOutput").ap()

    QKVA = dscr("s_qkva", [S, 1152], BF16)
    QBT = dscr("s_qbt", [128, NT, 384], BF16)
    KBT = dscr("s_kbt", [128, S], BF16)
    VB = dscr("s_vb", [S, 136], BF16)
    QIT = dscr("s_qit", [128, NT, 512], BF16)
    KIT = dscr("s_kit", [64, S], BF16)
    QCT = dscr("s_qct", [128, NT, 256], BF16)
    GT = dscr("s_gt", [24, 128, S], BF16)
    OA = dscr("s_oa", [3, S, 130], F32)
    OBT = dscr("s_obt", [3, 128, S], BF16)
    X1 = dscr("s_x1", [S, D], F32)

    A = Arena(nc, 206 * 1024)
    psum = nc.alloc_psum_tensor("psum", [128, 8, 512], F32)

    def bank(i):
        return psum[:, i, :]

    def bank_bf(i):
        return psum[:, i, :].bitcast(BF16)

    pb = [Buf("bank%d" % i, excl=True) for i in range(8)]

    ident = A.get([128], BF16)
    identf = A.get([128], F32)
    mcur = A.get([128], BF16)
    mprev = A.get([128], BF16)
    i3 = A.get([384], BF16)
    gains = A.get([384], F32)
    cs = A.get([NT, 16], F32)
    b_const = Buf("const")
    I_("pool", lambda e: e.memset(identf, 0.0), W=[b_const])
    I_("pool", lambda e: e.affine_select(out=identf, in_=identf, pattern=[[-1, 128]], compare_op=ALU.not_equal,
                                         fill=1.0, base=0, channel_multiplier=1), R=[b_const], W=[b_const])
    I_("pool", lambda e: e.tensor_copy(out=ident, in_=identf), R=[b_const], W=[b_const])
    for g in range(3):
        I_("pool", lambda e, g=g: e.tensor_copy(out=i3[:, g * 128:(g + 1) * 128], in_=identf), R=[b_const], W=[b_const])
    I_("pool", lambda e: e.memset(mcur, 0.0), W=[b_const], R=[b_const])
    I_("pool", lambda e: e.affine_select(out=mcur, in_=mcur, pattern=[[1, 128]], compare_op=ALU.is_ge,
                                         fill=NEG, base=0, channel_multiplier=-1), R=[b_const], W=[b_const])
    I_("pool", lambda e: e.memset(mprev, 0.0), W=[b_const], R=[b_const])
    I_("pool", lambda e: e.affine_select(out=mprev, in_=mprev, pattern=[[-1, 128]], compare_op=ALU.is_ge,
                                         fill=NEG, base=0, channel_multiplier=1), R=[b_const], W=[b_const])
    P.dma("sp", gains, gains_d.partition_broadcast(128), W=[b_const], grp="c0")
    A.mark()
    posi = A.get([NT], I32)
    posf = A.get([NT], F32)
    inv = A.get([8], F32)
    ang = A.get([NT, 8], F32)
    ang2 = A.get([NT, 16], F32)
    P.dma("sp", posi, pos_d, W=[b_const], grp="c0")
    I_("dve", lambda e: e.tensor_copy(out=posf, in_=posi), R=[b_const], W=[b_const])
    for i in range(8):
        I_("pool", lambda e, i=i: e.memset(inv[:, i:i + 1], float(np.float32(500000.0) ** np.float32(-i / 8.0))),
           R=[b_const], W=[b_const])
    I_("dve", lambda e: e.tensor_tensor(out=ang, in0=posf.unsqueeze(2).to_broadcast([128, NT, 8]),
                                        in1=inv.unsqueeze(1).to_broadcast([128, NT, 8]), op=ALU.mult),
       R=[b_const], W=[b_const])
    TWO_PI = float(2 * np.pi)
    I_("dve", lambda e: e.tensor_scalar(out=ang2[:, :, 0:8], in0=ang, scalar1=float(0.5 * np.pi), scalar2=None,
                                        op0=ALU.add), R=[b_const], W=[b_const])
    I_("dve", lambda e: e.tensor_copy(out=ang2[:, :, 8:16], in_=ang), R=[b_const], W=[b_const])
    angk = A.get([NT, 16], F32)
    angi = A.get([NT, 16], I32)
    I_("dve", lambda e: e.tensor_scalar(out=angk, in0=ang2, scalar1=float(1.0 / (2 * np.pi)), scalar2=None,
                                        op0=ALU.mult), R=[b_const], W=[b_const])
    I_("dve", lambda e: e.tensor_copy(out=angi, in_=angk), R=[b_const], W=[b_const])
    I_("dve", lambda e: e.tensor_copy(out=angk, in_=angi), R=[b_const], W=[b_const])
    I_("dve", lambda e: e.scalar_tensor_tensor(out=ang2, in0=angk, scalar=-TWO_PI, in1=ang2, op0=ALU.mult,
                                               op1=ALU.add), R=[b_const], W=[b_const])
    I_("dve", lambda e: e.tensor_scalar(out=angk, in0=ang2, scalar1=float(np.pi), scalar2=TWO_PI, op0=ALU.is_gt,
                                        op1=ALU.mult), R=[b_const], W=[b_const])
    I_("dve", lambda e: e.tensor_tensor(out=ang2, in0=ang2, in1=angk, op=ALU.subtract), R=[b_const], W=[b_const])
    I_("dve", lambda e: e.tensor_scalar(out=angk, in0=ang2, scalar1=float(-np.pi), scalar2=TWO_PI, op0=ALU.is_lt,
                                        op1=ALU.mult), R=[b_const], W=[b_const])
    I_("dve", lambda e: e.tensor_tensor(out=ang2, in0=ang2, in1=angk, op=ALU.add), R=[b_const], W=[b_const])
    I_("dve", lambda e: e.tensor_scalar(out=ang2, in0=ang2, scalar1=3.141592, scalar2=-3.141592, op0=ALU.min,
                                        op1=ALU.max), R=[b_const], W=[b_const])
    I_("act", lambda e: e.activation(out=cs, in_=ang2, func=AF.Sin), R=[b_const], W=[b_const])
    P.barrier()
    A.reset()

    def rstd_from_ss(ss, rs, n, bufs, width):
        I_("dve", lambda e: e.tensor_scalar(out=rs, in0=ss, scalar1=1.0 / n, scalar2=EPS, op0=ALU.mult, op1=ALU.add),
           R=bufs, W=bufs)
        I_("act", lambda e: e.activation(out=rs, in_=rs, func=AF.Sqrt), R=bufs, W=bufs)
        I_("dve", lambda e: e.reciprocal(out=rs, in_=rs), R=bufs, W=bufs)

    def make_stage(chunk, tag):
        return ([A.get([chunk], F32) for _ in range(2)], [Buf(tag + "stg%d" % i) for i in range(2)], tag)

    def load_weight_bf16(dst, src_d, nk, ncols, gain_d, tag, chunk=2048, stage=None):
        if stage is None:
            stage = make_stage(chunk, tag)
        stg, sb, stag = stage
        bw = Buf(tag)
        gt = None
        if gain_d is not None:
            gt = A.get([8], F32)
            P.dma("sp", gt, gain_d, W=[bw], grp=tag + "g")
        it = 0
        for k in range(nk):
            for c0 in range(0, ncols, chunk):
                cw = min(chunk, ncols - c0)
                s = it % 2
                P.dma("sp", stg[s][:, 0:cw], src_d[:, k, c0:c0 + cw], W=[sb[s]], grp=stag + "s%d" % s)
                eng = ("dve", "pool")[it % 2]
                if gt is not None:
                    I_(eng, lambda e, s=s, k=k, c0=c0, cw=cw: e.tensor_scalar(
                        out=dst[:, k, c0:c0 + cw], in0=stg[s][:, 0:cw], scalar1=gt[:, k:k + 1], scalar2=None,
                        op0=ALU.mult), R=[sb[s], bw], W=[bw])
                else:
                    I_(eng, lambda e, s=s, k=k, c0=c0, cw=cw: e.tensor_copy(out=dst[:, k, c0:c0 + cw],
                                                                          in_=stg[s][:, 0:cw]), R=[sb[s]], W=[bw])
                it += 1
        return bw

    def norm_rows_to_hT(src_tile_d, xt, xb, hT_dst, bx, bh, bxs, grp, pbank, bpb):
        P.dma("sp", xt, src_tile_d, W=[bx], grp=grp)
        st = bxs["st"]
        I_("pool", lambda e: e.memset(st[:, 0:1], 0.0), W=[bxs["b"]])
        I_("act", lambda e: e.activation(out=bxs["sq"], in_=xt, func=AF.Square, accum_out=st[:, 0:1]),
           R=[bx, bxs["b"]], W=[bxs["b"]])
        rstd_from_ss(st[:, 0:1], st[:, 1:2], D, [bxs["b"]], 1)
        I_("dve", lambda e: e.tensor_scalar(out=xb, in0=xt, scalar1=st[:, 1:2], scalar2=None, op0=ALU.mult),
           R=[bx, bxs["b"]], W=[bh])
        pT = bank_bf(pbank)
        for k in range(8):
            I_("pe", lambda e, k=k: e.transpose(out=pT[:, k * 128:(k + 1) * 128], in_=xb[:, k * 128:(k + 1) * 128],
                                                identity=ident), R=[bh, b_const], W=[bpb])

    if 1 in phases:
        W = A.get([8, IN_COLS], BF16)
        bW = load_weight_bf16(W, win_d, 8, IN_COLS, gmix_d, "win", chunk=1426)
        NB = 2
        xt = [A.get([D], F32) for _ in range(NB)]
        bx = [Buf("xt%d" % i) for i in range(NB)]
        xb = [A.get([D], BF16) for _ in range(NB)]
        bh = [Buf("xb%d" % i) for i in range(NB)]
        sq = A.get([D], F32)
        st = A.get([8], F32)
        bxs = dict(sq=sq, st=st, b=Buf("xstat"))
        hTg = [A.get([8, 512], BF16) for _ in range(2)]
        bhT = [Buf("hTg%d" % i) for i in range(2)]
        stg = A.get([2632], F32)
        bstg = Buf("stg")
        tmp = A.get([1536], F32)
        btmp = Buf("tmp")
        ss = A.get([24], F32)
        rs = A.get([24], F32)
        bss = Buf("ss")
        gainrow = A.get([1536], F32)
        bgr = Buf("gainrow")
        gsrc = [0] * 6 + [1] * 6 + [2] * 6 + [3] * 2 + [4] * 4
        for hh, gi in enumerate(gsrc):
            I_("pool", lambda e, hh=hh, gi=gi: e.tensor_copy(out=gainrow[:, hh * 64:(hh + 1) * 64],
                                                            in_=gains[:, gi * 64:(gi + 1) * 64]),
               R=[b_const], W=[bgr])
        rt = A.get([4, 29, 8], F32)
        brt = Buf("rt")
        oA = [A.get([1152], BF16) for _ in range(2)]
        boA = [Buf("oA%d" % i) for i in range(2)]
        oT = [A.get([1344], BF16) for _ in range(2)]
        boT = [Buf("oT%d" % i) for i in range(2)]
        oV = [A.get([136], BF16) for _ in range(2)]
        boV = [Buf("oV%d" % i) for i in range(2)]
        tT = [A.get([1408], BF16) for _ in range(2)]
        btT = [Buf("tT%d" % i) for i in range(2)]
        gsb = [A.get([512], BF16) for _ in range(3)]
        bgsb = [Buf("gsb%d" % i) for i in range(3)]
        WI_SCALE = float((8 ** -0.5) * (64 ** -0.5))
        gcount = 0
        for grp_i in range(ntiles // 4):
            hs = grp_i % 2
            for tt in range(4):
                n = grp_i * 4 + tt
                s = n % NB
                o = n % 2
                norm_rows_to_hT(x_d[n * 128:(n + 1) * 128, :], xt[s], xb[s], None, bx[s], bh[s], bxs, "x%d" % s, 5, pb[5])
                I_("act", lambda e, hs=hs, tt=tt: e.copy(
                    out=hTg[hs][:, :, tt * 128:(tt + 1) * 128],
                    in_=bank_bf(5).rearrange("p (k c) -> p k c", k=8)), R=[pb[5]], W=[bhT[hs]])
                for bnk in range(5):
                    for k in range(8):
                        I_("pe", lambda e, bnk=bnk, k=k, hs=hs, tt=tt: e.matmul(
                            bank(bnk), lhsT=hTg[hs][:, k, tt * 128:(tt + 1) * 128],
                            rhs=W[:, k, bnk * 512:(bnk + 1) * 512], start=(k == 0), stop=(k == 7)),
                           R=[bhT[hs], bW], W=[pb[bnk]])
                for k in range(8):
                    I_("pe", lambda e, k=k, hs=hs, tt=tt: e.matmul(
                        bank(7)[:, 0:N_SM], lhsT=hTg[hs][:, k, tt * 128:(tt + 1) * 128],
                        rhs=W[:, k, N_TM:N_TM + N_SM], start=(k == 0), stop=(k == 7)),
                       R=[bhT[hs], bW], W=[pb[7]])
                I_("act", lambda e: e.copy(out=stg[:, 0:1792], in_=psum[:, 0:4, :].rearrange("p a b -> p (a b)")[:, 0:1792]),
                   R=[pb[0], pb[1], pb[2], pb[3]], W=[bstg])
                I_("dve", lambda e: e.tensor_copy(out=stg[:, 1856:2624],
                                                  in_=psum[:, 3:5, :].rearrange("p a b -> p (a b)")[:, 256:1024]),
                   R=[pb[3], pb[4]], W=[bstg])
                I_("dve", lambda e: e.tensor_copy(out=stg[:, 1792:1856], in_=bank(7)[:, 0:64]), R=[pb[7]], W=[bstg])
                I_("dve", lambda e: e.tensor_scalar(out=stg[:, 2624:2632], in0=bank(7)[:, 64:72], scalar1=WI_SCALE,
                                                    scalar2=None, op0=ALU.mult), R=[pb[7]], W=[bstg])
                I_("pool", lambda e: e.tensor_tensor(out=tmp[:, 0:1280], in0=stg[:, 0:1280], in1=stg[:, 0:1280],
                                                     op=ALU.mult), R=[bstg], W=[btmp])
                I_("pool", lambda e: e.tensor_tensor(out=tmp[:, 1280:1536], in0=stg[:, 1856:2112],
                                                     in1=stg[:, 1856:2112], op=ALU.mult), R=[bstg], W=[btmp])
                I_("dve", lambda e: e.tensor_reduce(out=ss, in_=tmp.rearrange("p (h d) -> p h d", d=64), axis=AX.X,
                                                    op=ALU.add), R=[btmp], W=[bss])
                rstd_from_ss(ss, rs, 64, [bss], 24)
                I_("dve", lambda e: e.tensor_tensor(
                    out=tmp.rearrange("p (h d) -> p h d", d=64), in0=gainrow.rearrange("p (h d) -> p h d", d=64),
                    in1=rs.unsqueeze(2).to_broadcast([128, 24, 64]), op=ALU.mult), R=[bss, bgr, btmp], W=[btmp])
                I_("pool", lambda e: e.tensor_tensor(out=stg[:, 0:1280], in0=stg[:, 0:1280], in1=tmp[:, 0:1280],
                                                     op=ALU.mult), R=[bstg, btmp], W=[bstg])
                I_("pool", lambda e: e.tensor_tensor(out=stg[:, 1856:2112], in0=stg[:, 1856:2112],
                                                     in1=tmp[:, 1280:1536], op=ALU.mult), R=[bstg, btmp], W=[bstg])
                v = stg[:, 0:1856].rearrange("p (h d) -> p h d", d=64)
                x1 = v[:, :, 0:8]
                x2 = v[:, :, 8:16]
                cosb = cs[:, n, 0:8].unsqueeze(1).to_broadcast([128, 29, 8])
                sinb = cs[:, n, 8:16].unsqueeze(1).to_broadcast([128, 29, 8])
                I_("dve", lambda e, x1=x1, cosb=cosb: e.tensor_tensor(out=rt[:, 0], in0=x1, in1=cosb, op=ALU.mult),
                   R=[bstg, b_const], W=[brt])
                I_("dve", lambda e, x2=x2, sinb=sinb: e.tensor_tensor(out=rt[:, 1], in0=x2, in1=sinb, op=ALU.mult),
                   R=[bstg, b_const], W=[brt])
                I_("pool", lambda e, x2=x2, cosb=cosb: e.tensor_tensor(out=rt[:, 2], in0=x2, in1=cosb, op=ALU.mult),
                   R=[bstg, b_const], W=[brt])
                I_("pool", lambda e, x1=x1, sinb=sinb: e.tensor_tensor(out=rt[:, 3], in0=x1, in1=sinb, op=ALU.mult),
                   R=[bstg, b_const], W=[brt])
                I_("dve", lambda e, x1=x1: e.tensor_tensor(out=x1, in0=rt[:, 0], in1=rt[:, 1], op=ALU.subtract),
                   R=[brt], W=[bstg])
                I_("dve", lambda e, x2=x2: e.tensor_tensor(out=x2, in0=rt[:, 2], in1=rt[:, 3], op=ALU.add),
                   R=[brt], W=[bstg])
                I_("act", lambda e, o=o: e.copy(out=oA[o][:, 0:768], in_=stg[:, 0:768]), R=[bstg], W=[boA[o]])
                I_("act", lambda e, o=o: e.copy(out=oA[o][:, 768:1152], in_=stg[:, 2112:2496]), R=[bstg], W=[boA[o]])
                I_("pool", lambda e, o=o: e.tensor_copy(out=oT[o], in_=stg[:, 768:2112]), R=[bstg], W=[boT[o]])
                I_("pool", lambda e, o=o: e.tensor_copy(out=oV[o], in_=stg[:, 2496:2632]), R=[bstg], W=[boV[o]])
                P.dma("pool", QKVA[n * 128:(n + 1) * 128, :], oA[o], R=[boA[o]], grp="oA%d" % o)
                P.dma("pool", VB[n * 128:(n + 1) * 128, :], oV[o], R=[boV[o]], grp="oV%d" % o)
                pT6 = bank_bf(6)
                for c in range(8):
                    I_("pe", lambda e, c=c, o=o: e.transpose(out=pT6[:, c * 128:(c + 1) * 128],
                                                             in_=oT[o][:, c * 128:(c + 1) * 128], identity=ident),
                       R=[boT[o], b_const], W=[pb[6]])
                I_("dve", lambda e, o=o: e.tensor_copy(out=tT[o][:, 0:1024], in_=pT6), R=[pb[6]], W=[btT[o]])
                I_("pe", lambda e, o=o: e.transpose(out=pT6[0:64, 0:128], in_=oT[o][:, 1024:1088], identity=ident),
                   R=[boT[o], b_const], W=[pb[6]])
                for c in range(2):
                    I_("pe", lambda e, c=c, o=o: e.transpose(out=pT6[:, 128 + c * 128:256 + c * 128],
                                                             in_=oT[o][:, 1088 + c * 128:1216 + c * 128], identity=ident),
                       R=[boT[o], b_const], W=[pb[6]])
                I_("dve", lambda e, o=o: e.tensor_copy(out=tT[o][0:64, 1024:1152], in_=pT6[0:64, 0:128]),
                   R=[pb[6]], W=[btT[o]])
                I_("dve", lambda e, o=o: e.tensor_copy(out=tT[o][:, 1152:1408], in_=pT6[:, 128:384]),
                   R=[pb[6]], W=[btT[o]])
                g = "tT%d" % o
                P.dma("pool", QBT[:, n, :], tT[o][:, 0:384], R=[btT[o]], grp=g)
                P.dma("pool", KBT[:, n * 128:(n + 1) * 128], tT[o][:, 384:512], R=[btT[o]], grp=g)
                P.dma("pool", QIT[:, n, :], tT[o][:, 512:1024], R=[btT[o]], grp=g)
                P.dma("pool", KIT[:, n * 128:(n + 1) * 128], tT[o][0:64, 1024:1152], R=[btT[o]], grp=g)
                P.dma("pool", QCT[:, n, :], tT[o][:, 1152:1408], R=[btT[o]], grp=g)
            for c in range(24):
                for k in range(8):
                    I_("pe", lambda e, c=c, k=k, hs=hs: e.matmul(
                        bank(7), lhsT=W[:, k, N_TM + N_SM + c * 128:N_TM + N_SM + (c + 1) * 128],
                        rhs=hTg[hs][:, k, :], start=(k == 0), stop=(k == 7)), R=[bhT[hs], bW], W=[pb[7]])
                gs = gcount % 3
                gcount += 1
                I_("act", lambda e, gs=gs: e.activation(out=gsb[gs], in_=bank(7), func=AF.Sigmoid),
                   R=[pb[7]], W=[bgsb[gs]])
                P.dma("sp", GT[c, :, grp_i * 512:(grp_i + 1) * 512], gsb[gs], R=[bgsb[gs]], grp="gsb%d" % gs)
        P.barrier()
        A.reset()


    ntok = ntiles * 128
    if 2 in phases:
        blk = [A.get([3, 128], BF16) for _ in range(3)]
        bblk = [Buf("blk%d" % i) for i in range(3)]
        qT = [A.get([128], BF16) for _ in range(2)]
        bqT = [Buf("qT%d" % i) for i in range(2)]
        kT = [A.get([128], BF16) for _ in range(3)]
        bkT = [Buf("kT%d" % i) for i in range(3)]
        va_ = [A.get([2, 65], BF16) for _ in range(3)]
        bva = [Buf("va%d" % i) for i in range(3)]
        PT = [A.get([512], BF16) for _ in range(2)]
        bPT = [Buf("PT%d" % i) for i in range(2)]
        oas = [A.get([130], F32) for _ in range(2)]
        boas = [Buf("oas%d" % i) for i in range(2)]
        for i in range(3):
            I_("pool", lambda e, i=i: e.memset(va_[i], 1.0), W=[bva[i]])
        u = 0
        for g, dil in enumerate((1, 4, 16)):
            m = ntok // dil
            nblk = m // 128
            assert nblk >= 1
            for r in range(dil):
                for n in range(nblk):
                    sb3 = u % 3
                    sb2 = u % 2
                    prev3 = (u - 1) % 3
                    start = r + dil * 128 * n
                    rows = QKVA[start:start + dil * 127 + 1:dil, :].rearrange("t (s c) -> t s c", s=3)[:, :, g * 128:(g + 1) * 128]
                    P.dma("sp", blk[sb3], rows, W=[bblk[sb3]], grp="blk%d" % sb3)
                    pT = bank_bf(sb2)
                    I_("pe", lambda e, pT=pT, sb3=sb3: e.transpose(out=pT[:, 0:128], in_=blk[sb3][:, 0, :], identity=ident),
                       R=[bblk[sb3], b_const], W=[pb[sb2]])
                    I_("pe", lambda e, pT=pT, sb3=sb3: e.transpose(out=pT[:, 128:256], in_=blk[sb3][:, 1, :], identity=ident),
                       R=[bblk[sb3], b_const], W=[pb[sb2]])
                    I_("dve", lambda e, pT=pT, sb2=sb2: e.tensor_copy(out=qT[sb2], in_=pT[:, 0:128]), R=[pb[sb2]], W=[bqT[sb2]])
                    I_("dve", lambda e, pT=pT, sb3=sb3: e.tensor_copy(out=kT[sb3], in_=pT[:, 128:256]), R=[pb[sb2]], W=[bkT[sb3]])
                    I_("pool", lambda e, sb3=sb3: e.tensor_copy(out=va_[sb3][:, :, 0:64],
                                                                in_=blk[sb3][:, 2, :].rearrange("p (j d) -> p j d", j=2)),
                       R=[bblk[sb3]], W=[bva[sb3]])
                    psY = bank(2 + sb2)
                    has_prev = n > 0
                    for j in range(2):
                        if has_prev:
                            I_("pe", lambda e, j=j, psY=psY, prev3=prev3, sb2=sb2: e.matmul(
                                psY[:, j * 128:(j + 1) * 128], lhsT=kT[prev3][64 * j:64 * j + 64, :],
                                rhs=qT[sb2][64 * j:64 * j + 64, :], start=True, stop=False),
                               R=[bkT[prev3], bqT[sb2]], W=[pb[2 + sb2]])
                            I_("pe", lambda e, j=j, psY=psY: e.matmul(
                                psY[:, j * 128:(j + 1) * 128], lhsT=ident, rhs=mprev, start=False, stop=True),
                               R=[b_const], W=[pb[2 + sb2]])
                        I_("pe", lambda e, j=j, psY=psY, sb3=sb3, sb2=sb2: e.matmul(
                            psY[:, 256 + j * 128:256 + (j + 1) * 128], lhsT=kT[sb3][64 * j:64 * j + 64, :],
                            rhs=qT[sb2][64 * j:64 * j + 64, :], start=True, stop=False),
                           R=[bkT[sb3], bqT[sb2]], W=[pb[2 + sb2]])
                        I_("pe", lambda e, j=j, psY=psY: e.matmul(
                            psY[:, 256 + j * 128:256 + (j + 1) * 128], lhsT=ident, rhs=mcur, start=False, stop=True),
                           R=[b_const], W=[pb[2 + sb2]])
                    lo = 0 if has_prev else 256
                    I_("act", lambda e, psY=psY, sb2=sb2, lo=lo: e.activation(out=PT[sb2][:, lo:512], in_=psY[:, lo:512],
                                                                             func=AF.Exp, scale=0.125),
                       R=[pb[2 + sb2]], W=[bPT[sb2]])
                    psZ = bank(4 + sb2)
                    for j in range(2):
                        if has_prev:
                            I_("pe", lambda e, j=j, psZ=psZ, sb2=sb2, prev3=prev3: e.matmul(
                                psZ[:, j * 65:(j + 1) * 65], lhsT=PT[sb2][:, j * 128:(j + 1) * 128],
                                rhs=va_[prev3][:, j, :], start=True, stop=False),
                               R=[bPT[sb2], bva[prev3]], W=[pb[4 + sb2]])
                        I_("pe", lambda e, j=j, psZ=psZ, sb2=sb2, sb3=sb3, hp=has_prev: e.matmul(
                            psZ[:, j * 65:(j + 1) * 65], lhsT=PT[sb2][:, 256 + j * 128:256 + (j + 1) * 128],
                            rhs=va_[sb3][:, j, :], start=(not hp), stop=True),
                           R=[bPT[sb2], bva[sb3]], W=[pb[4 + sb2]])
                    I_("dve", lambda e, psZ=psZ, sb2=sb2: e.tensor_copy(out=oas[sb2], in_=psZ[:, 0:130]),
                       R=[pb[4 + sb2]], W=[boas[sb2]])
                    P.dma("pool", OA[g, start:start + dil * 127 + 1:dil, :], oas[sb2], R=[boas[sb2]], grp="oas%d" % sb2)
                    u += 1
        P.barrier()
        A.reset()

    if 3 in phases:
        U8 = mybir.dt.uint8
        kiT2 = A.get([S], BF16)
        kbT = A.get([S], BF16)
        vba = A.get([NT, 2, 65], BF16)
        wia = A.get([NT, 8], BF16)
        bK = Buf("dsaK")
        P.dma("sp", kiT2[0:64, 0:ntok], KIT[:, 0:ntok], W=[bK], grp="dk")
        P.dma("sp", kiT2[64:128, 0:ntok], KIT[:, 0:ntok], W=[bK], grp="dk")
        P.dma("sp", kbT[:, 0:ntok], KBT[:, 0:ntok], W=[bK], grp="dk")
        I_("pool", lambda e: e.memset(vba, 1.0), W=[bK])
        vst = [A.get([136], BF16) for _ in range(2)]
        bvst = [Buf("vst%d" % i) for i in range(2)]
        for n in range(ntiles):
            s2 = n % 2
            P.dma("sp", vst[s2], VB[n * 128:(n + 1) * 128, :], W=[bvst[s2]], grp="vst%d" % s2)
            I_("pool", lambda e, n=n, s2=s2: e.tensor_copy(out=vba[:, n, :, 0:64],
                                                          in_=vst[s2][:, 0:128].rearrange("p (c d) -> p c d", c=2)),
               R=[bvst[s2], bK], W=[bK])
            I_("pool", lambda e, n=n, s2=s2: e.tensor_copy(out=wia[:, n, :], in_=vst[s2][:, 128:136]),
               R=[bvst[s2], bK], W=[bK])
        NI = 3
        qiT = [A.get([4, 128], BF16) for _ in range(2)]
        bqiT = [Buf("qiT%d" % i) for i in range(2)]
        qbT = [A.get([384], BF16) for _ in range(3)]
        bqbT = [Buf("qbT%d" % i) for i in range(3)]
        Dh = [A.get([8, 128], BF16) for _ in range(2)]
        bDh = [Buf("Dh%d" % i) for i in range(2)]
        isc = [A.get([S], F32) for _ in range(NI)]
        bisc = [Buf("isc%d" % i) for i in range(NI)]
        junk_t = A.get([S // 2], BF16)
        junk = junk_t.bitcast(U8)
        bjunk = Buf("junk")
        MBF = A.get([S], BF16)
        bMBF = [Buf("MBF%d" % c) for c in range(S // 512)]
        Rb = [A.get([2, 512], BF16) for _ in range(3)]
        bRb = [Buf("Rb%d" % i) for i in range(3)]
        PTb = [A.get([384], BF16) for _ in range(3)]
        bPTb = [Buf("PTb%d" % i) for i in range(3)]
        bst = [A.get([8], F32) for _ in range(NI)]
        bbst = [Buf("bst%d" % i) for i in range(NI)]
        Wk = [A.get([NBIS + 1], F32) for _ in range(NI)]
        pow2 = A.get([NBIS + 1], F32)
        cntb = A.get([2], F32)
        bcnt = Buf("cnt")
        rden = A.get([6], F32)
        brden = Buf("rden")
        ob = [A.get([384], BF16) for _ in range(2)]
        bob = [Buf("ob%d" % i) for i in range(2)]
        obT = [A.get([384], BF16) for _ in range(2)]
        bobT = [Buf("obT%d" % i) for i in range(2)]
        bpow = Buf("pow2")
        for k in range(NBIS + 1):
            I_("pool", lambda e, k=k: e.memset(pow2[:, k:k + 1], float(2.0 ** -(k + 1))), W=[bpow], R=[bpow])
        pair = [psum[:, 0:2, :], psum[:, 2:4, :]]
        bpair = [Buf("pair0", excl=True), Buf("pair1", excl=True)]
        accI = bank(4)
        sbank = [bank(5), bank(6)]
        accO = bank(7)
        cnt_pair = [0]
        cnt_rb = [0]
        cnt_unit = [0]
        _fr = []

        def fill_reg(e):
            if not _fr:
                _fr.append(e.to_reg(-1e30))
            return _fr[0]

        def dsa_index(i):
            s2 = i % 2
            s3 = i % NI
            L = 128 * (i + 1)
            P.dma("sp", qiT[s2], QIT[:, i, :].rearrange("p (j t) -> p j t", j=4), W=[bqiT[s2]], grp="qiT%d" % s2)
            P.dma("sp", qbT[s3], QBT[:, i, :], W=[bqbT[s3]], grp="qbT%d" % s3)
            I_("pool", lambda e: e.tensor_tensor(out=Dh[s2], in0=identf.unsqueeze(1).to_broadcast([128, 8, 128]),
                                                 in1=wia[:, i, :].unsqueeze(2).to_broadcast([128, 8, 128]), op=ALU.mult),
               R=[bK, b_const], W=[bDh[s2]])
            for c0 in range(0, L, 512):
                cw = min(512, L - c0)
                for jj in range(4):
                    pp = cnt_pair[0] % 2
                    cnt_pair[0] += 1
                    rb = cnt_rb[0] % 3
                    cnt_rb[0] += 1
                    for hh in range(2):
                        I_("pe", lambda e, pp=pp, hh=hh, jj=jj, c0=c0, cw=cw: e.matmul(
                            pair[pp][:, hh, 0:cw], lhsT=qiT[s2][64 * hh:64 * hh + 64, jj, :],
                            rhs=kiT2[64 * hh:64 * hh + 64, c0:c0 + cw], start=True, stop=True),
                           R=[bqiT[s2], bK], W=[bpair[pp]])
                    I_("act", lambda e, pp=pp, rb=rb, cw=cw: e.activation(out=Rb[rb][:, :, 0:cw], in_=pair[pp][:, :, 0:cw],
                                                                         func=AF.Relu), R=[bpair[pp]], W=[bRb[rb]])
                    for hh in range(2):
                        I_("pe", lambda e, rb=rb, hh=hh, jj=jj, cw=cw: e.matmul(
                            accI[:, 0:cw], lhsT=Dh[s2][:, 2 * jj + hh, :], rhs=Rb[rb][:, hh, 0:cw],
                            start=(jj == 0 and hh == 0), stop=(jj == 3 and hh == 1)),
                           R=[bDh[s2], bRb[rb]], W=[pb[4]])
                    if jj == 3:
                        I_("act", lambda e, c0=c0, cw=cw: e.copy(out=isc[s3][:, c0:c0 + cw], in_=accI[:, 0:cw]),
                           R=[pb[4]], W=[bisc[s3]])
                    yield
            I_("pool", lambda e: e.affine_select(out=isc[s3][:, 128 * i:128 * (i + 1)], in_=isc[s3][:, 128 * i:128 * (i + 1)],
                                                 pattern=[[-1, 128]], compare_op=ALU.is_ge, fill=fill_reg(e), base=0,
                                                 channel_multiplier=1), R=[bisc[s3]], W=[bisc[s3]])
            yield

        def dsa_bisect(i):
            s3 = i % NI
            L = 128 * (i + 1)
            b = bst[s3]
            bb = bbst[s3]
            if i < 2:
                I_("dve", lambda e: e.memset(b[:, 3:4], -1e29), W=[bb])
                return
            I_("dve", lambda e: e.tensor_reduce(out=b[:, 0:1], in_=isc[s3][:, 0:L], axis=AX.X, op=ALU.max),
               R=[bisc[s3]], W=[bb])
            I_("dve", lambda e: e.tensor_reduce(out=b[:, 1:2], in_=isc[s3][:, 0:128 * i], axis=AX.X, op=ALU.min),
               R=[bisc[s3]], W=[bb])
            I_("dve", lambda e: e.tensor_tensor(out=b[:, 2:3], in0=b[:, 0:1], in1=b[:, 1:2], op=ALU.subtract),
               R=[bb], W=[bb])
            I_("dve", lambda e: e.tensor_scalar(out=Wk[s3], in0=pow2, scalar1=b[:, 2:3], scalar2=None, op0=ALU.mult),
               R=[bb, bpow], W=[bb])
            I_("dve", lambda e: e.tensor_tensor(out=b[:, 4:5], in0=b[:, 1:2], in1=Wk[s3][:, 0:1], op=ALU.add),
               R=[bb], W=[bb])
            for k in range(NBIS):
                I_("dve", lambda e: e.tensor_scalar(out=junk[:, 0:L], in0=isc[s3][:, 0:L], scalar1=b[:, 4:5], scalar2=None,
                                                    op0=ALU.is_ge, op1=ALU.add, accum_out=cntb[:, 0:1]),
                   R=[bisc[s3], bb], W=[bjunk, bcnt])
                I_("dve", lambda e: e.tensor_scalar(out=cntb[:, 1:2], in0=cntb[:, 0:1], scalar1=TOPK - 0.5, scalar2=0.5,
                                                    op0=ALU.is_ge, op1=ALU.subtract), R=[bcnt], W=[bcnt])
                I_("dve", lambda e, k=k: e.scalar_tensor_tensor(out=b[:, 4:5], in0=cntb[:, 1:2], scalar=Wk[s3][:, k:k + 1],
                                                               in1=b[:, 4:5], op0=ALU.mult, op1=ALU.add),
                   R=[bcnt, bb], W=[bb])
            I_("dve", lambda e: e.tensor_tensor(out=b[:, 3:4], in0=b[:, 4:5], in1=Wk[s3][:, NBIS:NBIS + 1], op=ALU.subtract),
               R=[bb], W=[bb])

        def dsa_maskgen(i):
            s3 = i % NI
            L = 128 * (i + 1)
            b = bst[s3]
            for c0 in range(0, L, 512):
                cw = min(512, L - c0)
                I_("dve", lambda e, c0=c0, cw=cw: e.tensor_scalar(out=MBF[:, c0:c0 + cw], in0=isc[s3][:, c0:c0 + cw],
                                                                 scalar1=b[:, 3:4], scalar2=NEG, op0=ALU.is_lt, op1=ALU.mult),
                   R=[bisc[s3], bbst[s3]], W=[bMBF[c0 // 512]])

        def dsa_attn(i):
            s2 = i % 2
            s3 = i % NI
            for kb in range(i + 1):
                bm = bMBF[kb // 4]
                for c in range(2):
                    u = cnt_unit[0]
                    cnt_unit[0] += 1
                    sb = u % 2
                    pt = u % 3
                    psS = sbank[sb]
                    I_("pe", lambda e, c=c, kb=kb, psS=psS: e.matmul(
                        psS[:, 0:384], lhsT=kbT[64 * c:64 * c + 64, kb * 128:(kb + 1) * 128],
                        rhs=qbT[s3][64 * c:64 * c + 64, :], start=True, stop=False),
                       R=[bK, bqbT[s3]], W=[pb[5 + sb]])
                    I_("pe", lambda e, kb=kb, psS=psS: e.matmul(psS[:, 0:384], lhsT=MBF[:, kb * 128:(kb + 1) * 128], rhs=i3,
                                                               start=False, stop=True),
                       R=[bm, b_const], W=[pb[5 + sb]])
                    I_("act", lambda e, pt=pt, psS=psS: e.activation(out=PTb[pt], in_=psS[:, 0:384], func=AF.Exp, scale=0.125),
                       R=[pb[5 + sb]], W=[bPTb[pt]])
                    for g in range(3):
                        h = 3 * c + g
                        I_("pe", lambda e, pt=pt, g=g, h=h, kb=kb, c=c: e.matmul(
                            accO[:, h * 65:(h + 1) * 65], lhsT=PTb[pt][:, g * 128:(g + 1) * 128], rhs=vba[:, kb, c, :],
                            start=(kb == 0 and h == 0), stop=(kb == i), skip_group_check=True),
                            R=[bPTb[pt], bK], W=[pb[7]])
                    yield

        def dsa_final(i):
            s2 = i % 2
            av = accO[:, 0:390].rearrange("p (h e) -> p h e", e=65)
            I_("dve", lambda e: e.reciprocal(out=rden, in_=av[:, :, 64]), R=[pb[7]], W=[brden])
            I_("dve", lambda e: e.tensor_tensor(out=ob[s2].rearrange("p (h d) -> p h d", d=64), in0=av[:, :, 0:64],
                                                in1=rden.unsqueeze(2).to_broadcast([128, 6, 64]), op=ALU.mult),
               R=[pb[7], brden], W=[bob[s2]])
            pT = bank_bf(4)
            for cc in range(3):
                I_("pe", lambda e, cc=cc: e.transpose(out=pT[:, cc * 128:(cc + 1) * 128], in_=ob[s2][:, cc * 128:(cc + 1) * 128],
                                                      identity=ident), R=[bob[s2], b_const], W=[pb[4]])
            I_("act", lambda e: e.copy(out=obT[s2], in_=pT[:, 0:384]), R=[pb[4]], W=[bobT[s2]])
            P.dma("pool", OBT[:, :, i * 128:(i + 1) * 128].rearrange("c p t -> p c t"),
                  obT[s2].rearrange("p (c t) -> p c t", c=3), R=[bobT[s2]], grp="obT%d" % s2)

        def run_all(gen):
            for _ in gen:
                pass

        def interleave(ga, na, gb, nb):
            da = db = 0
            ea = eb = False
            while not (ea and eb):
                fa = da / max(na, 1)
                fb = db / max(nb, 1)
                if (not ea) and (eb or fa <= fb):
                    try:
                        next(ga)
                        da += 1
                    except StopIteration:
                        ea = True
                else:
                    try:
                        next(gb)
                        db += 1
                    except StopIteration:
                        eb = True

        def n_idx_steps(i):
            return 4 * ((128 * (i + 1) + 511) // 512) + 1

        run_all(dsa_index(0))
        if ntiles > 1:
            run_all(dsa_index(1))
        dsa_bisect(0)
        for i in range(ntiles):
            dsa_maskgen(i)
            if i + 1 < ntiles:
                dsa_bisect(i + 1)
            ga = dsa_attn(i)
            if i + 2 < ntiles:
                interleave(dsa_index(i + 2), n_idx_steps(i + 2), ga, 2 * (i + 1))
            else:
                run_all(ga)
            dsa_final(i)
        P.barrier()
        A.reset()

    if 4 in phases:
        wa = A.get([1, D], BF16)
        wb_ = A.get([3, D], BF16)
        wc = A.get([2, D], BF16)
        wo = A.get([8, D], BF16)
        wkv = A.get([8, 512], BF16)
        stage4 = make_stage(1024, "p4")
        bwa = load_weight_bf16(wa, wa_d, 1, D, None, "wa", chunk=1024, stage=stage4)
        bwb = load_weight_bf16(wb_, wb_d, 3, D, None, "wb", chunk=1024, stage=stage4)
        bwc = load_weight_bf16(wc, wc_d, 2, D, None, "wc", chunk=1024, stage=stage4)
        bwo = load_weight_bf16(wo, wo_d, 8, D, None, "wo", chunk=1024, stage=stage4)
        bwkv = load_weight_bf16(wkv, wkv_d, 8, 512, gmem_d, "wkv", chunk=1024, stage=stage4)
        P4STOP = int(os.environ.get('P4STOP', '9'))
        xt4 = [A.get([D], F32) for _ in range(2)]
        bxt4 = [Buf("xt4%d" % i) for i in range(2)]
        xb4 = A.get([D], BF16)
        bxb4 = Buf("xb4")
        sq4 = A.get([D], F32)
        st4 = A.get([8], F32)
        bxs4 = dict(sq=sq4, st=st4, b=Buf("xstat4"))
        memT = A.get([8, 128], BF16)
        bmemT = Buf("memT")
        kmT = A.get([2, 256], BF16)
        vmb = A.get([2, 256], BF16)
        bkm = Buf("kmT")
        bvm = Buf("vmb")
        ksb = A.get([256], F32)
        ksq = A.get([256], F32)
        kss = A.get([8], F32)
        kgr = A.get([256], F32)
        kmb = A.get([256], BF16)
        bks = Buf("ksb")
        ones_bf = A.get([128], BF16)
        I_("pool", lambda e: e.memset(ones_bf, 1.0), W=[bks])
        for hh in range(4):
            I_("pool", lambda e, hh=hh: e.tensor_copy(out=kgr[:, hh * 64:(hh + 1) * 64], in_=gains[:, 5 * 64:6 * 64]),
               R=[b_const], W=[bks])
        for mt in range(2 if P4STOP >= 1 else 0):
            norm_rows_to_hT(mem_d[mt * 128:(mt + 1) * 128, :], xt4[0], xb4, None, bxt4[0], bxb4, bxs4, "xt40", 5, pb[5])
            I_("act", lambda e: e.copy(out=memT, in_=bank_bf(5).rearrange("p (k c) -> p k c", k=8)), R=[pb[5]], W=[bmemT])
            P4SUB = int(os.environ.get('P4SUB', '9'))
            if P4SUB < 1:
                continue
            for k in range(8):
                I_("pe", lambda e, k=k: e.matmul(bank(0), lhsT=memT[:, k, :], rhs=wkv[:, k, :], start=(k == 0), stop=(k == 7)),
                   R=[bmemT, bwkv], W=[pb[0]])
            P4V = int(os.environ.get('P4V', '3'))
            if P4V & 1:
                I_("act", lambda e, mt=mt: e.copy(out=vmb[:, mt, :], in_=bank(0)[:, 256:512]), R=[pb[0]], W=[bvm])
            if P4V & 2:
                I_("dve", lambda e: e.tensor_copy(out=ksb, in_=bank(0)[:, 0:256]), R=[pb[0]], W=[bks])
            if P4SUB < 2:
                continue
            I_("pool", lambda e: e.tensor_tensor(out=ksq, in0=ksb, in1=ksb, op=ALU.mult), R=[bks], W=[bks])
            I_("dve", lambda e: e.tensor_reduce(out=kss[:, 0:4], in_=ksq.rearrange("p (h d) -> p h d", d=64), axis=AX.X,
                                                op=ALU.add), R=[bks], W=[bks])
            rstd_from_ss(kss[:, 0:4], kss[:, 4:8], 64, [bks], 4)
            I_("dve", lambda e: e.tensor_tensor(out=ksq.rearrange("p (h d) -> p h d", d=64),
                                                in0=kgr.rearrange("p (h d) -> p h d", d=64),
                                                in1=kss[:, 4:8].unsqueeze(2).to_broadcast([128, 4, 64]), op=ALU.mult),
               R=[bks], W=[bks])
            I_("dve", lambda e: e.tensor_tensor(out=kmb, in0=ksb, in1=ksq, op=ALU.mult), R=[bks], W=[bks])
            if P4SUB < 3:
                continue
            for j in range(2):
                I_("pe", lambda e, j=j: e.transpose(out=bank_bf(1)[:, j * 128:(j + 1) * 128], in_=kmb[:, j * 128:(j + 1) * 128],
                                                    identity=ident), R=[bks, b_const], W=[pb[1]])
            I_("dve", lambda e, mt=mt: e.tensor_copy(out=kmT[:, :, mt * 128:(mt + 1) * 128],
                                                     in_=bank_bf(1)[:, 0:256].rearrange("p (j m) -> p j m", j=2)),
               R=[pb[1]], W=[bkm])
        gtg = [A.get([24, 512], BF16) for _ in range(2)]
        bgtg = [Buf("gtg%d" % i) for i in range(2)]
        obg = [A.get([3, 512], BF16) for _ in range(2)]
        bobg = [Buf("obg%d" % i) for i in range(2)]
        qcg = [A.get([2, 512], BF16) for _ in range(2)]
        bqcg = [Buf("qcg%d" % i) for i in range(2)]
        oaT = [A.get([512], BF16) for _ in range(2)]
        boaT = [Buf("oaT%d" % i) for i in range(2)]
        ocT = [A.get([2, 512], BF16) for _ in range(2)]
        bocT = [Buf("ocT%d" % i) for i in range(2)]
        mT = [A.get([8, 512], BF16) for _ in range(2)]
        bmT = [Buf("mT%d" % i) for i in range(2)]
        oa3 = [A.get([3, 130], F32) for _ in range(2)]
        boa3 = [Buf("oa3%d" % i) for i in range(2)]
        oasum = A.get([130], F32)
        oard = A.get([2], F32)
        oab = A.get([128], BF16)
        boas4 = Buf("oasum")
        PTc = [A.get([512], BF16) for _ in range(4)]
        bPTc = [Buf("PTc%d" % i) for i in range(4)]
        rdc = [A.get([512], F32) for _ in range(2)]
        brdc = [Buf("rdc%d" % i) for i in range(2)]
        tm = [A.get([512], F32) for _ in range(3)]
        btm = [Buf("tm%d" % i) for i in range(3)]
        x1t = [A.get([D], F32) for _ in range(2)]
        bx1t = [Buf("x1t%d" % i) for i in range(2)]
        ngrp = ntiles // 4
        cpt = [0]
        chd = [0]
        for G in range(ngrp if P4STOP >= 2 else 0):
            g2 = G % 2
            t0 = G * 512
            for c4 in range(0, 24, 4):
                P.dma("sp", gtg[g2][:, c4:c4 + 4, :], GT[c4:c4 + 4, :, t0:t0 + 512].rearrange("c p t -> p c t"),
                      W=[bgtg[g2]], grp="gtg%d" % g2)
            P.dma("sp", obg[g2], OBT[:, :, t0:t0 + 512].rearrange("c p t -> p c t"), W=[bobg[g2]], grp="obg%d" % g2)
            for j in range(2):
                P.dma("sp", qcg[g2][:, j, :].rearrange("p (n t) -> p n t", n=4), QCT[:, 4 * G:4 * G + 4, j * 128:(j + 1) * 128],
                      W=[bqcg[g2]], grp="qcg%d" % g2)
            for tt in range(4):
                n = 4 * G + tt
                o2 = n % 2
                P.dma("sp", oa3[o2], OA[:, n * 128:(n + 1) * 128, :].rearrange("g t e -> t g e"), W=[boa3[o2]],
                      grp="oa3%d" % o2)
                I_("pool", lambda e, o2=o2: e.tensor_tensor(out=oasum, in0=oa3[o2][:, 0, :], in1=oa3[o2][:, 1, :], op=ALU.add),
                   R=[boa3[o2]], W=[boas4])
                I_("pool", lambda e, o2=o2: e.tensor_tensor(out=oasum, in0=oasum, in1=oa3[o2][:, 2, :], op=ALU.add),
                   R=[boa3[o2], boas4], W=[boas4])
                osv = oasum.rearrange("p (j e) -> p j e", e=65)
                I_("dve", lambda e, osv=osv: e.reciprocal(out=oard, in_=osv[:, :, 64]), R=[boas4], W=[boas4])
                I_("dve", lambda e, osv=osv: e.tensor_tensor(out=oab.rearrange("p (j d) -> p j d", d=64), in0=osv[:, :, 0:64],
                                                             in1=oard.unsqueeze(2).to_broadcast([128, 2, 64]), op=ALU.mult),
                   R=[boas4], W=[boas4])
                I_("pe", lambda e, tt=tt: e.transpose(out=bank_bf(0)[:, tt * 128:(tt + 1) * 128], in_=oab, identity=ident),
                   R=[boas4, b_const], W=[pb[0]])
            I_("act", lambda e, g2=g2: e.copy(out=oaT[g2], in_=bank_bf(0)[:, 0:512]), R=[pb[0]], W=[boaT[g2]])
            for h in range(4 if P4STOP >= 3 else 0):
                j, hh = h // 2, h % 2
                hs2 = chd[0] % 2
                chd[0] += 1
                numb, denb = 4 + 2 * hs2, 5 + 2 * hs2
                pts = []
                for mt in range(2):
                    sbk = 2 * hs2 + mt
                    pt = cpt[0] % 4
                    cpt[0] += 1
                    pts.append(pt)
                    I_("pe", lambda e, sbk=sbk, hh=hh, j=j, mt=mt, g2=g2: e.matmul(
                        bank(sbk), lhsT=kmT[64 * hh:64 * hh + 64, j, mt * 128:(mt + 1) * 128],
                        rhs=qcg[g2][64 * hh:64 * hh + 64, j, :], start=True, stop=True),
                       R=[bkm, bqcg[g2]], W=[pb[sbk]])
                    I_("act", lambda e, sbk=sbk, pt=pt: e.activation(out=PTc[pt], in_=bank(sbk), func=AF.Exp, scale=0.125),
                       R=[pb[sbk]], W=[bPTc[pt]])
                for mt in range(2):
                    I_("pe", lambda e, mt=mt, j=j, numb=numb, pt=pts[mt]: e.matmul(
                        bank(numb), lhsT=vmb[:, mt, j * 128:(j + 1) * 128], rhs=PTc[pt], start=(mt == 0), stop=(mt == 1)),
                       R=[bvm, bPTc[pts[mt]]], W=[pb[numb]])
                for mt in range(2):
                    I_("pe", lambda e, mt=mt, denb=denb, pt=pts[mt]: e.matmul(
                        bank(denb), lhsT=ones_bf, rhs=PTc[pt], start=(mt == 0), stop=(mt == 1)),
                       R=[bks, bPTc[pts[mt]]], W=[pb[denb]])
                lo_, hi_ = 64 * hh, 64 * hh + 64
                I_("dve", lambda e, denb=denb, hs2=hs2, lo_=lo_, hi_=hi_: e.reciprocal(out=rdc[hs2][lo_:hi_, :],
                                                                                     in_=bank(denb)[lo_:hi_, :]),
                   R=[pb[denb]], W=[brdc[hs2]])
                I_("dve", lambda e, numb=numb, hs2=hs2, lo_=lo_, hi_=hi_, j=j, g2=g2: e.tensor_tensor(
                    out=ocT[g2][lo_:hi_, j, :], in0=bank(numb)[lo_:hi_, :], in1=rdc[hs2][lo_:hi_, :], op=ALU.mult),
                   R=[pb[numb], brdc[hs2]], W=[bocT[g2]])
            for oc in range(8 if P4STOP >= 4 else 0):
                bs = 3 * (oc % 2)
                cs_ = slice(oc * 128, (oc + 1) * 128)
                I_("pe", lambda e, bs=bs, cs_=cs_, g2=g2: e.matmul(bank(bs), lhsT=wa[:, 0, cs_], rhs=oaT[g2], start=True, stop=True),
                   R=[bwa, boaT[g2]], W=[pb[bs]])
                for k in range(3):
                    I_("pe", lambda e, bs=bs, cs_=cs_, k=k, g2=g2: e.matmul(bank(bs + 1), lhsT=wb_[:, k, cs_], rhs=obg[g2][:, k, :],
                                                                          start=(k == 0), stop=(k == 2)),
                       R=[bwb, bobg[g2]], W=[pb[bs + 1]])
                for k in range(2):
                    I_("pe", lambda e, bs=bs, cs_=cs_, k=k, g2=g2: e.matmul(bank(bs + 2), lhsT=wc[:, k, cs_], rhs=ocT[g2][:, k, :],
                                                                          start=(k == 0), stop=(k == 1)),
                       R=[bwc, bocT[g2]], W=[pb[bs + 2]])
                I_("dve", lambda e, bs=bs, oc=oc, g2=g2: e.tensor_tensor(out=tm[0], in0=bank(bs), in1=gtg[g2][:, oc, :], op=ALU.mult),
                   R=[pb[bs], bgtg[g2]], W=[btm[0]])
                I_("dve", lambda e, bs=bs, oc=oc, g2=g2: e.tensor_tensor(out=tm[1], in0=bank(bs + 1), in1=gtg[g2][:, 8 + oc, :],
                                                                       op=ALU.mult), R=[pb[bs + 1], bgtg[g2]], W=[btm[1]])
                I_("dve", lambda e, bs=bs, oc=oc, g2=g2: e.tensor_tensor(out=tm[2], in0=bank(bs + 2), in1=gtg[g2][:, 16 + oc, :],
                                                                       op=ALU.mult), R=[pb[bs + 2], bgtg[g2]], W=[btm[2]])
                I_("pool", lambda e: e.tensor_tensor(out=tm[0], in0=tm[0], in1=tm[1], op=ALU.add), R=[btm[0], btm[1]], W=[btm[0]])
                I_("pool", lambda e, oc=oc, g2=g2: e.tensor_tensor(out=mT[g2][:, oc, :], in0=tm[0], in1=tm[2], op=ALU.add),
                   R=[btm[0], btm[2]], W=[bmT[g2]])
            for tt in range(4 if P4STOP >= 5 else 0):
                n = 4 * G + tt
                o2 = n % 2
                P.dma("sp", xt4[o2], x_d[n * 128:(n + 1) * 128, :], W=[bxt4[o2]], grp="xt4%d" % o2)
                for hf in range(2):
                    for k in range(8):
                        I_("pe", lambda e, hf=hf, k=k, tt=tt, g2=g2: e.matmul(
                            bank(6 + hf), lhsT=mT[g2][:, k, tt * 128:(tt + 1) * 128], rhs=wo[:, k, hf * 512:(hf + 1) * 512],
                            start=(k == 0), stop=(k == 7)), R=[bmT[g2], bwo], W=[pb[6 + hf]])
                I_("dve", lambda e, o2=o2: e.tensor_tensor(out=x1t[o2], in0=psum[:, 6:8, :].rearrange("p a b -> p (a b)"),
                                                           in1=xt4[o2], op=ALU.add), R=[pb[6], pb[7], bxt4[o2]], W=[bx1t[o2]])
                P.dma("pool", X1[n * 128:(n + 1) * 128, :], x1t[o2], R=[bx1t[o2]], grp="x1t%d" % o2)
        P.barrier()
        A.reset()

    if 5 in phases:
        W1 = A.get([8, 4096], BF16)
        W2 = A.get([32, D], BF16)
        stage5 = make_stage(2048, "p5")
        bW1 = load_weight_bf16(W1, w1_d, 8, 4096, gmlp_d, "w1", chunk=2048, stage=stage5)
        bW2 = load_weight_bf16(W2, w2_d, 32, D, None, "w2", chunk=1024, stage=stage5)
        xt5 = [A.get([D], F32) for _ in range(4)]
        bxt5 = [Buf("xt5%d" % i) for i in range(4)]
        xb5 = [A.get([D], BF16) for _ in range(2)]
        bxb5 = [Buf("xb5%d" % i) for i in range(2)]
        sq5 = A.get([D], F32)
        st5 = A.get([8], F32)
        bxs5 = dict(sq=sq5, st=st5, b=Buf("xstat5"))
        h2T = [A.get([8, 256], BF16) for _ in range(2)]
        bh2T = [Buf("h2T%d" % i) for i in range(2)]
        rr = [A.get([256], BF16) for _ in range(3)]
        brr = [Buf("rr%d" % i) for i in range(3)]
        aa = [A.get([256], BF16) for _ in range(3)]
        baa = [Buf("aa%d" % i) for i in range(3)]
        ot = [A.get([D], F32) for _ in range(2)]
        bot = [Buf("ot%d" % i) for i in range(2)]
        ng5 = ntiles // 2
        cc5 = [0]
        for G in range(ng5):
            g2 = G % 2
            for tt in range(2):
                n = 2 * G + tt
                s4 = n % 4
                s2 = n % 2
                norm_rows_to_hT(X1[n * 128:(n + 1) * 128, :], xt5[s4], xb5[s2], None, bxt5[s4], bxb5[s2], bxs5,
                                "xt5%d" % s4, 7, pb[7])
                I_("act", lambda e, g2=g2, tt=tt: e.copy(out=h2T[g2][:, :, tt * 128:(tt + 1) * 128],
                                                         in_=bank_bf(7).rearrange("p (k c) -> p k c", k=8)),
                   R=[pb[7]], W=[bh2T[g2]])
            for c in range(32):
                ub = 4 + cc5[0] % 3
                r3 = cc5[0] % 3
                cc5[0] += 1
                for k in range(8):
                    I_("pe", lambda e, ub=ub, k=k, c=c, g2=g2: e.matmul(
                        bank(ub)[:, 0:256], lhsT=W1[:, k, c * 128:(c + 1) * 128], rhs=h2T[g2][:, k, :],
                        start=(k == 0), stop=(k == 7)), R=[bW1, bh2T[g2]], W=[pb[ub]])
                I_("act", lambda e, ub=ub, r3=r3: e.activation(out=rr[r3], in_=bank(ub)[:, 0:256], func=AF.Relu),
                   R=[pb[ub]], W=[brr[r3]])
                I_("dve", lambda e, r3=r3: e.tensor_tensor(out=aa[r3], in0=rr[r3], in1=rr[r3], op=ALU.mult),
                   R=[brr[r3]], W=[baa[r3]])
                for tt in range(2):
                    for hf in range(2):
                        I_("pe", lambda e, tt=tt, hf=hf, r3=r3, c=c: e.matmul(
                            bank(tt * 2 + hf), lhsT=aa[r3][:, tt * 128:(tt + 1) * 128], rhs=W2[:, c, hf * 512:(hf + 1) * 512],
                            start=(c == 0), stop=(c == 31)), R=[baa[r3], bW2], W=[pb[tt * 2 + hf]])
            for tt in range(2):
                n = 2 * G + tt
                s4 = n % 4
                s2 = n % 2
                I_("dve", lambda e, tt=tt, s2=s2, s4=s4: e.tensor_tensor(
                    out=ot[s2], in0=psum[:, 2 * tt:2 * tt + 2, :].rearrange("p a b -> p (a b)"), in1=xt5[s4], op=ALU.add),
                   R=[pb[2 * tt], pb[2 * tt + 1], bxt5[s4]], W=[bot[s2]])
                P.dma("pool", out_d[n * 128:(n + 1) * 128, :], ot[s2], R=[bot[s2]], grp="ot%d" % s2)
        P.barrier()
        A.reset()

    P.barrier()
    P.emit()
    return nc, P


def _host_layout(inputs, b):
    def kmaj(w):
        k = w.shape[0] // 128
        return np.ascontiguousarray(w.reshape(k, 128, w.shape[1]).transpose(1, 0, 2))
    perm = _win_perm()
    m = {
        "x": np.ascontiguousarray(inputs["x"][b]),
        "mem": np.ascontiguousarray(inputs["mem"][b]),
        "pos": np.ascontiguousarray(inputs["positions"][b].reshape(NT, 128).T),
        "w_in": kmaj(inputs["w_in"][0][:, perm]),
        "g_mix": np.ascontiguousarray(inputs["g_mix"][0].reshape(8, 128).T),
        "g_mem": np.ascontiguousarray(inputs["g_mem"][0].reshape(8, 128).T),
        "g_mlp": np.ascontiguousarray(inputs["g_mlp"][0].reshape(8, 128).T),
        "gains": np.concatenate([inputs[k][0] for k in ("g_qa", "g_ka", "g_qb", "g_kb", "g_qc", "g_kc")])[None, :],
        "w_mem_kv": kmaj(inputs["w_mem_kv"][0]),
        "w_a": kmaj(inputs["w_a"][0]),
        "w_b": kmaj(inputs["w_b"][0]),
        "w_c": kmaj(inputs["w_c"][0]),
        "w_o": kmaj(inputs["w_o"][0]),
        "w_1": kmaj(inputs["w_1"][0]),
        "w_2": kmaj(inputs["w_2"][0]),
    }
    return {k: np.ascontiguousarray(v) for k, v in m.items()}


def kernel(**inputs):
    inputs = {k: np.asarray(v) for k, v in inputs.items()}
    nc, _ = build_program()
    in_maps = [_host_layout(inputs, b) for b in range(8)]
    res = run_bass_kernel_spmd(nc, in_maps, core_ids=list(range(8)))
    return np.stack([np.asarray(r["out"]) for r in res.results], axis=0).astype(np.float32)
```

```python
import bisect
import os
import numpy as np
import concourse.bass as bass
import concourse.mybir as mybir
from concourse.bass_utils import run_bass_kernel_spmd

F32 = mybir.dt.float32
BF16 = mybir.dt.bfloat16
I32 = mybir.dt.int32
ALU = mybir.AluOpType
AF = mybir.ActivationFunctionType
AX = mybir.AxisListType

ENGS = ("pe", "act", "dve", "pool", "sp")

S = 8192
D = 1024
NT = S // 128
EPS = 1e-6
NEG = -30000.0
TOPK = 256
NBIS = 16


class Buf:
    __slots__ = ("name", "w", "r", "excl")

    def __init__(self, name="", excl=False):
        self.name = name
        self.w = None
        self.r = []
        self.excl = excl


class Prog:
    def __init__(self, nc):
        self.nc = nc
        self.ins = []
        self.by_eng = {e: [] for e in ENGS}
        self.groups = {}

    def _deps(self, R, W):
        deps = set()
        for b in R:
            if b.w is not None:
                deps.add(b.w)
        for b in W:
            if b.w is not None:
                deps.add(b.w)
            deps.update(b.r)
        return deps

    def _commit(self, iid, R, W):
        for b in R:
            b.r.append(iid)
        for b in W:
            b.w = iid
            b.r = []

    def I(self, eng, fn, R=(), W=()):
        iid = len(self.ins)
        W = list(W) + [b for b in R if b.excl and b not in W]
        deps = self._deps(R, W)
        raw = set(b.w for b in R if b.w is not None)
        self.ins.append(dict(eng=eng, fn=fn, deps=deps, raw=raw, grp=None))
        self.by_eng[eng].append(iid)
        self._commit(iid, R, W)
        return iid

    def dma(self, eng, out, in_, R=(), W=(), grp="g", **kw):
        iid = len(self.ins)
        deps = self._deps(R, W)
        raw = set(b.w for b in R if b.w is not None)
        self.groups.setdefault(grp, []).append(iid)
        self.ins.append(dict(eng=eng, fn=(lambda e, o=out, i=in_, k=kw: e.dma_start(out=o, in_=i, **k)),
                             deps=deps, raw=raw, grp=grp))
        self.by_eng[eng].append(iid)
        self._commit(iid, R, W)
        return iid

    def barrier(self):
        alld = set()
        for e in ENGS:
            for k in reversed(self.by_eng[e]):
                if self.ins[k]["fn"] is not None:
                    alld.add(k)
                    break
        for g, l in self.groups.items():
            if l:
                alld.add(l[-1])
        for e in ENGS:
            iid = len(self.ins)
            self.ins.append(dict(eng=e, fn=None, deps=set(alld), raw=set(alld), grp=None))
            self.by_eng[e].append(iid)

    def emit(self):
        nc = self.nc
        ins = self.ins
        n = len(ins)
        is_target = [False] * n
        for k in range(n):
            for d in ins[k]["deps"]:
                is_target[d] = True
        ms_val = [0] * n
        cnt = {e: 0 for e in ENGS}
        for k in range(n):
            it = ins[k]
            if it["grp"] is None and it["fn"] is not None and is_target[k]:
                cnt[it["eng"]] += 1
                ms_val[k] = cnt[it["eng"]]
        self.esem = {e: nc.alloc_semaphore("sem_" + e) for e in ENGS}
        self.gsem = {g: nc.alloc_semaphore("dsem_" + g) for g in self.groups}
        self.nwaits = 0
        prog = self

        def run_engine(ename, eobj):
            seen = {}
            for k in prog.by_eng[ename]:
                it = ins[k]
                need = {}
                for d in it["deps"]:
                    dd = ins[d]
                    if dd["grp"] is not None:
                        g = dd["grp"]
                        c = bisect.bisect_left(prog.groups[g], k)
                        key = ("g", g)
                        val = 16 * c
                    else:
                        if dd["fn"] is None:
                            continue
                        if dd["eng"] == ename and it["grp"] is None and it["fn"] is not None:
                            if ename == "pe":
                                continue
                            if d not in it["raw"]:
                                continue
                        key = ("e", dd["eng"])
                        val = ms_val[d]
                    if val > need.get(key, 0):
                        need[key] = val
                for key, val in need.items():
                    if seen.get(key, 0) >= val:
                        continue
                    seen[key] = val
                    sem = prog.esem[key[1]] if key[0] == "e" else prog.gsem[key[1]]
                    eobj.wait_ge(sem, val)
                    prog.nwaits += 1
                if it["fn"] is None:
                    continue
                bi = it["fn"](eobj)
                if it["grp"] is not None:
                    bi.then_inc(prog.gsem[it["grp"]], 16)
                elif is_target[k]:
                    bi.then_inc(prog.esem[ename], 1)

        with nc.Block() as block:
            @block.tensor
            def _(e):
                run_engine("pe", e)

            @block.scalar
            def _(e):
                run_engine("act", e)

            @block.vector
            def _(e):
                run_engine("dve", e)

            @block.gpsimd
            def _(e):
                run_engine("pool", e)

            @block.sync
            def _(e):
                run_engine("sp", e)


class Arena:
    def __init__(self, nc, nbytes):
        self.t = nc.alloc_sbuf_tensor("arena", [128, nbytes // 2], BF16)
        self.n = nbytes // 2
        self.base = 0
        self.p = 0

    def mark(self):
        self.base = self.p

    def reset(self):
        self.p = self.base

    def get(self, shape, dtype):
        ne = int(np.prod(shape))
        w = ne * (2 if dtype in (F32, I32) else 1)
        w = (w + 31) // 32 * 32
        assert self.p + w <= self.n, ("SBUF arena overflow", self.p, w, self.n)
        ap = self.t[:, self.p:self.p + w]
        self.p += w
        if dtype != BF16:
            ap = ap.bitcast(dtype)
        ap = ap[:, 0:ne]
        if len(shape) == 2:
            ap = ap.rearrange("p (a b) -> p a b", a=shape[0])
        elif len(shape) == 3:
            ap = ap.rearrange("p (a b c) -> p a b c", a=shape[0], b=shape[1])
        return ap


IN_COLS = 5704
N_TM = 2560
N_SM = 72
N_G = 3072


def _win_perm():
    o = {}
    acc = 0
    for nme, w in (("qa", 384), ("ka", 384), ("va", 384), ("qb", 384), ("kb", 128), ("vb", 128),
                   ("qi", 512), ("ki", 64), ("wi", 8), ("qc", 256), ("gl", 3072)):
        o[nme] = np.arange(acc, acc + w)
        acc += w
    qb = o["qb"].reshape(2, 3, 64).transpose(1, 0, 2).reshape(-1)
    return np.concatenate([o["qa"], o["ka"], qb, o["kb"], o["qi"], o["qc"], o["va"], o["vb"],
                           o["ki"], o["wi"], o["gl"]])


def build_program(debug=(), phases=(1, 2, 3, 4, 5), ntiles=NT):
    nc = bass.Bass("TRN2", target_bir_lowering=False)
    P = Prog(nc)
    I_ = P.I

    def din(name, shape, dt=F32):
        return nc.dram_tensor(name, shape, dt, kind="ExternalInput").ap()

    def dscr(name, shape, dt):
        return nc.dram_tensor(name, shape, dt, kind=("ExternalOutput" if name in debug else "Internal")).ap()

    x_d = din("x", [S, D])
    mem_d = din("mem", [256, D])
    pos_d = din("pos", [128, NT], I32)
    win_d = din("w_in", [128, 8, IN_COLS])
    gmix_d = din("g_mix", [128, 8])
    gmem_d = din("g_mem", [128, 8])
    gmlp_d = din("g_mlp", [128, 8])
    gains_d = din("gains", [1, 384])
    wkv_d = din("w_mem_kv", [128, 8, 512])
    wa_d = din("w_a", [128, 1, D])
    wb_d = din("w_b", [128, 3, D])
    wc_d = din("w_c", [128, 2, D])
    wo_d = din("w_o", [128, 8, D])
    w1_d = din("w_1", [128, 8, 4096])
    w2_d = din("w_2", [128, 32, D])
    out_d = nc.dram_tensor("out", [S, D], F32, kind="ExternalOutput").ap()

    QKVA = dscr("s_qkva", [S, 1152], BF16)
    QBT = dscr("s_qbt", [128, NT, 384], BF16)
    KBT = dscr("s_kbt", [128, S], BF16)
    VB = dscr("s_vb", [S, 136], BF16)
    QIT = dscr("s_qit", [128, NT, 512], BF16)
    KIT = dscr("s_kit", [64, S], BF16)
    QCT = dscr("s_qct", [128, NT, 256], BF16)
    GT = dscr("s_gt", [24, 128, S], BF16)
    OA = dscr("s_oa", [3, S, 130], F32)
    OBT = dscr("s_obt", [3, 128, S], BF16)
    X1 = dscr("s_x1", [S, D], F32)

    A = Arena(nc, 206 * 1024)
    psum = nc.alloc_psum_tensor("psum", [128, 8, 512], F32)

    def bank(i):
        return psum[:, i, :]

    def bank_bf(i):
        return psum[:, i, :].bitcast(BF16)

    pb = [Buf("bank%d" % i, excl=True) for i in range(8)]

    ident = A.get([128], BF16)
    identf = A.get([128], F32)
    mcur = A.get([128], BF16)
    mprev = A.get([128], BF16)
    i3 = A.get([384], BF16)
    gains = A.get([384], F32)
    cs = A.get([NT, 16], F32)
    b_const = Buf("const")
    I_("pool", lambda e: e.memset(identf, 0.0), W=[b_const])
    I_("pool", lambda e: e.affine_select(out=identf, in_=identf, pattern=[[-1, 128]], compare_op=ALU.not_equal,
                                         fill=1.0, base=0, channel_multiplier=1), R=[b_const], W=[b_const])
    I_("pool", lambda e: e.tensor_copy(out=ident, in_=identf), R=[b_const], W=[b_const])
    for g in range(3):
        I_("pool", lambda e, g=g: e.tensor_copy(out=i3[:, g * 128:(g + 1) * 128], in_=identf), R=[b_const], W=[b_const])
    I_("pool", lambda e: e.memset(mcur, 0.0), W=[b_const], R=[b_const])
    I_("pool", lambda e: e.affine_select(out=mcur, in_=mcur, pattern=[[1, 128]], compare_op=ALU.is_ge,
                                         fill=NEG, base=0, channel_multiplier=-1), R=[b_const], W=[b_const])
    I_("pool", lambda e: e.memset(mprev, 0.0), W=[b_const], R=[b_const])
    I_("pool", lambda e: e.affine_select(out=mprev, in_=mprev, pattern=[[-1, 128]], compare_op=ALU.is_ge,
                                         fill=NEG, base=0, channel_multiplier=1), R=[b_const], W=[b_const])
    P.dma("sp", gains, gains_d.partition_broadcast(128), W=[b_const], grp="c0")
    A.mark()
    posi = A.get([NT], I32)
    posf = A.get([NT], F32)
    inv = A.get([8], F32)
    ang = A.get([NT, 8], F32)
    ang2 = A.get([NT, 16], F32)
    P.dma("sp", posi, pos_d, W=[b_const], grp="c0")
    I_("dve", lambda e: e.tensor_copy(out=posf, in_=posi), R=[b_const], W=[b_const])
    for i in range(8):
        I_("pool", lambda e, i=i: e.memset(inv[:, i:i + 1], float(np.float32(500000.0) ** np.float32(-i / 8.0))),
           R=[b_const], W=[b_const])
    I_("dve", lambda e: e.tensor_tensor(out=ang, in0=posf.unsqueeze(2).to_broadcast([128, NT, 8]),
                                        in1=inv.unsqueeze(1).to_broadcast([128, NT, 8]), op=ALU.mult),
       R=[b_const], W=[b_const])
    TWO_PI = float(2 * np.pi)
    I_("dve", lambda e: e.tensor_scalar(out=ang2[:, :, 0:8], in0=ang, scalar1=float(0.5 * np.pi), scalar2=None,
                                        op0=ALU.add), R=[b_const], W=[b_const])
    I_("dve", lambda e: e.tensor_copy(out=ang2[:, :, 8:16], in_=ang), R=[b_const], W=[b_const])
    angk = A.get([NT, 16], F32)
    angi = A.get([NT, 16], I32)
    I_("dve", lambda e: e.tensor_scalar(out=angk, in0=ang2, scalar1=float(1.0 / (2 * np.pi)), scalar2=None,
                                        op0=ALU.mult), R=[b_const], W=[b_const])
    I_("dve", lambda e: e.tensor_copy(out=angi, in_=angk), R=[b_const], W=[b_const])
    I_("dve", lambda e: e.tensor_copy(out=angk, in_=angi), R=[b_const], W=[b_const])
    I_("dve", lambda e: e.scalar_tensor_tensor(out=ang2, in0=angk, scalar=-TWO_PI, in1=ang2, op0=ALU.mult,
                                               op1=ALU.add), R=[b_const], W=[b_const])
    I_("dve", lambda e: e.tensor_scalar(out=angk, in0=ang2, scalar1=float(np.pi), scalar2=TWO_PI, op0=ALU.is_gt,
                                        op1=ALU.mult), R=[b_const], W=[b_const])
    I_("dve", lambda e: e.tensor_tensor(out=ang2, in0=ang2, in1=angk, op=ALU.subtract), R=[b_const], W=[b_const])
    I_("dve", lambda e: e.tensor_scalar(out=angk, in0=ang2, scalar1=float(-np.pi), scalar2=TWO_PI, op0=ALU.is_lt,
                                        op1=ALU.mult), R=[b_const], W=[b_const])
    I_("dve", lambda e: e.tensor_tensor(out=ang2, in0=ang2, in1=angk, op=ALU.add), R=[b_const], W=[b_const])
    I_("dve", lambda e: e.tensor_scalar(out=ang2, in0=ang2, scalar1=3.141592, scalar2=-3.141592, op0=ALU.min,
                                        op1=ALU.max), R=[b_const], W=[b_const])
    I_("act", lambda e: e.activation(out=cs, in_=ang2, func=AF.Sin), R=[b_const], W=[b_const])
    P.barrier()
    A.reset()

    def rstd_from_ss(ss, rs, n, bufs, width):
        I_("dve", lambda e: e.tensor_scalar(out=rs, in0=ss, scalar1=1.0 / n, scalar2=EPS, op0=ALU.mult, op1=ALU.add),
           R=bufs, W=bufs)
        I_("act", lambda e: e.activation(out=rs, in_=rs, func=AF.Sqrt), R=bufs, W=bufs)
        I_("dve", lambda e: e.reciprocal(out=rs, in_=rs), R=bufs, W=bufs)

    def make_stage(chunk, tag):
        return ([A.get([chunk], F32) for _ in range(2)], [Buf(tag + "stg%d" % i) for i in range(2)], tag)

    def load_weight_bf16(dst, src_d, nk, ncols, gain_d, tag, chunk=2048, stage=None):
        if stage is None:
            stage = make_stage(chunk, tag)
        stg, sb, stag = stage
        bw = Buf(tag)
        gt = None
        if gain_d is not None:
            gt = A.get([8], F32)
            P.dma("sp", gt, gain_d, W=[bw], grp=tag + "g")
        it = 0
        for k in range(nk):
            for c0 in range(0, ncols, chunk):
                cw = min(chunk, ncols - c0)
                s = it % 2
                P.dma("sp", stg[s][:, 0:cw], src_d[:, k, c0:c0 + cw], W=[sb[s]], grp=stag + "s%d" % s)
                eng = ("dve", "pool")[it % 2]
                if gt is not None:
                    I_(eng, lambda e, s=s, k=k, c0=c0, cw=cw: e.tensor_scalar(
                        out=dst[:, k, c0:c0 + cw], in0=stg[s][:, 0:cw], scalar1=gt[:, k:k + 1], scalar2=None,
                        op0=ALU.mult), R=[sb[s], bw], W=[bw])
                else:
                    I_(eng, lambda e, s=s, k=k, c0=c0, cw=cw: e.tensor_copy(out=dst[:, k, c0:c0 + cw],
                                                                          in_=stg[s][:, 0:cw]), R=[sb[s]], W=[bw])
                it += 1
        return bw

    def norm_rows_to_hT(src_tile_d, xt, xb, hT_dst, bx, bh, bxs, grp, pbank, bpb):
        P.dma("sp", xt, src_tile_d, W=[bx], grp=grp)
        st = bxs["st"]
        I_("pool", lambda e: e.memset(st[:, 0:1], 0.0), W=[bxs["b"]])
        I_("act", lambda e: e.activation(out=bxs["sq"], in_=xt, func=AF.Square, accum_out=st[:, 0:1]),
           R=[bx, bxs["b"]], W=[bxs["b"]])
        rstd_from_ss(st[:, 0:1], st[:, 1:2], D, [bxs["b"]], 1)
        I_("dve", lambda e: e.tensor_scalar(out=xb, in0=xt, scalar1=st[:, 1:2], scalar2=None, op0=ALU.mult),
           R=[bx, bxs["b"]], W=[bh])
        pT = bank_bf(pbank)
        for k in range(8):
            I_("pe", lambda e, k=k: e.transpose(out=pT[:, k * 128:(k + 1) * 128], in_=xb[:, k * 128:(k + 1) * 128],
                                                identity=ident), R=[bh, b_const], W=[bpb])

    if 1 in phases:
        W = A.get([8, IN_COLS], BF16)
        bW = load_weight_bf16(W, win_d, 8, IN_COLS, gmix_d, "win", chunk=1426)
        NB = 2
        xt = [A.get([D], F32) for _ in range(NB)]
        bx = [Buf("xt%d" % i) for i in range(NB)]
        xb = [A.get([D], BF16) for _ in range(NB)]
        bh = [Buf("xb%d" % i) for i in range(NB)]
        sq = A.get([D], F32)
        st = A.get([8], F32)
        bxs = dict(sq=sq, st=st, b=Buf("xstat"))
        hTg = [A.get([8, 512], BF16) for _ in range(2)]
        bhT = [Buf("hTg%d" % i) for i in range(2)]
        stg = A.get([2632], F32)
        bstg = Buf("stg")
        tmp = A.get([1536], F32)
        btmp = Buf("tmp")
        ss = A.get([24], F32)
        rs = A.get([24], F32)
        bss = Buf("ss")
        gainrow = A.get([1536], F32)
        bgr = Buf("gainrow")
        gsrc = [0] * 6 + [1] * 6 + [2] * 6 + [3] * 2 + [4] * 4
        for hh, gi in enumerate(gsrc):
            I_("pool", lambda e, hh=hh, gi=gi: e.tensor_copy(out=gainrow[:, hh * 64:(hh + 1) * 64],
                                                            in_=gains[:, gi * 64:(gi + 1) * 64]),
               R=[b_const], W=[bgr])
        rt = A.get([4, 29, 8], F32)
        brt = Buf("rt")
        oA = [A.get([1152], BF16) for _ in range(2)]
        boA = [Buf("oA%d" % i) for i in range(2)]
        oT = [A.get([1344], BF16) for _ in range(2)]
        boT = [Buf("oT%d" % i) for i in range(2)]
        oV = [A.get([136], BF16) for _ in range(2)]
        boV = [Buf("oV%d" % i) for i in range(2)]
        tT = [A.get([1408], BF16) for _ in range(2)]
        btT = [Buf("tT%d" % i) for i in range(2)]
        gsb = [A.get([512], BF16) for _ in range(3)]
        bgsb = [Buf("gsb%d" % i) for i in range(3)]
        WI_SCALE = float((8 ** -0.5) * (64 ** -0.5))
        gcount = 0
        for grp_i in range(ntiles // 4):
            hs = grp_i % 2
            for tt in range(4):
                n = grp_i * 4 + tt
                s = n % NB
                o = n % 2
                norm_rows_to_hT(x_d[n * 128:(n + 1) * 128, :], xt[s], xb[s], None, bx[s], bh[s], bxs, "x%d" % s, 5, pb[5])
                I_("act", lambda e, hs=hs, tt=tt: e.copy(
                    out=hTg[hs][:, :, tt * 128:(tt + 1) * 128],
                    in_=bank_bf(5).rearrange("p (k c) -> p k c", k=8)), R=[pb[5]], W=[bhT[hs]])
                for bnk in range(5):
                    for k in range(8):
                        I_("pe", lambda e, bnk=bnk, k=k, hs=hs, tt=tt: e.matmul(
                            bank(bnk), lhsT=hTg[hs][:, k, tt * 128:(tt + 1) * 128],
                            rhs=W[:, k, bnk * 512:(bnk + 1) * 512], start=(k == 0), stop=(k == 7)),
                           R=[bhT[hs], bW], W=[pb[bnk]])
                for k in range(8):
                    I_("pe", lambda e, k=k, hs=hs, tt=tt: e.matmul(
                        bank(7)[:, 0:N_SM], lhsT=hTg[hs][:, k, tt * 128:(tt + 1) * 128],
                        rhs=W[:, k, N_TM:N_TM + N_SM], start=(k == 0), stop=(k == 7)),
                       R=[bhT[hs], bW], W=[pb[7]])
                I_("act", lambda e: e.copy(out=stg[:, 0:1792], in_=psum[:, 0:4, :].rearrange("p a b -> p (a b)")[:, 0:1792]),
                   R=[pb[0], pb[1], pb[2], pb[3]], W=[bstg])
                I_("dve", lambda e: e.tensor_copy(out=stg[:, 1856:2624],
                                                  in_=psum[:, 3:5, :].rearrange("p a b -> p (a b)")[:, 256:1024]),
                   R=[pb[3], pb[4]], W=[bstg])
                I_("dve", lambda e: e.tensor_copy(out=stg[:, 1792:1856], in_=bank(7)[:, 0:64]), R=[pb[7]], W=[bstg])
                I_("dve", lambda e: e.tensor_scalar(out=stg[:, 2624:2632], in0=bank(7)[:, 64:72], scalar1=WI_SCALE,
                                                    scalar2=None, op0=ALU.mult), R=[pb[7]], W=[bstg])
                I_("pool", lambda e: e.tensor_tensor(out=tmp[:, 0:1280], in0=stg[:, 0:1280], in1=stg[:, 0:1280],
                                                     op=ALU.mult), R=[bstg], W=[btmp])
                I_("pool", lambda e: e.tensor_tensor(out=tmp[:, 1280:1536], in0=stg[:, 1856:2112],
                                                     in1=stg[:, 1856:2112], op=ALU.mult), R=[bstg], W=[btmp])
                I_("dve", lambda e: e.tensor_reduce(out=ss, in_=tmp.rearrange("p (h d) -> p h d", d=64), axis=AX.X,
                                                    op=ALU.add), R=[btmp], W=[bss])
                rstd_from_ss(ss, rs, 64, [bss], 24)
                I_("dve", lambda e: e.tensor_tensor(
                    out=tmp.rearrange("p (h d) -> p h d", d=64), in0=gainrow.rearrange("p (h d) -> p h d", d=64),
                    in1=rs.unsqueeze(2).to_broadcast([128, 24, 64]), op=ALU.mult), R=[bss, bgr, btmp], W=[btmp])
                I_("pool", lambda e: e.tensor_tensor(out=stg[:, 0:1280], in0=stg[:, 0:1280], in1=tmp[:, 0:1280],
                                                     op=ALU.mult), R=[bstg, btmp], W=[bstg])
                I_("pool", lambda e: e.tensor_tensor(out=stg[:, 1856:2112], in0=stg[:, 1856:2112],
                                                     in1=tmp[:, 1280:1536], op=ALU.mult), R=[bstg, btmp], W=[bstg])
                v = stg[:, 0:1856].rearrange("p (h d) -> p h d", d=64)
                x1 = v[:, :, 0:8]
                x2 = v[:, :, 8:16]
                cosb = cs[:, n, 0:8].unsqueeze(1).to_broadcast([128, 29, 8])
                sinb = cs[:, n, 8:16].unsqueeze(1).to_broadcast([128, 29, 8])
                I_("dve", lambda e, x1=x1, cosb=cosb: e.tensor_tensor(out=rt[:, 0], in0=x1, in1=cosb, op=ALU.mult),
                   R=[bstg, b_const], W=[brt])
                I_("dve", lambda e, x2=x2, sinb=sinb: e.tensor_tensor(out=rt[:, 1], in0=x2, in1=sinb, op=ALU.mult),
                   R=[bstg, b_const], W=[brt])
                I_("pool", lambda e, x2=x2, cosb=cosb: e.tensor_tensor(out=rt[:, 2], in0=x2, in1=cosb, op=ALU.mult),
                   R=[bstg, b_const], W=[brt])
                I_("pool", lambda e, x1=x1, sinb=sinb: e.tensor_tensor(out=rt[:, 3], in0=x1, in1=sinb, op=ALU.mult),
                   R=[bstg, b_const], W=[brt])
                I_("dve", lambda e, x1=x1: e.tensor_tensor(out=x1, in0=rt[:, 0], in1=rt[:, 1], op=ALU.subtract),
                   R=[brt], W=[bstg])
                I_("dve", lambda e, x2=x2: e.tensor_tensor(out=x2, in0=rt[:, 2], in1=rt[:, 3], op=ALU.add),
                   R=[brt], W=[bstg])
                I_("act", lambda e, o=o: e.copy(out=oA[o][:, 0:768], in_=stg[:, 0:768]), R=[bstg], W=[boA[o]])
                I_("act", lambda e, o=o: e.copy(out=oA[o][:, 768:1152], in_=stg[:, 2112:2496]), R=[bstg], W=[boA[o]])
                I_("pool", lambda e, o=o: e.tensor_copy(out=oT[o], in_=stg[:, 768:2112]), R=[bstg], W=[boT[o]])
                I_("pool", lambda e, o=o: e.tensor_copy(out=oV[o], in_=stg[:, 2496:2632]), R=[bstg], W=[boV[o]])
                P.dma("pool", QKVA[n * 128:(n + 1) * 128, :], oA[o], R=[boA[o]], grp="oA%d" % o)
                P.dma("pool", VB[n * 128:(n + 1) * 128, :], oV[o], R=[boV[o]], grp="oV%d" % o)
                pT6 = bank_bf(6)
                for c in range(8):
                    I_("pe", lambda e, c=c, o=o: e.transpose(out=pT6[:, c * 128:(c + 1) * 128],
                                                             in_=oT[o][:, c * 128:(c + 1) * 128], identity=ident),
                       R=[boT[o], b_const], W=[pb[6]])
                I_("dve", lambda e, o=o: e.tensor_copy(out=tT[o][:, 0:1024], in_=pT6), R=[pb[6]], W=[btT[o]])
                I_("pe", lambda e, o=o: e.transpose(out=pT6[0:64, 0:128], in_=oT[o][:, 1024:1088], identity=ident),
                   R=[boT[o], b_const], W=[pb[6]])
                for c in range(2):
                    I_("pe", lambda e, c=c, o=o: e.transpose(out=pT6[:, 128 + c * 128:256 + c * 128],
                                                             in_=oT[o][:, 1088 + c * 128:1216 + c * 128], identity=ident),
                       R=[boT[o], b_const], W=[pb[6]])
                I_("dve", lambda e, o=o: e.tensor_copy(out=tT[o][0:64, 1024:1152], in_=pT6[0:64, 0:128]),
                   R=[pb[6]], W=[btT[o]])
                I_("dve", lambda e, o=o: e.tensor_copy(out=tT[o][:, 1152:1408], in_=pT6[:, 128:384]),
                   R=[pb[6]], W=[btT[o]])
                g = "tT%d" % o
                P.dma("pool", QBT[:, n, :], tT[o][:, 0:384], R=[btT[o]], grp=g)
                P.dma("pool", KBT[:, n * 128:(n + 1) * 128], tT[o][:, 384:512], R=[btT[o]], grp=g)
                P.dma("pool", QIT[:, n, :], tT[o][:, 512:1024], R=[btT[o]], grp=g)
                P.dma("pool", KIT[:, n * 128:(n + 1) * 128], tT[o][0:64, 1024:1152], R=[btT[o]], grp=g)
                P.dma("pool", QCT[:, n, :], tT[o][:, 1152:1408], R=[btT[o]], grp=g)
            for c in range(24):
                for k in range(8):
                    I_("pe", lambda e, c=c, k=k, hs=hs: e.matmul(
                        bank(7), lhsT=W[:, k, N_TM + N_SM + c * 128:N_TM + N_SM + (c + 1) * 128],
                        rhs=hTg[hs][:, k, :], start=(k == 0), stop=(k == 7)), R=[bhT[hs], bW], W=[pb[7]])
                gs = gcount % 3
                gcount += 1
                I_("act", lambda e, gs=gs: e.activation(out=gsb[gs], in_=bank(7), func=AF.Sigmoid),
                   R=[pb[7]], W=[bgsb[gs]])
                P.dma("sp", GT[c, :, grp_i * 512:(grp_i + 1) * 512], gsb[gs], R=[bgsb[gs]], grp="gsb%d" % gs)
        P.barrier()
        A.reset()


    ntok = ntiles * 128
    if 2 in phases:
        blk = [A.get([3, 128], BF16) for _ in range(3)]
        bblk = [Buf("blk%d" % i) for i in range(3)]
        qT = [A.get([128], BF16) for _ in range(2)]
        bqT = [Buf("qT%d" % i) for i in range(2)]
        kT = [A.get([128], BF16) for _ in range(3)]
        bkT = [Buf("kT%d" % i) for i in range(3)]
        va_ = [A.get([2, 65], BF16) for _ in range(3)]
        bva = [Buf("va%d" % i) for i in range(3)]
        PT = [A.get([512], BF16) for _ in range(2)]
        bPT = [Buf("PT%d" % i) for i in range(2)]
        oas = [A.get([130], F32) for _ in range(2)]
        boas = [Buf("oas%d" % i) for i in range(2)]
        for i in range(3):
            I_("pool", lambda e, i=i: e.memset(va_[i], 1.0), W=[bva[i]])
        u = 0
        for g, dil in enumerate((1, 4, 16)):
            m = ntok // dil
            nblk = m // 128
            assert nblk >= 1
            for r in range(dil):
                for n in range(nblk):
                    sb3 = u % 3
                    sb2 = u % 2
                    prev3 = (u - 1) % 3
                    start = r + dil * 128 * n
                    rows = QKVA[start:start + dil * 127 + 1:dil, :].rearrange("t (s c) -> t s c", s=3)[:, :, g * 128:(g + 1) * 128]
                    P.dma("sp", blk[sb3], rows, W=[bblk[sb3]], grp="blk%d" % sb3)
                    pT = bank_bf(sb2)
                    I_("pe", lambda e, pT=pT, sb3=sb3: e.transpose(out=pT[:, 0:128], in_=blk[sb3][:, 0, :], identity=ident),
                       R=[bblk[sb3], b_const], W=[pb[sb2]])
                    I_("pe", lambda e, pT=pT, sb3=sb3: e.transpose(out=pT[:, 128:256], in_=blk[sb3][:, 1, :], identity=ident),
                       R=[bblk[sb3], b_const], W=[pb[sb2]])
                    I_("dve", lambda e, pT=pT, sb2=sb2: e.tensor_copy(out=qT[sb2], in_=pT[:, 0:128]), R=[pb[sb2]], W=[bqT[sb2]])
                    I_("dve", lambda e, pT=pT, sb3=sb3: e.tensor_copy(out=kT[sb3], in_=pT[:, 128:256]), R=[pb[sb2]], W=[bkT[sb3]])
                    I_("pool", lambda e, sb3=sb3: e.tensor_copy(out=va_[sb3][:, :, 0:64],
                                                                in_=blk[sb3][:, 2, :].rearrange("p (j d) -> p j d", j=2)),
                       R=[bblk[sb3]], W=[bva[sb3]])
                    psY = bank(2 + sb2)
                    has_prev = n > 0
                    for j in range(2):
                        if has_prev:
                            I_("pe", lambda e, j=j, psY=psY, prev3=prev3, sb2=sb2: e.matmul(
                                psY[:, j * 128:(j + 1) * 128], lhsT=kT[prev3][64 * j:64 * j + 64, :],
                                rhs=qT[sb2][64 * j:64 * j + 64, :], start=True, stop=False),
                               R=[bkT[prev3], bqT[sb2]], W=[pb[2 + sb2]])
                            I_("pe", lambda e, j=j, psY=psY: e.matmul(
                                psY[:, j * 128:(j + 1) * 128], lhsT=ident, rhs=mprev, start=False, stop=True),
                               R=[b_const], W=[pb[2 + sb2]])
                        I_("pe", lambda e, j=j, psY=psY, sb3=sb3, sb2=sb2: e.matmul(
                            psY[:, 256 + j * 128:256 + (j + 1) * 128], lhsT=kT[sb3][64 * j:64 * j + 64, :],
                            rhs=qT[sb2][64 * j:64 * j + 64, :], start=True, stop=False),
                           R=[bkT[sb3], bqT[sb2]], W=[pb[2 + sb2]])
                        I_("pe", lambda e, j=j, psY=psY: e.matmul(
                            psY[:, 256 + j * 128:256 + (j + 1) * 128], lhsT=ident, rhs=mcur, start=False, stop=True),
                           R=[b_const], W=[pb[2 + sb2]])
                    lo = 0 if has_prev else 256
                    I_("act", lambda e, psY=psY, sb2=sb2, lo=lo: e.activation(out=PT[sb2][:, lo:512], in_=psY[:, lo:512],
                                                                             func=AF.Exp, scale=0.125),
                       R=[pb[2 + sb2]], W=[bPT[sb2]])
                    psZ = bank(4 + sb2)
                    for j in range(2):
                        if has_prev:
                            I_("pe", lambda e, j=j, psZ=psZ, sb2=sb2, prev3=prev3: e.matmul(
                                psZ[:, j * 65:(j + 1) * 65], lhsT=PT[sb2][:, j * 128:(j + 1) * 128],
                                rhs=va_[prev3][:, j, :], start=True, stop=False),
                               R=[bPT[sb2], bva[prev3]], W=[pb[4 + sb2]])
                        I_("pe", lambda e, j=j, psZ=psZ, sb2=sb2, sb3=sb3, hp=has_prev: e.matmul(
                            psZ[:, j * 65:(j + 1) * 65], lhsT=PT[sb2][:, 256 + j * 128:256 + (j + 1) * 128],
                            rhs=va_[sb3][:, j, :], start=(not hp), stop=True),
                           R=[bPT[sb2], bva[sb3]], W=[pb[4 + sb2]])
                    I_("dve", lambda e, psZ=psZ, sb2=sb2: e.tensor_copy(out=oas[sb2], in_=psZ[:, 0:130]),
                       R=[pb[4 + sb2]], W=[boas[sb2]])
                    P.dma("pool", OA[g, start:start + dil * 127 + 1:dil, :], oas[sb2], R=[boas[sb2]], grp="oas%d" % sb2)
                    u += 1
        P.barrier()
        A.reset()

    if 3 in phases:
        U8 = mybir.dt.uint8
        kiT2 = A.get([S], BF16)
        kbT = A.get([S], BF16)
        vba = A.get([NT, 2, 65], BF16)
        wia = A.get([NT, 8], BF16)
        bK = Buf("dsaK")
        P.dma("sp", kiT2[0:64, 0:ntok], KIT[:, 0:ntok], W=[bK], grp="dk")
        P.dma("sp", kiT2[64:128, 0:ntok], KIT[:, 0:ntok], W=[bK], grp="dk")
        P.dma("sp", kbT[:, 0:ntok], KBT[:, 0:ntok], W=[bK], grp="dk")
        I_("pool", lambda e: e.memset(vba, 1.0), W=[bK])
        vst = [A.get([136], BF16) for _ in range(2)]
        bvst = [Buf("vst%d" % i) for i in range(2)]
        for n in range(ntiles):
            s2 = n % 2
            P.dma("sp", vst[s2], VB[n * 128:(n + 1) * 128, :], W=[bvst[s2]], grp="vst%d" % s2)
            I_("pool", lambda e, n=n, s2=s2: e.tensor_copy(out=vba[:, n, :, 0:64],
                                                          in_=vst[s2][:, 0:128].rearrange("p (c d) -> p c d", c=2)),
               R=[bvst[s2], bK], W=[bK])
            I_("pool", lambda e, n=n, s2=s2: e.tensor_copy(out=wia[:, n, :], in_=vst[s2][:, 128:136]),
               R=[bvst[s2], bK], W=[bK])
        NI = 3
        qiT = [A.get([4, 128], BF16) for _ in range(2)]
        bqiT = [Buf("qiT%d" % i) for i in range(2)]
        qbT = [A.get([384], BF16) for _ in range(3)]
        bqbT = [Buf("qbT%d" % i) for i in range(3)]
        Dh = [A.get([8, 128], BF16) for _ in range(2)]
        bDh = [Buf("Dh%d" % i) for i in range(2)]
        isc = [A.get([S], F32) for _ in range(NI)]
        bisc = [Buf("isc%d" % i) for i in range(NI)]
        junk_t = A.get([S // 2], BF16)
        junk = junk_t.bitcast(U8)
        bjunk = Buf("junk")
        MBF = A.get([S], BF16)
        bMBF = [Buf("MBF%d" % c) for c in range(S // 512)]
        Rb = [A.get([2, 512], BF16) for _ in range(3)]
        bRb = [Buf("Rb%d" % i) for i in range(3)]
        PTb = [A.get([384], BF16) for _ in range(3)]
        bPTb = [Buf("PTb%d" % i) for i in range(3)]
        bst = [A.get([8], F32) for _ in range(NI)]
        bbst = [Buf("bst%d" % i) for i in range(NI)]
        Wk = [A.get([NBIS + 1], F32) for _ in range(NI)]
        pow2 = A.get([NBIS + 1], F32)
        cntb = A.get([2], F32)
        bcnt = Buf("cnt")
        rden = A.get([6], F32)
        brden = Buf("rden")
        ob = [A.get([384], BF16) for _ in range(2)]
        bob = [Buf("ob%d" % i) for i in range(2)]
        obT = [A.get([384], BF16) for _ in range(2)]
        bobT = [Buf("obT%d" % i) for i in range(2)]
        bpow = Buf("pow2")
        for k in range(NBIS + 1):
            I_("pool", lambda e, k=k: e.memset(pow2[:, k:k + 1], float(2.0 ** -(k + 1))), W=[bpow], R=[bpow])
        pair = [psum[:, 0:2, :], psum[:, 2:4, :]]
        bpair = [Buf("pair0", excl=True), Buf("pair1", excl=True)]
        accI = bank(4)
        sbank = [bank(5), bank(6)]
        accO = bank(7)
        cnt_pair = [0]
        cnt_rb = [0]
        cnt_unit = [0]
        _fr = []

        def fill_reg(e):
            if not _fr:
                _fr.append(e.to_reg(-1e30))
            return _fr[0]

        def dsa_index(i):
            s2 = i % 2
            s3 = i % NI
            L = 128 * (i + 1)
            P.dma("sp", qiT[s2], QIT[:, i, :].rearrange("p (j t) -> p j t", j=4), W=[bqiT[s2]], grp="qiT%d" % s2)
            P.dma("sp", qbT[s3], QBT[:, i, :], W=[bqbT[s3]], grp="qbT%d" % s3)
            I_("pool", lambda e: e.tensor_tensor(out=Dh[s2], in0=identf.unsqueeze(1).to_broadcast([128, 8, 128]),
                                                 in1=wia[:, i, :].unsqueeze(2).to_broadcast([128, 8, 128]), op=ALU.mult),
               R=[bK, b_const], W=[bDh[s2]])
            steps = [(c0, min(512, L - c0), jj) for c0 in range(0, L, 512) for jj in range(4)]
            slots = {}

            def QK(k):
                c0, cw, jj = steps[k]
                pp = cnt_pair[0] % 2
                cnt_pair[0] += 1
                rb = cnt_rb[0] % 3
                cnt_rb[0] += 1
                slots[k] = (pp, rb)
                for hh in range(2):
                    I_("pe", lambda e, pp=pp, hh=hh, jj=jj, c0=c0, cw=cw: e.matmul(
                        pair[pp][:, hh, 0:cw], lhsT=qiT[s2][64 * hh:64 * hh + 64, jj, :],
                        rhs=kiT2[64 * hh:64 * hh + 64, c0:c0 + cw], start=True, stop=True),
                       R=[bqiT[s2], bK], W=[bpair[pp]])
                I_("act", lambda e, pp=pp, rb=rb, cw=cw: e.activation(out=Rb[rb][:, :, 0:cw], in_=pair[pp][:, :, 0:cw],
                                                                     func=AF.Relu), R=[bpair[pp]], W=[bRb[rb]])

            def HS(k):
                c0, cw, jj = steps[k]
                pp, rb = slots.pop(k)
                for hh in range(2):
                    I_("pe", lambda e, rb=rb, hh=hh, jj=jj, cw=cw: e.matmul(
                        accI[:, 0:cw], lhsT=Dh[s2][:, 2 * jj + hh, :], rhs=Rb[rb][:, hh, 0:cw],
                        start=(jj == 0 and hh == 0), stop=(jj == 3 and hh == 1)),
                       R=[bDh[s2], bRb[rb]], W=[pb[4]])
                if jj == 3:
                    I_("act", lambda e, c0=c0, cw=cw: e.copy(out=isc[s3][:, c0:c0 + cw], in_=accI[:, 0:cw]),
                       R=[pb[4]], W=[bisc[s3]])

            QK(0)
            for k in range(len(steps)):
                if k + 1 < len(steps):
                    QK(k + 1)
                HS(k)
                yield
            I_("pool", lambda e: e.affine_select(out=isc[s3][:, 128 * i:128 * (i + 1)], in_=isc[s3][:, 128 * i:128 * (i + 1)],
                                                 pattern=[[-1, 128]], compare_op=ALU.is_ge, fill=fill_reg(e), base=0,
                                                 channel_multiplier=1), R=[bisc[s3]], W=[bisc[s3]])
            yield

        def dsa_bisect(i):
            s3 = i % NI
            L = 128 * (i + 1)
            b = bst[s3]
            bb = bbst[s3]
            if i < 2:
                I_("dve", lambda e: e.memset(b[:, 3:4], -1e29), W=[bb])
                return
            I_("dve", lambda e: e.tensor_reduce(out=b[:, 0:1], in_=isc[s3][:, 0:L], axis=AX.X, op=ALU.max),
               R=[bisc[s3]], W=[bb])
            I_("dve", lambda e: e.tensor_reduce(out=b[:, 1:2], in_=isc[s3][:, 0:128 * i], axis=AX.X, op=ALU.min),
               R=[bisc[s3]], W=[bb])
            I_("dve", lambda e: e.tensor_tensor(out=b[:, 2:3], in0=b[:, 0:1], in1=b[:, 1:2], op=ALU.subtract),
               R=[bb], W=[bb])
            I_("dve", lambda e: e.tensor_scalar(out=Wk[s3], in0=pow2, scalar1=b[:, 2:3], scalar2=None, op0=ALU.mult),
               R=[bb, bpow], W=[bb])
            I_("dve", lambda e: e.tensor_tensor(out=b[:, 4:5], in0=b[:, 1:2], in1=Wk[s3][:, 0:1], op=ALU.add),
               R=[bb], W=[bb])
            for k in range(NBIS):
                I_("dve", lambda e: e.tensor_scalar(out=junk[:, 0:L], in0=isc[s3][:, 0:L], scalar1=b[:, 4:5], scalar2=None,
                                                    op0=ALU.is_ge, op1=ALU.add, accum_out=cntb[:, 0:1]),
                   R=[bisc[s3], bb], W=[bjunk, bcnt])
                I_("dve", lambda e: e.tensor_scalar(out=cntb[:, 1:2], in0=cntb[:, 0:1], scalar1=TOPK - 0.5, scalar2=0.5,
                                                    op0=ALU.is_ge, op1=ALU.subtract), R=[bcnt], W=[bcnt])
                I_("dve", lambda e, k=k: e.scalar_tensor_tensor(out=b[:, 4:5], in0=cntb[:, 1:2], scalar=Wk[s3][:, k:k + 1],
                                                               in1=b[:, 4:5], op0=ALU.mult, op1=ALU.add),
                   R=[bcnt, bb], W=[bb])
            I_("dve", lambda e: e.tensor_tensor(out=b[:, 3:4], in0=b[:, 4:5], in1=Wk[s3][:, NBIS:NBIS + 1], op=ALU.subtract),
               R=[bb], W=[bb])

        def dsa_maskgen(i):
            s3 = i % NI
            L = 128 * (i + 1)
            b = bst[s3]
            for c0 in range(0, L, 512):
                cw = min(512, L - c0)
                I_("dve", lambda e, c0=c0, cw=cw: e.tensor_scalar(out=MBF[:, c0:c0 + cw], in0=isc[s3][:, c0:c0 + cw],
                                                                 scalar1=b[:, 3:4], scalar2=NEG, op0=ALU.is_lt, op1=ALU.mult),
                   R=[bisc[s3], bbst[s3]], W=[bMBF[c0 // 512]])

        def dsa_attn(i):
            s2 = i % 2
            s3 = i % NI
            units = [(kb, c) for kb in range(i + 1) for c in range(2)]
            slots = {}

            def SC(k):
                kb, c = units[k]
                bm = bMBF[kb // 4]
                u = cnt_unit[0]
                cnt_unit[0] += 1
                sb = u % 2
                pt = u % 3
                slots[k] = pt
                psS = sbank[sb]
                I_("pe", lambda e, c=c, kb=kb, psS=psS: e.matmul(
                    psS[:, 0:384], lhsT=kbT[64 * c:64 * c + 64, kb * 128:(kb + 1) * 128],
                    rhs=qbT[s3][64 * c:64 * c + 64, :], start=True, stop=False),
                   R=[bK, bqbT[s3]], W=[pb[5 + sb]])
                I_("pe", lambda e, kb=kb, psS=psS: e.matmul(psS[:, 0:384], lhsT=MBF[:, kb * 128:(kb + 1) * 128], rhs=i3,
                                                           start=False, stop=True),
                   R=[bm, b_const], W=[pb[5 + sb]])
                I_("act", lambda e, pt=pt, psS=psS: e.activation(out=PTb[pt], in_=psS[:, 0:384], func=AF.Exp, scale=0.125),
                   R=[pb[5 + sb]], W=[bPTb[pt]])

            def PV(k):
                kb, c = units[k]
                pt = slots.pop(k)
                for g in range(3):
                    h = 3 * c + g
                    I_("pe", lambda e, pt=pt, g=g, h=h, kb=kb, c=c: e.matmul(
                        accO[:, h * 65:(h + 1) * 65], lhsT=PTb[pt][:, g * 128:(g + 1) * 128], rhs=vba[:, kb, c, :],
                        start=(kb == 0 and h == 0), stop=(kb == i), skip_group_check=True),
                        R=[bPTb[pt], bK], W=[pb[7]])

            SC(0)
            for k in range(len(units)):
                if k + 1 < len(units):
                    SC(k + 1)
                PV(k)
                yield

        def dsa_final(i):
            s2 = i % 2
            av = accO[:, 0:390].rearrange("p (h e) -> p h e", e=65)
            I_("dve", lambda e: e.reciprocal(out=rden, in_=av[:, :, 64]), R=[pb[7]], W=[brden])
            I_("dve", lambda e: e.tensor_tensor(out=ob[s2].rearrange("p (h d) -> p h d", d=64), in0=av[:, :, 0:64],
                                                in1=rden.unsqueeze(2).to_broadcast([128, 6, 64]), op=ALU.mult),
               R=[pb[7], brden], W=[bob[s2]])
            pT = bank_bf(4)
            for cc in range(3):
                I_("pe", lambda e, cc=cc: e.transpose(out=pT[:, cc * 128:(cc + 1) * 128], in_=ob[s2][:, cc * 128:(cc + 1) * 128],
                                                      identity=ident), R=[bob[s2], b_const], W=[pb[4]])
            I_("act", lambda e: e.copy(out=obT[s2], in_=pT[:, 0:384]), R=[pb[4]], W=[bobT[s2]])
            P.dma("pool", OBT[:, :, i * 128:(i + 1) * 128].rearrange("c p t -> p c t"),
                  obT[s2].rearrange("p (c t) -> p c t", c=3), R=[bobT[s2]], grp="obT%d" % s2)

        def run_all(gen):
            for _ in gen:
                pass

        def interleave(ga, na, gb, nb):
            da = db = 0
            ea = eb = False
            while not (ea and eb):
                fa = da / max(na, 1)
                fb = db / max(nb, 1)
                if (not ea) and (eb or fa <= fb):
                    try:
                        next(ga)
                        da += 1
                    except StopIteration:
                        ea = True
                else:
                    try:
                        next(gb)
                        db += 1
                    except StopIteration:
                        eb = True

        def n_idx_steps(i):
            return 4 * ((128 * (i + 1) + 511) // 512) + 1

        run_all(dsa_index(0))
        if ntiles > 1:
            run_all(dsa_index(1))
        dsa_bisect(0)
        for i in range(ntiles):
            dsa_maskgen(i)
            if i + 1 < ntiles:
                dsa_bisect(i + 1)
            ga = dsa_attn(i)
            if i + 2 < ntiles:
                interleave(dsa_index(i + 2), n_idx_steps(i + 2), ga, 2 * (i + 1))
            else:
                run_all(ga)
            dsa_final(i)
        P.barrier()
        A.reset()

    if 4 in phases:
        wa = A.get([1, D], BF16)
        wb_ = A.get([3, D], BF16)
        wc = A.get([2, D], BF16)
        wo = A.get([8, D], BF16)
        wkv = A.get([8, 512], BF16)
        stage4 = make_stage(1024, "p4")
        bwa = load_weight_bf16(wa, wa_d, 1, D, None, "wa", chunk=1024, stage=stage4)
        bwb = load_weight_bf16(wb_, wb_d, 3, D, None, "wb", chunk=1024, stage=stage4)
        bwc = load_weight_bf16(wc, wc_d, 2, D, None, "wc", chunk=1024, stage=stage4)
        bwo = load_weight_bf16(wo, wo_d, 8, D, None, "wo", chunk=1024, stage=stage4)
        bwkv = load_weight_bf16(wkv, wkv_d, 8, 512, gmem_d, "wkv", chunk=1024, stage=stage4)
        P4STOP = int(os.environ.get('P4STOP', '9'))
        xt4 = [A.get([D], F32) for _ in range(2)]
        bxt4 = [Buf("xt4%d" % i) for i in range(2)]
        xb4 = A.get([D], BF16)
        bxb4 = Buf("xb4")
        sq4 = A.get([D], F32)
        st4 = A.get([8], F32)
        bxs4 = dict(sq=sq4, st=st4, b=Buf("xstat4"))
        memT = A.get([8, 128], BF16)
        bmemT = Buf("memT")
        kmT = A.get([2, 256], BF16)
        vmb = A.get([2, 256], BF16)
        bkm = Buf("kmT")
        bvm = Buf("vmb")
        ksb = A.get([256], F32)
        ksq = A.get([256], F32)
        kss = A.get([8], F32)
        kgr = A.get([256], F32)
        kmb = A.get([256], BF16)
        bks = Buf("ksb")
        ones_bf = A.get([128], BF16)
        I_("pool", lambda e: e.memset(ones_bf, 1.0), W=[bks])
        for hh in range(4):
            I_("pool", lambda e, hh=hh: e.tensor_copy(out=kgr[:, hh * 64:(hh + 1) * 64], in_=gains[:, 5 * 64:6 * 64]),
               R=[b_const], W=[bks])
        for mt in range(2 if P4STOP >= 1 else 0):
            norm_rows_to_hT(mem_d[mt * 128:(mt + 1) * 128, :], xt4[0], xb4, None, bxt4[0], bxb4, bxs4, "xt40", 5, pb[5])
            I_("act", lambda e: e.copy(out=memT, in_=bank_bf(5).rearrange("p (k c) -> p k c", k=8)), R=[pb[5]], W=[bmemT])
            P4SUB = int(os.environ.get('P4SUB', '9'))
            if P4SUB < 1:
                continue
            for k in range(8):
                I_("pe", lambda e, k=k: e.matmul(bank(0), lhsT=memT[:, k, :], rhs=wkv[:, k, :], start=(k == 0), stop=(k == 7)),
                   R=[bmemT, bwkv], W=[pb[0]])
            P4V = int(os.environ.get('P4V', '3'))
            if P4V & 1:
                I_("act", lambda e, mt=mt: e.copy(out=vmb[:, mt, :], in_=bank(0)[:, 256:512]), R=[pb[0]], W=[bvm])
            if P4V & 2:
                I_("dve", lambda e: e.tensor_copy(out=ksb, in_=bank(0)[:, 0:256]), R=[pb[0]], W=[bks])
            if P4SUB < 2:
                continue
            I_("pool", lambda e: e.tensor_tensor(out=ksq, in0=ksb, in1=ksb, op=ALU.mult), R=[bks], W=[bks])
            I_("dve", lambda e: e.tensor_reduce(out=kss[:, 0:4], in_=ksq.rearrange("p (h d) -> p h d", d=64), axis=AX.X,
                                                op=ALU.add), R=[bks], W=[bks])
            rstd_from_ss(kss[:, 0:4], kss[:, 4:8], 64, [bks], 4)
            I_("dve", lambda e: e.tensor_tensor(out=ksq.rearrange("p (h d) -> p h d", d=64),
                                                in0=kgr.rearrange("p (h d) -> p h d", d=64),
                                                in1=kss[:, 4:8].unsqueeze(2).to_broadcast([128, 4, 64]), op=ALU.mult),
               R=[bks], W=[bks])
            I_("dve", lambda e: e.tensor_tensor(out=kmb, in0=ksb, in1=ksq, op=ALU.mult), R=[bks], W=[bks])
            if P4SUB < 3:
                continue
            for j in range(2):
                I_("pe", lambda e, j=j: e.transpose(out=bank_bf(1)[:, j * 128:(j + 1) * 128], in_=kmb[:, j * 128:(j + 1) * 128],
                                                    identity=ident), R=[bks, b_const], W=[pb[1]])
            I_("dve", lambda e, mt=mt: e.tensor_copy(out=kmT[:, :, mt * 128:(mt + 1) * 128],
                                                     in_=bank_bf(1)[:, 0:256].rearrange("p (j m) -> p j m", j=2)),
               R=[pb[1]], W=[bkm])
        gtg = [A.get([24, 512], BF16) for _ in range(2)]
        bgtg = [Buf("gtg%d" % i) for i in range(2)]
        obg = [A.get([3, 512], BF16) for _ in range(2)]
        bobg = [Buf("obg%d" % i) for i in range(2)]
        qcg = [A.get([2, 512], BF16) for _ in range(2)]
        bqcg = [Buf("qcg%d" % i) for i in range(2)]
        oaT = [A.get([512], BF16) for _ in range(2)]
        boaT = [Buf("oaT%d" % i) for i in range(2)]
        ocT = [A.get([2, 512], BF16) for _ in range(2)]
        bocT = [Buf("ocT%d" % i) for i in range(2)]
        mT = [A.get([8, 512], BF16) for _ in range(2)]
        bmT = [Buf("mT%d" % i) for i in range(2)]
        oa3 = [A.get([3, 130], F32) for _ in range(2)]
        boa3 = [Buf("oa3%d" % i) for i in range(2)]
        oasum = A.get([130], F32)
        oard = A.get([2], F32)
        oab = A.get([128], BF16)
        boas4 = Buf("oasum")
        PTc = [A.get([512], BF16) for _ in range(4)]
        bPTc = [Buf("PTc%d" % i) for i in range(4)]
        rdc = [A.get([512], F32) for _ in range(2)]
        brdc = [Buf("rdc%d" % i) for i in range(2)]
        tm = [A.get([512], F32) for _ in range(3)]
        btm = [Buf("tm%d" % i) for i in range(3)]
        x1t = [A.get([D], F32) for _ in range(2)]
        bx1t = [Buf("x1t%d" % i) for i in range(2)]
        ngrp = ntiles // 4
        cpt = [0]
        chd = [0]
        for G in range(ngrp if P4STOP >= 2 else 0):
            g2 = G % 2
            t0 = G * 512
            for c4 in range(0, 24, 4):
                P.dma("sp", gtg[g2][:, c4:c4 + 4, :], GT[c4:c4 + 4, :, t0:t0 + 512].rearrange("c p t -> p c t"),
                      W=[bgtg[g2]], grp="gtg%d" % g2)
            P.dma("sp", obg[g2], OBT[:, :, t0:t0 + 512].rearrange("c p t -> p c t"), W=[bobg[g2]], grp="obg%d" % g2)
            for j in range(2):
                P.dma("sp", qcg[g2][:, j, :].rearrange("p (n t) -> p n t", n=4), QCT[:, 4 * G:4 * G + 4, j * 128:(j + 1) * 128],
                      W=[bqcg[g2]], grp="qcg%d" % g2)
            for tt in range(4):
                n = 4 * G + tt
                o2 = n % 2
                P.dma("sp", oa3[o2], OA[:, n * 128:(n + 1) * 128, :].rearrange("g t e -> t g e"), W=[boa3[o2]],
                      grp="oa3%d" % o2)
                I_("pool", lambda e, o2=o2: e.tensor_tensor(out=oasum, in0=oa3[o2][:, 0, :], in1=oa3[o2][:, 1, :], op=ALU.add),
                   R=[boa3[o2]], W=[boas4])
                I_("pool", lambda e, o2=o2: e.tensor_tensor(out=oasum, in0=oasum, in1=oa3[o2][:, 2, :], op=ALU.add),
                   R=[boa3[o2], boas4], W=[boas4])
                osv = oasum.rearrange("p (j e) -> p j e", e=65)
                I_("dve", lambda e, osv=osv: e.reciprocal(out=oard, in_=osv[:, :, 64]), R=[boas4], W=[boas4])
                I_("dve", lambda e, osv=osv: e.tensor_tensor(out=oab.rearrange("p (j d) -> p j d", d=64), in0=osv[:, :, 0:64],
                                                             in1=oard.unsqueeze(2).to_broadcast([128, 2, 64]), op=ALU.mult),
                   R=[boas4], W=[boas4])
                I_("pe", lambda e, tt=tt: e.transpose(out=bank_bf(0)[:, tt * 128:(tt + 1) * 128], in_=oab, identity=ident),
                   R=[boas4, b_const], W=[pb[0]])
            I_("act", lambda e, g2=g2: e.copy(out=oaT[g2], in_=bank_bf(0)[:, 0:512]), R=[pb[0]], W=[boaT[g2]])
            for h in range(4 if P4STOP >= 3 else 0):
                j, hh = h // 2, h % 2
                hs2 = chd[0] % 2
                chd[0] += 1
                numb, denb = 4 + 2 * hs2, 5 + 2 * hs2
                pts = []
                for mt in range(2):
                    sbk = 2 * hs2 + mt
                    pt = cpt[0] % 4
                    cpt[0] += 1
                    pts.append(pt)
                    I_("pe", lambda e, sbk=sbk, hh=hh, j=j, mt=mt, g2=g2: e.matmul(
                        bank(sbk), lhsT=kmT[64 * hh:64 * hh + 64, j, mt * 128:(mt + 1) * 128],
                        rhs=qcg[g2][64 * hh:64 * hh + 64, j, :], start=True, stop=True),
                       R=[bkm, bqcg[g2]], W=[pb[sbk]])
                    I_("act", lambda e, sbk=sbk, pt=pt: e.activation(out=PTc[pt], in_=bank(sbk), func=AF.Exp, scale=0.125),
                       R=[pb[sbk]], W=[bPTc[pt]])
                for mt in range(2):
                    I_("pe", lambda e, mt=mt, j=j, numb=numb, pt=pts[mt]: e.matmul(
                        bank(numb), lhsT=vmb[:, mt, j * 128:(j + 1) * 128], rhs=PTc[pt], start=(mt == 0), stop=(mt == 1)),
                       R=[bvm, bPTc[pts[mt]]], W=[pb[numb]])
                for mt in range(2):
                    I_("pe", lambda e, mt=mt, denb=denb, pt=pts[mt]: e.matmul(
                        bank(denb), lhsT=ones_bf, rhs=PTc[pt], start=(mt == 0), stop=(mt == 1)),
                       R=[bks, bPTc[pts[mt]]], W=[pb[denb]])
                lo_, hi_ = 64 * hh, 64 * hh + 64
                I_("dve", lambda e, denb=denb, hs2=hs2, lo_=lo_, hi_=hi_: e.reciprocal(out=rdc[hs2][lo_:hi_, :],
                                                                                     in_=bank(denb)[lo_:hi_, :]),
                   R=[pb[denb]], W=[brdc[hs2]])
                I_("dve", lambda e, numb=numb, hs2=hs2, lo_=lo_, hi_=hi_, j=j, g2=g2: e.tensor_tensor(
                    out=ocT[g2][lo_:hi_, j, :], in0=bank(numb)[lo_:hi_, :], in1=rdc[hs2][lo_:hi_, :], op=ALU.mult),
                   R=[pb[numb], brdc[hs2]], W=[bocT[g2]])
            for oc in range(8 if P4STOP >= 4 else 0):
                bs = 3 * (oc % 2)
                cs_ = slice(oc * 128, (oc + 1) * 128)
                I_("pe", lambda e, bs=bs, cs_=cs_, g2=g2: e.matmul(bank(bs), lhsT=wa[:, 0, cs_], rhs=oaT[g2], start=True, stop=True),
                   R=[bwa, boaT[g2]], W=[pb[bs]])
                for k in range(3):
                    I_("pe", lambda e, bs=bs, cs_=cs_, k=k, g2=g2: e.matmul(bank(bs + 1), lhsT=wb_[:, k, cs_], rhs=obg[g2][:, k, :],
                                                                          start=(k == 0), stop=(k == 2)),
                       R=[bwb, bobg[g2]], W=[pb[bs + 1]])
                for k in range(2):
                    I_("pe", lambda e, bs=bs, cs_=cs_, k=k, g2=g2: e.matmul(bank(bs + 2), lhsT=wc[:, k, cs_], rhs=ocT[g2][:, k, :],
                                                                          start=(k == 0), stop=(k == 1)),
                       R=[bwc, bocT[g2]], W=[pb[bs + 2]])
                I_("dve", lambda e, bs=bs, oc=oc, g2=g2: e.tensor_tensor(out=tm[0], in0=bank(bs), in1=gtg[g2][:, oc, :], op=ALU.mult),
                   R=[pb[bs], bgtg[g2]], W=[btm[0]])
                I_("dve", lambda e, bs=bs, oc=oc, g2=g2: e.tensor_tensor(out=tm[1], in0=bank(bs + 1), in1=gtg[g2][:, 8 + oc, :],
                                                                       op=ALU.mult), R=[pb[bs + 1], bgtg[g2]], W=[btm[1]])
                I_("dve", lambda e, bs=bs, oc=oc, g2=g2: e.tensor_tensor(out=tm[2], in0=bank(bs + 2), in1=gtg[g2][:, 16 + oc, :],
                                                                       op=ALU.mult), R=[pb[bs + 2], bgtg[g2]], W=[btm[2]])
                I_("pool", lambda e: e.tensor_tensor(out=tm[0], in0=tm[0], in1=tm[1], op=ALU.add), R=[btm[0], btm[1]], W=[btm[0]])
                I_("pool", lambda e, oc=oc, g2=g2: e.tensor_tensor(out=mT[g2][:, oc, :], in0=tm[0], in1=tm[2], op=ALU.add),
                   R=[btm[0], btm[2]], W=[bmT[g2]])
            for tt in range(4 if P4STOP >= 5 else 0):
                n = 4 * G + tt
                o2 = n % 2
                P.dma("sp", xt4[o2], x_d[n * 128:(n + 1) * 128, :], W=[bxt4[o2]], grp="xt4%d" % o2)
                for hf in range(2):
                    for k in range(8):
                        I_("pe", lambda e, hf=hf, k=k, tt=tt, g2=g2: e.matmul(
                            bank(6 + hf), lhsT=mT[g2][:, k, tt * 128:(tt + 1) * 128], rhs=wo[:, k, hf * 512:(hf + 1) * 512],
                            start=(k == 0), stop=(k == 7)), R=[bmT[g2], bwo], W=[pb[6 + hf]])
                I_("dve", lambda e, o2=o2: e.tensor_tensor(out=x1t[o2], in0=psum[:, 6:8, :].rearrange("p a b -> p (a b)"),
                                                           in1=xt4[o2], op=ALU.add), R=[pb[6], pb[7], bxt4[o2]], W=[bx1t[o2]])
                P.dma("pool", X1[n * 128:(n + 1) * 128, :], x1t[o2], R=[bx1t[o2]], grp="x1t%d" % o2)
        P.barrier()
        A.reset()

    if 5 in phases:
        W1 = A.get([8, 4096], BF16)
        W2 = A.get([32, D], BF16)
        stage5 = make_stage(2048, "p5")
        bW1 = load_weight_bf16(W1, w1_d, 8, 4096, gmlp_d, "w1", chunk=2048, stage=stage5)
        bW2 = load_weight_bf16(W2, w2_d, 32, D, None, "w2", chunk=1024, stage=stage5)
        xt5 = [A.get([D], F32) for _ in range(4)]
        bxt5 = [Buf("xt5%d" % i) for i in range(4)]
        xb5 = [A.get([D], BF16) for _ in range(2)]
        bxb5 = [Buf("xb5%d" % i) for i in range(2)]
        sq5 = A.get([D], F32)
        st5 = A.get([8], F32)
        bxs5 = dict(sq=sq5, st=st5, b=Buf("xstat5"))
        h2T = [A.get([8, 256], BF16) for _ in range(2)]
        bh2T = [Buf("h2T%d" % i) for i in range(2)]
        rr = [A.get([256], BF16) for _ in range(3)]
        brr = [Buf("rr%d" % i) for i in range(3)]
        aa = [A.get([256], BF16) for _ in range(3)]
        baa = [Buf("aa%d" % i) for i in range(3)]
        ot = [A.get([D], F32) for _ in range(2)]
        bot = [Buf("ot%d" % i) for i in range(2)]
        ng5 = ntiles // 2
        cc5 = [0]
        for G in range(ng5):
            g2 = G % 2
            for tt in range(2):
                n = 2 * G + tt
                s4 = n % 4
                s2 = n % 2
                norm_rows_to_hT(X1[n * 128:(n + 1) * 128, :], xt5[s4], xb5[s2], None, bxt5[s4], bxb5[s2], bxs5,
                                "xt5%d" % s4, 7, pb[7])
                I_("act", lambda e, g2=g2, tt=tt: e.copy(out=h2T[g2][:, :, tt * 128:(tt + 1) * 128],
                                                         in_=bank_bf(7).rearrange("p (k c) -> p k c", k=8)),
                   R=[pb[7]], W=[bh2T[g2]])
            slots5 = {}

            def U5(c):
                ub = 4 + cc5[0] % 3
                r3 = cc5[0] % 3
                cc5[0] += 1
                slots5[c] = r3
                for k in range(8):
                    I_("pe", lambda e, ub=ub, k=k, c=c, g2=g2: e.matmul(
                        bank(ub)[:, 0:256], lhsT=W1[:, k, c * 128:(c + 1) * 128], rhs=h2T[g2][:, k, :],
                        start=(k == 0), stop=(k == 7)), R=[bW1, bh2T[g2]], W=[pb[ub]])
                I_("act", lambda e, ub=ub, r3=r3: e.activation(out=rr[r3], in_=bank(ub)[:, 0:256], func=AF.Relu),
                   R=[pb[ub]], W=[brr[r3]])
                I_("dve", lambda e, r3=r3: e.tensor_tensor(out=aa[r3], in0=rr[r3], in1=rr[r3], op=ALU.mult),
                   R=[brr[r3]], W=[baa[r3]])

            def O5(c):
                r3 = slots5.pop(c)
                for tt in range(2):
                    for hf in range(2):
                        I_("pe", lambda e, tt=tt, hf=hf, r3=r3, c=c: e.matmul(
                            bank(tt * 2 + hf), lhsT=aa[r3][:, tt * 128:(tt + 1) * 128], rhs=W2[:, c, hf * 512:(hf + 1) * 512],
                            start=(c == 0), stop=(c == 31)), R=[baa[r3], bW2], W=[pb[tt * 2 + hf]])

            U5(0)
            for c in range(32):
                if c + 1 < 32:
                    U5(c + 1)
                O5(c)
            for tt in range(2):
                n = 2 * G + tt
                s4 = n % 4
                s2 = n % 2
                I_("dve", lambda e, tt=tt, s2=s2, s4=s4: e.tensor_tensor(
                    out=ot[s2], in0=psum[:, 2 * tt:2 * tt + 2, :].rearrange("p a b -> p (a b)"), in1=xt5[s4], op=ALU.add),
                   R=[pb[2 * tt], pb[2 * tt + 1], bxt5[s4]], W=[bot[s2]])
                P.dma("pool", out_d[n * 128:(n + 1) * 128, :], ot[s2], R=[bot[s2]], grp="ot%d" % s2)
        P.barrier()
        A.reset()

    P.barrier()
    P.emit()
    return nc, P


def _host_layout(inputs, b):
    def kmaj(w):
        k = w.shape[0] // 128
        return np.ascontiguousarray(w.reshape(k, 128, w.shape[1]).transpose(1, 0, 2))
    perm = _win_perm()
    m = {
        "x": np.ascontiguousarray(inputs["x"][b]),
        "mem": np.ascontiguousarray(inputs["mem"][b]),
        "pos": np.ascontiguousarray(inputs["positions"][b].reshape(NT, 128).T),
        "w_in": kmaj(inputs["w_in"][0][:, perm]),
        "g_mix": np.ascontiguousarray(inputs["g_mix"][0].reshape(8, 128).T),
        "g_mem": np.ascontiguousarray(inputs["g_mem"][0].reshape(8, 128).T),
        "g_mlp": np.ascontiguousarray(inputs["g_mlp"][0].reshape(8, 128).T),
        "gains": np.concatenate([inputs[k][0] for k in ("g_qa", "g_ka", "g_qb", "g_kb", "g_qc", "g_kc")])[None, :],
        "w_mem_kv": kmaj(inputs["w_mem_kv"][0]),
        "w_a": kmaj(inputs["w_a"][0]),
        "w_b": kmaj(inputs["w_b"][0]),
        "w_c": kmaj(inputs["w_c"][0]),
        "w_o": kmaj(inputs["w_o"][0]),
        "w_1": kmaj(inputs["w_1"][0]),
        "w_2": kmaj(inputs["w_2"][0]),
    }
    return {k: np.ascontiguousarray(v) for k, v in m.items()}


def kernel(**inputs):
    inputs = {k: np.asarray(v) for k, v in inputs.items()}
    nc, _ = build_program()
    in_maps = [_host_layout(inputs, b) for b in range(8)]
    res = run_bass_kernel_spmd(nc, in_maps, core_ids=list(range(8)))
    return np.stack([np.asarray(r["out"]) for r in res.results], axis=0).astype(np.float32)
```

```python
import bisect
import os
import numpy as np
import concourse.bass as bass
import concourse.mybir as mybir
from concourse.bass_utils import run_bass_kernel_spmd

F32 = mybir.dt.float32
BF16 = mybir.dt.bfloat16
I32 = mybir.dt.int32
ALU = mybir.AluOpType
AF = mybir.ActivationFunctionType
AX = mybir.AxisListType

ENGS = ("pe", "act", "dve", "pool", "sp")

S = 8192
D = 1024
NT = S // 128
EPS = 1e-6
NEG = -30000.0
TOPK = 256
NBIS = 16


class Buf:
    __slots__ = ("name", "w", "r", "excl")

    def __init__(self, name="", excl=False):
        self.name = name
        self.w = None
        self.r = []
        self.excl = excl


class Prog:
    def __init__(self, nc):
        self.nc = nc
        self.ins = []
        self.by_eng = {e: [] for e in ENGS}
        self.groups = {}

    def _deps(self, R, W):
        deps = set()
        for b in R:
            if b.w is not None:
                deps.add(b.w)
        for b in W:
            if b.w is not None:
                deps.add(b.w)
            deps.update(b.r)
        return deps

    def _commit(self, iid, R, W):
        for b in R:
            b.r.append(iid)
        for b in W:
            b.w = iid
            b.r = []

    def I(self, eng, fn, R=(), W=()):
        iid = len(self.ins)
        W = list(W) + [b for b in R if b.excl and b not in W]
        deps = self._deps(R, W)
        raw = set(b.w for b in R if b.w is not None)
        self.ins.append(dict(eng=eng, fn=fn, deps=deps, raw=raw, grp=None))
        self.by_eng[eng].append(iid)
        self._commit(iid, R, W)
        return iid

    def dma(self, eng, out, in_, R=(), W=(), grp="g", **kw):
        iid = len(self.ins)
        deps = self._deps(R, W)
        raw = set(b.w for b in R if b.w is not None)
        self.groups.setdefault(grp, []).append(iid)
        self.ins.append(dict(eng=eng, fn=(lambda e, o=out, i=in_, k=kw: e.dma_start(out=o, in_=i, **k)),
                             deps=deps, raw=raw, grp=grp))
        self.by_eng[eng].append(iid)
        self._commit(iid, R, W)
        return iid

    def barrier(self):
        alld = set()
        for e in ENGS:
            for k in reversed(self.by_eng[e]):
                if self.ins[k]["fn"] is not None:
                    alld.add(k)
                    break
        for g, l in self.groups.items():
            if l:
                alld.add(l[-1])
        for e in ENGS:
            iid = len(self.ins)
            self.ins.append(dict(eng=e, fn=None, deps=set(alld), raw=set(alld), grp=None))
            self.by_eng[e].append(iid)

    def emit(self):
        nc = self.nc
        ins = self.ins
        n = len(ins)
        is_target = [False] * n
        for k in range(n):
            for d in ins[k]["deps"]:
                is_target[d] = True
        ms_val = [0] * n
        cnt = {e: 0 for e in ENGS}
        for k in range(n):
            it = ins[k]
            if it["grp"] is None and it["fn"] is not None and is_target[k]:
                cnt[it["eng"]] += 1
                ms_val[k] = cnt[it["eng"]]
        self.esem = {e: nc.alloc_semaphore("sem_" + e) for e in ENGS}
        self.gsem = {g: nc.alloc_semaphore("dsem_" + g) for g in self.groups}
        self.nwaits = 0
        prog = self

        def run_engine(ename, eobj):
            seen = {}
            for k in prog.by_eng[ename]:
                it = ins[k]
                need = {}
                for d in it["deps"]:
                    dd = ins[d]
                    if dd["grp"] is not None:
                        g = dd["grp"]
                        c = bisect.bisect_left(prog.groups[g], k)
                        key = ("g", g)
                        val = 16 * c
                    else:
                        if dd["fn"] is None:
                            continue
                        if dd["eng"] == ename and it["grp"] is None and it["fn"] is not None:
                            if ename == "pe":
                                continue
                            if d not in it["raw"]:
                                continue
                        key = ("e", dd["eng"])
                        val = ms_val[d]
                    if val > need.get(key, 0):
                        need[key] = val
                for key, val in need.items():
                    if seen.get(key, 0) >= val:
                        continue
                    seen[key] = val
                    sem = prog.esem[key[1]] if key[0] == "e" else prog.gsem[key[1]]
                    eobj.wait_ge(sem, val)
                    prog.nwaits += 1
                if it["fn"] is None:
                    continue
                bi = it["fn"](eobj)
                if it["grp"] is not None:
                    bi.then_inc(prog.gsem[it["grp"]], 16)
                elif is_target[k]:
                    bi.then_inc(prog.esem[ename], 1)

        with nc.Block() as block:
            @block.tensor
            def _(e):
                run_engine("pe", e)

            @block.scalar
            def _(e):
                run_engine("act", e)

            @block.vector
            def _(e):
                run_engine("dve", e)

            @block.gpsimd
            def _(e):
                run_engine("pool", e)

            @block.sync
            def _(e):
                run_engine("sp", e)


def interleave_streams(streams):
    st = [[g, max(n, 1), 0, False] for g, n in streams]
    while True:
        live = [x for x in st if not x[3]]
        if not live:
            return
        x = min(live, key=lambda x: x[2] / x[1])
        try:
            next(x[0])
            x[2] += 1
        except StopIteration:
            x[3] = True


class Arena:
    def __init__(self, nc, nbytes):
        self.t = nc.alloc_sbuf_tensor("arena", [128, nbytes // 2], BF16)
        self.n = nbytes // 2
        self.base = 0
        self.p = 0

    def mark(self):
        self.base = self.p

    def reset(self):
        self.p = self.base

    def get(self, shape, dtype):
        ne = int(np.prod(shape))
        w = ne * (2 if dtype in (F32, I32) else 1)
        w = (w + 31) // 32 * 32
        assert self.p + w <= self.n, ("SBUF arena overflow", self.p, w, self.n)
        ap = self.t[:, self.p:self.p + w]
        self.p += w
        if dtype != BF16:
            ap = ap.bitcast(dtype)
        ap = ap[:, 0:ne]
        if len(shape) == 2:
            ap = ap.rearrange("p (a b) -> p a b", a=shape[0])
        elif len(shape) == 3:
            ap = ap.rearrange("p (a b c) -> p a b c", a=shape[0], b=shape[1])
        return ap


IN_COLS = 5704
N_TM = 2560
N_SM = 72
N_G = 3072


def _win_perm():
    o = {}
    acc = 0
    for nme, w in (("qa", 384), ("ka", 384), ("va", 384), ("qb", 384), ("kb", 128), ("vb", 128),
                   ("qi", 512), ("ki", 64), ("wi", 8), ("qc", 256), ("gl", 3072)):
        o[nme] = np.arange(acc, acc + w)
        acc += w
    qb = o["qb"].reshape(2, 3, 64).transpose(1, 0, 2).reshape(-1)
    return np.concatenate([o["qa"], o["ka"], qb, o["kb"], o["qi"], o["qc"], o["va"], o["vb"],
                           o["ki"], o["wi"], o["gl"]])


def build_program(debug=(), phases=(1, 2, 3, 4, 5), ntiles=NT):
    nc = bass.Bass("TRN2", target_bir_lowering=False)
    P = Prog(nc)
    I_ = P.I

    def din(name, shape, dt=F32):
        return nc.dram_tensor(name, shape, dt, kind="ExternalInput").ap()

    def dscr(name, shape, dt):
        return nc.dram_tensor(name, shape, dt, kind=("ExternalOutput" if name in debug else "Internal")).ap()

    x_d = din("x", [S, D])
    mem_d = din("mem", [256, D])
    pos_d = din("pos", [128, NT], I32)
    win_d = din("w_in", [128, 8, IN_COLS])
    gmix_d = din("g_mix", [128, 8])
    gmem_d = din("g_mem", [128, 8])
    gmlp_d = din("g_mlp", [128, 8])
    gains_d = din("gains", [1, 384])
    wkv_d = din("w_mem_kv", [128, 8, 512])
    wa_d = din("w_a", [128, 1, D])
    wb_d = din("w_b", [128, 3, D])
    wc_d = din("w_c", [128, 2, D])
    wo_d = din("w_o", [128, 8, D])
    w1_d = din("w_1", [128, 8, 4096])
    w2_d = din("w_2", [128, 32, D])
    out_d = nc.dram_tensor("out", [S, D], F32, kind="ExternalOutput").ap()

    QKVA = dscr("s_qkva", [S, 1152], BF16)
    QBT = dscr("s_qbt", [128, NT, 384], BF16)
    KBT = dscr("s_kbt", [128, S], BF16)
    VB = dscr("s_vb", [S, 136], BF16)
    QIT = dscr("s_qit", [128, NT, 512], BF16)
    KIT = dscr("s_kit", [64, S], BF16)
    QCT = dscr("s_qct", [128, NT, 256], BF16)
    GT = dscr("s_gt", [24, 128, S], BF16)
    OA = dscr("s_oa", [3, S, 130], F32)
    OBT = dscr("s_obt", [3, 128, S], BF16)
    X1 = dscr("s_x1", [S, D], F32)

    A = Arena(nc, 206 * 1024)
    psum = nc.alloc_psum_tensor("psum", [128, 8, 512], F32)

    def bank(i):
        return psum[:, i, :]

    def bank_bf(i):
        return psum[:, i, :].bitcast(BF16)

    pb = [Buf("bank%d" % i, excl=True) for i in range(8)]

    ident = A.get([128], BF16)
    identf = A.get([128], F32)
    mcur = A.get([128], BF16)
    mprev = A.get([128], BF16)
    i3 = A.get([384], BF16)
    gains = A.get([384], F32)
    cs = A.get([NT, 16], F32)
    b_const = Buf("const")
    I_("pool", lambda e: e.memset(identf, 0.0), W=[b_const])
    I_("pool", lambda e: e.affine_select(out=identf, in_=identf, pattern=[[-1, 128]], compare_op=ALU.not_equal,
                                         fill=1.0, base=0, channel_multiplier=1), R=[b_const], W=[b_const])
    I_("pool", lambda e: e.tensor_copy(out=ident, in_=identf), R=[b_const], W=[b_const])
    for g in range(3):
        I_("pool", lambda e, g=g: e.tensor_copy(out=i3[:, g * 128:(g + 1) * 128], in_=identf), R=[b_const], W=[b_const])
    I_("pool", lambda e: e.memset(mcur, 0.0), W=[b_const], R=[b_const])
    I_("pool", lambda e: e.affine_select(out=mcur, in_=mcur, pattern=[[1, 128]], compare_op=ALU.is_ge,
                                         fill=NEG, base=0, channel_multiplier=-1), R=[b_const], W=[b_const])
    I_("pool", lambda e: e.memset(mprev, 0.0), W=[b_const], R=[b_const])
    I_("pool", lambda e: e.affine_select(out=mprev, in_=mprev, pattern=[[-1, 128]], compare_op=ALU.is_ge,
                                         fill=NEG, base=0, channel_multiplier=1), R=[b_const], W=[b_const])
    P.dma("sp", gains, gains_d.partition_broadcast(128), W=[b_const], grp="c0")
    A.mark()
    posi = A.get([NT], I32)
    posf = A.get([NT], F32)
    inv = A.get([8], F32)
    ang = A.get([NT, 8], F32)
    ang2 = A.get([NT, 16], F32)
    P.dma("sp", posi, pos_d, W=[b_const], grp="c0")
    I_("dve", lambda e: e.tensor_copy(out=posf, in_=posi), R=[b_const], W=[b_const])
    for i in range(8):
        I_("pool", lambda e, i=i: e.memset(inv[:, i:i + 1], float(np.float32(500000.0) ** np.float32(-i / 8.0))),
           R=[b_const], W=[b_const])
    I_("dve", lambda e: e.tensor_tensor(out=ang, in0=posf.unsqueeze(2).to_broadcast([128, NT, 8]),
                                        in1=inv.unsqueeze(1).to_broadcast([128, NT, 8]), op=ALU.mult),
       R=[b_const], W=[b_const])
    TWO_PI = float(2 * np.pi)
    I_("dve", lambda e: e.tensor_scalar(out=ang2[:, :, 0:8], in0=ang, scalar1=float(0.5 * np.pi), scalar2=None,
                                        op0=ALU.add), R=[b_const], W=[b_const])
    I_("dve", lambda e: e.tensor_copy(out=ang2[:, :, 8:16], in_=ang), R=[b_const], W=[b_const])
    angk = A.get([NT, 16], F32)
    angi = A.get([NT, 16], I32)
    I_("dve", lambda e: e.tensor_scalar(out=angk, in0=ang2, scalar1=float(1.0 / (2 * np.pi)), scalar2=None,
                                        op0=ALU.mult), R=[b_const], W=[b_const])
    I_("dve", lambda e: e.tensor_copy(out=angi, in_=angk), R=[b_const], W=[b_const])
    I_("dve", lambda e: e.tensor_copy(out=angk, in_=angi), R=[b_const], W=[b_const])
    I_("dve", lambda e: e.scalar_tensor_tensor(out=ang2, in0=angk, scalar=-TWO_PI, in1=ang2, op0=ALU.mult,
                                               op1=ALU.add), R=[b_const], W=[b_const])
    I_("dve", lambda e: e.tensor_scalar(out=angk, in0=ang2, scalar1=float(np.pi), scalar2=TWO_PI, op0=ALU.is_gt,
                                        op1=ALU.mult), R=[b_const], W=[b_const])
    I_("dve", lambda e: e.tensor_tensor(out=ang2, in0=ang2, in1=angk, op=ALU.subtract), R=[b_const], W=[b_const])
    I_("dve", lambda e: e.tensor_scalar(out=angk, in0=ang2, scalar1=float(-np.pi), scalar2=TWO_PI, op0=ALU.is_lt,
                                        op1=ALU.mult), R=[b_const], W=[b_const])
    I_("dve", lambda e: e.tensor_tensor(out=ang2, in0=ang2, in1=angk, op=ALU.add), R=[b_const], W=[b_const])
    I_("dve", lambda e: e.tensor_scalar(out=ang2, in0=ang2, scalar1=3.141592, scalar2=-3.141592, op0=ALU.min,
                                        op1=ALU.max), R=[b_const], W=[b_const])
    I_("act", lambda e: e.activation(out=cs, in_=ang2, func=AF.Sin), R=[b_const], W=[b_const])
    P.barrier()
    A.reset()

    def rstd_from_ss(ss, rs, n, bufs, width):
        I_("dve", lambda e: e.tensor_scalar(out=rs, in0=ss, scalar1=1.0 / n, scalar2=EPS, op0=ALU.mult, op1=ALU.add),
           R=bufs, W=bufs)
        I_("act", lambda e: e.activation(out=rs, in_=rs, func=AF.Sqrt), R=bufs, W=bufs)
        I_("dve", lambda e: e.reciprocal(out=rs, in_=rs), R=bufs, W=bufs)

    def make_stage(chunk, tag):
        return ([A.get([chunk], F32) for _ in range(2)], [Buf(tag + "stg%d" % i) for i in range(2)], tag)

    def load_weight_bf16(dst, src_d, nk, ncols, gain_d, tag, chunk=2048, stage=None):
        if stage is None:
            stage = make_stage(chunk, tag)
        stg, sb, stag = stage
        bw = Buf(tag)
        gt = None
        if gain_d is not None:
            gt = A.get([8], F32)
            P.dma("sp", gt, gain_d, W=[bw], grp=tag + "g")
        it = 0
        for k in range(nk):
            for c0 in range(0, ncols, chunk):
                cw = min(chunk, ncols - c0)
                s = it % 2
                P.dma("sp", stg[s][:, 0:cw], src_d[:, k, c0:c0 + cw], W=[sb[s]], grp=stag + "s%d" % s)
                eng = ("dve", "pool")[it % 2]
                if gt is not None:
                    I_(eng, lambda e, s=s, k=k, c0=c0, cw=cw: e.tensor_scalar(
                        out=dst[:, k, c0:c0 + cw], in0=stg[s][:, 0:cw], scalar1=gt[:, k:k + 1], scalar2=None,
                        op0=ALU.mult), R=[sb[s], bw], W=[bw])
                else:
                    I_(eng, lambda e, s=s, k=k, c0=c0, cw=cw: e.tensor_copy(out=dst[:, k, c0:c0 + cw],
                                                                          in_=stg[s][:, 0:cw]), R=[sb[s]], W=[bw])
                it += 1
        return bw

    def norm_rows_to_hT(src_tile_d, xt, xb, hT_dst, bx, bh, bxs, grp, pbank, bpb):
        P.dma("sp", xt, src_tile_d, W=[bx], grp=grp)
        st = bxs["st"]
        I_("pool", lambda e: e.memset(st[:, 0:1], 0.0), W=[bxs["b"]])
        I_("act", lambda e: e.activation(out=bxs["sq"], in_=xt, func=AF.Square, accum_out=st[:, 0:1]),
           R=[bx, bxs["b"]], W=[bxs["b"]])
        rstd_from_ss(st[:, 0:1], st[:, 1:2], D, [bxs["b"]], 1)
        I_("dve", lambda e: e.tensor_scalar(out=xb, in0=xt, scalar1=st[:, 1:2], scalar2=None, op0=ALU.mult),
           R=[bx, bxs["b"]], W=[bh])
        pT = bank_bf(pbank)
        for k in range(8):
            I_("pe", lambda e, k=k: e.transpose(out=pT[:, k * 128:(k + 1) * 128], in_=xb[:, k * 128:(k + 1) * 128],
                                                identity=ident), R=[bh, b_const], W=[bpb])

    if 1 in phases:
        W = A.get([8, IN_COLS], BF16)
        p1_pos0 = A.p
        bW = load_weight_bf16(W, win_d, 8, IN_COLS, gmix_d, "win", chunk=1426)
        P.barrier()
        A.p = p1_pos0
        NB = 2
        xt = [A.get([D], F32) for _ in range(NB)]
        bx = [Buf("xt%d" % i) for i in range(NB)]
        xb = [A.get([D], BF16) for _ in range(NB)]
        bh = [Buf("xb%d" % i) for i in range(NB)]
        sq = A.get([D], F32)
        st = A.get([8], F32)
        bxs = dict(sq=sq, st=st, b=Buf("xstat"))
        hTg = [A.get([8, 512], BF16) for _ in range(2)]
        bhT = [Buf("hTg%d" % i) for i in range(2)]
        stg2 = [A.get([2632], F32) for _ in range(2)]
        bstg2 = [Buf("stg%d" % i) for i in range(2)]
        tmp2 = [A.get([1536], F32) for _ in range(2)]
        btmp2 = [Buf("tmp%d" % i) for i in range(2)]
        ss2 = [A.get([24], F32) for _ in range(2)]
        rs2 = [A.get([24], F32) for _ in range(2)]
        bss2 = [Buf("ss%d" % i) for i in range(2)]
        gainrow = A.get([1536], F32)
        bgr = Buf("gainrow")
        gsrc = [0] * 6 + [1] * 6 + [2] * 6 + [3] * 2 + [4] * 4
        for hh, gi in enumerate(gsrc):
            I_("pool", lambda e, hh=hh, gi=gi: e.tensor_copy(out=gainrow[:, hh * 64:(hh + 1) * 64],
                                                            in_=gains[:, gi * 64:(gi + 1) * 64]),
               R=[b_const], W=[bgr])
        rt2 = [A.get([4, 29, 8], F32) for _ in range(2)]
        brt2 = [Buf("rt%d" % i) for i in range(2)]
        oA = [A.get([1152], BF16) for _ in range(2)]
        boA = [Buf("oA%d" % i) for i in range(2)]
        oT = [A.get([1344], BF16) for _ in range(2)]
        boT = [Buf("oT%d" % i) for i in range(2)]
        oV = [A.get([136], BF16) for _ in range(2)]
        boV = [Buf("oV%d" % i) for i in range(2)]
        tT = [A.get([1408], BF16) for _ in range(2)]
        btT = [Buf("tT%d" % i) for i in range(2)]
        gsb = [A.get([512], BF16) for _ in range(3)]
        bgsb = [Buf("gsb%d" % i) for i in range(3)]
        WI_SCALE = float((8 ** -0.5) * (64 ** -0.5))
        gcount = [0]

        def p1_A(n):
            grp_i, tt = n // 4, n % 4
            hs = grp_i % 2
            s = n % NB
            o = n % 2
            stg = stg2[o]
            bstg = bstg2[o]
            norm_rows_to_hT(x_d[n * 128:(n + 1) * 128, :], xt[s], xb[s], None, bx[s], bh[s], bxs, "x%d" % s, 5, pb[5])
            I_("act", lambda e: e.copy(out=hTg[hs][:, :, tt * 128:(tt + 1) * 128],
                                       in_=bank_bf(5).rearrange("p (k c) -> p k c", k=8)), R=[pb[5]], W=[bhT[hs]])
            yield
            for bnk in range(5):
                for k in range(8):
                    I_("pe", lambda e, bnk=bnk, k=k: e.matmul(
                        bank(bnk), lhsT=hTg[hs][:, k, tt * 128:(tt + 1) * 128],
                        rhs=W[:, k, bnk * 512:(bnk + 1) * 512], start=(k == 0), stop=(k == 7)),
                       R=[bhT[hs], bW], W=[pb[bnk]])
                yield
            for k in range(8):
                I_("pe", lambda e, k=k: e.matmul(
                    bank(7)[:, 0:N_SM], lhsT=hTg[hs][:, k, tt * 128:(tt + 1) * 128],
                    rhs=W[:, k, N_TM:N_TM + N_SM], start=(k == 0), stop=(k == 7)),
                   R=[bhT[hs], bW], W=[pb[7]])
            I_("act", lambda e: e.copy(out=stg[:, 0:1792], in_=psum[:, 0:4, :].rearrange("p a b -> p (a b)")[:, 0:1792]),
               R=[pb[0], pb[1], pb[2], pb[3]], W=[bstg])
            I_("dve", lambda e: e.tensor_copy(out=stg[:, 1856:2624],
                                              in_=psum[:, 3:5, :].rearrange("p a b -> p (a b)")[:, 256:1024]),
               R=[pb[3], pb[4]], W=[bstg])
            I_("dve", lambda e: e.tensor_copy(out=stg[:, 1792:1856], in_=bank(7)[:, 0:64]), R=[pb[7]], W=[bstg])
            I_("dve", lambda e: e.tensor_scalar(out=stg[:, 2624:2632], in0=bank(7)[:, 64:72], scalar1=WI_SCALE,
                                                scalar2=None, op0=ALU.mult), R=[pb[7]], W=[bstg])
            yield

        def p1_B(n):
            o = n % 2
            stg, bstg = stg2[o], bstg2[o]
            tmp, btmp = tmp2[o], btmp2[o]
            ss, rs, bss = ss2[o], rs2[o], bss2[o]
            rt, brt = rt2[o], brt2[o]
            I_("pool", lambda e: e.tensor_tensor(out=tmp[:, 0:1280], in0=stg[:, 0:1280], in1=stg[:, 0:1280],
                                                 op=ALU.mult), R=[bstg], W=[btmp])
            I_("pool", lambda e: e.tensor_tensor(out=tmp[:, 1280:1536], in0=stg[:, 1856:2112],
                                                 in1=stg[:, 1856:2112], op=ALU.mult), R=[bstg], W=[btmp])
            yield
            I_("dve", lambda e: e.tensor_reduce(out=ss, in_=tmp.rearrange("p (h d) -> p h d", d=64), axis=AX.X,
                                                op=ALU.add), R=[btmp], W=[bss])
            yield
            rstd_from_ss(ss, rs, 64, [bss], 24)
            yield
            I_("dve", lambda e: e.tensor_tensor(
                out=tmp.rearrange("p (h d) -> p h d", d=64), in0=gainrow.rearrange("p (h d) -> p h d", d=64),
                in1=rs.unsqueeze(2).to_broadcast([128, 24, 64]), op=ALU.mult), R=[bss, bgr, btmp], W=[btmp])
            yield
            I_("pool", lambda e: e.tensor_tensor(out=stg[:, 0:1280], in0=stg[:, 0:1280], in1=tmp[:, 0:1280],
                                                 op=ALU.mult), R=[bstg, btmp], W=[bstg])
            I_("pool", lambda e: e.tensor_tensor(out=stg[:, 1856:2112], in0=stg[:, 1856:2112],
                                                 in1=tmp[:, 1280:1536], op=ALU.mult), R=[bstg, btmp], W=[bstg])
            yield
            v = stg[:, 0:1856].rearrange("p (h d) -> p h d", d=64)
            x1 = v[:, :, 0:8]
            x2 = v[:, :, 8:16]
            cosb = cs[:, n, 0:8].unsqueeze(1).to_broadcast([128, 29, 8])
            sinb = cs[:, n, 8:16].unsqueeze(1).to_broadcast([128, 29, 8])
            I_("dve", lambda e: e.tensor_tensor(out=rt[:, 0], in0=x1, in1=cosb, op=ALU.mult), R=[bstg, b_const], W=[brt])
            I_("dve", lambda e: e.tensor_tensor(out=rt[:, 1], in0=x2, in1=sinb, op=ALU.mult), R=[bstg, b_const], W=[brt])
            I_("pool", lambda e: e.tensor_tensor(out=rt[:, 2], in0=x2, in1=cosb, op=ALU.mult), R=[bstg, b_const], W=[brt])
            I_("pool", lambda e: e.tensor_tensor(out=rt[:, 3], in0=x1, in1=sinb, op=ALU.mult), R=[bstg, b_const], W=[brt])
            yield
            I_("dve", lambda e: e.tensor_tensor(out=x1, in0=rt[:, 0], in1=rt[:, 1], op=ALU.subtract), R=[brt], W=[bstg])
            I_("dve", lambda e: e.tensor_tensor(out=x2, in0=rt[:, 2], in1=rt[:, 3], op=ALU.add), R=[brt], W=[bstg])
            yield
            I_("act", lambda e: e.copy(out=oA[o][:, 0:768], in_=stg[:, 0:768]), R=[bstg], W=[boA[o]])
            I_("act", lambda e: e.copy(out=oA[o][:, 768:1152], in_=stg[:, 2112:2496]), R=[bstg], W=[boA[o]])
            I_("pool", lambda e: e.tensor_copy(out=oT[o], in_=stg[:, 768:2112]), R=[bstg], W=[boT[o]])
            I_("pool", lambda e: e.tensor_copy(out=oV[o], in_=stg[:, 2496:2632]), R=[bstg], W=[boV[o]])
            P.dma("pool", QKVA[n * 128:(n + 1) * 128, :], oA[o], R=[boA[o]], grp="oA%d" % o)
            P.dma("pool", VB[n * 128:(n + 1) * 128, :], oV[o], R=[boV[o]], grp="oV%d" % o)
            yield
            pT6 = bank_bf(6)
            for c in range(8):
                I_("pe", lambda e, c=c: e.transpose(out=pT6[:, c * 128:(c + 1) * 128],
                                                    in_=oT[o][:, c * 128:(c + 1) * 128], identity=ident),
                   R=[boT[o], b_const], W=[pb[6]])
            I_("dve", lambda e: e.tensor_copy(out=tT[o][:, 0:1024], in_=pT6), R=[pb[6]], W=[btT[o]])
            yield
            I_("pe", lambda e: e.transpose(out=pT6[0:64, 0:128], in_=oT[o][:, 1024:1088], identity=ident),
               R=[boT[o], b_const], W=[pb[6]])
            for c in range(2):
                I_("pe", lambda e, c=c: e.transpose(out=pT6[:, 128 + c * 128:256 + c * 128],
                                                    in_=oT[o][:, 1088 + c * 128:1216 + c * 128], identity=ident),
                   R=[boT[o], b_const], W=[pb[6]])
            I_("dve", lambda e: e.tensor_copy(out=tT[o][0:64, 1024:1152], in_=pT6[0:64, 0:128]), R=[pb[6]], W=[btT[o]])
            I_("dve", lambda e: e.tensor_copy(out=tT[o][:, 1152:1408], in_=pT6[:, 128:384]), R=[pb[6]], W=[btT[o]])
            g = "tT%d" % o
            P.dma("pool", QBT[:, n, :], tT[o][:, 0:384], R=[btT[o]], grp=g)
            P.dma("pool", KBT[:, n * 128:(n + 1) * 128], tT[o][:, 384:512], R=[btT[o]], grp=g)
            P.dma("pool", QIT[:, n, :], tT[o][:, 512:1024], R=[btT[o]], grp=g)
            P.dma("pool", KIT[:, n * 128:(n + 1) * 128], tT[o][0:64, 1024:1152], R=[btT[o]], grp=g)
            P.dma("pool", QCT[:, n, :], tT[o][:, 1152:1408], R=[btT[o]], grp=g)
            yield

        def p1_G(grp_i):
            hs = grp_i % 2
            for c in range(24):
                for k in range(8):
                    I_("pe", lambda e, c=c, k=k: e.matmul(
                        bank(7), lhsT=W[:, k, N_TM + N_SM + c * 128:N_TM + N_SM + (c + 1) * 128],
                        rhs=hTg[hs][:, k, :], start=(k == 0), stop=(k == 7)), R=[bhT[hs], bW], W=[pb[7]])
                gs = gcount[0] % 3
                gcount[0] += 1
                I_("act", lambda e, gs=gs: e.activation(out=gsb[gs], in_=bank(7), func=AF.Sigmoid),
                   R=[pb[7]], W=[bgsb[gs]])
                P.dma("sp", GT[c, :, grp_i * 512:(grp_i + 1) * 512], gsb[gs], R=[bgsb[gs]], grp="gsb%d" % gs)
                yield

        for _ in p1_A(0):
            pass
        for n in range(ntiles):
            streams = [(p1_B(n), 11)]
            if n + 1 < ntiles:
                streams.append((p1_A(n + 1), 7))
            if n % 4 == 3:
                streams.append((p1_G(n // 4), 24))
            interleave_streams(streams)
        P.barrier()
        A.reset()


    ntok = ntiles * 128
    if 2 in phases:
        blk = [A.get([3, 128], BF16) for _ in range(3)]
        bblk = [Buf("blk%d" % i) for i in range(3)]
        qT = [A.get([128], BF16) for _ in range(2)]
        bqT = [Buf("qT%d" % i) for i in range(2)]
        kT = [A.get([128], BF16) for _ in range(3)]
        bkT = [Buf("kT%d" % i) for i in range(3)]
        va_ = [A.get([2, 65], BF16) for _ in range(3)]
        bva = [Buf("va%d" % i) for i in range(3)]
        PT = [A.get([512], BF16) for _ in range(2)]
        bPT = [Buf("PT%d" % i) for i in range(2)]
        oas = [A.get([130], F32) for _ in range(2)]
        boas = [Buf("oas%d" % i) for i in range(2)]
        for i in range(3):
            I_("pool", lambda e, i=i: e.memset(va_[i], 1.0), W=[bva[i]])
        u = 0
        for g, dil in enumerate((1, 4, 16)):
            m = ntok // dil
            nblk = m // 128
            assert nblk >= 1
            for r in range(dil):
                for n in range(nblk):
                    sb3 = u % 3
                    sb2 = u % 2
                    prev3 = (u - 1) % 3
                    start = r + dil * 128 * n
                    rows = QKVA[start:start + dil * 127 + 1:dil, :].rearrange("t (s c) -> t s c", s=3)[:, :, g * 128:(g + 1) * 128]
                    P.dma("sp", blk[sb3], rows, W=[bblk[sb3]], grp="blk%d" % sb3)
                    pT = bank_bf(sb2)
                    I_("pe", lambda e, pT=pT, sb3=sb3: e.transpose(out=pT[:, 0:128], in_=blk[sb3][:, 0, :], identity=ident),
                       R=[bblk[sb3], b_const], W=[pb[sb2]])
                    I_("pe", lambda e, pT=pT, sb3=sb3: e.transpose(out=pT[:, 128:256], in_=blk[sb3][:, 1, :], identity=ident),
                       R=[bblk[sb3], b_const], W=[pb[sb2]])
                    I_("dve", lambda e, pT=pT, sb2=sb2: e.tensor_copy(out=qT[sb2], in_=pT[:, 0:128]), R=[pb[sb2]], W=[bqT[sb2]])
                    I_("dve", lambda e, pT=pT, sb3=sb3: e.tensor_copy(out=kT[sb3], in_=pT[:, 128:256]), R=[pb[sb2]], W=[bkT[sb3]])
                    I_("pool", lambda e, sb3=sb3: e.tensor_copy(out=va_[sb3][:, :, 0:64],
                                                                in_=blk[sb3][:, 2, :].rearrange("p (j d) -> p j d", j=2)),
                       R=[bblk[sb3]], W=[bva[sb3]])
                    psY = bank(2 + sb2)
                    has_prev = n > 0
                    for j in range(2):
                        if has_prev:
                            I_("pe", lambda e, j=j, psY=psY, prev3=prev3, sb2=sb2: e.matmul(
                                psY[:, j * 128:(j + 1) * 128], lhsT=kT[prev3][64 * j:64 * j + 64, :],
                                rhs=qT[sb2][64 * j:64 * j + 64, :], start=True, stop=False),
                               R=[bkT[prev3], bqT[sb2]], W=[pb[2 + sb2]])
                            I_("pe", lambda e, j=j, psY=psY: e.matmul(
                                psY[:, j * 128:(j + 1) * 128], lhsT=ident, rhs=mprev, start=False, stop=True),
                               R=[b_const], W=[pb[2 + sb2]])
                        I_("pe", lambda e, j=j, psY=psY, sb3=sb3, sb2=sb2: e.matmul(
                            psY[:, 256 + j * 128:256 + (j + 1) * 128], lhsT=kT[sb3][64 * j:64 * j + 64, :],
                            rhs=qT[sb2][64 * j:64 * j + 64, :], start=True, stop=False),
                           R=[bkT[sb3], bqT[sb2]], W=[pb[2 + sb2]])
                        I_("pe", lambda e, j=j, psY=psY: e.matmul(
                            psY[:, 256 + j * 128:256 + (j + 1) * 128], lhsT=ident, rhs=mcur, start=False, stop=True),
                           R=[b_const], W=[pb[2 + sb2]])
                    lo = 0 if has_prev else 256
                    I_("act", lambda e, psY=psY, sb2=sb2, lo=lo: e.activation(out=PT[sb2][:, lo:512], in_=psY[:, lo:512],
                                                                             func=AF.Exp, scale=0.125),
                       R=[pb[2 + sb2]], W=[bPT[sb2]])
                    psZ = bank(4 + sb2)
                    for j in range(2):
                        if has_prev:
                            I_("pe", lambda e, j=j, psZ=psZ, sb2=sb2, prev3=prev3: e.matmul(
                                psZ[:, j * 65:(j + 1) * 65], lhsT=PT[sb2][:, j * 128:(j + 1) * 128],
                                rhs=va_[prev3][:, j, :], start=True, stop=False),
                               R=[bPT[sb2], bva[prev3]], W=[pb[4 + sb2]])
                        I_("pe", lambda e, j=j, psZ=psZ, sb2=sb2, sb3=sb3, hp=has_prev: e.matmul(
                            psZ[:, j * 65:(j + 1) * 65], lhsT=PT[sb2][:, 256 + j * 128:256 + (j + 1) * 128],
                            rhs=va_[sb3][:, j, :], start=(not hp), stop=True),
                           R=[bPT[sb2], bva[sb3]], W=[pb[4 + sb2]])
                    I_("dve", lambda e, psZ=psZ, sb2=sb2: e.tensor_copy(out=oas[sb2], in_=psZ[:, 0:130]),
                       R=[pb[4 + sb2]], W=[boas[sb2]])
                    P.dma("pool", OA[g, start:start + dil * 127 + 1:dil, :], oas[sb2], R=[boas[sb2]], grp="oas%d" % sb2)
                    u += 1
        P.barrier()
        A.reset()

    if 3 in phases:
        U8 = mybir.dt.uint8
        kiT2 = A.get([S], BF16)
        kbT = A.get([S], BF16)
        vba = A.get([NT, 2, 65], BF16)
        wia = A.get([NT, 8], BF16)
        bK = Buf("dsaK")
        P.dma("sp", kiT2[0:64, 0:ntok], KIT[:, 0:ntok], W=[bK], grp="dk")
        P.dma("sp", kiT2[64:128, 0:ntok], KIT[:, 0:ntok], W=[bK], grp="dk")
        P.dma("sp", kbT[:, 0:ntok], KBT[:, 0:ntok], W=[bK], grp="dk")
        I_("pool", lambda e: e.memset(vba, 1.0), W=[bK])
        vst = [A.get([136], BF16) for _ in range(2)]
        bvst = [Buf("vst%d" % i) for i in range(2)]
        for n in range(ntiles):
            s2 = n % 2
            P.dma("sp", vst[s2], VB[n * 128:(n + 1) * 128, :], W=[bvst[s2]], grp="vst%d" % s2)
            I_("pool", lambda e, n=n, s2=s2: e.tensor_copy(out=vba[:, n, :, 0:64],
                                                          in_=vst[s2][:, 0:128].rearrange("p (c d) -> p c d", c=2)),
               R=[bvst[s2], bK], W=[bK])
            I_("pool", lambda e, n=n, s2=s2: e.tensor_copy(out=wia[:, n, :], in_=vst[s2][:, 128:136]),
               R=[bvst[s2], bK], W=[bK])
        NI = 3
        qiT = [A.get([4, 128], BF16) for _ in range(2)]
        bqiT = [Buf("qiT%d" % i) for i in range(2)]
        qbT = [A.get([384], BF16) for _ in range(3)]
        bqbT = [Buf("qbT%d" % i) for i in range(3)]
        Dh = [A.get([8, 128], BF16) for _ in range(2)]
        bDh = [Buf("Dh%d" % i) for i in range(2)]
        isc = [A.get([S], F32) for _ in range(NI)]
        bisc = [Buf("isc%d" % i) for i in range(NI)]
        junk_t = A.get([S // 2], BF16)
        junk = junk_t.bitcast(U8)
        bjunk = Buf("junk")
        MBF = A.get([S], BF16)
        bMBF = [Buf("MBF%d" % c) for c in range(S // 512)]
        Rb = [A.get([2, 512], BF16) for _ in range(3)]
        bRb = [Buf("Rb%d" % i) for i in range(3)]
        PTb = [A.get([384], BF16) for _ in range(3)]
        bPTb = [Buf("PTb%d" % i) for i in range(3)]
        bst = [A.get([8], F32) for _ in range(NI)]
        bbst = [Buf("bst%d" % i) for i in range(NI)]
        Wk = [A.get([NBIS + 1], F32) for _ in range(NI)]
        pow2 = A.get([NBIS + 1], F32)
        cntb = A.get([2], F32)
        bcnt = Buf("cnt")
        cntp = A.get([2], F32)
        bcntp = Buf("cntp")
        bjunkp = Buf("junkp")
        rden = A.get([6], F32)
        brden = Buf("rden")
        ob = [A.get([384], BF16) for _ in range(2)]
        bob = [Buf("ob%d" % i) for i in range(2)]
        obT = [A.get([384], BF16) for _ in range(2)]
        bobT = [Buf("obT%d" % i) for i in range(2)]
        bpow = Buf("pow2")
        for k in range(NBIS + 1):
            I_("pool", lambda e, k=k: e.memset(pow2[:, k:k + 1], float(2.0 ** -(k + 1))), W=[bpow], R=[bpow])
        pair = [psum[:, 0:2, :], psum[:, 2:4, :]]
        bpair = [Buf("pair0", excl=True), Buf("pair1", excl=True)]
        accI = bank(4)
        sbank = [bank(5), bank(6)]
        accO = bank(7)
        cnt_pair = [0]
        cnt_rb = [0]
        cnt_unit = [0]
        _fr = []

        def fill_reg(e):
            if not _fr:
                _fr.append(e.to_reg(-1e30))
            return _fr[0]

        def dsa_index(i):
            s2 = i % 2
            s3 = i % NI
            L = 128 * (i + 1)
            P.dma("sp", qiT[s2], QIT[:, i, :].rearrange("p (j t) -> p j t", j=4), W=[bqiT[s2]], grp="qiT%d" % s2)
            P.dma("sp", qbT[s3], QBT[:, i, :], W=[bqbT[s3]], grp="qbT%d" % s3)
            I_("pool", lambda e: e.tensor_tensor(out=Dh[s2], in0=identf.unsqueeze(1).to_broadcast([128, 8, 128]),
                                                 in1=wia[:, i, :].unsqueeze(2).to_broadcast([128, 8, 128]), op=ALU.mult),
               R=[bK, b_const], W=[bDh[s2]])
            steps = [(c0, min(512, L - c0), jj) for c0 in range(0, L, 512) for jj in range(4)]
            slots = {}

            def QK(k):
                c0, cw, jj = steps[k]
                pp = cnt_pair[0] % 2
                cnt_pair[0] += 1
                rb = cnt_rb[0] % 3
                cnt_rb[0] += 1
                slots[k] = (pp, rb)
                for hh in range(2):
                    I_("pe", lambda e, pp=pp, hh=hh, jj=jj, c0=c0, cw=cw: e.matmul(
                        pair[pp][:, hh, 0:cw], lhsT=qiT[s2][64 * hh:64 * hh + 64, jj, :],
                        rhs=kiT2[64 * hh:64 * hh + 64, c0:c0 + cw], start=True, stop=True),
                       R=[bqiT[s2], bK], W=[bpair[pp]])
                I_("act", lambda e, pp=pp, rb=rb, cw=cw: e.activation(out=Rb[rb][:, :, 0:cw], in_=pair[pp][:, :, 0:cw],
                                                                     func=AF.Relu), R=[bpair[pp]], W=[bRb[rb]])

            def HS(k):
                c0, cw, jj = steps[k]
                pp, rb = slots.pop(k)
                for hh in range(2):
                    I_("pe", lambda e, rb=rb, hh=hh, jj=jj, cw=cw: e.matmul(
                        accI[:, 0:cw], lhsT=Dh[s2][:, 2 * jj + hh, :], rhs=Rb[rb][:, hh, 0:cw],
                        start=(jj == 0 and hh == 0), stop=(jj == 3 and hh == 1)),
                       R=[bDh[s2], bRb[rb]], W=[pb[4]])
                if jj == 3:
                    I_("act", lambda e, c0=c0, cw=cw: e.copy(out=isc[s3][:, c0:c0 + cw], in_=accI[:, 0:cw]),
                       R=[pb[4]], W=[bisc[s3]])

            QK(0)
            for k in range(len(steps)):
                if k + 1 < len(steps):
                    QK(k + 1)
                HS(k)
                yield
            I_("pool", lambda e: e.affine_select(out=isc[s3][:, 128 * i:128 * (i + 1)], in_=isc[s3][:, 128 * i:128 * (i + 1)],
                                                 pattern=[[-1, 128]], compare_op=ALU.is_ge, fill=fill_reg(e), base=0,
                                                 channel_multiplier=1), R=[bisc[s3]], W=[bisc[s3]])
            yield

        def dsa_bisect(i):
            s3 = i % NI
            L = 128 * (i + 1)
            b = bst[s3]
            bb = bbst[s3]
            if i < 2:
                I_("dve", lambda e: e.memset(b[:, 3:4], -1e29), W=[bb])
                return
            I_("dve", lambda e: e.tensor_reduce(out=b[:, 0:1], in_=isc[s3][:, 0:L], axis=AX.X, op=ALU.max),
               R=[bisc[s3]], W=[bb])
            I_("dve", lambda e: e.tensor_reduce(out=b[:, 1:2], in_=isc[s3][:, 0:128 * i], axis=AX.X, op=ALU.min),
               R=[bisc[s3]], W=[bb])
            I_("dve", lambda e: e.tensor_tensor(out=b[:, 2:3], in0=b[:, 0:1], in1=b[:, 1:2], op=ALU.subtract),
               R=[bb], W=[bb])
            I_("dve", lambda e: e.tensor_scalar(out=Wk[s3], in0=pow2, scalar1=b[:, 2:3], scalar2=None, op0=ALU.mult),
               R=[bb, bpow], W=[bb])
            I_("dve", lambda e: e.tensor_tensor(out=b[:, 4:5], in0=b[:, 1:2], in1=Wk[s3][:, 0:1], op=ALU.add),
               R=[bb], W=[bb])
            for k in range(NBIS):
                I_("dve", lambda e: e.tensor_scalar(out=junk[:, 0:L], in0=isc[s3][:, 0:L], scalar1=b[:, 4:5], scalar2=None,
                                                    op0=ALU.is_ge, op1=ALU.add, accum_out=cntb[:, 0:1]),
                   R=[bisc[s3], bb], W=[bjunk, bcnt])
                I_("dve", lambda e: e.tensor_scalar(out=cntb[:, 1:2], in0=cntb[:, 0:1], scalar1=TOPK - 0.5, scalar2=0.5,
                                                    op0=ALU.is_ge, op1=ALU.subtract), R=[bcnt], W=[bcnt])
                I_("dve", lambda e, k=k: e.scalar_tensor_tensor(out=b[:, 4:5], in0=cntb[:, 1:2], scalar=Wk[s3][:, k:k + 1],
                                                               in1=b[:, 4:5], op0=ALU.mult, op1=ALU.add),
                   R=[bcnt, bb], W=[bb])
            I_("dve", lambda e: e.tensor_tensor(out=b[:, 3:4], in0=b[:, 4:5], in1=Wk[s3][:, NBIS:NBIS + 1], op=ALU.subtract),
               R=[bb], W=[bb])

        def dsa_maskgen(i):
            s3 = i % NI
            L = 128 * (i + 1)
            b = bst[s3]
            for c0 in range(0, L, 512):
                cw = min(512, L - c0)
                I_("dve", lambda e, c0=c0, cw=cw: e.tensor_scalar(out=MBF[:, c0:c0 + cw], in0=isc[s3][:, c0:c0 + cw],
                                                                 scalar1=b[:, 3:4], scalar2=NEG, op0=ALU.is_lt, op1=ALU.mult),
                   R=[bisc[s3], bbst[s3]], W=[bMBF[c0 // 512]])

        def dsa_attn(i):
            s2 = i % 2
            s3 = i % NI
            units = [(kb, c) for kb in range(i + 1) for c in range(2)]
            slots = {}

            def SC(k):
                kb, c = units[k]
                bm = bMBF[kb // 4]
                u = cnt_unit[0]
                cnt_unit[0] += 1
                sb = u % 2
                pt = u % 3
                slots[k] = pt
                psS = sbank[sb]
                I_("pe", lambda e, c=c, kb=kb, psS=psS: e.matmul(
                    psS[:, 0:384], lhsT=kbT[64 * c:64 * c + 64, kb * 128:(kb + 1) * 128],
                    rhs=qbT[s3][64 * c:64 * c + 64, :], start=True, stop=False),
                   R=[bK, bqbT[s3]], W=[pb[5 + sb]])
                I_("pe", lambda e, kb=kb, psS=psS: e.matmul(psS[:, 0:384], lhsT=MBF[:, kb * 128:(kb + 1) * 128], rhs=i3,
                                                           start=False, stop=True),
                   R=[bm, b_const], W=[pb[5 + sb]])
                I_("act", lambda e, pt=pt, psS=psS: e.activation(out=PTb[pt], in_=psS[:, 0:384], func=AF.Exp, scale=0.125),
                   R=[pb[5 + sb]], W=[bPTb[pt]])

            def PV(k):
                kb, c = units[k]
                pt = slots.pop(k)
                for g in range(3):
                    h = 3 * c + g
                    I_("pe", lambda e, pt=pt, g=g, h=h, kb=kb, c=c: e.matmul(
                        accO[:, h * 65:(h + 1) * 65], lhsT=PTb[pt][:, g * 128:(g + 1) * 128], rhs=vba[:, kb, c, :],
                        start=(kb == 0 and h == 0), stop=(kb == i), skip_group_check=True),
                        R=[bPTb[pt], bK], W=[pb[7]])

            SC(0)
            for k in range(len(units)):
                if k + 1 < len(units):
                    SC(k + 1)
                PV(k)
                yield

        def dsa_final(i):
            s2 = i % 2
            av = accO[:, 0:390].rearrange("p (h e) -> p h e", e=65)
            I_("dve", lambda e: e.reciprocal(out=rden, in_=av[:, :, 64]), R=[pb[7]], W=[brden])
            I_("dve", lambda e: e.tensor_tensor(out=ob[s2].rearrange("p (h d) -> p h d", d=64), in0=av[:, :, 0:64],
                                                in1=rden.unsqueeze(2).to_broadcast([128, 6, 64]), op=ALU.mult),
               R=[pb[7], brden], W=[bob[s2]])
            pT = bank_bf(4)
            for cc in range(3):
                I_("pe", lambda e, cc=cc: e.transpose(out=pT[:, cc * 128:(cc + 1) * 128], in_=ob[s2][:, cc * 128:(cc + 1) * 128],
                                                      identity=ident), R=[bob[s2], b_const], W=[pb[4]])
            I_("act", lambda e: e.copy(out=obT[s2], in_=pT[:, 0:384]), R=[pb[4]], W=[bobT[s2]])
            P.dma("pool", OBT[:, :, i * 128:(i + 1) * 128].rearrange("c p t -> p c t"),
                  obT[s2].rearrange("p (c t) -> p c t", c=3), R=[bobT[s2]], grp="obT%d" % s2)

        def run_all(gen):
            for _ in gen:
                pass

        def interleave(ga, na, gb, nb):
            da = db = 0
            ea = eb = False
            while not (ea and eb):
                fa = da / max(na, 1)
                fb = db / max(nb, 1)
                if (not ea) and (eb or fa <= fb):
                    try:
                        next(ga)
                        da += 1
                    except StopIteration:
                        ea = True
                else:
                    try:
                        next(gb)
                        db += 1
                    except StopIteration:
                        eb = True

        def n_idx_steps(i):
            return 4 * ((128 * (i + 1) + 511) // 512) + 1

        run_all(dsa_index(0))
        if ntiles > 1:
            run_all(dsa_index(1))
        dsa_bisect(0)
        for i in range(ntiles):
            dsa_maskgen(i)
            if i + 1 < ntiles:
                dsa_bisect(i + 1)
            ga = dsa_attn(i)
            if i + 2 < ntiles:
                interleave(dsa_index(i + 2), n_idx_steps(i + 2), ga, 2 * (i + 1))
            else:
                run_all(ga)
            dsa_final(i)
        P.barrier()
        A.reset()

    if 4 in phases:
        wa = A.get([1, D], BF16)
        wb_ = A.get([3, D], BF16)
        wc = A.get([2, D], BF16)
        wo = A.get([8, D], BF16)
        wkv = A.get([8, 512], BF16)
        stage4 = make_stage(1024, "p4")
        bwa = load_weight_bf16(wa, wa_d, 1, D, None, "wa", chunk=1024, stage=stage4)
        bwb = load_weight_bf16(wb_, wb_d, 3, D, None, "wb", chunk=1024, stage=stage4)
        bwc = load_weight_bf16(wc, wc_d, 2, D, None, "wc", chunk=1024, stage=stage4)
        bwo = load_weight_bf16(wo, wo_d, 8, D, None, "wo", chunk=1024, stage=stage4)
        bwkv = load_weight_bf16(wkv, wkv_d, 8, 512, gmem_d, "wkv", chunk=1024, stage=stage4)
        P4STOP = int(os.environ.get('P4STOP', '9'))
        xt4 = [A.get([D], F32) for _ in range(2)]
        bxt4 = [Buf("xt4%d" % i) for i in range(2)]
        xb4 = A.get([D], BF16)
        bxb4 = Buf("xb4")
        sq4 = A.get([D], F32)
        st4 = A.get([8], F32)
        bxs4 = dict(sq=sq4, st=st4, b=Buf("xstat4"))
        memT = A.get([8, 128], BF16)
        bmemT = Buf("memT")
        kmT = A.get([2, 256], BF16)
        vmb = A.get([2, 256], BF16)
        bkm = Buf("kmT")
        bvm = Buf("vmb")
        ksb = A.get([256], F32)
        ksq = A.get([256], F32)
        kss = A.get([8], F32)
        kgr = A.get([256], F32)
        kmb = A.get([256], BF16)
        bks = Buf("ksb")
        ones_bf = A.get([128], BF16)
        I_("pool", lambda e: e.memset(ones_bf, 1.0), W=[bks])
        for hh in range(4):
            I_("pool", lambda e, hh=hh: e.tensor_copy(out=kgr[:, hh * 64:(hh + 1) * 64], in_=gains[:, 5 * 64:6 * 64]),
               R=[b_const], W=[bks])
        for mt in range(2 if P4STOP >= 1 else 0):
            norm_rows_to_hT(mem_d[mt * 128:(mt + 1) * 128, :], xt4[0], xb4, None, bxt4[0], bxb4, bxs4, "xt40", 5, pb[5])
            I_("act", lambda e: e.copy(out=memT, in_=bank_bf(5).rearrange("p (k c) -> p k c", k=8)), R=[pb[5]], W=[bmemT])
            P4SUB = int(os.environ.get('P4SUB', '9'))
            if P4SUB < 1:
                continue
            for k in range(8):
                I_("pe", lambda e, k=k: e.matmul(bank(0), lhsT=memT[:, k, :], rhs=wkv[:, k, :], start=(k == 0), stop=(k == 7)),
                   R=[bmemT, bwkv], W=[pb[0]])
            P4V = int(os.environ.get('P4V', '3'))
            if P4V & 1:
                I_("act", lambda e, mt=mt: e.copy(out=vmb[:, mt, :], in_=bank(0)[:, 256:512]), R=[pb[0]], W=[bvm])
            if P4V & 2:
                I_("dve", lambda e: e.tensor_copy(out=ksb, in_=bank(0)[:, 0:256]), R=[pb[0]], W=[bks])
            if P4SUB < 2:
                continue
            I_("pool", lambda e: e.tensor_tensor(out=ksq, in0=ksb, in1=ksb, op=ALU.mult), R=[bks], W=[bks])
            I_("dve", lambda e: e.tensor_reduce(out=kss[:, 0:4], in_=ksq.rearrange("p (h d) -> p h d", d=64), axis=AX.X,
                                                op=ALU.add), R=[bks], W=[bks])
            rstd_from_ss(kss[:, 0:4], kss[:, 4:8], 64, [bks], 4)
            I_("dve", lambda e: e.tensor_tensor(out=ksq.rearrange("p (h d) -> p h d", d=64),
                                                in0=kgr.rearrange("p (h d) -> p h d", d=64),
                                                in1=kss[:, 4:8].unsqueeze(2).to_broadcast([128, 4, 64]), op=ALU.mult),
               R=[bks], W=[bks])
            I_("dve", lambda e: e.tensor_tensor(out=kmb, in0=ksb, in1=ksq, op=ALU.mult), R=[bks], W=[bks])
            if P4SUB < 3:
                continue
            for j in range(2):
                I_("pe", lambda e, j=j: e.transpose(out=bank_bf(1)[:, j * 128:(j + 1) * 128], in_=kmb[:, j * 128:(j + 1) * 128],
                                                    identity=ident), R=[bks, b_const], W=[pb[1]])
            I_("dve", lambda e, mt=mt: e.tensor_copy(out=kmT[:, :, mt * 128:(mt + 1) * 128],
                                                     in_=bank_bf(1)[:, 0:256].rearrange("p (j m) -> p j m", j=2)),
               R=[pb[1]], W=[bkm])
        gtg = [A.get([24, 512], BF16) for _ in range(2)]
        bgtg = [Buf("gtg%d" % i) for i in range(2)]
        obg = [A.get([3, 512], BF16) for _ in range(2)]
        bobg = [Buf("obg%d" % i) for i in range(2)]
        qcg = [A.get([2, 512], BF16) for _ in range(2)]
        bqcg = [Buf("qcg%d" % i) for i in range(2)]
        oaT = [A.get([512], BF16) for _ in range(2)]
        boaT = [Buf("oaT%d" % i) for i in range(2)]
        ocT = [A.get([2, 512], BF16) for _ in range(2)]
        bocT = [Buf("ocT%d" % i) for i in range(2)]
        mT = [A.get([8, 512], BF16) for _ in range(2)]
        bmT = [Buf("mT%d" % i) for i in range(2)]
        oa3 = [A.get([3, 130], F32) for _ in range(2)]
        boa3 = [Buf("oa3%d" % i) for i in range(2)]
        oasum = A.get([130], F32)
        oard = A.get([2], F32)
        oab = A.get([128], BF16)
        boas4 = Buf("oasum")
        PTc = [A.get([512], BF16) for _ in range(4)]
        bPTc = [Buf("PTc%d" % i) for i in range(4)]
        rdc = [A.get([512], F32) for _ in range(2)]
        brdc = [Buf("rdc%d" % i) for i in range(2)]
        tm = [A.get([512], F32) for _ in range(3)]
        btm = [Buf("tm%d" % i) for i in range(3)]
        x1t = [A.get([D], F32) for _ in range(2)]
        bx1t = [Buf("x1t%d" % i) for i in range(2)]
        ngrp = ntiles // 4
        cpt = [0]
        chd = [0]
        for G in range(ngrp if P4STOP >= 2 else 0):
            g2 = G % 2
            t0 = G * 512
            for c4 in range(0, 24, 4):
                P.dma("sp", gtg[g2][:, c4:c4 + 4, :], GT[c4:c4 + 4, :, t0:t0 + 512].rearrange("c p t -> p c t"),
                      W=[bgtg[g2]], grp="gtg%d" % g2)
            P.dma("sp", obg[g2], OBT[:, :, t0:t0 + 512].rearrange("c p t -> p c t"), W=[bobg[g2]], grp="obg%d" % g2)
            for j in range(2):
                P.dma("sp", qcg[g2][:, j, :].rearrange("p (n t) -> p n t", n=4), QCT[:, 4 * G:4 * G + 4, j * 128:(j + 1) * 128],
                      W=[bqcg[g2]], grp="qcg%d" % g2)
            for tt in range(4):
                n = 4 * G + tt
                o2 = n % 2
                P.dma("sp", oa3[o2], OA[:, n * 128:(n + 1) * 128, :].rearrange("g t e -> t g e"), W=[boa3[o2]],
                      grp="oa3%d" % o2)
                I_("pool", lambda e, o2=o2: e.tensor_tensor(out=oasum, in0=oa3[o2][:, 0, :], in1=oa3[o2][:, 1, :], op=ALU.add),
                   R=[boa3[o2]], W=[boas4])
                I_("pool", lambda e, o2=o2: e.tensor_tensor(out=oasum, in0=oasum, in1=oa3[o2][:, 2, :], op=ALU.add),
                   R=[boa3[o2], boas4], W=[boas4])
                osv = oasum.rearrange("p (j e) -> p j e", e=65)
                I_("dve", lambda e, osv=osv: e.reciprocal(out=oard, in_=osv[:, :, 64]), R=[boas4], W=[boas4])
                I_("dve", lambda e, osv=osv: e.tensor_tensor(out=oab.rearrange("p (j d) -> p j d", d=64), in0=osv[:, :, 0:64],
                                                             in1=oard.unsqueeze(2).to_broadcast([128, 2, 64]), op=ALU.mult),
                   R=[boas4], W=[boas4])
                I_("pe", lambda e, tt=tt: e.transpose(out=bank_bf(0)[:, tt * 128:(tt + 1) * 128], in_=oab, identity=ident),
                   R=[boas4, b_const], W=[pb[0]])
            I_("act", lambda e, g2=g2: e.copy(out=oaT[g2], in_=bank_bf(0)[:, 0:512]), R=[pb[0]], W=[boaT[g2]])
            for h in range(4 if P4STOP >= 3 else 0):
                j, hh = h // 2, h % 2
                hs2 = chd[0] % 2
                chd[0] += 1
                numb, denb = 4 + 2 * hs2, 5 + 2 * hs2
                pts = []
                for mt in range(2):
                    sbk = 2 * hs2 + mt
                    pt = cpt[0] % 4
                    cpt[0] += 1
                    pts.append(pt)
                    I_("pe", lambda e, sbk=sbk, hh=hh, j=j, mt=mt, g2=g2: e.matmul(
                        bank(sbk), lhsT=kmT[64 * hh:64 * hh + 64, j, mt * 128:(mt + 1) * 128],
                        rhs=qcg[g2][64 * hh:64 * hh + 64, j, :], start=True, stop=True),
                       R=[bkm, bqcg[g2]], W=[pb[sbk]])
                    I_("act", lambda e, sbk=sbk, pt=pt: e.activation(out=PTc[pt], in_=bank(sbk), func=AF.Exp, scale=0.125),
                       R=[pb[sbk]], W=[bPTc[pt]])
                for mt in range(2):
                    I_("pe", lambda e, mt=mt, j=j, numb=numb, pt=pts[mt]: e.matmul(
                        bank(numb), lhsT=vmb[:, mt, j * 128:(j + 1) * 128], rhs=PTc[pt], start=(mt == 0), stop=(mt == 1)),
                       R=[bvm, bPTc[pts[mt]]], W=[pb[numb]])
                for mt in range(2):
                    I_("pe", lambda e, mt=mt, denb=denb, pt=pts[mt]: e.matmul(
                        bank(denb), lhsT=ones_bf, rhs=PTc[pt], start=(mt == 0), stop=(mt == 1)),
                       R=[bks, bPTc[pts[mt]]], W=[pb[denb]])
                lo_, hi_ = 64 * hh, 64 * hh + 64
                I_("dve", lambda e, denb=denb, hs2=hs2, lo_=lo_, hi_=hi_: e.reciprocal(out=rdc[hs2][lo_:hi_, :],
                                                                                     in_=bank(denb)[lo_:hi_, :]),
                   R=[pb[denb]], W=[brdc[hs2]])
                I_("dve", lambda e, numb=numb, hs2=hs2, lo_=lo_, hi_=hi_, j=j, g2=g2: e.tensor_tensor(
                    out=ocT[g2][lo_:hi_, j, :], in0=bank(numb)[lo_:hi_, :], in1=rdc[hs2][lo_:hi_, :], op=ALU.mult),
                   R=[pb[numb], brdc[hs2]], W=[bocT[g2]])
            for oc in range(8 if P4STOP >= 4 else 0):
                bs = 3 * (oc % 2)
                cs_ = slice(oc * 128, (oc + 1) * 128)
                I_("pe", lambda e, bs=bs, cs_=cs_, g2=g2: e.matmul(bank(bs), lhsT=wa[:, 0, cs_], rhs=oaT[g2], start=True, stop=True),
                   R=[bwa, boaT[g2]], W=[pb[bs]])
                for k in range(3):
                    I_("pe", lambda e, bs=bs, cs_=cs_, k=k, g2=g2: e.matmul(bank(bs + 1), lhsT=wb_[:, k, cs_], rhs=obg[g2][:, k, :],
                                                                          start=(k == 0), stop=(k == 2)),
                       R=[bwb, bobg[g2]], W=[pb[bs + 1]])
                for k in range(2):
                    I_("pe", lambda e, bs=bs, cs_=cs_, k=k, g2=g2: e.matmul(bank(bs + 2), lhsT=wc[:, k, cs_], rhs=ocT[g2][:, k, :],
                                                                          start=(k == 0), stop=(k == 1)),
                       R=[bwc, bocT[g2]], W=[pb[bs + 2]])
                I_("dve", lambda e, bs=bs, oc=oc, g2=g2: e.tensor_tensor(out=tm[0], in0=bank(bs), in1=gtg[g2][:, oc, :], op=ALU.mult),
                   R=[pb[bs], bgtg[g2]], W=[btm[0]])
                I_("dve", lambda e, bs=bs, oc=oc, g2=g2: e.tensor_tensor(out=tm[1], in0=bank(bs + 1), in1=gtg[g2][:, 8 + oc, :],
                                                                       op=ALU.mult), R=[pb[bs + 1], bgtg[g2]], W=[btm[1]])
                I_("dve", lambda e, bs=bs, oc=oc, g2=g2: e.tensor_tensor(out=tm[2], in0=bank(bs + 2), in1=gtg[g2][:, 16 + oc, :],
                                                                       op=ALU.mult), R=[pb[bs + 2], bgtg[g2]], W=[btm[2]])
                I_("pool", lambda e: e.tensor_tensor(out=tm[0], in0=tm[0], in1=tm[1], op=ALU.add), R=[btm[0], btm[1]], W=[btm[0]])
                I_("pool", lambda e, oc=oc, g2=g2: e.tensor_tensor(out=mT[g2][:, oc, :], in0=tm[0], in1=tm[2], op=ALU.add),
                   R=[btm[0], btm[2]], W=[bmT[g2]])
            for tt in range(4 if P4STOP >= 5 else 0):
                n = 4 * G + tt
                o2 = n % 2
                P.dma("sp", xt4[o2], x_d[n * 128:(n + 1) * 128, :], W=[bxt4[o2]], grp="xt4%d" % o2)
                for hf in range(2):
                    for k in range(8):
                        I_("pe", lambda e, hf=hf, k=k, tt=tt, g2=g2: e.matmul(
                            bank(6 + hf), lhsT=mT[g2][:, k, tt * 128:(tt + 1) * 128], rhs=wo[:, k, hf * 512:(hf + 1) * 512],
                            start=(k == 0), stop=(k == 7)), R=[bmT[g2], bwo], W=[pb[6 + hf]])
                I_("dve", lambda e, o2=o2: e.tensor_tensor(out=x1t[o2], in0=psum[:, 6:8, :].rearrange("p a b -> p (a b)"),
                                                           in1=xt4[o2], op=ALU.add), R=[pb[6], pb[7], bxt4[o2]], W=[bx1t[o2]])
                P.dma("pool", X1[n * 128:(n + 1) * 128, :], x1t[o2], R=[bx1t[o2]], grp="x1t%d" % o2)
        P.barrier()
        A.reset()

    if 5 in phases:
        W1 = A.get([8, 4096], BF16)
        W2 = A.get([32, D], BF16)
        stage5 = make_stage(2048, "p5")
        bW1 = load_weight_bf16(W1, w1_d, 8, 4096, gmlp_d, "w1", chunk=2048, stage=stage5)
        bW2 = load_weight_bf16(W2, w2_d, 32, D, None, "w2", chunk=1024, stage=stage5)
        xt5 = [A.get([D], F32) for _ in range(4)]
        bxt5 = [Buf("xt5%d" % i) for i in range(4)]
        xb5 = [A.get([D], BF16) for _ in range(2)]
        bxb5 = [Buf("xb5%d" % i) for i in range(2)]
        sq5 = A.get([D], F32)
        st5 = A.get([8], F32)
        bxs5 = dict(sq=sq5, st=st5, b=Buf("xstat5"))
        h2T = [A.get([8, 256], BF16) for _ in range(2)]
        bh2T = [Buf("h2T%d" % i) for i in range(2)]
        rr = [A.get([256], BF16) for _ in range(3)]
        brr = [Buf("rr%d" % i) for i in range(3)]
        aa = [A.get([256], BF16) for _ in range(3)]
        baa = [Buf("aa%d" % i) for i in range(3)]
        ot = [A.get([D], F32) for _ in range(2)]
        bot = [Buf("ot%d" % i) for i in range(2)]
        ng5 = ntiles // 2
        cc5 = [0]
        for G in range(ng5):
            g2 = G % 2
            for tt in range(2):
                n = 2 * G + tt
                s4 = n % 4
                s2 = n % 2
                norm_rows_to_hT(X1[n * 128:(n + 1) * 128, :], xt5[s4], xb5[s2], None, bxt5[s4], bxb5[s2], bxs5,
                                "xt5%d" % s4, 7, pb[7])
                I_("act", lambda e, g2=g2, tt=tt: e.copy(out=h2T[g2][:, :, tt * 128:(tt + 1) * 128],
                                                         in_=bank_bf(7).rearrange("p (k c) -> p k c", k=8)),
                   R=[pb[7]], W=[bh2T[g2]])
            slots5 = {}

            def U5(c):
                ub = 4 + cc5[0] % 3
                r3 = cc5[0] % 3
                cc5[0] += 1
                slots5[c] = r3
                for k in range(8):
                    I_("pe", lambda e, ub=ub, k=k, c=c, g2=g2: e.matmul(
                        bank(ub)[:, 0:256], lhsT=W1[:, k, c * 128:(c + 1) * 128], rhs=h2T[g2][:, k, :],
                        start=(k == 0), stop=(k == 7)), R=[bW1, bh2T[g2]], W=[pb[ub]])
                I_("act", lambda e, ub=ub, r3=r3: e.activation(out=rr[r3], in_=bank(ub)[:, 0:256], func=AF.Relu),
                   R=[pb[ub]], W=[brr[r3]])
                I_("dve", lambda e, r3=r3: e.tensor_tensor(out=aa[r3], in0=rr[r3], in1=rr[r3], op=ALU.mult),
                   R=[brr[r3]], W=[baa[r3]])

            def O5(c):
                r3 = slots5.pop(c)
                for tt in range(2):
                    for hf in range(2):
                        I_("pe", lambda e, tt=tt, hf=hf, r3=r3, c=c: e.matmul(
                            bank(tt * 2 + hf), lhsT=aa[r3][:, tt * 128:(tt + 1) * 128], rhs=W2[:, c, hf * 512:(hf + 1) * 512],
                            start=(c == 0), stop=(c == 31)), R=[baa[r3], bW2], W=[pb[tt * 2 + hf]])

            U5(0)
            for c in range(32):
                if c + 1 < 32:
                    U5(c + 1)
                O5(c)
            for tt in range(2):
                n = 2 * G + tt
                s4 = n % 4
                s2 = n % 2
                I_("dve", lambda e, tt=tt, s2=s2, s4=s4: e.tensor_tensor(
                    out=ot[s2], in0=psum[:, 2 * tt:2 * tt + 2, :].rearrange("p a b -> p (a b)"), in1=xt5[s4], op=ALU.add),
                   R=[pb[2 * tt], pb[2 * tt + 1], bxt5[s4]], W=[bot[s2]])
                P.dma("pool", out_d[n * 128:(n + 1) * 128, :], ot[s2], R=[bot[s2]], grp="ot%d" % s2)
        P.barrier()
        A.reset()

    P.barrier()
    P.emit()
    return nc, P


def _host_layout(inputs, b):
    def kmaj(w):
        k = w.shape[0] // 128
        return np.ascontiguousarray(w.reshape(k, 128, w.shape[1]).transpose(1, 0, 2))
    perm = _win_perm()
    m = {
        "x": np.ascontiguousarray(inputs["x"][b]),
        "mem": np.ascontiguousarray(inputs["mem"][b]),
        "pos": np.ascontiguousarray(inputs["positions"][b].reshape(NT, 128).T),
        "w_in": kmaj(inputs["w_in"][0][:, perm]),
        "g_mix": np.ascontiguousarray(inputs["g_mix"][0].reshape(8, 128).T),
        "g_mem": np.ascontiguousarray(inputs["g_mem"][0].reshape(8, 128).T),
        "g_mlp": np.ascontiguousarray(inputs["g_mlp"][0].reshape(8, 128).T),
        "gains": np.concatenate([inputs[k][0] for k in ("g_qa", "g_ka", "g_qb", "g_kb", "g_qc", "g_kc")])[None, :],
        "w_mem_kv": kmaj(inputs["w_mem_kv"][0]),
        "w_a": kmaj(inputs["w_a"][0]),
        "w_b": kmaj(inputs["w_b"][0]),
        "w_c": kmaj(inputs["w_c"][0]),
        "w_o": kmaj(inputs["w_o"][0]),
        "w_1": kmaj(inputs["w_1"][0]),
        "w_2": kmaj(inputs["w_2"][0]),
    }
    return {k: np.ascontiguousarray(v) for k, v in m.items()}


def kernel(**inputs):
    inputs = {k: np.asarray(v) for k, v in inputs.items()}
    nc, _ = build_program()
    in_maps = [_host_layout(inputs, b) for b in range(8)]
    res = run_bass_kernel_spmd(nc, in_maps, core_ids=list(range(8)))
    return np.stack([np.asarray(r["out"]) for r in res.results], axis=0).astype(np.float32)
```

```python
import bisect
import os
import numpy as np
import concourse.bass as bass
import concourse.mybir as mybir
from concourse.bass_utils import run_bass_kernel_spmd

F32 = mybir.dt.float32
BF16 = mybir.dt.bfloat16
I32 = mybir.dt.int32
ALU = mybir.AluOpType
AF = mybir.ActivationFunctionType
AX = mybir.AxisListType

ENGS = ("pe", "act", "dve", "pool", "sp")

S = 8192
D = 1024
NT = S // 128
EPS = 1e-6
NEG = -30000.0
TOPK = 256
NBIS = 16


class Buf:
    __slots__ = ("name", "w", "r", "excl")

    def __init__(self, name="", excl=False):
        self.name = name
        self.w = None
        self.r = []
        self.excl = excl


class Prog:
    def __init__(self, nc):
        self.nc = nc
        self.ins = []
        self.by_eng = {e: [] for e in ENGS}
        self.groups = {}

    def _deps(self, R, W):
        deps = set()
        for b in R:
            if b.w is not None:
                deps.add(b.w)
        for b in W:
            if b.w is not None:
                deps.add(b.w)
            deps.update(b.r)
        return deps

    def _commit(self, iid, R, W):
        for b in R:
            b.r.append(iid)
        for b in W:
            b.w = iid
            b.r = []

    def I(self, eng, fn, R=(), W=()):
        iid = len(self.ins)
        W = list(W) + [b for b in R if b.excl and b not in W]
        deps = self._deps(R, W)
        raw = set(b.w for b in R if b.w is not None)
        self.ins.append(dict(eng=eng, fn=fn, deps=deps, raw=raw, grp=None))
        self.by_eng[eng].append(iid)
        self._commit(iid, R, W)
        return iid

    def dma(self, eng, out, in_, R=(), W=(), grp="g", **kw):
        iid = len(self.ins)
        deps = self._deps(R, W)
        raw = set(b.w for b in R if b.w is not None)
        self.groups.setdefault(grp, []).append(iid)
        self.ins.append(dict(eng=eng, fn=(lambda e, o=out, i=in_, k=kw: e.dma_start(out=o, in_=i, **k)),
                             deps=deps, raw=raw, grp=grp))
        self.by_eng[eng].append(iid)
        self._commit(iid, R, W)
        return iid

    def barrier(self):
        alld = set()
        for e in ENGS:
            for k in reversed(self.by_eng[e]):
                if self.ins[k]["fn"] is not None:
                    alld.add(k)
                    break
        for g, l in self.groups.items():
            if l:
                alld.add(l[-1])
        for e in ENGS:
            iid = len(self.ins)
            self.ins.append(dict(eng=e, fn=None, deps=set(alld), raw=set(alld), grp=None))
            self.by_eng[e].append(iid)

    def emit(self):
        nc = self.nc
        ins = self.ins
        n = len(ins)
        is_target = [False] * n
        for k in range(n):
            for d in ins[k]["deps"]:
                is_target[d] = True
        ms_val = [0] * n
        cnt = {e: 0 for e in ENGS}
        for k in range(n):
            it = ins[k]
            if it["grp"] is None and it["fn"] is not None and is_target[k]:
                cnt[it["eng"]] += 1
                ms_val[k] = cnt[it["eng"]]
        self.esem = {e: nc.alloc_semaphore("sem_" + e) for e in ENGS}
        self.gsem = {g: nc.alloc_semaphore("dsem_" + g) for g in self.groups}
        self.nwaits = 0
        prog = self

        def run_engine(ename, eobj):
            seen = {}
            for k in prog.by_eng[ename]:
                it = ins[k]
                need = {}
                for d in it["deps"]:
                    dd = ins[d]
                    if dd["grp"] is not None:
                        g = dd["grp"]
                        c = bisect.bisect_left(prog.groups[g], k)
                        key = ("g", g)
                        val = 16 * c
                    else:
                        if dd["fn"] is None:
                            continue
                        if dd["eng"] == ename and it["grp"] is None and it["fn"] is not None:
                            if ename == "pe":
                                continue
                            if d not in it["raw"]:
                                continue
                        key = ("e", dd["eng"])
                        val = ms_val[d]
                    if val > need.get(key, 0):
                        need[key] = val
                for key, val in need.items():
                    if seen.get(key, 0) >= val:
                        continue
                    seen[key] = val
                    sem = prog.esem[key[1]] if key[0] == "e" else prog.gsem[key[1]]
                    eobj.wait_ge(sem, val)
                    prog.nwaits += 1
                if it["fn"] is None:
                    continue
                bi = it["fn"](eobj)
                if it["grp"] is not None:
                    bi.then_inc(prog.gsem[it["grp"]], 16)
                elif is_target[k]:
                    bi.then_inc(prog.esem[ename], 1)

        with nc.Block() as block:
            @block.tensor
            def _(e):
                run_engine("pe", e)

            @block.scalar
            def _(e):
                run_engine("act", e)

            @block.vector
            def _(e):
                run_engine("dve", e)

            @block.gpsimd
            def _(e):
                run_engine("pool", e)

            @block.sync
            def _(e):
                run_engine("sp", e)


def interleave_streams(streams):
    st = [[g, max(n, 1), 0, False] for g, n in streams]
    while True:
        live = [x for x in st if not x[3]]
        if not live:
            return
        x = min(live, key=lambda x: x[2] / x[1])
        try:
            next(x[0])
            x[2] += 1
        except StopIteration:
            x[3] = True


class Arena:
    def __init__(self, nc, nbytes):
        self.t = nc.alloc_sbuf_tensor("arena", [128, nbytes // 2], BF16)
        self.n = nbytes // 2
        self.base = 0
        self.p = 0

    def mark(self):
        self.base = self.p

    def reset(self):
        self.p = self.base

    def get(self, shape, dtype):
        ne = int(np.prod(shape))
        w = ne * (2 if dtype in (F32, I32) else 1)
        w = (w + 31) // 32 * 32
        assert self.p + w <= self.n, ("SBUF arena overflow", self.p, w, self.n)
        ap = self.t[:, self.p:self.p + w]
        self.p += w
        if dtype != BF16:
            ap = ap.bitcast(dtype)
        ap = ap[:, 0:ne]
        if len(shape) == 2:
            ap = ap.rearrange("p (a b) -> p a b", a=shape[0])
        elif len(shape) == 3:
            ap = ap.rearrange("p (a b c) -> p a b c", a=shape[0], b=shape[1])
        return ap


IN_COLS = 5704
N_TM = 2560
N_SM = 72
N_G = 3072


def _win_perm():
    o = {}
    acc = 0
    for nme, w in (("qa", 384), ("ka", 384), ("va", 384), ("qb", 384), ("kb", 128), ("vb", 128),
                   ("qi", 512), ("ki", 64), ("wi", 8), ("qc", 256), ("gl", 3072)):
        o[nme] = np.arange(acc, acc + w)
        acc += w
    qb = o["qb"].reshape(2, 3, 64).transpose(1, 0, 2).reshape(-1)
    return np.concatenate([o["qa"], o["ka"], qb, o["kb"], o["qi"], o["qc"], o["va"], o["vb"],
                           o["ki"], o["wi"], o["gl"]])


def build_program(debug=(), phases=(1, 2, 3, 4, 5), ntiles=NT):
    nc = bass.Bass("TRN2", target_bir_lowering=False)
    P = Prog(nc)
    I_ = P.I

    def din(name, shape, dt=F32):
        return nc.dram_tensor(name, shape, dt, kind="ExternalInput").ap()

    def dscr(name, shape, dt):
        return nc.dram_tensor(name, shape, dt, kind=("ExternalOutput" if name in debug else "Internal")).ap()

    x_d = din("x", [S, D])
    mem_d = din("mem", [256, D])
    pos_d = din("pos", [128, NT], I32)
    win_d = din("w_in", [128, 8, IN_COLS])
    gmix_d = din("g_mix", [128, 8])
    gmem_d = din("g_mem", [128, 8])
    gmlp_d = din("g_mlp", [128, 8])
    gains_d = din("gains", [1, 384])
    wkv_d = din("w_mem_kv", [128, 8, 512])
    wa_d = din("w_a", [128, 1, D])
    wb_d = din("w_b", [128, 3, D])
    wc_d = din("w_c", [128, 2, D])
    wo_d = din("w_o", [128, 8, D])
    w1_d = din("w_1", [128, 8, 4096])
    w2_d = din("w_2", [128, 32, D])
    out_d = nc.dram_tensor("out", [S, D], F32, kind="ExternalOutput").ap()

    QKVA = dscr("s_qkva", [S, 1152], BF16)
    QBT = dscr("s_qbt", [128, NT, 384], BF16)
    KBT = dscr("s_kbt", [128, S], BF16)
    VB = dscr("s_vb", [S, 136], BF16)
    QIT = dscr("s_qit", [128, NT, 512], BF16)
    KIT = dscr("s_kit", [64, S], BF16)
    QCT = dscr("s_qct", [128, NT, 256], BF16)
    GT = dscr("s_gt", [24, 128, S], BF16)
    OA = dscr("s_oa", [3, S, 130], F32)
    OBT = dscr("s_obt", [3, 128, S], BF16)
    X1 = dscr("s_x1", [S, D], F32)

    A = Arena(nc, 206 * 1024)
    psum = nc.alloc_psum_tensor("psum", [128, 8, 512], F32)

    def bank(i):
        return psum[:, i, :]

    def bank_bf(i):
        return psum[:, i, :].bitcast(BF16)

    pb = [Buf("bank%d" % i, excl=True) for i in range(8)]

    ident = A.get([128], BF16)
    identf = A.get([128], F32)
    mcur = A.get([128], BF16)
    mprev = A.get([128], BF16)
    i3 = A.get([384], BF16)
    gains = A.get([384], F32)
    cs = A.get([NT, 16], F32)
    b_const = Buf("const")
    I_("pool", lambda e: e.memset(identf, 0.0), W=[b_const])
    I_("pool", lambda e: e.affine_select(out=identf, in_=identf, pattern=[[-1, 128]], compare_op=ALU.not_equal,
                                         fill=1.0, base=0, channel_multiplier=1), R=[b_const], W=[b_const])
    I_("pool", lambda e: e.tensor_copy(out=ident, in_=identf), R=[b_const], W=[b_const])
    for g in range(3):
        I_("pool", lambda e, g=g: e.tensor_copy(out=i3[:, g * 128:(g + 1) * 128], in_=identf), R=[b_const], W=[b_const])
    I_("pool", lambda e: e.memset(mcur, 0.0), W=[b_const], R=[b_const])
    I_("pool", lambda e: e.affine_select(out=mcur, in_=mcur, pattern=[[1, 128]], compare_op=ALU.is_ge,
                                         fill=NEG, base=0, channel_multiplier=-1), R=[b_const], W=[b_const])
    I_("pool", lambda e: e.memset(mprev, 0.0), W=[b_const], R=[b_const])
    I_("pool", lambda e: e.affine_select(out=mprev, in_=mprev, pattern=[[-1, 128]], compare_op=ALU.is_ge,
                                         fill=NEG, base=0, channel_multiplier=1), R=[b_const], W=[b_const])
    P.dma("sp", gains, gains_d.partition_broadcast(128), W=[b_const], grp="c0")
    A.mark()
    posi = A.get([NT], I32)
    posf = A.get([NT], F32)
    inv = A.get([8], F32)
    ang = A.get([NT, 8], F32)
    ang2 = A.get([NT, 16], F32)
    P.dma("sp", posi, pos_d, W=[b_const], grp="c0")
    I_("dve", lambda e: e.tensor_copy(out=posf, in_=posi), R=[b_const], W=[b_const])
    for i in range(8):
        I_("pool", lambda e, i=i: e.memset(inv[:, i:i + 1], float(np.float32(500000.0) ** np.float32(-i / 8.0))),
           R=[b_const], W=[b_const])
    I_("dve", lambda e: e.tensor_tensor(out=ang, in0=posf.unsqueeze(2).to_broadcast([128, NT, 8]),
                                        in1=inv.unsqueeze(1).to_broadcast([128, NT, 8]), op=ALU.mult),
       R=[b_const], W=[b_const])
    TWO_PI = float(2 * np.pi)
    I_("dve", lambda e: e.tensor_scalar(out=ang2[:, :, 0:8], in0=ang, scalar1=float(0.5 * np.pi), scalar2=None,
                                        op0=ALU.add), R=[b_const], W=[b_const])
    I_("dve", lambda e: e.tensor_copy(out=ang2[:, :, 8:16], in_=ang), R=[b_const], W=[b_const])
    angk = A.get([NT, 16], F32)
    angi = A.get([NT, 16], I32)
    I_("dve", lambda e: e.tensor_scalar(out=angk, in0=ang2, scalar1=float(1.0 / (2 * np.pi)), scalar2=None,
                                        op0=ALU.mult), R=[b_const], W=[b_const])
    I_("dve", lambda e: e.tensor_copy(out=angi, in_=angk), R=[b_const], W=[b_const])
    I_("dve", lambda e: e.tensor_copy(out=angk, in_=angi), R=[b_const], W=[b_const])
    I_("dve", lambda e: e.scalar_tensor_tensor(out=ang2, in0=angk, scalar=-TWO_PI, in1=ang2, op0=ALU.mult,
                                               op1=ALU.add), R=[b_const], W=[b_const])
    I_("dve", lambda e: e.tensor_scalar(out=angk, in0=ang2, scalar1=float(np.pi), scalar2=TWO_PI, op0=ALU.is_gt,
                                        op1=ALU.mult), R=[b_const], W=[b_const])
    I_("dve", lambda e: e.tensor_tensor(out=ang2, in0=ang2, in1=angk, op=ALU.subtract), R=[b_const], W=[b_const])
    I_("dve", lambda e: e.tensor_scalar(out=angk, in0=ang2, scalar1=float(-np.pi), scalar2=TWO_PI, op0=ALU.is_lt,
                                        op1=ALU.mult), R=[b_const], W=[b_const])
    I_("dve", lambda e: e.tensor_tensor(out=ang2, in0=ang2, in1=angk, op=ALU.add), R=[b_const], W=[b_const])
    I_("dve", lambda e: e.tensor_scalar(out=ang2, in0=ang2, scalar1=3.141592, scalar2=-3.141592, op0=ALU.min,
                                        op1=ALU.max), R=[b_const], W=[b_const])
    I_("act", lambda e: e.activation(out=cs, in_=ang2, func=AF.Sin), R=[b_const], W=[b_const])
    P.barrier()
    A.reset()

    def rstd_from_ss(ss, rs, n, bufs, width):
        I_("dve", lambda e: e.tensor_scalar(out=rs, in0=ss, scalar1=1.0 / n, scalar2=EPS, op0=ALU.mult, op1=ALU.add),
           R=bufs, W=bufs)
        I_("act", lambda e: e.activation(out=rs, in_=rs, func=AF.Sqrt), R=bufs, W=bufs)
        I_("dve", lambda e: e.reciprocal(out=rs, in_=rs), R=bufs, W=bufs)

    def make_stage(chunk, tag):
        return ([A.get([chunk], F32) for _ in range(2)], [Buf(tag + "stg%d" % i) for i in range(2)], tag)

    def load_weight_bf16(dst, src_d, nk, ncols, gain_d, tag, chunk=2048, stage=None):
        if stage is None:
            stage = make_stage(chunk, tag)
        stg, sb, stag = stage
        bw = Buf(tag)
        gt = None
        if gain_d is not None:
            gt = A.get([8], F32)
            P.dma("sp", gt, gain_d, W=[bw], grp=tag + "g")
        it = 0
        for k in range(nk):
            for c0 in range(0, ncols, chunk):
                cw = min(chunk, ncols - c0)
                s = it % 2
                P.dma("sp", stg[s][:, 0:cw], src_d[:, k, c0:c0 + cw], W=[sb[s]], grp=stag + "s%d" % s)
                eng = ("dve", "pool")[it % 2]
                if gt is not None:
                    I_(eng, lambda e, s=s, k=k, c0=c0, cw=cw: e.tensor_scalar(
                        out=dst[:, k, c0:c0 + cw], in0=stg[s][:, 0:cw], scalar1=gt[:, k:k + 1], scalar2=None,
                        op0=ALU.mult), R=[sb[s], bw], W=[bw])
                else:
                    I_(eng, lambda e, s=s, k=k, c0=c0, cw=cw: e.tensor_copy(out=dst[:, k, c0:c0 + cw],
                                                                          in_=stg[s][:, 0:cw]), R=[sb[s]], W=[bw])
                it += 1
        return bw

    def norm_rows_to_hT(src_tile_d, xt, xb, hT_dst, bx, bh, bxs, grp, pbank, bpb):
        P.dma("sp", xt, src_tile_d, W=[bx], grp=grp)
        st = bxs["st"]
        I_("pool", lambda e: e.memset(st[:, 0:1], 0.0), W=[bxs["b"]])
        I_("act", lambda e: e.activation(out=bxs["sq"], in_=xt, func=AF.Square, accum_out=st[:, 0:1]),
           R=[bx, bxs["b"]], W=[bxs["b"]])
        rstd_from_ss(st[:, 0:1], st[:, 1:2], D, [bxs["b"]], 1)
        I_("dve", lambda e: e.tensor_scalar(out=xb, in0=xt, scalar1=st[:, 1:2], scalar2=None, op0=ALU.mult),
           R=[bx, bxs["b"]], W=[bh])
        pT = bank_bf(pbank)
        for k in range(8):
            I_("pe", lambda e, k=k: e.transpose(out=pT[:, k * 128:(k + 1) * 128], in_=xb[:, k * 128:(k + 1) * 128],
                                                identity=ident), R=[bh, b_const], W=[bpb])

    if 1 in phases:
        W = A.get([8, IN_COLS], BF16)
        p1_pos0 = A.p
        bW = load_weight_bf16(W, win_d, 8, IN_COLS, gmix_d, "win", chunk=1426)
        P.barrier()
        A.p = p1_pos0
        NB = 2
        xt = [A.get([D], F32) for _ in range(NB)]
        bx = [Buf("xt%d" % i) for i in range(NB)]
        xb = [A.get([D], BF16) for _ in range(NB)]
        bh = [Buf("xb%d" % i) for i in range(NB)]
        sq = A.get([D], F32)
        st = A.get([8], F32)
        bxs = dict(sq=sq, st=st, b=Buf("xstat"))
        hTg = [A.get([8, 512], BF16) for _ in range(2)]
        bhT = [Buf("hTg%d" % i) for i in range(2)]
        stg2 = [A.get([2632], F32) for _ in range(2)]
        bstg2 = [Buf("stg%d" % i) for i in range(2)]
        tmp2 = [A.get([1536], F32) for _ in range(2)]
        btmp2 = [Buf("tmp%d" % i) for i in range(2)]
        ss2 = [A.get([24], F32) for _ in range(2)]
        rs2 = [A.get([24], F32) for _ in range(2)]
        bss2 = [Buf("ss%d" % i) for i in range(2)]
        gainrow = A.get([1536], F32)
        bgr = Buf("gainrow")
        gsrc = [0] * 6 + [1] * 6 + [2] * 6 + [3] * 2 + [4] * 4
        for hh, gi in enumerate(gsrc):
            I_("pool", lambda e, hh=hh, gi=gi: e.tensor_copy(out=gainrow[:, hh * 64:(hh + 1) * 64],
                                                            in_=gains[:, gi * 64:(gi + 1) * 64]),
               R=[b_const], W=[bgr])
        rt2 = [A.get([4, 29, 8], F32) for _ in range(2)]
        brt2 = [Buf("rt%d" % i) for i in range(2)]
        oA = [A.get([1152], BF16) for _ in range(2)]
        boA = [Buf("oA%d" % i) for i in range(2)]
        oT = [A.get([1344], BF16) for _ in range(2)]
        boT = [Buf("oT%d" % i) for i in range(2)]
        oV = [A.get([136], BF16) for _ in range(2)]
        boV = [Buf("oV%d" % i) for i in range(2)]
        tT = [A.get([1408], BF16) for _ in range(2)]
        btT = [Buf("tT%d" % i) for i in range(2)]
        gsb = [A.get([512], BF16) for _ in range(3)]
        bgsb = [Buf("gsb%d" % i) for i in range(3)]
        WI_SCALE = float((8 ** -0.5) * (64 ** -0.5))
        gcount = [0]

        def p1_A(n):
            grp_i, tt = n // 4, n % 4
            hs = grp_i % 2
            s = n % NB
            o = n % 2
            stg = stg2[o]
            bstg = bstg2[o]
            norm_rows_to_hT(x_d[n * 128:(n + 1) * 128, :], xt[s], xb[s], None, bx[s], bh[s], bxs, "x%d" % s, 5, pb[5])
            I_("act", lambda e: e.copy(out=hTg[hs][:, :, tt * 128:(tt + 1) * 128],
                                       in_=bank_bf(5).rearrange("p (k c) -> p k c", k=8)), R=[pb[5]], W=[bhT[hs]])
            yield
            for bnk in range(5):
                for k in range(8):
                    I_("pe", lambda e, bnk=bnk, k=k: e.matmul(
                        bank(bnk), lhsT=hTg[hs][:, k, tt * 128:(tt + 1) * 128],
                        rhs=W[:, k, bnk * 512:(bnk + 1) * 512], start=(k == 0), stop=(k == 7)),
                       R=[bhT[hs], bW], W=[pb[bnk]])
                yield
            for k in range(8):
                I_("pe", lambda e, k=k: e.matmul(
                    bank(7)[:, 0:N_SM], lhsT=hTg[hs][:, k, tt * 128:(tt + 1) * 128],
                    rhs=W[:, k, N_TM:N_TM + N_SM], start=(k == 0), stop=(k == 7)),
                   R=[bhT[hs], bW], W=[pb[7]])
            I_("act", lambda e: e.copy(out=stg[:, 0:1792], in_=psum[:, 0:4, :].rearrange("p a b -> p (a b)")[:, 0:1792]),
               R=[pb[0], pb[1], pb[2], pb[3]], W=[bstg])
            I_("dve", lambda e: e.tensor_copy(out=stg[:, 1856:2624],
                                              in_=psum[:, 3:5, :].rearrange("p a b -> p (a b)")[:, 256:1024]),
               R=[pb[3], pb[4]], W=[bstg])
            I_("dve", lambda e: e.tensor_copy(out=stg[:, 1792:1856], in_=bank(7)[:, 0:64]), R=[pb[7]], W=[bstg])
            I_("dve", lambda e: e.tensor_scalar(out=stg[:, 2624:2632], in0=bank(7)[:, 64:72], scalar1=WI_SCALE,
                                                scalar2=None, op0=ALU.mult), R=[pb[7]], W=[bstg])
            yield

        def p1_B(n):
            o = n % 2
            stg, bstg = stg2[o], bstg2[o]
            tmp, btmp = tmp2[o], btmp2[o]
            ss, rs, bss = ss2[o], rs2[o], bss2[o]
            rt, brt = rt2[o], brt2[o]
            I_("pool", lambda e: e.tensor_tensor(out=tmp[:, 0:1280], in0=stg[:, 0:1280], in1=stg[:, 0:1280],
                                                 op=ALU.mult), R=[bstg], W=[btmp])
            I_("pool", lambda e: e.tensor_tensor(out=tmp[:, 1280:1536], in0=stg[:, 1856:2112],
                                                 in1=stg[:, 1856:2112], op=ALU.mult), R=[bstg], W=[btmp])
            yield
            I_("dve", lambda e: e.tensor_reduce(out=ss, in_=tmp.rearrange("p (h d) -> p h d", d=64), axis=AX.X,
                                                op=ALU.add), R=[btmp], W=[bss])
            yield
            rstd_from_ss(ss, rs, 64, [bss], 24)
            yield
            I_("dve", lambda e: e.tensor_tensor(
                out=tmp.rearrange("p (h d) -> p h d", d=64), in0=gainrow.rearrange("p (h d) -> p h d", d=64),
                in1=rs.unsqueeze(2).to_broadcast([128, 24, 64]), op=ALU.mult), R=[bss, bgr, btmp], W=[btmp])
            yield
            I_("pool", lambda e: e.tensor_tensor(out=stg[:, 0:1280], in0=stg[:, 0:1280], in1=tmp[:, 0:1280],
                                                 op=ALU.mult), R=[bstg, btmp], W=[bstg])
            I_("pool", lambda e: e.tensor_tensor(out=stg[:, 1856:2112], in0=stg[:, 1856:2112],
                                                 in1=tmp[:, 1280:1536], op=ALU.mult), R=[bstg, btmp], W=[bstg])
            yield
            v = stg[:, 0:1856].rearrange("p (h d) -> p h d", d=64)
            x1 = v[:, :, 0:8]
            x2 = v[:, :, 8:16]
            cosb = cs[:, n, 0:8].unsqueeze(1).to_broadcast([128, 29, 8])
            sinb = cs[:, n, 8:16].unsqueeze(1).to_broadcast([128, 29, 8])
            I_("dve", lambda e: e.tensor_tensor(out=rt[:, 0], in0=x1, in1=cosb, op=ALU.mult), R=[bstg, b_const], W=[brt])
            I_("dve", lambda e: e.tensor_tensor(out=rt[:, 1], in0=x2, in1=sinb, op=ALU.mult), R=[bstg, b_const], W=[brt])
            I_("pool", lambda e: e.tensor_tensor(out=rt[:, 2], in0=x2, in1=cosb, op=ALU.mult), R=[bstg, b_const], W=[brt])
            I_("pool", lambda e: e.tensor_tensor(out=rt[:, 3], in0=x1, in1=sinb, op=ALU.mult), R=[bstg, b_const], W=[brt])
            yield
            I_("dve", lambda e: e.tensor_tensor(out=x1, in0=rt[:, 0], in1=rt[:, 1], op=ALU.subtract), R=[brt], W=[bstg])
            I_("dve", lambda e: e.tensor_tensor(out=x2, in0=rt[:, 2], in1=rt[:, 3], op=ALU.add), R=[brt], W=[bstg])
            yield
            I_("act", lambda e: e.copy(out=oA[o][:, 0:768], in_=stg[:, 0:768]), R=[bstg], W=[boA[o]])
            I_("act", lambda e: e.copy(out=oA[o][:, 768:1152], in_=stg[:, 2112:2496]), R=[bstg], W=[boA[o]])
            I_("pool", lambda e: e.tensor_copy(out=oT[o], in_=stg[:, 768:2112]), R=[bstg], W=[boT[o]])
            I_("pool", lambda e: e.tensor_copy(out=oV[o], in_=stg[:, 2496:2632]), R=[bstg], W=[boV[o]])
            P.dma("pool", QKVA[n * 128:(n + 1) * 128, :], oA[o], R=[boA[o]], grp="oA%d" % o)
            P.dma("pool", VB[n * 128:(n + 1) * 128, :], oV[o], R=[boV[o]], grp="oV%d" % o)
            yield
            pT6 = bank_bf(6)
            for c in range(8):
                I_("pe", lambda e, c=c: e.transpose(out=pT6[:, c * 128:(c + 1) * 128],
                                                    in_=oT[o][:, c * 128:(c + 1) * 128], identity=ident),
                   R=[boT[o], b_const], W=[pb[6]])
            I_("dve", lambda e: e.tensor_copy(out=tT[o][:, 0:1024], in_=pT6), R=[pb[6]], W=[btT[o]])
            yield
            I_("pe", lambda e: e.transpose(out=pT6[0:64, 0:128], in_=oT[o][:, 1024:1088], identity=ident),
               R=[boT[o], b_const], W=[pb[6]])
            for c in range(2):
                I_("pe", lambda e, c=c: e.transpose(out=pT6[:, 128 + c * 128:256 + c * 128],
                                                    in_=oT[o][:, 1088 + c * 128:1216 + c * 128], identity=ident),
                   R=[boT[o], b_const], W=[pb[6]])
            I_("dve", lambda e: e.tensor_copy(out=tT[o][0:64, 1024:1152], in_=pT6[0:64, 0:128]), R=[pb[6]], W=[btT[o]])
            I_("dve", lambda e: e.tensor_copy(out=tT[o][:, 1152:1408], in_=pT6[:, 128:384]), R=[pb[6]], W=[btT[o]])
            g = "tT%d" % o
            P.dma("pool", QBT[:, n, :], tT[o][:, 0:384], R=[btT[o]], grp=g)
            P.dma("pool", KBT[:, n * 128:(n + 1) * 128], tT[o][:, 384:512], R=[btT[o]], grp=g)
            P.dma("pool", QIT[:, n, :], tT[o][:, 512:1024], R=[btT[o]], grp=g)
            P.dma("pool", KIT[:, n * 128:(n + 1) * 128], tT[o][0:64, 1024:1152], R=[btT[o]], grp=g)
            P.dma("pool", QCT[:, n, :], tT[o][:, 1152:1408], R=[btT[o]], grp=g)
            yield

        def p1_G(grp_i):
            hs = grp_i % 2
            for c in range(24):
                for k in range(8):
                    I_("pe", lambda e, c=c, k=k: e.matmul(
                        bank(7), lhsT=W[:, k, N_TM + N_SM + c * 128:N_TM + N_SM + (c + 1) * 128],
                        rhs=hTg[hs][:, k, :], start=(k == 0), stop=(k == 7)), R=[bhT[hs], bW], W=[pb[7]])
                gs = gcount[0] % 3
                gcount[0] += 1
                I_("act", lambda e, gs=gs: e.activation(out=gsb[gs], in_=bank(7), func=AF.Sigmoid),
                   R=[pb[7]], W=[bgsb[gs]])
                P.dma("sp", GT[c, :, grp_i * 512:(grp_i + 1) * 512], gsb[gs], R=[bgsb[gs]], grp="gsb%d" % gs)
                yield

        for _ in p1_A(0):
            pass
        for n in range(ntiles):
            streams = [(p1_B(n), 11)]
            if n + 1 < ntiles:
                streams.append((p1_A(n + 1), 7))
            if n % 4 == 3:
                streams.append((p1_G(n // 4), 24))
            interleave_streams(streams)
        P.barrier()
        A.reset()


    ntok = ntiles * 128
    if 2 in phases:
        blk = [A.get([3, 128], BF16) for _ in range(3)]
        bblk = [Buf("blk%d" % i) for i in range(3)]
        qT = [A.get([128], BF16) for _ in range(2)]
        bqT = [Buf("qT%d" % i) for i in range(2)]
        kT = [A.get([128], BF16) for _ in range(3)]
        bkT = [Buf("kT%d" % i) for i in range(3)]
        va_ = [A.get([2, 65], BF16) for _ in range(3)]
        bva = [Buf("va%d" % i) for i in range(3)]
        PT = [A.get([512], BF16) for _ in range(2)]
        bPT = [Buf("PT%d" % i) for i in range(2)]
        oas = [A.get([130], F32) for _ in range(2)]
        boas = [Buf("oas%d" % i) for i in range(2)]
        for i in range(3):
            I_("pool", lambda e, i=i: e.memset(va_[i], 1.0), W=[bva[i]])
        u = 0
        for g, dil in enumerate((1, 4, 16)):
            m = ntok // dil
            nblk = m // 128
            assert nblk >= 1
            for r in range(dil):
                for n in range(nblk):
                    sb3 = u % 3
                    sb2 = u % 2
                    prev3 = (u - 1) % 3
                    start = r + dil * 128 * n
                    rows = QKVA[start:start + dil * 127 + 1:dil, :].rearrange("t (s c) -> t s c", s=3)[:, :, g * 128:(g + 1) * 128]
                    P.dma("sp", blk[sb3], rows, W=[bblk[sb3]], grp="blk%d" % sb3)
                    pT = bank_bf(sb2)
                    I_("pe", lambda e, pT=pT, sb3=sb3: e.transpose(out=pT[:, 0:128], in_=blk[sb3][:, 0, :], identity=ident),
                       R=[bblk[sb3], b_const], W=[pb[sb2]])
                    I_("pe", lambda e, pT=pT, sb3=sb3: e.transpose(out=pT[:, 128:256], in_=blk[sb3][:, 1, :], identity=ident),
                       R=[bblk[sb3], b_const], W=[pb[sb2]])
                    I_("dve", lambda e, pT=pT, sb2=sb2: e.tensor_copy(out=qT[sb2], in_=pT[:, 0:128]), R=[pb[sb2]], W=[bqT[sb2]])
                    I_("dve", lambda e, pT=pT, sb3=sb3: e.tensor_copy(out=kT[sb3], in_=pT[:, 128:256]), R=[pb[sb2]], W=[bkT[sb3]])
                    I_("pool", lambda e, sb3=sb3: e.tensor_copy(out=va_[sb3][:, :, 0:64],
                                                                in_=blk[sb3][:, 2, :].rearrange("p (j d) -> p j d", j=2)),
                       R=[bblk[sb3]], W=[bva[sb3]])
                    psY = bank(2 + sb2)
                    has_prev = n > 0
                    for j in range(2):
                        if has_prev:
                            I_("pe", lambda e, j=j, psY=psY, prev3=prev3, sb2=sb2: e.matmul(
                                psY[:, j * 128:(j + 1) * 128], lhsT=kT[prev3][64 * j:64 * j + 64, :],
                                rhs=qT[sb2][64 * j:64 * j + 64, :], start=True, stop=False),
                               R=[bkT[prev3], bqT[sb2]], W=[pb[2 + sb2]])
                            I_("pe", lambda e, j=j, psY=psY: e.matmul(
                                psY[:, j * 128:(j + 1) * 128], lhsT=ident, rhs=mprev, start=False, stop=True),
                               R=[b_const], W=[pb[2 + sb2]])
                        I_("pe", lambda e, j=j, psY=psY, sb3=sb3, sb2=sb2: e.matmul(
                            psY[:, 256 + j * 128:256 + (j + 1) * 128], lhsT=kT[sb3][64 * j:64 * j + 64, :],
                            rhs=qT[sb2][64 * j:64 * j + 64, :], start=True, stop=False),
                           R=[bkT[sb3], bqT[sb2]], W=[pb[2 + sb2]])
                        I_("pe", lambda e, j=j, psY=psY: e.matmul(
                            psY[:, 256 + j * 128:256 + (j + 1) * 128], lhsT=ident, rhs=mcur, start=False, stop=True),
                           R=[b_const], W=[pb[2 + sb2]])
                    lo = 0 if has_prev else 256
                    I_("act", lambda e, psY=psY, sb2=sb2, lo=lo: e.activation(out=PT[sb2][:, lo:512], in_=psY[:, lo:512],
                                                                             func=AF.Exp, scale=0.125),
                       R=[pb[2 + sb2]], W=[bPT[sb2]])
                    psZ = bank(4 + sb2)
                    for j in range(2):
                        if has_prev:
                            I_("pe", lambda e, j=j, psZ=psZ, sb2=sb2, prev3=prev3: e.matmul(
                                psZ[:, j * 65:(j + 1) * 65], lhsT=PT[sb2][:, j * 128:(j + 1) * 128],
                                rhs=va_[prev3][:, j, :], start=True, stop=False),
                               R=[bPT[sb2], bva[prev3]], W=[pb[4 + sb2]])
                        I_("pe", lambda e, j=j, psZ=psZ, sb2=sb2, sb3=sb3, hp=has_prev: e.matmul(
                            psZ[:, j * 65:(j + 1) * 65], lhsT=PT[sb2][:, 256 + j * 128:256 + (j + 1) * 128],
                            rhs=va_[sb3][:, j, :], start=(not hp), stop=True),
                           R=[bPT[sb2], bva[sb3]], W=[pb[4 + sb2]])
                    I_("dve", lambda e, psZ=psZ, sb2=sb2: e.tensor_copy(out=oas[sb2], in_=psZ[:, 0:130]),
                       R=[pb[4 + sb2]], W=[boas[sb2]])
                    P.dma("pool", OA[g, start:start + dil * 127 + 1:dil, :], oas[sb2], R=[boas[sb2]], grp="oas%d" % sb2)
                    u += 1
        P.barrier()
        A.reset()

    if 3 in phases:
        U8 = mybir.dt.uint8
        kiT2 = A.get([S], BF16)
        kbT = A.get([S], BF16)
        vba = A.get([NT, 2, 65], BF16)
        wia = A.get([NT, 8], BF16)
        bK = Buf("dsaK")
        P.dma("sp", kiT2[0:64, 0:ntok], KIT[:, 0:ntok], W=[bK], grp="dk")
        P.dma("sp", kiT2[64:128, 0:ntok], KIT[:, 0:ntok], W=[bK], grp="dk")
        P.dma("sp", kbT[:, 0:ntok], KBT[:, 0:ntok], W=[bK], grp="dk")
        I_("pool", lambda e: e.memset(vba, 1.0), W=[bK])
        vst = [A.get([136], BF16) for _ in range(2)]
        bvst = [Buf("vst%d" % i) for i in range(2)]
        for n in range(ntiles):
            s2 = n % 2
            P.dma("sp", vst[s2], VB[n * 128:(n + 1) * 128, :], W=[bvst[s2]], grp="vst%d" % s2)
            I_("pool", lambda e, n=n, s2=s2: e.tensor_copy(out=vba[:, n, :, 0:64],
                                                          in_=vst[s2][:, 0:128].rearrange("p (c d) -> p c d", c=2)),
               R=[bvst[s2], bK], W=[bK])
            I_("pool", lambda e, n=n, s2=s2: e.tensor_copy(out=wia[:, n, :], in_=vst[s2][:, 128:136]),
               R=[bvst[s2], bK], W=[bK])
        NI = 3
        qiT = [A.get([4, 128], BF16) for _ in range(2)]
        bqiT = [Buf("qiT%d" % i) for i in range(2)]
        qbT = [A.get([384], BF16) for _ in range(3)]
        bqbT = [Buf("qbT%d" % i) for i in range(3)]
        Dh = [A.get([8, 128], BF16) for _ in range(2)]
        bDh = [Buf("Dh%d" % i) for i in range(2)]
        isc = [A.get([S], F32) for _ in range(NI)]
        bisc = [Buf("isc%d" % i) for i in range(NI)]
        junk_t = A.get([S // 2], BF16)
        junk = junk_t.bitcast(U8)
        bjunk = Buf("junk")
        MBF = A.get([S], BF16)
        bMBF = [Buf("MBF%d" % c) for c in range(S // 512)]
        Rb = [A.get([2, 512], BF16) for _ in range(3)]
        bRb = [Buf("Rb%d" % i) for i in range(3)]
        PTb = [A.get([384], BF16) for _ in range(3)]
        bPTb = [Buf("PTb%d" % i) for i in range(3)]
        bst = [A.get([8], F32) for _ in range(NI)]
        bbst = [Buf("bst%d" % i) for i in range(NI)]
        Wk = [A.get([NBIS + 1], F32) for _ in range(NI)]
        pow2 = A.get([NBIS + 1], F32)
        cntb = A.get([2], F32)
        bcnt = Buf("cnt")
        cntp = A.get([2], F32)
        bcntp = Buf("cntp")
        bjunkp = Buf("junkp")
        rden = A.get([6], F32)
        brden = Buf("rden")
        ob = [A.get([384], BF16) for _ in range(2)]
        bob = [Buf("ob%d" % i) for i in range(2)]
        obT = [A.get([384], BF16) for _ in range(2)]
        bobT = [Buf("obT%d" % i) for i in range(2)]
        bpow = Buf("pow2")
        for k in range(NBIS + 1):
            I_("pool", lambda e, k=k: e.memset(pow2[:, k:k + 1], float(2.0 ** -(k + 1))), W=[bpow], R=[bpow])
        pair = [psum[:, 0:2, :], psum[:, 2:4, :]]
        bpair = [Buf("pair0", excl=True), Buf("pair1", excl=True)]
        accI = bank(4)
        sbank = [bank(5), bank(6)]
        accO = bank(7)
        cnt_pair = [0]
        cnt_rb = [0]
        cnt_unit = [0]
        _fr = []

        def fill_reg(e):
            if not _fr:
                _fr.append(e.to_reg(-1e30))
            return _fr[0]

        def dsa_index(i):
            s2 = i % 2
            s3 = i % NI
            L = 128 * (i + 1)
            P.dma("sp", qiT[s2], QIT[:, i, :].rearrange("p (j t) -> p j t", j=4), W=[bqiT[s2]], grp="qiT%d" % s2)
            P.dma("sp", qbT[s3], QBT[:, i, :], W=[bqbT[s3]], grp="qbT%d" % s3)
            I_("pool", lambda e: e.tensor_tensor(out=Dh[s2], in0=identf.unsqueeze(1).to_broadcast([128, 8, 128]),
                                                 in1=wia[:, i, :].unsqueeze(2).to_broadcast([128, 8, 128]), op=ALU.mult),
               R=[bK, b_const], W=[bDh[s2]])
            steps = [(c0, min(512, L - c0), jj) for c0 in range(0, L, 512) for jj in range(4)]
            slots = {}

            def QK(k):
                c0, cw, jj = steps[k]
                pp = cnt_pair[0] % 2
                cnt_pair[0] += 1
                rb = cnt_rb[0] % 3
                cnt_rb[0] += 1
                slots[k] = (pp, rb)
                for hh in range(2):
                    I_("pe", lambda e, pp=pp, hh=hh, jj=jj, c0=c0, cw=cw: e.matmul(
                        pair[pp][:, hh, 0:cw], lhsT=qiT[s2][64 * hh:64 * hh + 64, jj, :],
                        rhs=kiT2[64 * hh:64 * hh + 64, c0:c0 + cw], start=True, stop=True),
                       R=[bqiT[s2], bK], W=[bpair[pp]])
                I_("act", lambda e, pp=pp, rb=rb, cw=cw: e.activation(out=Rb[rb][:, :, 0:cw], in_=pair[pp][:, :, 0:cw],
                                                                     func=AF.Relu), R=[bpair[pp]], W=[bRb[rb]])

            def HS(k):
                c0, cw, jj = steps[k]
                pp, rb = slots.pop(k)
                for hh in range(2):
                    I_("pe", lambda e, rb=rb, hh=hh, jj=jj, cw=cw: e.matmul(
                        accI[:, 0:cw], lhsT=Dh[s2][:, 2 * jj + hh, :], rhs=Rb[rb][:, hh, 0:cw],
                        start=(jj == 0 and hh == 0), stop=(jj == 3 and hh == 1)),
                       R=[bDh[s2], bRb[rb]], W=[pb[4]])
                if jj == 3:
                    I_("act", lambda e, c0=c0, cw=cw: e.copy(out=isc[s3][:, c0:c0 + cw], in_=accI[:, 0:cw]),
                       R=[pb[4]], W=[bisc[s3]])

            QK(0)
            for k in range(len(steps)):
                if k + 1 < len(steps):
                    QK(k + 1)
                HS(k)
                yield
            I_("pool", lambda e: e.affine_select(out=isc[s3][:, 128 * i:128 * (i + 1)], in_=isc[s3][:, 128 * i:128 * (i + 1)],
                                                 pattern=[[-1, 128]], compare_op=ALU.is_ge, fill=fill_reg(e), base=0,
                                                 channel_multiplier=1), R=[bisc[s3]], W=[bisc[s3]])
            yield

        def dsa_bisect(i):
            s3 = i % NI
            L = 128 * (i + 1)
            b = bst[s3]
            bb = bbst[s3]
            if i < 2:
                I_("dve", lambda e: e.memset(b[:, 3:4], -1e29), W=[bb])
                return
            I_("dve", lambda e: e.tensor_reduce(out=b[:, 0:1], in_=isc[s3][:, 0:L], axis=AX.X, op=ALU.max),
               R=[bisc[s3]], W=[bb])
            I_("dve", lambda e: e.tensor_reduce(out=b[:, 1:2], in_=isc[s3][:, 0:128 * i], axis=AX.X, op=ALU.min),
               R=[bisc[s3]], W=[bb])
            I_("dve", lambda e: e.tensor_tensor(out=b[:, 2:3], in0=b[:, 0:1], in1=b[:, 1:2], op=ALU.subtract),
               R=[bb], W=[bb])
            I_("dve", lambda e: e.tensor_scalar(out=Wk[s3], in0=pow2, scalar1=b[:, 2:3], scalar2=None, op0=ALU.mult),
               R=[bb, bpow], W=[bb])
            I_("dve", lambda e: e.tensor_tensor(out=b[:, 4:5], in0=b[:, 1:2], in1=Wk[s3][:, 0:1], op=ALU.add),
               R=[bb], W=[bb])
            for k in range(NBIS):
                I_("dve", lambda e: e.tensor_scalar(out=junk[:, 0:L], in0=isc[s3][:, 0:L], scalar1=b[:, 4:5], scalar2=None,
                                                    op0=ALU.is_ge, op1=ALU.add, accum_out=cntb[:, 0:1]),
                   R=[bisc[s3], bb], W=[bjunk, bcnt])
                I_("dve", lambda e: e.tensor_scalar(out=cntb[:, 1:2], in0=cntb[:, 0:1], scalar1=TOPK - 0.5, scalar2=0.5,
                                                    op0=ALU.is_ge, op1=ALU.subtract), R=[bcnt], W=[bcnt])
                I_("dve", lambda e, k=k: e.scalar_tensor_tensor(out=b[:, 4:5], in0=cntb[:, 1:2], scalar=Wk[s3][:, k:k + 1],
                                                               in1=b[:, 4:5], op0=ALU.mult, op1=ALU.add),
                   R=[bcnt, bb], W=[bb])
            I_("dve", lambda e: e.tensor_tensor(out=b[:, 3:4], in0=b[:, 4:5], in1=Wk[s3][:, NBIS:NBIS + 1], op=ALU.subtract),
               R=[bb], W=[bb])

        def dsa_maskgen(i):
            s3 = i % NI
            L = 128 * (i + 1)
            b = bst[s3]
            for c0 in range(0, L, 512):
                cw = min(512, L - c0)
                I_("dve", lambda e, c0=c0, cw=cw: e.tensor_scalar(out=MBF[:, c0:c0 + cw], in0=isc[s3][:, c0:c0 + cw],
                                                                 scalar1=b[:, 3:4], scalar2=NEG, op0=ALU.is_lt, op1=ALU.mult),
                   R=[bisc[s3], bbst[s3]], W=[bMBF[c0 // 512]])

        def dsa_attn(i):
            s2 = i % 2
            s3 = i % NI
            units = [(kb, c) for kb in range(i + 1) for c in range(2)]
            slots = {}

            def SC(k):
                kb, c = units[k]
                bm = bMBF[kb // 4]
                u = cnt_unit[0]
                cnt_unit[0] += 1
                sb = u % 2
                pt = u % 3
                slots[k] = pt
                psS = sbank[sb]
                I_("pe", lambda e, c=c, kb=kb, psS=psS: e.matmul(
                    psS[:, 0:384], lhsT=kbT[64 * c:64 * c + 64, kb * 128:(kb + 1) * 128],
                    rhs=qbT[s3][64 * c:64 * c + 64, :], start=True, stop=False),
                   R=[bK, bqbT[s3]], W=[pb[5 + sb]])
                I_("pe", lambda e, kb=kb, psS=psS: e.matmul(psS[:, 0:384], lhsT=MBF[:, kb * 128:(kb + 1) * 128], rhs=i3,
                                                           start=False, stop=True),
                   R=[bm, b_const], W=[pb[5 + sb]])
                I_("act", lambda e, pt=pt, psS=psS: e.activation(out=PTb[pt], in_=psS[:, 0:384], func=AF.Exp, scale=0.125),
                   R=[pb[5 + sb]], W=[bPTb[pt]])

            def PV(k):
                kb, c = units[k]
                pt = slots.pop(k)
                for g in range(3):
                    h = 3 * c + g
                    I_("pe", lambda e, pt=pt, g=g, h=h, kb=kb, c=c: e.matmul(
                        accO[:, h * 65:(h + 1) * 65], lhsT=PTb[pt][:, g * 128:(g + 1) * 128], rhs=vba[:, kb, c, :],
                        start=(kb == 0 and h == 0), stop=(kb == i), skip_group_check=True),
                        R=[bPTb[pt], bK], W=[pb[7]])

            SC(0)
            for k in range(len(units)):
                if k + 1 < len(units):
                    SC(k + 1)
                PV(k)
                yield

        def dsa_final(i):
            s2 = i % 2
            av = accO[:, 0:390].rearrange("p (h e) -> p h e", e=65)
            I_("dve", lambda e: e.reciprocal(out=rden, in_=av[:, :, 64]), R=[pb[7]], W=[brden])
            I_("dve", lambda e: e.tensor_tensor(out=ob[s2].rearrange("p (h d) -> p h d", d=64), in0=av[:, :, 0:64],
                                                in1=rden.unsqueeze(2).to_broadcast([128, 6, 64]), op=ALU.mult),
               R=[pb[7], brden], W=[bob[s2]])
            pT = bank_bf(4)
            for cc in range(3):
                I_("pe", lambda e, cc=cc: e.transpose(out=pT[:, cc * 128:(cc + 1) * 128], in_=ob[s2][:, cc * 128:(cc + 1) * 128],
                                                      identity=ident), R=[bob[s2], b_const], W=[pb[4]])
            I_("act", lambda e: e.copy(out=obT[s2], in_=pT[:, 0:384]), R=[pb[4]], W=[bobT[s2]])
            P.dma("pool", OBT[:, :, i * 128:(i + 1) * 128].rearrange("c p t -> p c t"),
                  obT[s2].rearrange("p (c t) -> p c t", c=3), R=[bobT[s2]], grp="obT%d" % s2)

        def run_all(gen):
            for _ in gen:
                pass

        def interleave(ga, na, gb, nb):
            da = db = 0
            ea = eb = False
            while not (ea and eb):
                fa = da / max(na, 1)
                fb = db / max(nb, 1)
                if (not ea) and (eb or fa <= fb):
                    try:
                        next(ga)
                        da += 1
                    except StopIteration:
                        ea = True
                else:
                    try:
                        next(gb)
                        db += 1
                    except StopIteration:
                        eb = True

        def n_idx_steps(i):
            return 4 * ((128 * (i + 1) + 511) // 512) + 1

        run_all(dsa_index(0))
        if ntiles > 1:
            run_all(dsa_index(1))
        dsa_bisect(0)
        for i in range(ntiles):
            dsa_maskgen(i)
            if i + 1 < ntiles:
                dsa_bisect(i + 1)
            ga = dsa_attn(i)
            if i + 2 < ntiles:
                interleave(dsa_index(i + 2), n_idx_steps(i + 2), ga, 2 * (i + 1))
            else:
                run_all(ga)
            dsa_final(i)
        P.barrier()
        A.reset()

    if 4 in phases:
        wa = A.get([1, D], BF16)
        wb_ = A.get([3, D], BF16)
        wc = A.get([2, D], BF16)
        wo = A.get([8, D], BF16)
        wkv = A.get([8, 512], BF16)
        stage4 = make_stage(1024, "p4")
        bwa = load_weight_bf16(wa, wa_d, 1, D, None, "wa", chunk=1024, stage=stage4)
        bwb = load_weight_bf16(wb_, wb_d, 3, D, None, "wb", chunk=1024, stage=stage4)
        bwc = load_weight_bf16(wc, wc_d, 2, D, None, "wc", chunk=1024, stage=stage4)
        bwo = load_weight_bf16(wo, wo_d, 8, D, None, "wo", chunk=1024, stage=stage4)
        bwkv = load_weight_bf16(wkv, wkv_d, 8, 512, gmem_d, "wkv", chunk=1024, stage=stage4)
        P4STOP = int(os.environ.get('P4STOP', '9'))
        xt4 = [A.get([D], F32) for _ in range(2)]
        bxt4 = [Buf("xt4%d" % i) for i in range(2)]
        xb4 = A.get([D], BF16)
        bxb4 = Buf("xb4")
        sq4 = A.get([D], F32)
        st4 = A.get([8], F32)
        bxs4 = dict(sq=sq4, st=st4, b=Buf("xstat4"))
        memT = A.get([8, 128], BF16)
        bmemT = Buf("memT")
        kmT = A.get([2, 256], BF16)
        vmb = A.get([2, 256], BF16)
        bkm = Buf("kmT")
        bvm = Buf("vmb")
        ksb = A.get([256], F32)
        ksq = A.get([256], F32)
        kss = A.get([8], F32)
        kgr = A.get([256], F32)
        kmb = A.get([256], BF16)
        bks = Buf("ksb")
        ones_bf = A.get([128], BF16)
        I_("pool", lambda e: e.memset(ones_bf, 1.0), W=[bks])
        for hh in range(4):
            I_("pool", lambda e, hh=hh: e.tensor_copy(out=kgr[:, hh * 64:(hh + 1) * 64], in_=gains[:, 5 * 64:6 * 64]),
               R=[b_const], W=[bks])
        for mt in range(2 if P4STOP >= 1 else 0):
            norm_rows_to_hT(mem_d[mt * 128:(mt + 1) * 128, :], xt4[0], xb4, None, bxt4[0], bxb4, bxs4, "xt40", 5, pb[5])
            I_("act", lambda e: e.copy(out=memT, in_=bank_bf(5).rearrange("p (k c) -> p k c", k=8)), R=[pb[5]], W=[bmemT])
            P4SUB = int(os.environ.get('P4SUB', '9'))
            if P4SUB < 1:
                continue
            for k in range(8):
                I_("pe", lambda e, k=k: e.matmul(bank(0), lhsT=memT[:, k, :], rhs=wkv[:, k, :], start=(k == 0), stop=(k == 7)),
                   R=[bmemT, bwkv], W=[pb[0]])
            P4V = int(os.environ.get('P4V', '3'))
            if P4V & 1:
                I_("act", lambda e, mt=mt: e.copy(out=vmb[:, mt, :], in_=bank(0)[:, 256:512]), R=[pb[0]], W=[bvm])
            if P4V & 2:
                I_("dve", lambda e: e.tensor_copy(out=ksb, in_=bank(0)[:, 0:256]), R=[pb[0]], W=[bks])
            if P4SUB < 2:
                continue
            I_("pool", lambda e: e.tensor_tensor(out=ksq, in0=ksb, in1=ksb, op=ALU.mult), R=[bks], W=[bks])
            I_("dve", lambda e: e.tensor_reduce(out=kss[:, 0:4], in_=ksq.rearrange("p (h d) -> p h d", d=64), axis=AX.X,
                                                op=ALU.add), R=[bks], W=[bks])
            rstd_from_ss(kss[:, 0:4], kss[:, 4:8], 64, [bks], 4)
            I_("dve", lambda e: e.tensor_tensor(out=ksq.rearrange("p (h d) -> p h d", d=64),
                                                in0=kgr.rearrange("p (h d) -> p h d", d=64),
                                                in1=kss[:, 4:8].unsqueeze(2).to_broadcast([128, 4, 64]), op=ALU.mult),
               R=[bks], W=[bks])
            I_("dve", lambda e: e.tensor_tensor(out=kmb, in0=ksb, in1=ksq, op=ALU.mult), R=[bks], W=[bks])
            if P4SUB < 3:
                continue
            for j in range(2):
                I_("pe", lambda e, j=j: e.transpose(out=bank_bf(1)[:, j * 128:(j + 1) * 128], in_=kmb[:, j * 128:(j + 1) * 128],
                                                    identity=ident), R=[bks, b_const], W=[pb[1]])
            I_("dve", lambda e, mt=mt: e.tensor_copy(out=kmT[:, :, mt * 128:(mt + 1) * 128],
                                                     in_=bank_bf(1)[:, 0:256].rearrange("p (j m) -> p j m", j=2)),
               R=[pb[1]], W=[bkm])
        gtg = [A.get([24, 512], BF16) for _ in range(2)]
        bgtg = [Buf("gtg%d" % i) for i in range(2)]
        obg = [A.get([3, 512], BF16) for _ in range(2)]
        bobg = [Buf("obg%d" % i) for i in range(2)]
        qcg = [A.get([2, 512], BF16) for _ in range(2)]
        bqcg = [Buf("qcg%d" % i) for i in range(2)]
        oaT = [A.get([512], BF16) for _ in range(2)]
        boaT = [Buf("oaT%d" % i) for i in range(2)]
        ocT = [A.get([2, 512], BF16) for _ in range(2)]
        bocT = [Buf("ocT%d" % i) for i in range(2)]
        mT = [A.get([8, 512], BF16) for _ in range(2)]
        bmT = [Buf("mT%d" % i) for i in range(2)]
        oa3 = [A.get([3, 130], F32) for _ in range(2)]
        boa3 = [Buf("oa3%d" % i) for i in range(2)]
        oasum = A.get([130], F32)
        oard = A.get([2], F32)
        oab = A.get([128], BF16)
        boas4 = Buf("oasum")
        PTc = [A.get([512], BF16) for _ in range(4)]
        bPTc = [Buf("PTc%d" % i) for i in range(4)]
        rdc = [A.get([512], F32) for _ in range(2)]
        brdc = [Buf("rdc%d" % i) for i in range(2)]
        tm = [A.get([512], F32) for _ in range(3)]
        btm = [Buf("tm%d" % i) for i in range(3)]
        x1t = [A.get([D], F32) for _ in range(2)]
        bx1t = [Buf("x1t%d" % i) for i in range(2)]
        ngrp = ntiles // 4
        cpt = [0]
        oasum2 = [A.get([130], F32) for _ in range(2)]
        oard2 = [A.get([2], F32) for _ in range(2)]
        oab2 = [A.get([128], BF16) for _ in range(2)]
        boas2 = [Buf("oasum%d" % i) for i in range(2)]

        def p4_prep(G):
            g2 = G % 2
            t0 = G * 512
            for c4 in range(0, 24, 4):
                P.dma("sp", gtg[g2][:, c4:c4 + 4, :], GT[c4:c4 + 4, :, t0:t0 + 512].rearrange("c p t -> p c t"),
                      W=[bgtg[g2]], grp="gtg%d" % g2)
            P.dma("sp", obg[g2], OBT[:, :, t0:t0 + 512].rearrange("c p t -> p c t"), W=[bobg[g2]], grp="obg%d" % g2)
            for j in range(2):
                P.dma("sp", qcg[g2][:, j, :].rearrange("p (n t) -> p n t", n=4), QCT[:, 4 * G:4 * G + 4, j * 128:(j + 1) * 128],
                      W=[bqcg[g2]], grp="qcg%d" % g2)
            for tt in range(4):
                n = 4 * G + tt
                o2 = n % 2
                oasum, oard, oab, boas4 = oasum2[o2], oard2[o2], oab2[o2], boas2[o2]
                P.dma("sp", oa3[o2], OA[:, n * 128:(n + 1) * 128, :].rearrange("g t e -> t g e"), W=[boa3[o2]],
                      grp="oa3%d" % o2)
                I_("pool", lambda e, o2=o2, oasum=oasum: e.tensor_tensor(out=oasum, in0=oa3[o2][:, 0, :], in1=oa3[o2][:, 1, :],
                                                                        op=ALU.add), R=[boa3[o2]], W=[boas4])
                I_("pool", lambda e, o2=o2, oasum=oasum: e.tensor_tensor(out=oasum, in0=oasum, in1=oa3[o2][:, 2, :], op=ALU.add),
                   R=[boa3[o2], boas4], W=[boas4])
                osv = oasum.rearrange("p (j e) -> p j e", e=65)
                I_("dve", lambda e, osv=osv, oard=oard: e.reciprocal(out=oard, in_=osv[:, :, 64]), R=[boas4], W=[boas4])
                I_("dve", lambda e, osv=osv, oard=oard, oab=oab: e.tensor_tensor(
                    out=oab.rearrange("p (j d) -> p j d", d=64), in0=osv[:, :, 0:64],
                    in1=oard.unsqueeze(2).to_broadcast([128, 2, 64]), op=ALU.mult), R=[boas4], W=[boas4])
                I_("pe", lambda e, tt=tt, oab=oab: e.transpose(out=bank_bf(4)[:, tt * 128:(tt + 1) * 128], in_=oab, identity=ident),
                   R=[boas4, b_const], W=[pb[4]])
                yield
            I_("act", lambda e: e.copy(out=oaT[g2], in_=bank_bf(4)[:, 0:512]), R=[pb[4]], W=[boaT[g2]])
            yield
            for h in range(4):
                j, hh = h // 2, h % 2
                pts = []
                for mt in range(2):
                    sbk = 6 + mt
                    pt = cpt[0] % 4
                    cpt[0] += 1
                    pts.append(pt)
                    I_("pe", lambda e, sbk=sbk, mt=mt, hh=hh, j=j: e.matmul(
                        bank(sbk), lhsT=kmT[64 * hh:64 * hh + 64, j, mt * 128:(mt + 1) * 128],
                        rhs=qcg[g2][64 * hh:64 * hh + 64, j, :], start=True, stop=True),
                       R=[bkm, bqcg[g2]], W=[pb[sbk]])
                    I_("act", lambda e, sbk=sbk, pt=pt: e.activation(out=PTc[pt], in_=bank(sbk), func=AF.Exp, scale=0.125),
                       R=[pb[sbk]], W=[bPTc[pt]])
                for mt in range(2):
                    I_("pe", lambda e, mt=mt, pt=pts[mt], j=j: e.matmul(
                        bank(4), lhsT=vmb[:, mt, j * 128:(j + 1) * 128], rhs=PTc[pt], start=(mt == 0), stop=(mt == 1)),
                       R=[bvm, bPTc[pts[mt]]], W=[pb[4]])
                for mt in range(2):
                    I_("pe", lambda e, mt=mt, pt=pts[mt]: e.matmul(
                        bank(5), lhsT=ones_bf, rhs=PTc[pt], start=(mt == 0), stop=(mt == 1)),
                       R=[bks, bPTc[pts[mt]]], W=[pb[5]])
                lo_, hi_ = 64 * hh, 64 * hh + 64
                hs2 = h % 2
                I_("dve", lambda e, hs2=hs2, lo_=lo_, hi_=hi_: e.reciprocal(out=rdc[hs2][lo_:hi_, :], in_=bank(5)[lo_:hi_, :]),
                   R=[pb[5]], W=[brdc[hs2]])
                I_("dve", lambda e, hs2=hs2, lo_=lo_, hi_=hi_, j=j: e.tensor_tensor(
                    out=ocT[g2][lo_:hi_, j, :], in0=bank(4)[lo_:hi_, :], in1=rdc[hs2][lo_:hi_, :], op=ALU.mult),
                   R=[pb[4], brdc[hs2]], W=[bocT[g2]])
                yield

        def p4_ef(G):
            g2 = G % 2
            for oc in range(8):
                cs_ = slice(oc * 128, (oc + 1) * 128)
                I_("pe", lambda e, cs_=cs_: e.matmul(bank(0), lhsT=wa[:, 0, cs_], rhs=oaT[g2], start=True, stop=True),
                   R=[bwa, boaT[g2]], W=[pb[0]])
                for k in range(3):
                    I_("pe", lambda e, cs_=cs_, k=k: e.matmul(bank(1), lhsT=wb_[:, k, cs_], rhs=obg[g2][:, k, :],
                                                            start=(k == 0), stop=(k == 2)),
                       R=[bwb, bobg[g2]], W=[pb[1]])
                for k in range(2):
                    I_("pe", lambda e, cs_=cs_, k=k: e.matmul(bank(2), lhsT=wc[:, k, cs_], rhs=ocT[g2][:, k, :],
                                                            start=(k == 0), stop=(k == 1)),
                       R=[bwc, bocT[g2]], W=[pb[2]])
                I_("dve", lambda e, oc=oc: e.tensor_tensor(out=tm[0], in0=bank(0), in1=gtg[g2][:, oc, :], op=ALU.mult),
                   R=[pb[0], bgtg[g2]], W=[btm[0]])
                I_("dve", lambda e, oc=oc: e.tensor_tensor(out=tm[1], in0=bank(1), in1=gtg[g2][:, 8 + oc, :], op=ALU.mult),
                   R=[pb[1], bgtg[g2]], W=[btm[1]])
                I_("dve", lambda e, oc=oc: e.tensor_tensor(out=tm[2], in0=bank(2), in1=gtg[g2][:, 16 + oc, :], op=ALU.mult),
                   R=[pb[2], bgtg[g2]], W=[btm[2]])
                I_("pool", lambda e: e.tensor_tensor(out=tm[0], in0=tm[0], in1=tm[1], op=ALU.add), R=[btm[0], btm[1]], W=[btm[0]])
                I_("pool", lambda e, oc=oc: e.tensor_tensor(out=mT[g2][:, oc, :], in0=tm[0], in1=tm[2], op=ALU.add),
                   R=[btm[0], btm[2]], W=[bmT[g2]])
                yield
            for tt in range(4):
                n = 4 * G + tt
                o2 = n % 2
                P.dma("sp", xt4[o2], x_d[n * 128:(n + 1) * 128, :], W=[bxt4[o2]], grp="xt4%d" % o2)
                for hf in range(2):
                    for k in range(8):
                        I_("pe", lambda e, hf=hf, k=k, tt=tt: e.matmul(
                            bank(3), lhsT=mT[g2][:, k, tt * 128:(tt + 1) * 128], rhs=wo[:, k, hf * 512:(hf + 1) * 512],
                            start=(k == 0), stop=(k == 7)), R=[bmT[g2], bwo], W=[pb[3]])
                    I_("dve", lambda e, o2=o2, hf=hf: e.tensor_tensor(out=x1t[o2][:, hf * 512:(hf + 1) * 512], in0=bank(3),
                                                                     in1=xt4[o2][:, hf * 512:(hf + 1) * 512], op=ALU.add),
                       R=[pb[3], bxt4[o2]], W=[bx1t[o2]])
                P.dma("pool", X1[n * 128:(n + 1) * 128, :], x1t[o2], R=[bx1t[o2]], grp="x1t%d" % o2)
                yield

        if ngrp > 0:
            for _ in p4_prep(0):
                pass
        for G in range(ngrp):
            streams = [(p4_ef(G), 12)]
            if G + 1 < ngrp:
                streams.append((p4_prep(G + 1), 9))
            interleave_streams(streams)
        P.barrier()
        A.reset()

    if 5 in phases:
        W1 = A.get([8, 4096], BF16)
        W2 = A.get([32, D], BF16)
        stage5 = make_stage(2048, "p5")
        bW1 = load_weight_bf16(W1, w1_d, 8, 4096, gmlp_d, "w1", chunk=2048, stage=stage5)
        bW2 = load_weight_bf16(W2, w2_d, 32, D, None, "w2", chunk=1024, stage=stage5)
        xt5 = [A.get([D], F32) for _ in range(4)]
        bxt5 = [Buf("xt5%d" % i) for i in range(4)]
        xb5 = [A.get([D], BF16) for _ in range(2)]
        bxb5 = [Buf("xb5%d" % i) for i in range(2)]
        sq5 = A.get([D], F32)
        st5 = A.get([8], F32)
        bxs5 = dict(sq=sq5, st=st5, b=Buf("xstat5"))
        h2T = [A.get([8, 256], BF16) for _ in range(2)]
        bh2T = [Buf("h2T%d" % i) for i in range(2)]
        rr = [A.get([256], BF16) for _ in range(3)]
        brr = [Buf("rr%d" % i) for i in range(3)]
        aa = [A.get([256], BF16) for _ in range(3)]
        baa = [Buf("aa%d" % i) for i in range(3)]
        ot = [A.get([D], F32) for _ in range(2)]
        bot = [Buf("ot%d" % i) for i in range(2)]
        ng5 = ntiles // 2
        cc5 = [0]
        for G in range(ng5):
            g2 = G % 2
            for tt in range(2):
                n = 2 * G + tt
                s4 = n % 4
                s2 = n % 2
                norm_rows_to_hT(X1[n * 128:(n + 1) * 128, :], xt5[s4], xb5[s2], None, bxt5[s4], bxb5[s2], bxs5,
                                "xt5%d" % s4, 7, pb[7])
                I_("act", lambda e, g2=g2, tt=tt: e.copy(out=h2T[g2][:, :, tt * 128:(tt + 1) * 128],
                                                         in_=bank_bf(7).rearrange("p (k c) -> p k c", k=8)),
                   R=[pb[7]], W=[bh2T[g2]])
            slots5 = {}

            def U5(c):
                ub = 4 + cc5[0] % 3
                r3 = cc5[0] % 3
                cc5[0] += 1
                slots5[c] = r3
                for k in range(8):
                    I_("pe", lambda e, ub=ub, k=k, c=c, g2=g2: e.matmul(
                        bank(ub)[:, 0:256], lhsT=W1[:, k, c * 128:(c + 1) * 128], rhs=h2T[g2][:, k, :],
                        start=(k == 0), stop=(k == 7)), R=[bW1, bh2T[g2]], W=[pb[ub]])
                I_("act", lambda e, ub=ub, r3=r3: e.activation(out=rr[r3], in_=bank(ub)[:, 0:256], func=AF.Relu),
                   R=[pb[ub]], W=[brr[r3]])
                I_("dve", lambda e, r3=r3: e.tensor_tensor(out=aa[r3], in0=rr[r3], in1=rr[r3], op=ALU.mult),
                   R=[brr[r3]], W=[baa[r3]])

            def O5(c):
                r3 = slots5.pop(c)
                for tt in range(2):
                    for hf in range(2):
                        I_("pe", lambda e, tt=tt, hf=hf, r3=r3, c=c: e.matmul(
                            bank(tt * 2 + hf), lhsT=aa[r3][:, tt * 128:(tt + 1) * 128], rhs=W2[:, c, hf * 512:(hf + 1) * 512],
                            start=(c == 0), stop=(c == 31)), R=[baa[r3], bW2], W=[pb[tt * 2 + hf]])

            U5(0)
            for c in range(32):
                if c + 1 < 32:
                    U5(c + 1)
                O5(c)
            for tt in range(2):
                n = 2 * G + tt
                s4 = n % 4
                s2 = n % 2
                I_("dve", lambda e, tt=tt, s2=s2, s4=s4: e.tensor_tensor(
                    out=ot[s2], in0=psum[:, 2 * tt:2 * tt + 2, :].rearrange("p a b -> p (a b)"), in1=xt5[s4], op=ALU.add),
                   R=[pb[2 * tt], pb[2 * tt + 1], bxt5[s4]], W=[bot[s2]])
                P.dma("pool", out_d[n * 128:(n + 1) * 128, :], ot[s2], R=[bot[s2]], grp="ot%d" % s2)
        P.barrier()
        A.reset()

    P.barrier()
    P.emit()
    return nc, P


def _host_layout(inputs, b):
    def kmaj(w):
        k = w.shape[0] // 128
        return np.ascontiguousarray(w.reshape(k, 128, w.shape[1]).transpose(1, 0, 2))
    perm = _win_perm()
    m = {
        "x": np.ascontiguousarray(inputs["x"][b]),
        "mem": np.ascontiguousarray(inputs["mem"][b]),
        "pos": np.ascontiguousarray(inputs["positions"][b].reshape(NT, 128).T),
        "w_in": kmaj(inputs["w_in"][0][:, perm]),
        "g_mix": np.ascontiguousarray(inputs["g_mix"][0].reshape(8, 128).T),
        "g_mem": np.ascontiguousarray(inputs["g_mem"][0].reshape(8, 128).T),
        "g_mlp": np.ascontiguousarray(inputs["g_mlp"][0].reshape(8, 128).T),
        "gains": np.concatenate([inputs[k][0] for k in ("g_qa", "g_ka", "g_qb", "g_kb", "g_qc", "g_kc")])[None, :],
        "w_mem_kv": kmaj(inputs["w_mem_kv"][0]),
        "w_a": kmaj(inputs["w_a"][0]),
        "w_b": kmaj(inputs["w_b"][0]),
        "w_c": kmaj(inputs["w_c"][0]),
        "w_o": kmaj(inputs["w_o"][0]),
        "w_1": kmaj(inputs["w_1"][0]),
        "w_2": kmaj(inputs["w_2"][0]),
    }
    return {k: np.ascontiguousarray(v) for k, v in m.items()}


def kernel(**inputs):
    inputs = {k: np.asarray(v) for k, v in inputs.items()}
    nc, _ = build_program()
    in_maps = [_host_layout(inputs, b) for b in range(8)]
    res = run_bass_kernel_spmd(nc, in_maps, core_ids=list(range(8)))
    return np.stack([np.asarray(r["out"]) for r in res.results], axis=0).astype(np.float32)
```

```python
import bisect
import numpy as np
import concourse.bass as bass
import concourse.mybir as mybir
from concourse.bass_utils import run_bass_kernel_spmd

F32 = mybir.dt.float32
BF16 = mybir.dt.bfloat16
I32 = mybir.dt.int32
ALU = mybir.AluOpType
AF = mybir.ActivationFunctionType
AX = mybir.AxisListType

ENGS = ("pe", "act", "dve", "pool", "sp")

S = 8192
D = 1024
NT = S // 128
EPS = 1e-6
NEG = -30000.0
TOPK = 256
NBIS = 16


class Buf:
    __slots__ = ("name", "w", "r", "excl")

    def __init__(self, name="", excl=False):
        self.name = name
        self.w = None
        self.r = []
        self.excl = excl


class Prog:
    def __init__(self, nc):
        self.nc = nc
        self.ins = []
        self.by_eng = {e: [] for e in ENGS}
        self.groups = {}

    def _deps(self, R, W):
        deps = set()
        for b in R:
            if b.w is not None:
                deps.add(b.w)
        for b in W:
            if b.w is not None:
                deps.add(b.w)
            deps.update(b.r)
        return deps

    def _commit(self, iid, R, W):
        for b in R:
            b.r.append(iid)
        for b in W:
            b.w = iid
            b.r = []

    def I(self, eng, fn, R=(), W=()):
        iid = len(self.ins)
        W = list(W) + [b for b in R if b.excl and b not in W]
        deps = self._deps(R, W)
        raw = set(b.w for b in R if b.w is not None)
        self.ins.append(dict(eng=eng, fn=fn, deps=deps, raw=raw, grp=None))
        self.by_eng[eng].append(iid)
        self._commit(iid, R, W)
        return iid

    def dma(self, eng, out, in_, R=(), W=(), grp="g", **kw):
        iid = len(self.ins)
        deps = self._deps(R, W)
        raw = set(b.w for b in R if b.w is not None)
        self.groups.setdefault(grp, []).append(iid)
        self.ins.append(dict(eng=eng, fn=(lambda e, o=out, i=in_, k=kw: e.dma_start(out=o, in_=i, **k)),
                             deps=deps, raw=raw, grp=grp))
        self.by_eng[eng].append(iid)
        self._commit(iid, R, W)
        return iid

    def barrier(self):
        alld = set()
        for e in ENGS:
            for k in reversed(self.by_eng[e]):
                if self.ins[k]["fn"] is not None:
                    alld.add(k)
                    break
        for g, l in self.groups.items():
            if l:
                alld.add(l[-1])
        for e in ENGS:
            iid = len(self.ins)
            self.ins.append(dict(eng=e, fn=None, deps=set(alld), raw=set(alld), grp=None))
            self.by_eng[e].append(iid)

    def emit(self):
        nc = self.nc
        ins = self.ins
        n = len(ins)
        is_target = [False] * n
        for k in range(n):
            for d in ins[k]["deps"]:
                is_target[d] = True
        ms_val = [0] * n
        cnt = {e: 0 for e in ENGS}
        for k in range(n):
            it = ins[k]
            if it["grp"] is None and it["fn"] is not None and is_target[k]:
                cnt[it["eng"]] += 1
                ms_val[k] = cnt[it["eng"]]
        self.esem = {e: nc.alloc_semaphore("sem_" + e) for e in ENGS}
        self.gsem = {g: nc.alloc_semaphore("dsem_" + g) for g in self.groups}
        self.nwaits = 0
        prog = self

        def run_engine(ename, eobj):
            seen = {}
            for k in prog.by_eng[ename]:
                it = ins[k]
                need = {}
                for d in it["deps"]:
                    dd = ins[d]
                    if dd["grp"] is not None:
                        g = dd["grp"]
                        c = bisect.bisect_left(prog.groups[g], k)
                        key = ("g", g)
                        val = 16 * c
                    else:
                        if dd["fn"] is None:
                            continue
                        if dd["eng"] == ename and it["grp"] is None and it["fn"] is not None:
                            if ename == "pe":
                                continue
                            if d not in it["raw"]:
                                continue
                        key = ("e", dd["eng"])
                        val = ms_val[d]
                    if val > need.get(key, 0):
                        need[key] = val
                for key, val in need.items():
                    if seen.get(key, 0) >= val:
                        continue
                    seen[key] = val
                    sem = prog.esem[key[1]] if key[0] == "e" else prog.gsem[key[1]]
                    eobj.wait_ge(sem, val)
                    prog.nwaits += 1
                if it["fn"] is None:
                    continue
                bi = it["fn"](eobj)
                if it["grp"] is not None:
                    bi.then_inc(prog.gsem[it["grp"]], 16)
                elif is_target[k]:
                    bi.then_inc(prog.esem[ename], 1)

        with nc.Block() as block:
            @block.tensor
            def _(e):
                run_engine("pe", e)

            @block.scalar
            def _(e):
                run_engine("act", e)

            @block.vector
            def _(e):
                run_engine("dve", e)

            @block.gpsimd
            def _(e):
                run_engine("pool", e)

            @block.sync
            def _(e):
                run_engine("sp", e)


def interleave_streams(streams):
    st = [[g, max(n, 1), 0, False] for g, n in streams]
    while True:
        live = [x for x in st if not x[3]]
        if not live:
            return
        x = min(live, key=lambda x: x[2] / x[1])
        try:
            next(x[0])
            x[2] += 1
        except StopIteration:
            x[3] = True


class Arena:
    def __init__(self, nc, nbytes):
        self.t = nc.alloc_sbuf_tensor("arena", [128, nbytes // 2], BF16)
        self.n = nbytes // 2
        self.base = 0
        self.p = 0

    def mark(self):
        self.base = self.p

    def reset(self):
        self.p = self.base

    def get(self, shape, dtype):
        ne = int(np.prod(shape))
        w = ne * (2 if dtype in (F32, I32) else 1)
        w = (w + 31) // 32 * 32
        assert self.p + w <= self.n, ("SBUF arena overflow", self.p, w, self.n)
        ap = self.t[:, self.p:self.p + w]
        self.p += w
        if dtype != BF16:
            ap = ap.bitcast(dtype)
        ap = ap[:, 0:ne]
        if len(shape) == 2:
            ap = ap.rearrange("p (a b) -> p a b", a=shape[0])
        elif len(shape) == 3:
            ap = ap.rearrange("p (a b c) -> p a b c", a=shape[0], b=shape[1])
        return ap


IN_COLS = 5704
N_TM = 2560
N_SM = 72
N_G = 3072


def _win_perm():
    o = {}
    acc = 0
    for nme, w in (("qa", 384), ("ka", 384), ("va", 384), ("qb", 384), ("kb", 128), ("vb", 128),
                   ("qi", 512), ("ki", 64), ("wi", 8), ("qc", 256), ("gl", 3072)):
        o[nme] = np.arange(acc, acc + w)
        acc += w
    qb = o["qb"].reshape(2, 3, 64).transpose(1, 0, 2).reshape(-1)
    return np.concatenate([o["qa"], o["ka"], qb, o["kb"], o["qi"], o["qc"], o["va"], o["vb"],
                           o["ki"], o["wi"], o["gl"]])


def build_program(debug=(), phases=(1, 2, 3, 4, 5), ntiles=NT):
    nc = bass.Bass("TRN2", target_bir_lowering=False)
    P = Prog(nc)
    I_ = P.I

    def din(name, shape, dt=F32):
        return nc.dram_tensor(name, shape, dt, kind="ExternalInput").ap()

    def dscr(name, shape, dt):
        return nc.dram_tensor(name, shape, dt, kind=("ExternalOutput" if name in debug else "Internal")).ap()

    x_d = din("x", [S, D])
    mem_d = din("mem", [256, D])
    pos_d = din("pos", [128, NT], I32)
    win_d = din("w_in", [128, 8, IN_COLS])
    gmix_d = din("g_mix", [128, 8])
    gmem_d = din("g_mem", [128, 8])
    gmlp_d = din("g_mlp", [128, 8])
    gains_d = din("gains", [1, 384])
    wkv_d = din("w_mem_kv", [128, 8, 512])
    wa_d = din("w_a", [128, 1, D])
    wb_d = din("w_b", [128, 3, D])
    wc_d = din("w_c", [128, 2, D])
    wo_d = din("w_o", [128, 8, D])
    w1_d = din("w_1", [128, 8, 4096])
    w2_d = din("w_2", [128, 32, D])
    out_d = nc.dram_tensor("out", [S, D], F32, kind="ExternalOutput").ap()

    QKVA = dscr("s_qkva", [S, 1152], BF16)
    QBT = dscr("s_qbt", [128, NT, 384], BF16)
    KBT = dscr("s_kbt", [128, S], BF16)
    VB = dscr("s_vb", [S, 136], BF16)
    QIT = dscr("s_qit", [128, NT, 512], BF16)
    KIT = dscr("s_kit", [64, S], BF16)
    QCT = dscr("s_qct", [128, NT, 256], BF16)
    GT = dscr("s_gt", [24, 128, S], BF16)
    OA = dscr("s_oa", [3, S, 130], F32)
    OBT = dscr("s_obt", [3, 128, S], BF16)
    X1 = dscr("s_x1", [S, D], F32)

    A = Arena(nc, 206 * 1024)
    psum = nc.alloc_psum_tensor("psum", [128, 8, 512], F32)

    def bank(i):
        return psum[:, i, :]

    def bank_bf(i):
        return psum[:, i, :].bitcast(BF16)

    pb = [Buf("bank%d" % i, excl=True) for i in range(8)]

    ident = A.get([128], BF16)
    identf = A.get([128], F32)
    mcur = A.get([128], BF16)
    mprev = A.get([128], BF16)
    i3 = A.get([384], BF16)
    gains = A.get([384], F32)
    cs = A.get([NT, 16], F32)
    b_const = Buf("const")
    I_("pool", lambda e: e.memset(identf, 0.0), W=[b_const])
    I_("pool", lambda e: e.affine_select(out=identf, in_=identf, pattern=[[-1, 128]], compare_op=ALU.not_equal,
                                         fill=1.0, base=0, channel_multiplier=1), R=[b_const], W=[b_const])
    I_("pool", lambda e: e.tensor_copy(out=ident, in_=identf), R=[b_const], W=[b_const])
    for g in range(3):
        I_("pool", lambda e, g=g: e.tensor_copy(out=i3[:, g * 128:(g + 1) * 128], in_=identf), R=[b_const], W=[b_const])
    I_("pool", lambda e: e.memset(mcur, 0.0), W=[b_const], R=[b_const])
    I_("pool", lambda e: e.affine_select(out=mcur, in_=mcur, pattern=[[1, 128]], compare_op=ALU.is_ge,
                                         fill=NEG, base=0, channel_multiplier=-1), R=[b_const], W=[b_const])
    I_("pool", lambda e: e.memset(mprev, 0.0), W=[b_const], R=[b_const])
    I_("pool", lambda e: e.affine_select(out=mprev, in_=mprev, pattern=[[-1, 128]], compare_op=ALU.is_ge,
                                         fill=NEG, base=0, channel_multiplier=1), R=[b_const], W=[b_const])
    P.dma("sp", gains, gains_d.partition_broadcast(128), W=[b_const], grp="c0")
    A.mark()
    posi = A.get([NT], I32)
    posf = A.get([NT], F32)
    inv = A.get([8], F32)
    ang = A.get([NT, 8], F32)
    ang2 = A.get([NT, 16], F32)
    P.dma("sp", posi, pos_d, W=[b_const], grp="c0")
    I_("dve", lambda e: e.tensor_copy(out=posf, in_=posi), R=[b_const], W=[b_const])
    for i in range(8):
        I_("pool", lambda e, i=i: e.memset(inv[:, i:i + 1], float(np.float32(500000.0) ** np.float32(-i / 8.0))),
           R=[b_const], W=[b_const])
    I_("dve", lambda e: e.tensor_tensor(out=ang, in0=posf.unsqueeze(2).to_broadcast([128, NT, 8]),
                                        in1=inv.unsqueeze(1).to_broadcast([128, NT, 8]), op=ALU.mult),
       R=[b_const], W=[b_const])
    TWO_PI = float(2 * np.pi)
    I_("dve", lambda e: e.tensor_scalar(out=ang2[:, :, 0:8], in0=ang, scalar1=float(0.5 * np.pi), scalar2=None,
                                        op0=ALU.add), R=[b_const], W=[b_const])
    I_("dve", lambda e: e.tensor_copy(out=ang2[:, :, 8:16], in_=ang), R=[b_const], W=[b_const])
    angk = A.get([NT, 16], F32)
    angi = A.get([NT, 16], I32)
    I_("dve", lambda e: e.tensor_scalar(out=angk, in0=ang2, scalar1=float(1.0 / (2 * np.pi)), scalar2=None,
                                        op0=ALU.mult), R=[b_const], W=[b_const])
    I_("dve", lambda e: e.tensor_copy(out=angi, in_=angk), R=[b_const], W=[b_const])
    I_("dve", lambda e: e.tensor_copy(out=angk, in_=angi), R=[b_const], W=[b_const])
    I_("dve", lambda e: e.scalar_tensor_tensor(out=ang2, in0=angk, scalar=-TWO_PI, in1=ang2, op0=ALU.mult,
                                               op1=ALU.add), R=[b_const], W=[b_const])
    I_("dve", lambda e: e.tensor_scalar(out=angk, in0=ang2, scalar1=float(np.pi), scalar2=TWO_PI, op0=ALU.is_gt,
                                        op1=ALU.mult), R=[b_const], W=[b_const])
    I_("dve", lambda e: e.tensor_tensor(out=ang2, in0=ang2, in1=angk, op=ALU.subtract), R=[b_const], W=[b_const])
    I_("dve", lambda e: e.tensor_scalar(out=angk, in0=ang2, scalar1=float(-np.pi), scalar2=TWO_PI, op0=ALU.is_lt,
                                        op1=ALU.mult), R=[b_const], W=[b_const])
    I_("dve", lambda e: e.tensor_tensor(out=ang2, in0=ang2, in1=angk, op=ALU.add), R=[b_const], W=[b_const])
    I_("dve", lambda e: e.tensor_scalar(out=ang2, in0=ang2, scalar1=3.141592, scalar2=-3.141592, op0=ALU.min,
                                        op1=ALU.max), R=[b_const], W=[b_const])
    I_("act", lambda e: e.activation(out=cs, in_=ang2, func=AF.Sin), R=[b_const], W=[b_const])
    P.barrier()
    A.reset()

    def rstd_from_ss(ss, rs, n, bufs, width):
        I_("dve", lambda e: e.tensor_scalar(out=rs, in0=ss, scalar1=1.0 / n, scalar2=EPS, op0=ALU.mult, op1=ALU.add),
           R=bufs, W=bufs)
        I_("act", lambda e: e.activation(out=rs, in_=rs, func=AF.Sqrt), R=bufs, W=bufs)
        I_("dve", lambda e: e.reciprocal(out=rs, in_=rs), R=bufs, W=bufs)

    def make_stage(chunk, tag):
        return ([A.get([chunk], F32) for _ in range(2)], [Buf(tag + "stg%d" % i) for i in range(2)], tag)

    def load_weight_bf16(dst, src_d, nk, ncols, gain_d, tag, chunk=2048, stage=None):
        if stage is None:
            stage = make_stage(chunk, tag)
        stg, sb, stag = stage
        bw = Buf(tag)
        gt = None
        if gain_d is not None:
            gt = A.get([8], F32)
            P.dma("sp", gt, gain_d, W=[bw], grp=tag + "g")
        it = 0
        for k in range(nk):
            for c0 in range(0, ncols, chunk):
                cw = min(chunk, ncols - c0)
                s = it % 2
                P.dma("sp", stg[s][:, 0:cw], src_d[:, k, c0:c0 + cw], W=[sb[s]], grp=stag + "s%d" % s)
                eng = ("dve", "pool")[it % 2]
                if gt is not None:
                    I_(eng, lambda e, s=s, k=k, c0=c0, cw=cw: e.tensor_scalar(
                        out=dst[:, k, c0:c0 + cw], in0=stg[s][:, 0:cw], scalar1=gt[:, k:k + 1], scalar2=None,
                        op0=ALU.mult), R=[sb[s], bw], W=[bw])
                else:
                    I_(eng, lambda e, s=s, k=k, c0=c0, cw=cw: e.tensor_copy(out=dst[:, k, c0:c0 + cw],
                                                                          in_=stg[s][:, 0:cw]), R=[sb[s]], W=[bw])
                it += 1
        return bw

    def norm_rows_to_hT(src_tile_d, xt, xb, hT_dst, bx, bh, bxs, grp, pbank, bpb):
        P.dma("sp", xt, src_tile_d, W=[bx], grp=grp)
        st = bxs["st"]
        I_("pool", lambda e: e.memset(st[:, 0:1], 0.0), W=[bxs["b"]])
        I_("act", lambda e: e.activation(out=bxs["sq"], in_=xt, func=AF.Square, accum_out=st[:, 0:1]),
           R=[bx, bxs["b"]], W=[bxs["b"]])
        rstd_from_ss(st[:, 0:1], st[:, 1:2], D, [bxs["b"]], 1)
        I_("dve", lambda e: e.tensor_scalar(out=xb, in0=xt, scalar1=st[:, 1:2], scalar2=None, op0=ALU.mult),
           R=[bx, bxs["b"]], W=[bh])
        pT = bank_bf(pbank)
        for k in range(8):
            I_("pe", lambda e, k=k: e.transpose(out=pT[:, k * 128:(k + 1) * 128], in_=xb[:, k * 128:(k + 1) * 128],
                                                identity=ident), R=[bh, b_const], W=[bpb])

    if 1 in phases:
        W = A.get([8, IN_COLS], BF16)
        p1_pos0 = A.p
        bW = load_weight_bf16(W, win_d, 8, IN_COLS, gmix_d, "win", chunk=1426)
        P.barrier()
        A.p = p1_pos0
        NB = 2
        xt = [A.get([D], F32) for _ in range(NB)]
        bx = [Buf("xt%d" % i) for i in range(NB)]
        xb = [A.get([D], BF16) for _ in range(NB)]
        bh = [Buf("xb%d" % i) for i in range(NB)]
        sq = A.get([D], F32)
        st = A.get([8], F32)
        bxs = dict(sq=sq, st=st, b=Buf("xstat"))
        hTg = [A.get([8, 512], BF16) for _ in range(2)]
        bhT = [Buf("hTg%d" % i) for i in range(2)]
        stg2 = [A.get([2632], F32) for _ in range(2)]
        bstg2 = [Buf("stg%d" % i) for i in range(2)]
        tmp2 = [A.get([1536], F32) for _ in range(2)]
        btmp2 = [Buf("tmp%d" % i) for i in range(2)]
        ss2 = [A.get([24], F32) for _ in range(2)]
        rs2 = [A.get([24], F32) for _ in range(2)]
        bss2 = [Buf("ss%d" % i) for i in range(2)]
        gainrow = A.get([1536], F32)
        bgr = Buf("gainrow")
        gsrc = [0] * 6 + [1] * 6 + [2] * 6 + [3] * 2 + [4] * 4
        for hh, gi in enumerate(gsrc):
            I_("pool", lambda e, hh=hh, gi=gi: e.tensor_copy(out=gainrow[:, hh * 64:(hh + 1) * 64],
                                                            in_=gains[:, gi * 64:(gi + 1) * 64]),
               R=[b_const], W=[bgr])
        rt2 = [A.get([4, 29, 8], F32) for _ in range(2)]
        brt2 = [Buf("rt%d" % i) for i in range(2)]
        oA = [A.get([1152], BF16) for _ in range(2)]
        boA = [Buf("oA%d" % i) for i in range(2)]
        oT = [A.get([1344], BF16) for _ in range(2)]
        boT = [Buf("oT%d" % i) for i in range(2)]
        oV = [A.get([136], BF16) for _ in range(2)]
        boV = [Buf("oV%d" % i) for i in range(2)]
        tT = [A.get([1408], BF16) for _ in range(2)]
        btT = [Buf("tT%d" % i) for i in range(2)]
        gsb = [A.get([512], BF16) for _ in range(3)]
        bgsb = [Buf("gsb%d" % i) for i in range(3)]
        WI_SCALE = float((8 ** -0.5) * (64 ** -0.5))
        gcount = [0]

        def p1_A(n):
            grp_i, tt = n // 4, n % 4
            hs = grp_i % 2
            s = n % NB
            o = n % 2
            stg = stg2[o]
            bstg = bstg2[o]
            norm_rows_to_hT(x_d[n * 128:(n + 1) * 128, :], xt[s], xb[s], None, bx[s], bh[s], bxs, "x%d" % s, 5, pb[5])
            I_("act", lambda e: e.copy(out=hTg[hs][:, :, tt * 128:(tt + 1) * 128],
                                       in_=bank_bf(5).rearrange("p (k c) -> p k c", k=8)), R=[pb[5]], W=[bhT[hs]])
            yield
            for bnk in range(5):
                for k in range(8):
                    I_("pe", lambda e, bnk=bnk, k=k: e.matmul(
                        bank(bnk), lhsT=hTg[hs][:, k, tt * 128:(tt + 1) * 128],
                        rhs=W[:, k, bnk * 512:(bnk + 1) * 512], start=(k == 0), stop=(k == 7)),
                       R=[bhT[hs], bW], W=[pb[bnk]])
                yield
            for k in range(8):
                I_("pe", lambda e, k=k: e.matmul(
                    bank(7)[:, 0:N_SM], lhsT=hTg[hs][:, k, tt * 128:(tt + 1) * 128],
                    rhs=W[:, k, N_TM:N_TM + N_SM], start=(k == 0), stop=(k == 7)),
                   R=[bhT[hs], bW], W=[pb[7]])
            I_("act", lambda e: e.copy(out=stg[:, 0:1792], in_=psum[:, 0:4, :].rearrange("p a b -> p (a b)")[:, 0:1792]),
               R=[pb[0], pb[1], pb[2], pb[3]], W=[bstg])
            I_("dve", lambda e: e.tensor_copy(out=stg[:, 1856:2624],
                                              in_=psum[:, 3:5, :].rearrange("p a b -> p (a b)")[:, 256:1024]),
               R=[pb[3], pb[4]], W=[bstg])
            I_("dve", lambda e: e.tensor_copy(out=stg[:, 1792:1856], in_=bank(7)[:, 0:64]), R=[pb[7]], W=[bstg])
            I_("dve", lambda e: e.tensor_scalar(out=stg[:, 2624:2632], in0=bank(7)[:, 64:72], scalar1=WI_SCALE,
                                                scalar2=None, op0=ALU.mult), R=[pb[7]], W=[bstg])
            yield

        def p1_B(n):
            o = n % 2
            stg, bstg = stg2[o], bstg2[o]
            tmp, btmp = tmp2[o], btmp2[o]
            ss, rs, bss = ss2[o], rs2[o], bss2[o]
            rt, brt = rt2[o], brt2[o]
            I_("pool", lambda e: e.tensor_tensor(out=tmp[:, 0:1280], in0=stg[:, 0:1280], in1=stg[:, 0:1280],
                                                 op=ALU.mult), R=[bstg], W=[btmp])
            I_("pool", lambda e: e.tensor_tensor(out=tmp[:, 1280:1536], in0=stg[:, 1856:2112],
                                                 in1=stg[:, 1856:2112], op=ALU.mult), R=[bstg], W=[btmp])
            yield
            I_("dve", lambda e: e.tensor_reduce(out=ss, in_=tmp.rearrange("p (h d) -> p h d", d=64), axis=AX.X,
                                                op=ALU.add), R=[btmp], W=[bss])
            yield
            rstd_from_ss(ss, rs, 64, [bss], 24)
            yield
            I_("dve", lambda e: e.tensor_tensor(
                out=tmp.rearrange("p (h d) -> p h d", d=64), in0=gainrow.rearrange("p (h d) -> p h d", d=64),
                in1=rs.unsqueeze(2).to_broadcast([128, 24, 64]), op=ALU.mult), R=[bss, bgr, btmp], W=[btmp])
            yield
            I_("pool", lambda e: e.tensor_tensor(out=stg[:, 0:1280], in0=stg[:, 0:1280], in1=tmp[:, 0:1280],
                                                 op=ALU.mult), R=[bstg, btmp], W=[bstg])
            I_("pool", lambda e: e.tensor_tensor(out=stg[:, 1856:2112], in0=stg[:, 1856:2112],
                                                 in1=tmp[:, 1280:1536], op=ALU.mult), R=[bstg, btmp], W=[bstg])
            yield
            v = stg[:, 0:1856].rearrange("p (h d) -> p h d", d=64)
            x1 = v[:, :, 0:8]
            x2 = v[:, :, 8:16]
            cosb = cs[:, n, 0:8].unsqueeze(1).to_broadcast([128, 29, 8])
            sinb = cs[:, n, 8:16].unsqueeze(1).to_broadcast([128, 29, 8])
            I_("dve", lambda e: e.tensor_tensor(out=rt[:, 0], in0=x1, in1=cosb, op=ALU.mult), R=[bstg, b_const], W=[brt])
            I_("dve", lambda e: e.tensor_tensor(out=rt[:, 1], in0=x2, in1=sinb, op=ALU.mult), R=[bstg, b_const], W=[brt])
            I_("pool", lambda e: e.tensor_tensor(out=rt[:, 2], in0=x2, in1=cosb, op=ALU.mult), R=[bstg, b_const], W=[brt])
            I_("pool", lambda e: e.tensor_tensor(out=rt[:, 3], in0=x1, in1=sinb, op=ALU.mult), R=[bstg, b_const], W=[brt])
            yield
            I_("dve", lambda e: e.tensor_tensor(out=x1, in0=rt[:, 0], in1=rt[:, 1], op=ALU.subtract), R=[brt], W=[bstg])
            I_("dve", lambda e: e.tensor_tensor(out=x2, in0=rt[:, 2], in1=rt[:, 3], op=ALU.add), R=[brt], W=[bstg])
            yield
            I_("act", lambda e: e.copy(out=oA[o][:, 0:768], in_=stg[:, 0:768]), R=[bstg], W=[boA[o]])
            I_("act", lambda e: e.copy(out=oA[o][:, 768:1152], in_=stg[:, 2112:2496]), R=[bstg], W=[boA[o]])
            I_("pool", lambda e: e.tensor_copy(out=oT[o], in_=stg[:, 768:2112]), R=[bstg], W=[boT[o]])
            I_("pool", lambda e: e.tensor_copy(out=oV[o], in_=stg[:, 2496:2632]), R=[bstg], W=[boV[o]])
            P.dma("pool", QKVA[n * 128:(n + 1) * 128, :], oA[o], R=[boA[o]], grp="oA%d" % o)
            P.dma("pool", VB[n * 128:(n + 1) * 128, :], oV[o], R=[boV[o]], grp="oV%d" % o)
            yield
            pT6 = bank_bf(6)
            for c in range(8):
                I_("pe", lambda e, c=c: e.transpose(out=pT6[:, c * 128:(c + 1) * 128],
                                                    in_=oT[o][:, c * 128:(c + 1) * 128], identity=ident),
                   R=[boT[o], b_const], W=[pb[6]])
            I_("dve", lambda e: e.tensor_copy(out=tT[o][:, 0:1024], in_=pT6), R=[pb[6]], W=[btT[o]])
            yield
            I_("pe", lambda e: e.transpose(out=pT6[0:64, 0:128], in_=oT[o][:, 1024:1088], identity=ident),
               R=[boT[o], b_const], W=[pb[6]])
            for c in range(2):
                I_("pe", lambda e, c=c: e.transpose(out=pT6[:, 128 + c * 128:256 + c * 128],
                                                    in_=oT[o][:, 1088 + c * 128:1216 + c * 128], identity=ident),
                   R=[boT[o], b_const], W=[pb[6]])
            I_("dve", lambda e: e.tensor_copy(out=tT[o][0:64, 1024:1152], in_=pT6[0:64, 0:128]), R=[pb[6]], W=[btT[o]])
            I_("dve", lambda e: e.tensor_copy(out=tT[o][:, 1152:1408], in_=pT6[:, 128:384]), R=[pb[6]], W=[btT[o]])
            g = "tT%d" % o
            P.dma("pool", QBT[:, n, :], tT[o][:, 0:384], R=[btT[o]], grp=g)
            P.dma("pool", KBT[:, n * 128:(n + 1) * 128], tT[o][:, 384:512], R=[btT[o]], grp=g)
            P.dma("pool", QIT[:, n, :], tT[o][:, 512:1024], R=[btT[o]], grp=g)
            P.dma("pool", KIT[:, n * 128:(n + 1) * 128], tT[o][0:64, 1024:1152], R=[btT[o]], grp=g)
            P.dma("pool", QCT[:, n, :], tT[o][:, 1152:1408], R=[btT[o]], grp=g)
            yield

        def p1_G(grp_i):
            hs = grp_i % 2
            for c in range(24):
                for k in range(8):
                    I_("pe", lambda e, c=c, k=k: e.matmul(
                        bank(7), lhsT=W[:, k, N_TM + N_SM + c * 128:N_TM + N_SM + (c + 1) * 128],
                        rhs=hTg[hs][:, k, :], start=(k == 0), stop=(k == 7)), R=[bhT[hs], bW], W=[pb[7]])
                gs = gcount[0] % 3
                gcount[0] += 1
                I_("act", lambda e, gs=gs: e.activation(out=gsb[gs], in_=bank(7), func=AF.Sigmoid),
                   R=[pb[7]], W=[bgsb[gs]])
                P.dma("sp", GT[c, :, grp_i * 512:(grp_i + 1) * 512], gsb[gs], R=[bgsb[gs]], grp="gsb%d" % gs)
                yield

        for _ in p1_A(0):
            pass
        for n in range(ntiles):
            streams = [(p1_B(n), 11)]
            if n + 1 < ntiles:
                streams.append((p1_A(n + 1), 7))
            if n % 4 == 3:
                streams.append((p1_G(n // 4), 24))
            interleave_streams(streams)
        P.barrier()
        A.reset()


    ntok = ntiles * 128
    if 2 in phases:
        blk = [A.get([3, 128], BF16) for _ in range(3)]
        bblk = [Buf("blk%d" % i) for i in range(3)]
        qT = [A.get([128], BF16) for _ in range(2)]
        bqT = [Buf("qT%d" % i) for i in range(2)]
        kT = [A.get([128], BF16) for _ in range(3)]
        bkT = [Buf("kT%d" % i) for i in range(3)]
        va_ = [A.get([2, 65], BF16) for _ in range(3)]
        bva = [Buf("va%d" % i) for i in range(3)]
        PT = [A.get([512], BF16) for _ in range(2)]
        bPT = [Buf("PT%d" % i) for i in range(2)]
        oas = [A.get([130], F32) for _ in range(2)]
        boas = [Buf("oas%d" % i) for i in range(2)]
        for i in range(3):
            I_("pool", lambda e, i=i: e.memset(va_[i], 1.0), W=[bva[i]])
        u = 0
        for g, dil in enumerate((1, 4, 16)):
            m = ntok // dil
            nblk = m // 128
            assert nblk >= 1
            for r in range(dil):
                for n in range(nblk):
                    sb3 = u % 3
                    sb2 = u % 2
                    prev3 = (u - 1) % 3
                    start = r + dil * 128 * n
                    rows = QKVA[start:start + dil * 127 + 1:dil, :].rearrange("t (s c) -> t s c", s=3)[:, :, g * 128:(g + 1) * 128]
                    P.dma("sp", blk[sb3], rows, W=[bblk[sb3]], grp="blk%d" % sb3)
                    pT = bank_bf(sb2)
                    I_("pe", lambda e, pT=pT, sb3=sb3: e.transpose(out=pT[:, 0:128], in_=blk[sb3][:, 0, :], identity=ident),
                       R=[bblk[sb3], b_const], W=[pb[sb2]])
                    I_("pe", lambda e, pT=pT, sb3=sb3: e.transpose(out=pT[:, 128:256], in_=blk[sb3][:, 1, :], identity=ident),
                       R=[bblk[sb3], b_const], W=[pb[sb2]])
                    I_("dve", lambda e, pT=pT, sb2=sb2: e.tensor_copy(out=qT[sb2], in_=pT[:, 0:128]), R=[pb[sb2]], W=[bqT[sb2]])
                    I_("dve", lambda e, pT=pT, sb3=sb3: e.tensor_copy(out=kT[sb3], in_=pT[:, 128:256]), R=[pb[sb2]], W=[bkT[sb3]])
                    I_("pool", lambda e, sb3=sb3: e.tensor_copy(out=va_[sb3][:, :, 0:64],
                                                                in_=blk[sb3][:, 2, :].rearrange("p (j d) -> p j d", j=2)),
                       R=[bblk[sb3]], W=[bva[sb3]])
                    psY = bank(2 + sb2)
                    has_prev = n > 0
                    for j in range(2):
                        if has_prev:
                            I_("pe", lambda e, j=j, psY=psY, prev3=prev3, sb2=sb2: e.matmul(
                                psY[:, j * 128:(j + 1) * 128], lhsT=kT[prev3][64 * j:64 * j + 64, :],
                                rhs=qT[sb2][64 * j:64 * j + 64, :], start=True, stop=False),
                               R=[bkT[prev3], bqT[sb2]], W=[pb[2 + sb2]])
                            I_("pe", lambda e, j=j, psY=psY: e.matmul(
                                psY[:, j * 128:(j + 1) * 128], lhsT=ident, rhs=mprev, start=False, stop=True),
                               R=[b_const], W=[pb[2 + sb2]])
                        I_("pe", lambda e, j=j, psY=psY, sb3=sb3, sb2=sb2: e.matmul(
                            psY[:, 256 + j * 128:256 + (j + 1) * 128], lhsT=kT[sb3][64 * j:64 * j + 64, :],
                            rhs=qT[sb2][64 * j:64 * j + 64, :], start=True, stop=False),
                           R=[bkT[sb3], bqT[sb2]], W=[pb[2 + sb2]])
                        I_("pe", lambda e, j=j, psY=psY: e.matmul(
                            psY[:, 256 + j * 128:256 + (j + 1) * 128], lhsT=ident, rhs=mcur, start=False, stop=True),
                           R=[b_const], W=[pb[2 + sb2]])
                    lo = 0 if has_prev else 256
                    I_("act", lambda e, psY=psY, sb2=sb2, lo=lo: e.activation(out=PT[sb2][:, lo:512], in_=psY[:, lo:512],
                                                                             func=AF.Exp, scale=0.125),
                       R=[pb[2 + sb2]], W=[bPT[sb2]])
                    psZ = bank(4 + sb2)
                    for j in range(2):
                        if has_prev:
                            I_("pe", lambda e, j=j, psZ=psZ, sb2=sb2, prev3=prev3: e.matmul(
                                psZ[:, j * 65:(j + 1) * 65], lhsT=PT[sb2][:, j * 128:(j + 1) * 128],
                                rhs=va_[prev3][:, j, :], start=True, stop=False),
                               R=[bPT[sb2], bva[prev3]], W=[pb[4 + sb2]])
                        I_("pe", lambda e, j=j, psZ=psZ, sb2=sb2, sb3=sb3, hp=has_prev: e.matmul(
                            psZ[:, j * 65:(j + 1) * 65], lhsT=PT[sb2][:, 256 + j * 128:256 + (j + 1) * 128],
                            rhs=va_[sb3][:, j, :], start=(not hp), stop=True),
                           R=[bPT[sb2], bva[sb3]], W=[pb[4 + sb2]])
                    I_("dve", lambda e, psZ=psZ, sb2=sb2: e.tensor_copy(out=oas[sb2], in_=psZ[:, 0:130]),
                       R=[pb[4 + sb2]], W=[boas[sb2]])
                    P.dma("pool", OA[g, start:start + dil * 127 + 1:dil, :], oas[sb2], R=[boas[sb2]], grp="oas%d" % sb2)
                    u += 1
        P.barrier()
        A.reset()

    if 3 in phases:
        U8 = mybir.dt.uint8
        kiT2 = A.get([S], BF16)
        kbT = A.get([S], BF16)
        vba = A.get([NT, 2, 65], BF16)
        wia = A.get([NT, 8], BF16)
        bK = Buf("dsaK")
        P.dma("sp", kiT2[0:64, 0:ntok], KIT[:, 0:ntok], W=[bK], grp="dk")
        P.dma("sp", kiT2[64:128, 0:ntok], KIT[:, 0:ntok], W=[bK], grp="dk")
        P.dma("sp", kbT[:, 0:ntok], KBT[:, 0:ntok], W=[bK], grp="dk")
        I_("pool", lambda e: e.memset(vba, 1.0), W=[bK])
        vst = [A.get([136], BF16) for _ in range(2)]
        bvst = [Buf("vst%d" % i) for i in range(2)]
        for n in range(ntiles):
            s2 = n % 2
            P.dma("sp", vst[s2], VB[n * 128:(n + 1) * 128, :], W=[bvst[s2]], grp="vst%d" % s2)
            I_("pool", lambda e, n=n, s2=s2: e.tensor_copy(out=vba[:, n, :, 0:64],
                                                          in_=vst[s2][:, 0:128].rearrange("p (c d) -> p c d", c=2)),
               R=[bvst[s2], bK], W=[bK])
            I_("pool", lambda e, n=n, s2=s2: e.tensor_copy(out=wia[:, n, :], in_=vst[s2][:, 128:136]),
               R=[bvst[s2], bK], W=[bK])
        NI = 3
        qiT = [A.get([4, 128], BF16) for _ in range(2)]
        bqiT = [Buf("qiT%d" % i) for i in range(2)]
        qbT = [A.get([384], BF16) for _ in range(3)]
        bqbT = [Buf("qbT%d" % i) for i in range(3)]
        Dh = [A.get([8, 128], BF16) for _ in range(2)]
        bDh = [Buf("Dh%d" % i) for i in range(2)]
        isc = [A.get([S], F32) for _ in range(NI)]
        bisc = [Buf("isc%d" % i) for i in range(NI)]
        junk_t = A.get([S // 2], BF16)
        junk = junk_t.bitcast(U8)
        bjunk = Buf("junk")
        MBF = A.get([S], BF16)
        bMBF = [Buf("MBF%d" % c) for c in range(S // 512)]
        Rb = [A.get([2, 512], BF16) for _ in range(3)]
        bRb = [Buf("Rb%d" % i) for i in range(3)]
        PTb = [A.get([384], BF16) for _ in range(3)]
        bPTb = [Buf("PTb%d" % i) for i in range(3)]
        bst = [A.get([8], F32) for _ in range(NI)]
        bbst = [Buf("bst%d" % i) for i in range(NI)]
        Wk = [A.get([NBIS + 1], F32) for _ in range(NI)]
        pow2 = A.get([NBIS + 1], F32)
        cntb = A.get([2], F32)
        bcnt = Buf("cnt")
        cntp = A.get([2], F32)
        bcntp = Buf("cntp")
        bjunkp = Buf("junkp")
        rden = A.get([6], F32)
        brden = Buf("rden")
        ob = [A.get([384], BF16) for _ in range(2)]
        bob = [Buf("ob%d" % i) for i in range(2)]
        obT = [A.get([384], BF16) for _ in range(2)]
        bobT = [Buf("obT%d" % i) for i in range(2)]
        bpow = Buf("pow2")
        for k in range(NBIS + 1):
            I_("pool", lambda e, k=k: e.memset(pow2[:, k:k + 1], float(2.0 ** -(k + 1))), W=[bpow], R=[bpow])
        pair = [psum[:, 0:2, :], psum[:, 2:4, :]]
        bpair = [Buf("pair0", excl=True), Buf("pair1", excl=True)]
        accI = bank(4)
        sbank = [bank(5), bank(6)]
        accO = bank(7)
        cnt_pair = [0]
        cnt_rb = [0]
        cnt_unit = [0]
        _fr = []

        def fill_reg(e):
            if not _fr:
                _fr.append(e.to_reg(-1e30))
            return _fr[0]

        def dsa_index(i):
            s2 = i % 2
            s3 = i % NI
            L = 128 * (i + 1)
            P.dma("sp", qiT[s2], QIT[:, i, :].rearrange("p (j t) -> p j t", j=4), W=[bqiT[s2]], grp="qiT%d" % s2)
            P.dma("sp", qbT[s3], QBT[:, i, :], W=[bqbT[s3]], grp="qbT%d" % s3)
            I_("pool", lambda e: e.tensor_tensor(out=Dh[s2], in0=identf.unsqueeze(1).to_broadcast([128, 8, 128]),
                                                 in1=wia[:, i, :].unsqueeze(2).to_broadcast([128, 8, 128]), op=ALU.mult),
               R=[bK, b_const], W=[bDh[s2]])
            steps = [(c0, min(512, L - c0), jj) for c0 in range(0, L, 512) for jj in range(4)]
            slots = {}

            def QK(k):
                c0, cw, jj = steps[k]
                pp = cnt_pair[0] % 2
                cnt_pair[0] += 1
                rb = cnt_rb[0] % 3
                cnt_rb[0] += 1
                slots[k] = (pp, rb)
                for hh in range(2):
                    I_("pe", lambda e, pp=pp, hh=hh, jj=jj, c0=c0, cw=cw: e.matmul(
                        pair[pp][:, hh, 0:cw], lhsT=qiT[s2][64 * hh:64 * hh + 64, jj, :],
                        rhs=kiT2[64 * hh:64 * hh + 64, c0:c0 + cw], start=True, stop=True),
                       R=[bqiT[s2], bK], W=[bpair[pp]])
                I_("act", lambda e, pp=pp, rb=rb, cw=cw: e.activation(out=Rb[rb][:, :, 0:cw], in_=pair[pp][:, :, 0:cw],
                                                                     func=AF.Relu), R=[bpair[pp]], W=[bRb[rb]])

            def HS(k):
                c0, cw, jj = steps[k]
                pp, rb = slots.pop(k)
                for hh in range(2):
                    I_("pe", lambda e, rb=rb, hh=hh, jj=jj, cw=cw: e.matmul(
                        accI[:, 0:cw], lhsT=Dh[s2][:, 2 * jj + hh, :], rhs=Rb[rb][:, hh, 0:cw],
                        start=(jj == 0 and hh == 0), stop=(jj == 3 and hh == 1)),
                       R=[bDh[s2], bRb[rb]], W=[pb[4]])
                if jj == 3:
                    I_("act", lambda e, c0=c0, cw=cw: e.copy(out=isc[s3][:, c0:c0 + cw], in_=accI[:, 0:cw]),
                       R=[pb[4]], W=[bisc[s3]])

            QK(0)
            for k in range(len(steps)):
                if k + 1 < len(steps):
                    QK(k + 1)
                HS(k)
                yield
            I_("pool", lambda e: e.affine_select(out=isc[s3][:, 128 * i:128 * (i + 1)], in_=isc[s3][:, 128 * i:128 * (i + 1)],
                                                 pattern=[[-1, 128]], compare_op=ALU.is_ge, fill=fill_reg(e), base=0,
                                                 channel_multiplier=1), R=[bisc[s3]], W=[bisc[s3]])
            yield

        def dsa_bisect(i):
            s3 = i % NI
            L = 128 * (i + 1)
            b = bst[s3]
            bb = bbst[s3]
            if i < 2:
                I_("dve", lambda e: e.memset(b[:, 3:4], -1e29), W=[bb])
                return
            I_("dve", lambda e: e.tensor_reduce(out=b[:, 0:1], in_=isc[s3][:, 0:L], axis=AX.X, op=ALU.max),
               R=[bisc[s3]], W=[bb])
            I_("dve", lambda e: e.tensor_reduce(out=b[:, 1:2], in_=isc[s3][:, 0:TOPK], axis=AX.X, op=ALU.min),
               R=[bisc[s3]], W=[bb])
            I_("dve", lambda e: e.tensor_tensor(out=b[:, 2:3], in0=b[:, 0:1], in1=b[:, 1:2], op=ALU.subtract),
               R=[bb], W=[bb])
            I_("dve", lambda e: e.tensor_scalar(out=Wk[s3], in0=pow2, scalar1=b[:, 2:3], scalar2=None, op0=ALU.mult),
               R=[bb, bpow], W=[bb])
            I_("dve", lambda e: e.tensor_tensor(out=b[:, 4:5], in0=b[:, 1:2], in1=Wk[s3][:, 0:1], op=ALU.add),
               R=[bb], W=[bb])
            for k in range(NBIS):
                I_("dve", lambda e: e.tensor_scalar(out=junk[:, 0:L], in0=isc[s3][:, 0:L], scalar1=b[:, 4:5], scalar2=None,
                                                    op0=ALU.is_ge, op1=ALU.add, accum_out=cntb[:, 0:1]),
                   R=[bisc[s3], bb], W=[bjunk, bcnt])
                I_("dve", lambda e: e.tensor_scalar(out=cntb[:, 1:2], in0=cntb[:, 0:1], scalar1=TOPK - 0.5, scalar2=0.5,
                                                    op0=ALU.is_ge, op1=ALU.subtract), R=[bcnt], W=[bcnt])
                I_("dve", lambda e, k=k: e.scalar_tensor_tensor(out=b[:, 4:5], in0=cntb[:, 1:2], scalar=Wk[s3][:, k:k + 1],
                                                               in1=b[:, 4:5], op0=ALU.mult, op1=ALU.add),
                   R=[bcnt, bb], W=[bb])
            I_("dve", lambda e: e.tensor_tensor(out=b[:, 3:4], in0=b[:, 4:5], in1=Wk[s3][:, NBIS:NBIS + 1], op=ALU.subtract),
               R=[bb], W=[bb])

        def dsa_maskgen(i):
            s3 = i % NI
            L = 128 * (i + 1)
            b = bst[s3]
            for c0 in range(0, L, 512):
                cw = min(512, L - c0)
                I_("dve", lambda e, c0=c0, cw=cw: e.tensor_scalar(out=MBF[:, c0:c0 + cw], in0=isc[s3][:, c0:c0 + cw],
                                                                 scalar1=b[:, 3:4], scalar2=NEG, op0=ALU.is_lt, op1=ALU.mult),
                   R=[bisc[s3], bbst[s3]], W=[bMBF[c0 // 512]])

        def dsa_attn(i):
            s2 = i % 2
            s3 = i % NI
            units = [(kb, c) for kb in range(i + 1) for c in range(2)]
            slots = {}

            def SC(k):
                kb, c = units[k]
                bm = bMBF[kb // 4]
                u = cnt_unit[0]
                cnt_unit[0] += 1
                sb = u % 2
                pt = u % 3
                slots[k] = pt
                psS = sbank[sb]
                I_("pe", lambda e, c=c, kb=kb, psS=psS: e.matmul(
                    psS[:, 0:384], lhsT=kbT[64 * c:64 * c + 64, kb * 128:(kb + 1) * 128],
                    rhs=qbT[s3][64 * c:64 * c + 64, :], start=True, stop=False),
                   R=[bK, bqbT[s3]], W=[pb[5 + sb]])
                I_("pe", lambda e, kb=kb, psS=psS: e.matmul(psS[:, 0:384], lhsT=MBF[:, kb * 128:(kb + 1) * 128], rhs=i3,
                                                           start=False, stop=True),
                   R=[bm, b_const], W=[pb[5 + sb]])
                I_("act", lambda e, pt=pt, psS=psS: e.activation(out=PTb[pt], in_=psS[:, 0:384], func=AF.Exp, scale=0.125),
                   R=[pb[5 + sb]], W=[bPTb[pt]])

            def PV(k):
                kb, c = units[k]
                pt = slots.pop(k)
                for g in range(3):
                    h = 3 * c + g
                    I_("pe", lambda e, pt=pt, g=g, h=h, kb=kb, c=c: e.matmul(
                        accO[:, h * 65:(h + 1) * 65], lhsT=PTb[pt][:, g * 128:(g + 1) * 128], rhs=vba[:, kb, c, :],
                        start=(kb == 0 and h == 0), stop=(kb == i), skip_group_check=True),
                        R=[bPTb[pt], bK], W=[pb[7]])

            SC(0)
            for k in range(len(units)):
                if k + 1 < len(units):
                    SC(k + 1)
                PV(k)
                yield

        def dsa_final(i):
            s2 = i % 2
            av = accO[:, 0:390].rearrange("p (h e) -> p h e", e=65)
            I_("dve", lambda e: e.reciprocal(out=rden, in_=av[:, :, 64]), R=[pb[7]], W=[brden])
            I_("dve", lambda e: e.tensor_tensor(out=ob[s2].rearrange("p (h d) -> p h d", d=64), in0=av[:, :, 0:64],
                                                in1=rden.unsqueeze(2).to_broadcast([128, 6, 64]), op=ALU.mult),
               R=[pb[7], brden], W=[bob[s2]])
            pT = bank_bf(4)
            for cc in range(3):
                I_("pe", lambda e, cc=cc: e.transpose(out=pT[:, cc * 128:(cc + 1) * 128], in_=ob[s2][:, cc * 128:(cc + 1) * 128],
                                                      identity=ident), R=[bob[s2], b_const], W=[pb[4]])
            I_("act", lambda e: e.copy(out=obT[s2], in_=pT[:, 0:384]), R=[pb[4]], W=[bobT[s2]])
            P.dma("pool", OBT[:, :, i * 128:(i + 1) * 128].rearrange("c p t -> p c t"),
                  obT[s2].rearrange("p (c t) -> p c t", c=3), R=[bobT[s2]], grp="obT%d" % s2)

        def run_all(gen):
            for _ in gen:
                pass

        def interleave(ga, na, gb, nb):
            da = db = 0
            ea = eb = False
            while not (ea and eb):
                fa = da / max(na, 1)
                fb = db / max(nb, 1)
                if (not ea) and (eb or fa <= fb):
                    try:
                        next(ga)
                        da += 1
                    except StopIteration:
                        ea = True
                else:
                    try:
                        next(gb)
                        db += 1
                    except StopIteration:
                        eb = True

        def n_idx_steps(i):
            return 4 * ((128 * (i + 1) + 511) // 512) + 1

        run_all(dsa_index(0))
        if ntiles > 1:
            run_all(dsa_index(1))
        dsa_bisect(0)
        for i in range(ntiles):
            dsa_maskgen(i)
            if i + 1 < ntiles:
                dsa_bisect(i + 1)
            ga = dsa_attn(i)
            if i + 2 < ntiles:
                interleave(dsa_index(i + 2), n_idx_steps(i + 2), ga, 2 * (i + 1))
            else:
                run_all(ga)
            dsa_final(i)
        P.barrier()
        A.reset()

    if 4 in phases:
        wa = A.get([1, D], BF16)
        wb_ = A.get([3, D], BF16)
        wc = A.get([2, D], BF16)
        wo = A.get([8, D], BF16)
        wkv = A.get([8, 512], BF16)
        stage4 = make_stage(1024, "p4")
        bwa = load_weight_bf16(wa, wa_d, 1, D, None, "wa", chunk=1024, stage=stage4)
        bwb = load_weight_bf16(wb_, wb_d, 3, D, None, "wb", chunk=1024, stage=stage4)
        bwc = load_weight_bf16(wc, wc_d, 2, D, None, "wc", chunk=1024, stage=stage4)
        bwo = load_weight_bf16(wo, wo_d, 8, D, None, "wo", chunk=1024, stage=stage4)
        bwkv = load_weight_bf16(wkv, wkv_d, 8, 512, gmem_d, "wkv", chunk=1024, stage=stage4)
        P4STOP = 9
        xt4 = [A.get([D], F32) for _ in range(2)]
        bxt4 = [Buf("xt4%d" % i) for i in range(2)]
        xb4 = A.get([D], BF16)
        bxb4 = Buf("xb4")
        sq4 = A.get([D], F32)
        st4 = A.get([8], F32)
        bxs4 = dict(sq=sq4, st=st4, b=Buf("xstat4"))
        memT = A.get([8, 128], BF16)
        bmemT = Buf("memT")
        kmT = A.get([2, 256], BF16)
        vmb = A.get([2, 256], BF16)
        bkm = Buf("kmT")
        bvm = Buf("vmb")
        ksb = A.get([256], F32)
        ksq = A.get([256], F32)
        kss = A.get([8], F32)
        kgr = A.get([256], F32)
        kmb = A.get([256], BF16)
        bks = Buf("ksb")
        ones_bf = A.get([128], BF16)
        I_("pool", lambda e: e.memset(ones_bf, 1.0), W=[bks])
        for hh in range(4):
            I_("pool", lambda e, hh=hh: e.tensor_copy(out=kgr[:, hh * 64:(hh + 1) * 64], in_=gains[:, 5 * 64:6 * 64]),
               R=[b_const], W=[bks])
        for mt in range(2 if P4STOP >= 1 else 0):
            norm_rows_to_hT(mem_d[mt * 128:(mt + 1) * 128, :], xt4[0], xb4, None, bxt4[0], bxb4, bxs4, "xt40", 5, pb[5])
            I_("act", lambda e: e.copy(out=memT, in_=bank_bf(5).rearrange("p (k c) -> p k c", k=8)), R=[pb[5]], W=[bmemT])
            P4SUB = 9
            if P4SUB < 1:
                continue
            for k in range(8):
                I_("pe", lambda e, k=k: e.matmul(bank(0), lhsT=memT[:, k, :], rhs=wkv[:, k, :], start=(k == 0), stop=(k == 7)),
                   R=[bmemT, bwkv], W=[pb[0]])
            P4V = 3
            if P4V & 1:
                I_("act", lambda e, mt=mt: e.copy(out=vmb[:, mt, :], in_=bank(0)[:, 256:512]), R=[pb[0]], W=[bvm])
            if P4V & 2:
                I_("dve", lambda e: e.tensor_copy(out=ksb, in_=bank(0)[:, 0:256]), R=[pb[0]], W=[bks])
            if P4SUB < 2:
                continue
            I_("pool", lambda e: e.tensor_tensor(out=ksq, in0=ksb, in1=ksb, op=ALU.mult), R=[bks], W=[bks])
            I_("dve", lambda e: e.tensor_reduce(out=kss[:, 0:4], in_=ksq.rearrange("p (h d) -> p h d", d=64), axis=AX.X,
                                                op=ALU.add), R=[bks], W=[bks])
            rstd_from_ss(kss[:, 0:4], kss[:, 4:8], 64, [bks], 4)
            I_("dve", lambda e: e.tensor_tensor(out=ksq.rearrange("p (h d) -> p h d", d=64),
                                                in0=kgr.rearrange("p (h d) -> p h d", d=64),
                                                in1=kss[:, 4:8].unsqueeze(2).to_broadcast([128, 4, 64]), op=ALU.mult),
               R=[bks], W=[bks])
            I_("dve", lambda e: e.tensor_tensor(out=kmb, in0=ksb, in1=ksq, op=ALU.mult), R=[bks], W=[bks])
            if P4SUB < 3:
                continue
            for j in range(2):
                I_("pe", lambda e, j=j: e.transpose(out=bank_bf(1)[:, j * 128:(j + 1) * 128], in_=kmb[:, j * 128:(j + 1) * 128],
                                                    identity=ident), R=[bks, b_const], W=[pb[1]])
            I_("dve", lambda e, mt=mt: e.tensor_copy(out=kmT[:, :, mt * 128:(mt + 1) * 128],
                                                     in_=bank_bf(1)[:, 0:256].rearrange("p (j m) -> p j m", j=2)),
               R=[pb[1]], W=[bkm])
        gtg = [A.get([24, 512], BF16) for _ in range(2)]
        bgtg = [Buf("gtg%d" % i) for i in range(2)]
        obg = [A.get([3, 512], BF16) for _ in range(2)]
        bobg = [Buf("obg%d" % i) for i in range(2)]
        qcg = [A.get([2, 512], BF16) for _ in range(2)]
        bqcg = [Buf("qcg%d" % i) for i in range(2)]
        oaT = [A.get([512], BF16) for _ in range(2)]
        boaT = [Buf("oaT%d" % i) for i in range(2)]
        ocT = [A.get([2, 512], BF16) for _ in range(2)]
        bocT = [Buf("ocT%d" % i) for i in range(2)]
        mT = [A.get([8, 512], BF16) for _ in range(2)]
        bmT = [Buf("mT%d" % i) for i in range(2)]
        oa3 = [A.get([3, 130], F32) for _ in range(2)]
        boa3 = [Buf("oa3%d" % i) for i in range(2)]
        oasum = A.get([130], F32)
        oard = A.get([2], F32)
        oab = A.get([128], BF16)
        boas4 = Buf("oasum")
        PTc = [A.get([512], BF16) for _ in range(4)]
        bPTc = [Buf("PTc%d" % i) for i in range(4)]
        rdc = [A.get([512], F32) for _ in range(2)]
        brdc = [Buf("rdc%d" % i) for i in range(2)]
        tm = [A.get([512], F32) for _ in range(3)]
        btm = [Buf("tm%d" % i) for i in range(3)]
        x1t = [A.get([D], F32) for _ in range(2)]
        bx1t = [Buf("x1t%d" % i) for i in range(2)]
        ngrp = ntiles // 4
        cpt = [0]
        oasum2 = [A.get([130], F32) for _ in range(2)]
        oard2 = [A.get([2], F32) for _ in range(2)]
        oab2 = [A.get([128], BF16) for _ in range(2)]
        boas2 = [Buf("oasum%d" % i) for i in range(2)]

        def p4_prep(G):
            g2 = G % 2
            t0 = G * 512
            for c4 in range(0, 24, 4):
                P.dma("sp", gtg[g2][:, c4:c4 + 4, :], GT[c4:c4 + 4, :, t0:t0 + 512].rearrange("c p t -> p c t"),
                      W=[bgtg[g2]], grp="gtg%d" % g2)
            P.dma("sp", obg[g2], OBT[:, :, t0:t0 + 512].rearrange("c p t -> p c t"), W=[bobg[g2]], grp="obg%d" % g2)
            for j in range(2):
                P.dma("sp", qcg[g2][:, j, :].rearrange("p (n t) -> p n t", n=4), QCT[:, 4 * G:4 * G + 4, j * 128:(j + 1) * 128],
                      W=[bqcg[g2]], grp="qcg%d" % g2)
            for tt in range(4):
                n = 4 * G + tt
                o2 = n % 2
                oasum, oard, oab, boas4 = oasum2[o2], oard2[o2], oab2[o2], boas2[o2]
                P.dma("sp", oa3[o2], OA[:, n * 128:(n + 1) * 128, :].rearrange("g t e -> t g e"), W=[boa3[o2]],
                      grp="oa3%d" % o2)
                I_("pool", lambda e, o2=o2, oasum=oasum: e.tensor_tensor(out=oasum, in0=oa3[o2][:, 0, :], in1=oa3[o2][:, 1, :],
                                                                        op=ALU.add), R=[boa3[o2]], W=[boas4])
                I_("pool", lambda e, o2=o2, oasum=oasum: e.tensor_tensor(out=oasum, in0=oasum, in1=oa3[o2][:, 2, :], op=ALU.add),
                   R=[boa3[o2], boas4], W=[boas4])
                osv = oasum.rearrange("p (j e) -> p j e", e=65)
                I_("dve", lambda e, osv=osv, oard=oard: e.reciprocal(out=oard, in_=osv[:, :, 64]), R=[boas4], W=[boas4])
                I_("dve", lambda e, osv=osv, oard=oard, oab=oab: e.tensor_tensor(
                    out=oab.rearrange("p (j d) -> p j d", d=64), in0=osv[:, :, 0:64],
                    in1=oard.unsqueeze(2).to_broadcast([128, 2, 64]), op=ALU.mult), R=[boas4], W=[boas4])
                I_("pe", lambda e, tt=tt, oab=oab: e.transpose(out=bank_bf(4)[:, tt * 128:(tt + 1) * 128], in_=oab, identity=ident),
                   R=[boas4, b_const], W=[pb[4]])
                yield
            I_("act", lambda e: e.copy(out=oaT[g2], in_=bank_bf(4)[:, 0:512]), R=[pb[4]], W=[boaT[g2]])
            yield
            for h in range(4):
                j, hh = h // 2, h % 2
                pts = []
                for mt in range(2):
                    sbk = 6 + mt
                    pt = cpt[0] % 4
                    cpt[0] += 1
                    pts.append(pt)
                    I_("pe", lambda e, sbk=sbk, mt=mt, hh=hh, j=j: e.matmul(
                        bank(sbk), lhsT=kmT[64 * hh:64 * hh + 64, j, mt * 128:(mt + 1) * 128],
                        rhs=qcg[g2][64 * hh:64 * hh + 64, j, :], start=True, stop=True),
                       R=[bkm, bqcg[g2]], W=[pb[sbk]])
                    I_("act", lambda e, sbk=sbk, pt=pt: e.activation(out=PTc[pt], in_=bank(sbk), func=AF.Exp, scale=0.125),
                       R=[pb[sbk]], W=[bPTc[pt]])
                for mt in range(2):
                    I_("pe", lambda e, mt=mt, pt=pts[mt], j=j: e.matmul(
                        bank(4), lhsT=vmb[:, mt, j * 128:(j + 1) * 128], rhs=PTc[pt], start=(mt == 0), stop=(mt == 1)),
                       R=[bvm, bPTc[pts[mt]]], W=[pb[4]])
                for mt in range(2):
                    I_("pe", lambda e, mt=mt, pt=pts[mt]: e.matmul(
                        bank(5), lhsT=ones_bf, rhs=PTc[pt], start=(mt == 0), stop=(mt == 1)),
                       R=[bks, bPTc[pts[mt]]], W=[pb[5]])
                lo_, hi_ = 64 * hh, 64 * hh + 64
                hs2 = h % 2
                I_("dve", lambda e, hs2=hs2, lo_=lo_, hi_=hi_: e.reciprocal(out=rdc[hs2][lo_:hi_, :], in_=bank(5)[lo_:hi_, :]),
                   R=[pb[5]], W=[brdc[hs2]])
                I_("dve", lambda e, hs2=hs2, lo_=lo_, hi_=hi_, j=j: e.tensor_tensor(
                    out=ocT[g2][lo_:hi_, j, :], in0=bank(4)[lo_:hi_, :], in1=rdc[hs2][lo_:hi_, :], op=ALU.mult),
                   R=[pb[4], brdc[hs2]], W=[bocT[g2]])
                yield

        def p4_ef(G):
            g2 = G % 2
            for oc in range(8):
                cs_ = slice(oc * 128, (oc + 1) * 128)
                I_("pe", lambda e, cs_=cs_: e.matmul(bank(0), lhsT=wa[:, 0, cs_], rhs=oaT[g2], start=True, stop=True),
                   R=[bwa, boaT[g2]], W=[pb[0]])
                for k in range(3):
                    I_("pe", lambda e, cs_=cs_, k=k: e.matmul(bank(1), lhsT=wb_[:, k, cs_], rhs=obg[g2][:, k, :],
                                                            start=(k == 0), stop=(k == 2)),
                       R=[bwb, bobg[g2]], W=[pb[1]])
                for k in range(2):
                    I_("pe", lambda e, cs_=cs_, k=k: e.matmul(bank(2), lhsT=wc[:, k, cs_], rhs=ocT[g2][:, k, :],
                                                            start=(k == 0), stop=(k == 1)),
                       R=[bwc, bocT[g2]], W=[pb[2]])
                I_("dve", lambda e, oc=oc: e.tensor_tensor(out=tm[0], in0=bank(0), in1=gtg[g2][:, oc, :], op=ALU.mult),
                   R=[pb[0], bgtg[g2]], W=[btm[0]])
                I_("dve", lambda e, oc=oc: e.tensor_tensor(out=tm[1], in0=bank(1), in1=gtg[g2][:, 8 + oc, :], op=ALU.mult),
                   R=[pb[1], bgtg[g2]], W=[btm[1]])
                I_("dve", lambda e, oc=oc: e.tensor_tensor(out=tm[2], in0=bank(2), in1=gtg[g2][:, 16 + oc, :], op=ALU.mult),
                   R=[pb[2], bgtg[g2]], W=[btm[2]])
                I_("pool", lambda e: e.tensor_tensor(out=tm[0], in0=tm[0], in1=tm[1], op=ALU.add), R=[btm[0], btm[1]], W=[btm[0]])
                I_("pool", lambda e, oc=oc: e.tensor_tensor(out=mT[g2][:, oc, :], in0=tm[0], in1=tm[2], op=ALU.add),
                   R=[btm[0], btm[2]], W=[bmT[g2]])
                yield
            for tt in range(4):
                n = 4 * G + tt
                o2 = n % 2
                P.dma("sp", xt4[o2], x_d[n * 128:(n + 1) * 128, :], W=[bxt4[o2]], grp="xt4%d" % o2)
                for hf in range(2):
                    for k in range(8):
                        I_("pe", lambda e, hf=hf, k=k, tt=tt: e.matmul(
                            bank(3), lhsT=mT[g2][:, k, tt * 128:(tt + 1) * 128], rhs=wo[:, k, hf * 512:(hf + 1) * 512],
                            start=(k == 0), stop=(k == 7)), R=[bmT[g2], bwo], W=[pb[3]])
                    I_("dve", lambda e, o2=o2, hf=hf: e.tensor_tensor(out=x1t[o2][:, hf * 512:(hf + 1) * 512], in0=bank(3),
                                                                     in1=xt4[o2][:, hf * 512:(hf + 1) * 512], op=ALU.add),
                       R=[pb[3], bxt4[o2]], W=[bx1t[o2]])
                P.dma("pool", X1[n * 128:(n + 1) * 128, :], x1t[o2], R=[bx1t[o2]], grp="x1t%d" % o2)
                yield

        if ngrp > 0:
            for _ in p4_prep(0):
                pass
        for G in range(ngrp):
            streams = [(p4_ef(G), 12)]
            if G + 1 < ngrp:
                streams.append((p4_prep(G + 1), 9))
            interleave_streams(streams)
        P.barrier()
        A.reset()

    if 5 in phases:
        W1 = A.get([8, 4096], BF16)
        W2 = A.get([32, D], BF16)
        stage5 = make_stage(2048, "p5")
        bW1 = load_weight_bf16(W1, w1_d, 8, 4096, gmlp_d, "w1", chunk=2048, stage=stage5)
        bW2 = load_weight_bf16(W2, w2_d, 32, D, None, "w2", chunk=1024, stage=stage5)
        xt5 = [A.get([D], F32) for _ in range(4)]
        bxt5 = [Buf("xt5%d" % i) for i in range(4)]
        xb5 = [A.get([D], BF16) for _ in range(2)]
        bxb5 = [Buf("xb5%d" % i) for i in range(2)]
        sq5 = A.get([D], F32)
        st5 = A.get([8], F32)
        bxs5 = dict(sq=sq5, st=st5, b=Buf("xstat5"))
        h2T = [A.get([8, 256], BF16) for _ in range(2)]
        bh2T = [Buf("h2T%d" % i) for i in range(2)]
        rr = [A.get([256], BF16) for _ in range(3)]
        brr = [Buf("rr%d" % i) for i in range(3)]
        aa = [A.get([256], BF16) for _ in range(3)]
        baa = [Buf("aa%d" % i) for i in range(3)]
        ot = [A.get([D], F32) for _ in range(2)]
        bot = [Buf("ot%d" % i) for i in range(2)]
        ng5 = ntiles // 2
        cc5 = [0]
        for G in range(ng5):
            g2 = G % 2
            for tt in range(2):
                n = 2 * G + tt
                s4 = n % 4
                s2 = n % 2
                norm_rows_to_hT(X1[n * 128:(n + 1) * 128, :], xt5[s4], xb5[s2], None, bxt5[s4], bxb5[s2], bxs5,
                                "xt5%d" % s4, 7, pb[7])
                I_("act", lambda e, g2=g2, tt=tt: e.copy(out=h2T[g2][:, :, tt * 128:(tt + 1) * 128],
                                                         in_=bank_bf(7).rearrange("p (k c) -> p k c", k=8)),
                   R=[pb[7]], W=[bh2T[g2]])
            slots5 = {}

            def U5(c):
                ub = 4 + cc5[0] % 3
                r3 = cc5[0] % 3
                cc5[0] += 1
                slots5[c] = r3
                for k in range(8):
                    I_("pe", lambda e, ub=ub, k=k, c=c, g2=g2: e.matmul(
                        bank(ub)[:, 0:256], lhsT=W1[:, k, c * 128:(c + 1) * 128], rhs=h2T[g2][:, k, :],
                        start=(k == 0), stop=(k == 7)), R=[bW1, bh2T[g2]], W=[pb[ub]])
                I_("act", lambda e, ub=ub, r3=r3: e.activation(out=rr[r3], in_=bank(ub)[:, 0:256], func=AF.Relu),
                   R=[pb[ub]], W=[brr[r3]])
                I_("dve", lambda e, r3=r3: e.tensor_tensor(out=aa[r3], in0=rr[r3], in1=rr[r3], op=ALU.mult),
                   R=[brr[r3]], W=[baa[r3]])

            def O5(c):
                r3 = slots5.pop(c)
                for tt in range(2):
                    for hf in range(2):
                        I_("pe", lambda e, tt=tt, hf=hf, r3=r3, c=c: e.matmul(
                            bank(tt * 2 + hf), lhsT=aa[r3][:, tt * 128:(tt + 1) * 128], rhs=W2[:, c, hf * 512:(hf + 1) * 512],
                            start=(c == 0), stop=(c == 31)), R=[baa[r3], bW2], W=[pb[tt * 2 + hf]])

            U5(0)
            for c in range(32):
                if c + 1 < 32:
                    U5(c + 1)
                O5(c)
            for tt in range(2):
                n = 2 * G + tt
                s4 = n % 4
                s2 = n % 2
                I_("dve", lambda e, tt=tt, s2=s2, s4=s4: e.tensor_tensor(
                    out=ot[s2], in0=psum[:, 2 * tt:2 * tt + 2, :].rearrange("p a b -> p (a b)"), in1=xt5[s4], op=ALU.add),
                   R=[pb[2 * tt], pb[2 * tt + 1], bxt5[s4]], W=[bot[s2]])
                P.dma("pool", out_d[n * 128:(n + 1) * 128, :], ot[s2], R=[bot[s2]], grp="ot%d" % s2)
        P.barrier()
        A.reset()

    P.barrier()
    P.emit()
    return nc, P


def _host_layout(inputs, b):
    def kmaj(w):
        k = w.shape[0] // 128
        return np.ascontiguousarray(w.reshape(k, 128, w.shape[1]).transpose(1, 0, 2))
    perm = _win_perm()
    m = {
        "x": np.ascontiguousarray(inputs["x"][b]),
        "mem": np.ascontiguousarray(inputs["mem"][b]),
        "pos": np.ascontiguousarray(inputs["positions"][b].reshape(NT, 128).T),
        "w_in": kmaj(inputs["w_in"][0][:, perm]),
        "g_mix": np.ascontiguousarray(inputs["g_mix"][0].reshape(8, 128).T),
        "g_mem": np.ascontiguousarray(inputs["g_mem"][0].reshape(8, 128).T),
        "g_mlp": np.ascontiguousarray(inputs["g_mlp"][0].reshape(8, 128).T),
        "gains": np.concatenate([inputs[k][0] for k in ("g_qa", "g_ka", "g_qb", "g_kb", "g_qc", "g_kc")])[None, :],
        "w_mem_kv": kmaj(inputs["w_mem_kv"][0]),
        "w_a": kmaj(inputs["w_a"][0]),
        "w_b": kmaj(inputs["w_b"][0]),
        "w_c": kmaj(inputs["w_c"][0]),
        "w_o": kmaj(inputs["w_o"][0]),
        "w_1": kmaj(inputs["w_1"][0]),
        "w_2": kmaj(inputs["w_2"][0]),
    }
    return {k: np.ascontiguousarray(v) for k, v in m.items()}


def kernel(**inputs):
    inputs = {k: np.asarray(v) for k, v in inputs.items()}
    nc, _ = build_program()
    in_maps = [_host_layout(inputs, b) for b in range(8)]
    res = run_bass_kernel_spmd(nc, in_maps, core_ids=list(range(8)))
    return np.stack([np.asarray(r["out"]) for r in res.results], axis=0).astype(np.float32)
```
